# Optimizing a Trainium2 kernel written in Bass

```python
import math
import jax
import jax.numpy as jnp
from jax import lax
import numpy as np

D_MODEL = 2048
BATCH = 8
SEQ = 4096
DEPTH = 2
DEC_BATCH = 16
DEC_SEQ = 16
PAST_LEN = 4096

CHUNK = 64
N_META = 16
Q_BLOCK = 128
N_EVEN = (DEPTH + 1) // 2
N_ODD = DEPTH // 2
H_A = 16
DH_A = 128
D_A = H_A * DH_A
H_B = 32
P_B = 64
D_SSM = H_B * P_B
G_B = 4
HG_B = H_B // G_B
N_B = 128
CONV_W = 4
CONV_DIM = D_SSM + 2 * G_B * N_B
H_C = 8
DK_C = 128
DV_C = 256
D_QK_C = H_C * DK_C
D_V_C = H_C * DV_C
D_FF = -(-8 * D_MODEL // (3 * 256)) * 256
E_A = 3 * D_A + H_A + D_SSM + CONV_DIM + H_B
E_C = 2 * D_QK_C + 2 * D_V_C + 2 * H_C
ALPHA = (2 * DEPTH) ** 0.25
BETA = (8 * DEPTH) ** -0.25
NEG = -1e30

kernel_name = 'fox_ssd_mlstm_streaming_step'


def _split(p, sizes):
    idx = [int(s) for s in np.cumsum(sizes)[:-1]]
    return jnp.split(p, idx, axis=-1)


def _front_pad(a, pad, value=0.0):
    widths = [(0, 0), (pad, 0)] + [(0, 0)] * (a.ndim - 2)
    return jnp.pad(a, widths, constant_values=value)


def layer_norm(x, g, b, eps=1e-5):
    xf = x.astype(jnp.float32)
    xc = xf - jnp.mean(xf, -1, keepdims=True)
    var = jnp.mean(xc * xc, -1, keepdims=True)
    return (xc * lax.rsqrt(var + eps) * g + b).astype(x.dtype)


def group_rms_norm(y, g, groups, eps=1e-5):
    shp = y.shape
    yg = y.astype(jnp.float32).reshape(shp[:-1] + (groups, shp[-1] // groups))
    yg = yg * lax.rsqrt(jnp.mean(yg * yg, -1, keepdims=True) + eps)
    return yg.reshape(shp) * g


def swiglu(x, wg, wu, wd):
    h = jax.nn.silu(jnp.einsum('btd,df->btf', x, wg)) * jnp.einsum('btd,df->btf', x, wu)
    return jnp.einsum('btf,fd->btd', h, wd)


def fox_attention(q, k_all, v_all, logf_all):
    bsz, t = q.shape[:2]
    s_len = k_all.shape[1]
    past = s_len - t
    f_cum = jnp.cumsum(logf_all.astype(jnp.float32), axis=1)
    f_k = jnp.transpose(f_cum, (0, 2, 1))
    f_q = f_cum[:, past:]
    k_pos = jnp.arange(s_len)
    scale = DH_A ** -0.5

    def attend(q_blk, fq_blk, qpos_blk):
        s = jnp.einsum('bqhd,bkhd->bhqk', q_blk, k_all, preferred_element_type=jnp.float32) * scale
        s = s + jnp.transpose(fq_blk, (0, 2, 1))[..., None] - f_k[:, :, None, :]
        s = jnp.where(k_pos[None, :] <= qpos_blk[:, None], s, NEG)
        p = jax.nn.softmax(s, axis=-1)
        return jnp.einsum('bhqk,bkhd->bqhd', p.astype(v_all.dtype), v_all)

    if t <= Q_BLOCK:
        return attend(q, f_q, past + jnp.arange(t))
    nb = -(-t // Q_BLOCK)
    pad = nb * Q_BLOCK - t
    q_p = jnp.pad(q, ((0, 0), (0, pad), (0, 0), (0, 0)))
    fq_p = jnp.pad(f_q, ((0, 0), (0, pad), (0, 0)))
    q_b = jnp.moveaxis(q_p.reshape(bsz, nb, Q_BLOCK, H_A, DH_A), 1, 0)
    fq_b = jnp.moveaxis(fq_p.reshape(bsz, nb, Q_BLOCK, H_A), 1, 0)
    pos_b = (past + jnp.arange(nb * Q_BLOCK)).reshape(nb, Q_BLOCK)
    out = lax.map(lambda a: attend(a[0], a[1], a[2]), (q_b, fq_b, pos_b))
    out = jnp.moveaxis(out, 0, 1).reshape(bsz, nb * Q_BLOCK, H_A, DH_A)
    return out[:, :t]


def causal_depthwise_conv(xp, w, b, t):
    out = b
    for i in range(CONV_W):
        out = out + w[i] * xp[:, i:i + t]
    return out


def ssd_chunked(x, dt, a, bm, cm, h0):
    bsz, t = x.shape[:2]
    nc = -(-t // CHUNK)
    pad = nc * CHUNK - t
    f32 = jnp.float32
    x = _front_pad(x.astype(f32), pad).reshape(bsz, nc, CHUNK, G_B, HG_B, P_B)
    dt = _front_pad(dt.astype(f32), pad).reshape(bsz, nc, CHUNK, G_B, HG_B)
    bm = _front_pad(bm.astype(f32), pad).reshape(bsz, nc, CHUNK, G_B, N_B)
    cm = _front_pad(cm.astype(f32), pad).reshape(bsz, nc, CHUNK, G_B, N_B)
    dt_t = jnp.moveaxis(dt, 2, -1)
    a_cum = jnp.cumsum(dt_t * a[:, :, None], axis=-1)
    causal = jnp.tril(jnp.ones((CHUNK, CHUNK), bool))
    decay = jnp.exp(jnp.where(causal, a_cum[..., :, None] - a_cum[..., None, :], NEG))
    cb = jnp.einsum('bclgn,bcsgn->bcgls', cm, bm)
    w = cb[:, :, :, None] * decay * dt_t[..., None, :]
    y_diag = jnp.einsum('bcghls,bcsghp->bclghp', w, x)
    to_end = jnp.exp(a_cum[..., -1:] - a_cum) * dt_t
    states = jnp.einsum('bclgn,bcghl,bclghp->bcghpn', bm, to_end, x)
    chunk_decay = jnp.exp(a_cum[..., -1])

    def step(h, inp):
        dec, st = inp
        return dec[..., None, None] * h + st, h

    h_final, h_in = lax.scan(step, h0.astype(f32), (jnp.moveaxis(chunk_decay, 1, 0), jnp.moveaxis(states, 1, 0)))
    h_in = jnp.moveaxis(h_in, 0, 1)
    y_off = jnp.einsum('bclgn,bcghpn,bcghl->bclghp', cm, h_in, jnp.exp(a_cum))
    y = (y_diag + y_off).reshape(bsz, nc * CHUNK, G_B, HG_B, P_B)[:, pad:]
    return y, h_final


def mlstm_chunked(q, k, v, logi, logf, c0, n0, m0):
    bsz, t = q.shape[:2]
    nc = -(-t // CHUNK)
    pad = nc * CHUNK - t

    def chunks(a, value=0.0):
        a = _front_pad(a, pad, value)
        a = a.reshape((bsz, nc, CHUNK) + a.shape[2:])
        return jnp.moveaxis(jnp.moveaxis(a, 1, 0), 2, 3)

    qc, kc, vc = chunks(q), chunks(k), chunks(v)
    li, lf = chunks(logi, NEG), chunks(logf)
    causal = jnp.tril(jnp.ones((CHUNK, CHUNK), bool))

    def step(carry, inp):
        c_st, n_st, m_st = carry
        qb, kb, vb, lib, lfb = inp
        b_cum = jnp.cumsum(lfb, -1)
        d_mat = jnp.where(causal, b_cum[..., :, None] - b_cum[..., None, :] + lib[..., None, :], NEG)
        inter = b_cum + m_st[..., None]
        m_row = jnp.maximum(inter, jnp.max(d_mat, -1))
        w = jnp.exp(d_mat - m_row[..., None]) * jnp.einsum('bhld,bhsd->bhls', qb, kb)
        g_inter = jnp.exp(inter - m_row)
        num = jnp.einsum('bhls,bhsv->bhlv', w, vb) + g_inter[..., None] * jnp.einsum('bhvd,bhld->bhlv', c_st, qb)
        den = jnp.sum(w, -1) + g_inter * jnp.einsum('bhd,bhld->bhl', n_st, qb)
        h = num / jnp.maximum(jnp.abs(den), jnp.exp(-m_row))[..., None]
        b_tot = b_cum[..., -1]
        d_end = b_tot[..., None] - b_cum + lib
        m_new = jnp.maximum(b_tot + m_st, jnp.max(d_end, -1))
        w_end = jnp.exp(d_end - m_new[..., None])
        g_old = jnp.exp(b_tot + m_st - m_new)
        c_new = g_old[..., None, None] * c_st + jnp.einsum('bhs,bhsv,bhsd->bhvd', w_end, vb, kb)
        n_new = g_old[..., None] * n_st + jnp.einsum('bhs,bhsd->bhd', w_end, kb)
        return (c_new, n_new, m_new), h

    (c_f, n_f, m_f), h = lax.scan(step, (c0, n0, m0), (qc, kc, vc, li, lf))
    h = jnp.transpose(h, (1, 0, 3, 2, 4)).reshape(bsz, nc * CHUNK, H_C, DV_C)[:, pad:]
    return h, c_f, n_f, m_f


def even_mixer(x, k_past, v_past, logf_past, conv_past, h0, w_in, b_f, conv_w, conv_b, dt_bias, a_log, d_skip, norm_g, w_out):
    bsz, t, _ = x.shape
    f32 = jnp.float32
    proj = jnp.einsum('btd,de->bte', x, w_in)
    q, k, v, fg, z, xbc, dt_raw = _split(proj, [D_A, D_A, D_A, H_A, D_SSM, CONV_DIM, H_B])
    q = q.reshape(bsz, t, H_A, DH_A)
    k = k.reshape(bsz, t, H_A, DH_A)
    v = v.reshape(bsz, t, H_A, DH_A)
    logf = jax.nn.log_sigmoid((fg + b_f).astype(f32))
    k_all = jnp.concatenate([k_past.astype(k.dtype), k], axis=1)
    v_all = jnp.concatenate([v_past.astype(v.dtype), v], axis=1)
    logf_all = jnp.concatenate([logf_past.astype(f32), logf], axis=1)
    attn = fox_attention(q, k_all, v_all, logf_all).reshape(bsz, t, D_A)
    xbc_ext = jnp.concatenate([conv_past.astype(xbc.dtype), xbc], axis=1)
    xbc_c = jax.nn.silu(causal_depthwise_conv(xbc_ext, conv_w, conv_b, t))
    xs, bm, cm = _split(xbc_c, [D_SSM, G_B * N_B, G_B * N_B])
    dt = jax.nn.softplus((dt_raw + dt_bias).astype(f32))
    a = -jnp.exp(a_log.astype(f32))
    xs_h = xs.reshape(bsz, t, G_B, HG_B, P_B)
    y, h_new = ssd_chunked(xs_h, dt.reshape(bsz, t, G_B, HG_B), a.reshape(G_B, HG_B),
                           bm.reshape(bsz, t, G_B, N_B), cm.reshape(bsz, t, G_B, N_B),
                           h0.reshape(bsz, G_B, HG_B, P_B, N_B))
    y = y + d_skip.reshape(G_B, HG_B)[:, :, None].astype(f32) * xs_h.astype(f32)
    y = y.reshape(bsz, t, D_SSM) * jax.nn.silu(z.astype(f32))
    y = group_rms_norm(y, norm_g, G_B).astype(x.dtype)
    out = jnp.einsum('bte,ed->btd', jnp.concatenate([attn, y], axis=-1), w_out)
    return out, k, v, logf, xbc_ext[:, -(CONV_W - 1):], h_new.reshape(bsz, H_B, P_B, N_B)


def odd_mixer(x, c0, n0, m0, w_in, b_i, b_f, norm_g, w_out):
    bsz, t, _ = x.shape
    f32 = jnp.float32
    proj = jnp.einsum('btd,de->bte', x, w_in)
    q, k, v, o, ig, fg = _split(proj, [D_QK_C, D_QK_C, D_V_C, D_V_C, H_C, H_C])
    q = q.reshape(bsz, t, H_C, DK_C).astype(f32)
    k = k.reshape(bsz, t, H_C, DK_C).astype(f32) * (DK_C ** -0.5)
    v = v.reshape(bsz, t, H_C, DV_C).astype(f32)
    logi = (ig + b_i).astype(f32)
    logf = jax.nn.log_sigmoid((fg + b_f).astype(f32))
    h, c_new, n_new, m_new = mlstm_chunked(q, k, v, logi, logf, c0.astype(f32), n0.astype(f32), m0.astype(f32))
    h = group_rms_norm(h.reshape(bsz, t, D_V_C), norm_g, H_C) * jax.nn.sigmoid(o.astype(f32))
    out = jnp.einsum('bte,ed->btd', h.astype(x.dtype), w_out)
    return out, c_new, n_new, m_new


def post_norm_layer(x, mix, g1, b1, wg, wu, wd, g2, b2):
    x = layer_norm(ALPHA * x + mix, g1, b1)
    return layer_norm(ALPHA * x + swiglu(x, wg, wu, wd), g2, b2)


def setup_inputs(seed: int = 0) -> dict:
    key = jax.random.key(seed)
    ks = jax.random.split(key, 32)
    f32 = jnp.float32

    def nrm(k, shape, scale):
        return jax.random.normal(k, shape, f32) * scale

    dt_init = jnp.exp(jax.random.uniform(ks[15], (N_EVEN, H_B), f32, math.log(1e-3), math.log(1e-1)))
    return {
        'x_prompt': nrm(ks[0], (BATCH, SEQ, D_MODEL), 1.0),
        'x_sample': nrm(ks[1], (DEC_BATCH, DEC_SEQ, D_MODEL), 1.0),
        'cache_fox_k': nrm(ks[2], (N_EVEN, DEC_BATCH, PAST_LEN, H_A, DH_A), 1.0),
        'cache_fox_v': nrm(ks[3], (N_EVEN, DEC_BATCH, PAST_LEN, H_A, DH_A), 1.0),
        'cache_fox_logf': jax.nn.log_sigmoid(3.0 + nrm(ks[4], (N_EVEN, DEC_BATCH, PAST_LEN, H_A), 1.0)),
        'state_ssd_conv': nrm(ks[5], (N_EVEN, DEC_BATCH, CONV_W - 1, CONV_DIM), 1.0),
        'state_ssd': nrm(ks[6], (N_EVEN, DEC_BATCH, H_B, P_B, N_B), 0.5),
        'state_mlstm_c': nrm(ks[7], (N_ODD, DEC_BATCH, H_C, DV_C, DK_C), 0.5),
        'state_mlstm_n': nrm(ks[8], (N_ODD, DEC_BATCH, H_C, DK_C), 0.5),
        'state_mlstm_m': nrm(ks[9], (N_ODD, DEC_BATCH, H_C), 1.0),
        'meta_tokens': nrm(ks[10], (N_META, D_MODEL), 1.0),
        'w_in_a': nrm(ks[11], (N_EVEN, D_MODEL, E_A), D_MODEL ** -0.5),
        'b_fgate_a': 3.0 + nrm(ks[12], (N_EVEN, H_A), 0.1),
        'conv_w': nrm(ks[13], (N_EVEN, CONV_W, CONV_DIM), CONV_W ** -0.5),
        'conv_b': nrm(ks[14], (N_EVEN, CONV_DIM), 0.02),
        'dt_bias': dt_init + jnp.log(-jnp.expm1(-dt_init)),
        'a_log': jnp.log(jax.random.uniform(ks[16], (N_EVEN, H_B), f32, 1.0, 16.0)),
        'd_skip': 1.0 + nrm(ks[17], (N_EVEN, H_B), 0.1),
        'ssd_norm_g': 1.0 + nrm(ks[18], (N_EVEN, D_SSM), 0.02),
        'w_out_a': nrm(ks[19], (N_EVEN, D_A + D_SSM, D_MODEL), BETA * (D_A + D_SSM) ** -0.5),
        'w_in_c': nrm(ks[20], (N_ODD, D_MODEL, E_C), D_MODEL ** -0.5),
        'b_igate_c': nrm(ks[21], (N_ODD, H_C), 0.1),
        'b_fgate_c': jnp.linspace(3.0, 6.0, H_C, dtype=f32)[None] + nrm(ks[22], (N_ODD, H_C), 0.1),
        'mlstm_norm_g': 1.0 + nrm(ks[23], (N_ODD, D_V_C), 0.02),
        'w_out_c': nrm(ks[24], (N_ODD, D_V_C, D_MODEL), BETA * D_V_C ** -0.5),
        'ln_mix_g': 1.0 + nrm(ks[25], (DEPTH, D_MODEL), 0.02),
        'ln_mix_b': nrm(ks[26], (DEPTH, D_MODEL), 0.02),
        'ln_ffn_g': 1.0 + nrm(ks[27], (DEPTH, D_MODEL), 0.02),
        'ln_ffn_b': nrm(ks[28], (DEPTH, D_MODEL), 0.02),
        'w_ffn_gate': nrm(ks[29], (DEPTH, D_MODEL, D_FF), D_MODEL ** -0.5),
        'w_ffn_up': nrm(ks[30], (DEPTH, D_MODEL, D_FF), D_MODEL ** -0.5),
        'w_ffn_down': nrm(ks[31], (DEPTH, D_FF, D_MODEL), BETA * D_FF ** -0.5),
    }


def reference(x_prompt, x_sample, cache_fox_k, cache_fox_v, cache_fox_logf, state_ssd_conv, state_ssd,
              state_mlstm_c, state_mlstm_n, state_mlstm_m, meta_tokens, w_in_a, b_fgate_a, conv_w, conv_b,
              dt_bias, a_log, d_skip, ssd_norm_g, w_out_a, w_in_c, b_igate_c, b_fgate_c, mlstm_norm_g, w_out_c,
              ln_mix_g, ln_mix_b, ln_ffn_g, ln_ffn_b, w_ffn_gate, w_ffn_up, w_ffn_down):
    f32 = jnp.float32
    bsz = x_prompt.shape[0]
    dtype = x_prompt.dtype
    meta = jnp.broadcast_to(meta_tokens.astype(dtype)[None], (bsz, N_META, D_MODEL))
    xp = jnp.concatenate([meta, x_prompt], axis=1)
    xs = x_sample
    zk = jnp.zeros((bsz, 0, H_A, DH_A), dtype)
    zf = jnp.zeros((bsz, 0, H_A), f32)
    zconv = jnp.zeros((bsz, CONV_W - 1, CONV_DIM), dtype)
    zh = jnp.zeros((bsz, H_B, P_B, N_B), f32)
    zc = jnp.zeros((bsz, H_C, DV_C, DK_C), f32)
    zn = jnp.zeros((bsz, H_C, DK_C), f32)
    zm = jnp.zeros((bsz, H_C), f32)
    kp_l, vp_l, fp_l, cvp_l, hp_l, cp_l, np_l, mp_l = [], [], [], [], [], [], [], []
    ks_l, vs_l, fs_l, cvs_l, hs_l, cs_l, ns_l, ms_l = [], [], [], [], [], [], [], []
    for layer in range(DEPTH):
        if layer % 2 == 0:
            e = layer // 2
            wts = (w_in_a[e], b_fgate_a[e], conv_w[e], conv_b[e], dt_bias[e], a_log[e], d_skip[e], ssd_norm_g[e], w_out_a[e])
            mix_p, kp, vp, fp, cvp, hp = even_mixer(xp, zk, zk, zf, zconv, zh, *wts)
            mix_s, kss, vss, fss, cvs, hss = even_mixer(xs, cache_fox_k[e], cache_fox_v[e], cache_fox_logf[e],
                                                        state_ssd_conv[e], state_ssd[e], *wts)
            kp_l.append(kp); vp_l.append(vp); fp_l.append(fp); cvp_l.append(cvp); hp_l.append(hp)
            ks_l.append(kss); vs_l.append(vss); fs_l.append(fss); cvs_l.append(cvs); hs_l.append(hss)
        else:
            o = layer // 2
            wts = (w_in_c[o], b_igate_c[o], b_fgate_c[o], mlstm_norm_g[o], w_out_c[o])
            mix_p, cp, np_, mp = odd_mixer(xp, zc, zn, zm, *wts)
            mix_s, css, nss, mss = odd_mixer(xs, state_mlstm_c[o], state_mlstm_n[o], state_mlstm_m[o], *wts)
            cp_l.append(cp); np_l.append(np_); mp_l.append(mp)
            cs_l.append(css); ns_l.append(nss); ms_l.append(mss)
        xp = post_norm_layer(xp, mix_p, ln_mix_g[layer], ln_mix_b[layer], w_ffn_gate[layer], w_ffn_up[layer],
                             w_ffn_down[layer], ln_ffn_g[layer], ln_ffn_b[layer])
        xs = post_norm_layer(xs, mix_s, ln_mix_g[layer], ln_mix_b[layer], w_ffn_gate[layer], w_ffn_up[layer],
                             w_ffn_down[layer], ln_ffn_g[layer], ln_ffn_b[layer])
    y_prompt = xp[:, N_META:]
    y_sample = xs
    return (y_prompt, y_sample,
            jnp.stack(kp_l), jnp.stack(vp_l), jnp.stack(fp_l), jnp.stack(cvp_l), jnp.stack(hp_l),
            jnp.stack(cp_l), jnp.stack(np_l), jnp.stack(mp_l),
            jnp.stack(ks_l), jnp.stack(vs_l), jnp.stack(fs_l), jnp.stack(cvs_l), jnp.stack(hs_l),
            jnp.stack(cs_l), jnp.stack(ns_l), jnp.stack(ms_l))
```

```python
import math
import numpy as np
from contextlib import ExitStack
import concourse.bass as bass
import concourse.mybir as mybir
from concourse.bass_utils import run_bass_kernel_spmd

F32 = mybir.dt.float32
BF16 = mybir.dt.bfloat16
AF = mybir.ActivationFunctionType
ALU = mybir.AluOpType
AX = mybir.AxisListType

D = 2048
N_META = 16
DEC = 16
H_A, DH_A = 16, 128
H_B, P_B, G_B, N_B = 32, 64, 4, 128
D_SSM = 2048
CONV_DIM = 3072
H_C, DK_C, DV_C = 8, 128, 256
D_FF = 5632
E_A = 11312
E_C = 6160
DEPTH = 2
ALPHA = (2 * DEPTH) ** 0.25
NEG = -1e30

ENGS = ("pe", "act", "dve", "pool", "sp")


class Buf:
    __slots__ = ("name", "t", "last_w", "readers", "is_psum")

    def __init__(self, name, t, is_psum=False):
        self.name = name
        self.t = t
        self.last_w = None
        self.readers = []
        self.is_psum = is_psum

    def __getitem__(self, idx):
        return self.t[idx]


class Op:
    __slots__ = ("eng", "fn", "deps", "is_dma", "key", "sig", "tick", "dcount")

    def __init__(self, eng, fn, is_dma=False, key=None):
        self.eng = eng
        self.fn = fn
        self.deps = []
        self.is_dma = is_dma
        self.key = key
        self.sig = False
        self.tick = 0
        self.dcount = 0


class Prog:
    def __init__(self, nc, stack):
        self.nc = nc
        self.stack = stack
        self.esem = {e: stack.enter_context(nc.semaphore("sem_" + e)) for e in ENGS}
        self.ecnt = {e: 0 for e in ENGS}
        self.ksem = {}
        self.kcnt = {}
        self.ops = {e: [] for e in ENGS}
        self.known = {e: {} for e in ENGS}
        self.bufs = []
        self.nops = 0
        self.uid = 0
        self.dmaq = 0
        self.free_dsems = {"sw": [], "hw": []}
        self.kkind = {}
        self.all_dsems = []
        self.keep_names = {"ident", "tri", "negtri", "negtri4", "tri", "ones", "sel0", "sellast"}

    def sb(self, stack, name, shape, dt):
        self.uid += 1
        nm = f"{name}_{self.uid}"
        t = stack.enter_context(self.nc.sbuf_tensor(nm, list(shape), dt))
        b = Buf(nm, t)
        self.bufs.append(b)
        return b

    def ps(self, stack, name, shape, dt=F32):
        self.uid += 1
        nm = f"{name}_{self.uid}"
        t = stack.enter_context(self.nc.psum_tensor(nm, list(shape), dt))
        b = Buf(nm, t, True)
        self.bufs.append(b)
        return b

    def _deps(self, op, reads, writes):
        for b in reads:
            if b.last_w is not None:
                op.deps.append(b.last_w)
            if b.is_psum:
                for r in b.readers:
                    if r.eng != op.eng:
                        op.deps.append(r)
        for b in writes:
            lw = b.last_w
            if lw is not None:
                if op.is_dma and lw.is_dma and lw.key is op.key:
                    op.deps.extend(lw.deps)
                else:
                    op.deps.append(lw)
            for r in b.readers:
                op.deps.append(r)
        for b in reads:
            b.readers.append(op)
        for b in writes:
            b.last_w = op
            b.readers = []

    def op(self, eng, fn, reads=(), writes=()):
        o = Op(eng, fn)
        self._deps(o, reads, writes)
        self.ops[eng].append(o)
        self.nops += 1
        return o

    def dma(self, out, in_, key, reads=(), writes=(), q=None):
        if q is None:
            q = "sp"
        kind = "sw" if q == "pool" else "hw"
        if key in self.kkind:
            assert self.kkind[key] == kind, key.name
        if key not in self.ksem:
            self.kkind[key] = kind
            if self.free_dsems[kind]:
                self.ksem[key], self.kcnt[key] = self.free_dsems[kind].pop()
            else:
                self.ksem[key] = self.stack.enter_context(self.nc.semaphore(f"dk{len(self.all_dsems)}"))
                self.kcnt[key] = 0
                self.all_dsems.append(self.ksem[key])
        o = Op(q, lambda e, out=out, in_=in_: e.dma_start(out=out, in_=in_), True, key)
        self.kcnt[key] += 16
        o.dcount = self.kcnt[key]
        self._deps(o, reads, writes)
        self.ops[q].append(o)
        self.nops += 1
        return o

    def load(self, buf, dst, src, q=None):
        return self.dma(dst, src, buf, writes=[buf], q=q)

    def store(self, buf, dst, src, q=None):
        return self.dma(dst, src, buf, reads=[buf], q=q)

    def end_phase(self):
        nc = self.nc
        for e in ENGS:
            for o in self.ops[e]:
                for d in o.deps:
                    if not d.is_dma and (d.eng != o.eng or o.eng != "pe"):
                        d.sig = True
        for e in ENGS:
            c = self.ecnt[e]
            for o in self.ops[e]:
                if o.sig:
                    c += 1
                    o.tick = c
            self.ecnt[e] = c
        ops, esem, ksem, known, kcnt = self.ops, self.esem, self.ksem, self.known, self.kcnt

        def replay(eng_obj, name):
            kn = known[name]
            for o in ops[name]:
                need = {}
                for d in o.deps:
                    if d.is_dma:
                        s, v = ksem[d.key], d.dcount
                    else:
                        if d.eng == name and name == "pe":
                            continue
                        s, v = esem[d.eng], d.tick
                    if kn.get(id(s), 0) >= v:
                        continue
                    if need.get(id(s), (None, 0))[1] < v:
                        need[id(s)] = (s, v)
                for sid, (s, v) in need.items():
                    eng_obj.wait_ge(s, v)
                    kn[sid] = v
                ins = o.fn(eng_obj)
                if o.is_dma:
                    ins.then_inc(ksem[o.key], 16)
                elif o.sig:
                    ins.then_inc(esem[name], 1)
            if name == "sp":
                for k, c in kcnt.items():
                    s = ksem[k]
                    if kn.get(id(s), 0) < c:
                        eng_obj.wait_ge(s, c)
                        kn[id(s)] = c

        with nc.Block() as block:
            @block.tensor
            def _(e):
                replay(e, "pe")

            @block.scalar
            def _(e):
                replay(e, "act")

            @block.vector
            def _(e):
                replay(e, "dve")

            @block.gpsimd
            def _(e):
                replay(e, "pool")

            @block.sync
            def _(e):
                replay(e, "sp")

        for e in ENGS:
            self.ops[e] = []
            for e2 in ENGS:
                self.known[e][id(self.esem[e2])] = self.ecnt[e2]
            for k, c in self.kcnt.items():
                self.known[e][id(self.ksem[k])] = c
        for k, c in self.kcnt.items():
            self.free_dsems[self.kkind[k]].append((self.ksem[k], c))
        self.ksem = {}
        self.kcnt = {}
        self.kkind = {}
        for b in self.bufs:
            b.last_w = None
            b.readers = []
        self.bufs = [b for b in self.bufs if b.is_psum or b.name.split("_")[0] in self.keep_names]


class Ring:
    def __init__(self, items):
        self.items = items
        self.i = 0

    def next(self):
        b = self.items[self.i % len(self.items)]
        self.i += 1
        return b


def _tiles(n, step):
    return [(i, min(step, n - i)) for i in range(0, n, step)]


class Ctx:
    pass


def build_xT(P, C, pst, xT, tok0, n, off, loader, xin_ring, psT_ring):
    xb = xin_ring.next()
    loader(xb, tok0, n)
    for c4 in range(4):
        ps = psT_ring.next()
        for j in range(4):
            c = c4 * 4 + j
            P.op("pe", lambda e, ps=ps, xb=xb, c=c, j=j, n=n: e.transpose(
                ps[0:128, j * 128:j * 128 + 128], xb[0:128, c * 128:(c + 1) * 128], C.ident[:, :]),
                reads=[xb, C.ident], writes=[ps])
        P.op("dve", lambda e, ps=ps, c4=c4, n=n, off=off: e.tensor_copy(
            xT[:, c4 * 4:(c4 + 1) * 4, off:off + n],
            ps[:].rearrange("p (j m) -> p j m", j=4)[:, :, 0:n]), reads=[ps], writes=[xT])


def gemm_segments(P, C, xT, Kc, ntok, W, segs, wring, ps_ring, wcols):
    Wv = W.rearrange("(c p) n -> p c n", p=128)
    for (c0, c1, orient, evac) in segs:
        for (cc, ncw) in _tiles(c1 - c0, wcols):
            wb = wring.next()
            P.load(wb, wb[:, 0:Kc, 0:ncw], Wv[:, :, c0 + cc:c0 + cc + ncw], q="pool")
            if orient == "tok":
                for (t0, n) in _tiles(ntok, 128):
                    ps = ps_ring.next()
                    for c in range(Kc):
                        P.op("pe", lambda e, ps=ps, c=c, t0=t0, n=n, wb=wb, ncw=ncw: e.matmul(
                            ps[0:128, 0:ncw], xT[:, c, t0:t0 + 128], wb[:, c, 0:ncw],
                            start=(c == 0), stop=(c == Kc - 1)), reads=[xT, wb], writes=[ps])
                    evac(ps, t0, n, c0 + cc, ncw)
            else:
                for (s0, ns) in _tiles(ncw, 128):
                    for (t0, n) in _tiles(ntok, 512):
                        ps = ps_ring.next()
                        for c in range(Kc):
                            P.op("pe", lambda e, ps=ps, c=c, t0=t0, n=n, wb=wb, s0=s0, ns=ns: e.matmul(
                                ps[0:ns, 0:n], wb[:, c, s0:s0 + ns], xT[:, c, t0:t0 + n],
                                start=(c == 0), stop=(c == Kc - 1)), reads=[xT, wb], writes=[ps])
                        evac(ps, t0, n, c0 + cc + s0, ns)


def supertiles(ntok, tt):
    n128 = (ntok + 127) // 128
    per = tt // 128
    nsup = (n128 + per - 1) // per
    base, extra = divmod(n128, nsup)
    out = []
    a = 0
    for i in range(nsup):
        k = base + (1 if i < extra else 0)
        b = min(ntok, a + k * 128)
        out.append((a, b - a))
        a = b
    return out


def layer_norm_tile(P, C, st_bufs, xa, xb_, n, g_b, b_b, out):
    stats, mv, rstd = st_bufs
    P.op("dve", lambda e: e.scalar_tensor_tensor(out=xa[0:n, :], in0=xa[0:n, :], scalar=float(ALPHA), in1=xb_[0:n, :],
                                                 op0=ALU.mult, op1=ALU.add), reads=[xa, xb_], writes=[xa])
    for k in range(4):
        P.op("dve", lambda e, k=k: e.bn_stats(stats[0:n, k * 6:(k + 1) * 6], xa[0:n, k * 512:(k + 1) * 512]),
             reads=[xa], writes=[stats])
    P.op("dve", lambda e: e.bn_aggr(mv[0:n, :], stats[0:n, :]), reads=[stats], writes=[mv])
    P.op("dve", lambda e: e.tensor_scalar(rstd[0:n, :], mv[0:n, 1:2], 1e-5, None, ALU.add), reads=[mv], writes=[rstd])
    P.op("act", lambda e: e.activation(rstd[0:n, :], rstd[0:n, :], AF.Sqrt), reads=[rstd], writes=[rstd])
    P.op("dve", lambda e: e.reciprocal(rstd[0:n, :], rstd[0:n, :]), reads=[rstd], writes=[rstd])
    P.op("dve", lambda e: e.tensor_scalar(xa[0:n, :], xa[0:n, :], mv[0:n, 0:1], rstd[0:n, 0:1], ALU.subtract, ALU.mult),
         reads=[xa, mv, rstd], writes=[xa])
    P.op("pool", lambda e: e.tensor_tensor(xa[0:n, :], xa[0:n, :], g_b[0:n, :], ALU.mult), reads=[xa, g_b], writes=[xa])
    P.op("pool", lambda e: e.tensor_tensor(out[0:n, :], xa[0:n, :], b_b[0:n, :], ALU.add), reads=[xa, b_b], writes=[out])


def build_program(SEQ=4096, PAST=4096, NS=2, stop_after=None, debug=False):
    nc = bass.Bass("TRN2", target_bir_lowering=False)
    C = Ctx()
    NP = N_META + SEQ
    NTOK = NP + NS * DEC
    C.NP, C.NTOK, C.SEQ, C.PAST, C.NS = NP, NTOK, SEQ, PAST, NS

    early = {"x_prompt", "x_sample", "meta_tokens", "cache_fox_k", "cache_fox_v", "cache_fox_logf", "w_in_a",
             "b_fgate_a", "c_ident", "c_tri", "c_negtri"}
    if stop_after == "l0":
        early |= {"w_out_a", "ln_mix_g", "ln_mix_b", "ln_ffn_g", "ln_ffn_b", "w_ffn_gate", "w_ffn_up", "w_ffn_down"}
    if stop_after in ("ssd", "l0"):
        early |= {"state_ssd_conv", "state_ssd", "conv_w", "conv_b", "dt_bias", "a_log", "d_skip", "ssd_norm_g"}
    C.early = early

    def din(name, shape, dt=F32):
        if stop_after in ("inproj_a", "attn", "cache", "ssd", "l0") and name not in early:
            return None
        return nc.dram_tensor(name, list(shape), dt, kind="ExternalInput").ap()

    def dout(name, shape, dt=F32):
        return nc.dram_tensor(name, list(shape), dt, kind="ExternalOutput").ap()

    def dscr(name, shape, dt=F32):
        if debug:
            return nc.dram_tensor(name, list(shape), dt, kind="ExternalOutput").ap()
        return nc.dram_tensor(name, list(shape), dt, kind="Internal").ap()

    I = Ctx()
    I.x_prompt = din("x_prompt", [SEQ, D])
    I.x_sample = din("x_sample", [NS * DEC, D])
    I.meta = din("meta_tokens", [N_META, D])
    I.cache_k = din("cache_fox_k", [NS, PAST, 2048])
    I.cache_v = din("cache_fox_v", [NS, PAST, 2048])
    I.cache_logf = din("cache_fox_logf", [NS, PAST, 16])
    I.st_conv = din("state_ssd_conv", [NS, 3, CONV_DIM])
    I.st_ssd = din("state_ssd", [NS, 2048, 128])
    I.st_c = din("state_mlstm_c", [NS, 2048, 128])
    I.st_n = din("state_mlstm_n", [NS, 8, 128])
    I.st_m = din("state_mlstm_m", [NS, 8])
    I.w_in_a = din("w_in_a", [D, E_A])
    I.b_fgate_a = din("b_fgate_a", [1, 16])
    I.conv_w = din("conv_w", [4, CONV_DIM])
    I.conv_b = din("conv_b", [1, CONV_DIM])
    I.dt_bias = din("dt_bias", [1, 32])
    I.a_log = din("a_log", [1, 32])
    I.d_skip = din("d_skip", [1, 32])
    I.ssd_norm_g = din("ssd_norm_g", [1, 2048])
    I.w_out_a = din("w_out_a", [4096, D])
    I.w_in_c = din("w_in_c", [D, E_C])
    I.b_igate_c = din("b_igate_c", [1, 8])
    I.b_fgate_c = din("b_fgate_c", [1, 8])
    I.mlstm_norm_g = din("mlstm_norm_g", [1, 2048])
    I.w_out_c = din("w_out_c", [2048, D])
    I.ln_mix_g = din("ln_mix_g", [2, D])
    I.ln_mix_b = din("ln_mix_b", [2, D])
    I.ln_ffn_g = din("ln_ffn_g", [2, D])
    I.ln_ffn_b = din("ln_ffn_b", [2, D])
    I.w_ffn_gate = din("w_ffn_gate", [2, D, D_FF])
    I.w_ffn_up = din("w_ffn_up", [2, D, D_FF])
    I.w_ffn_down = din("w_ffn_down", [2, D_FF, D])
    I.c_ident = din("c_ident", [128, 128])
    I.c_tri = din("c_tri", [128, 128])
    I.c_negtri = din("c_negtri", [128, 128])

    O = Ctx()
    O.y = dout("o_y", [NTOK, D])
    O.fox_k = dout("o_fox_k", [NTOK, 2048])
    O.fox_v = dout("o_fox_v", [NTOK, 2048])
    O.fox_logf = dout("o_fox_logf", [NTOK, 16])
    O.ssd_conv = dout("o_ssd_conv", [1 + NS, 3, CONV_DIM])
    O.ssd_state = dout("o_ssd_state", [1 + NS, 2048, 128])
    O.mlstm_c = dout("o_mlstm_c", [1 + NS, 2048, 128])
    O.mlstm_n = dout("o_mlstm_n", [1 + NS, 8, 128])
    O.mlstm_m = dout("o_mlstm_m", [1 + NS, 8])

    S = Ctx()
    S.qT = dscr("s_qT", [2048, NTOK], BF16)
    S.kT = dscr("s_kT", [2048, NTOK], BF16)
    S.vbf = dscr("s_vbf", [NTOK, 2048], BF16)
    S.kTp = dscr("s_kTp", [NS, 2048, PAST], BF16)
    S.vp = dscr("s_vp", [NS, PAST, 2048], BF16)
    S.logf = dscr("s_logf", [NTOK, 16])
    S.z = dscr("s_z", [NTOK, 2048])
    S.xbcT = dscr("s_xbcT", [CONV_DIM, NTOK])
    S.dtraw = dscr("s_dtraw", [NTOK, 32])
    S.xtok = dscr("s_xtok", [NTOK, 2048])
    S.bmtok = dscr("s_bmtok", [NTOK, 512], BF16)
    S.bcT = dscr("s_bcT", [1024, NTOK], BF16)
    S.mixT = dscr("s_mixT", [4096, NTOK], BF16)
    S.mix = dscr("s_mix", [NTOK, D])
    S.mix2 = dscr("s_mix2", [NTOK, D])
    S.x0 = dscr("s_x0", [NTOK, D])
    S.x1T = dscr("s_x1T", [D, NTOK], BF16)
    S.ktok = dscr("s_ktok", [NTOK, 1024], BF16)
    S.gates = dscr("s_gates", [NTOK, 16])
    S.x1 = dscr("s_x1", [NTOK, D])
    S.x2 = dscr("s_x2", [NTOK, D])

    NKT_ = (max(NP, PAST + DEC) + 127) // 128
    C.dbgF = dscr("s_dbgF", [128, NKT_ * 16]) if debug else None
    C.dbgL = dscr("s_dbgL", [128, NKT_ * 16]) if debug else None
    seqs = [(0, NP, 0, 0)] + [(NP + s * DEC, DEC, PAST, 1 + s) for s in range(NS)]

    with ExitStack() as gst:
        P = Prog(nc, gst)
        C.ident = P.sb(gst, "ident", [128, 128], F32)
        C.tri = P.sb(gst, "tri", [128, 128], F32)
        C.negtri = P.sb(gst, "negtri", [128, 128], F32)
        C.tri_bf = P.sb(gst, "tri_bf", [128, 128], BF16)
        C.ones_bf = P.sb(gst, "ones_bf", [128, 128], BF16)
        C.ones_f = P.sb(gst, "ones_f", [128, 128], F32)
        C.sel0 = P.sb(gst, "sel0", [128, 128], F32)
        P.load(C.ident, C.ident[:], I.c_ident)
        P.load(C.tri, C.tri[:], I.c_tri)
        P.load(C.negtri, C.negtri[:], I.c_negtri)
        P.op("dve", lambda e: e.tensor_copy(C.tri_bf[:], C.tri[:]), reads=[C.tri], writes=[C.tri_bf])
        P.op("dve", lambda e: e.memset(C.ones_bf[:], 1.0), writes=[C.ones_bf])
        P.op("dve", lambda e: e.memset(C.ones_f[:], 1.0), writes=[C.ones_f])
        P.op("dve", lambda e: e.tensor_copy(C.sel0[:], C.ident[:, 0:1].to_broadcast([128, 128])), reads=[C.ident],
             writes=[C.sel0])
        C.negtri4 = P.sb(gst, "negtri4", [128, 512], F32)
        P.op("dve", lambda e: e.tensor_copy(C.negtri4[:].rearrange("p (h l) -> p h l", h=4),
                                            C.negtri[:, :].unsqueeze(1).to_broadcast([128, 4, 128])),
             reads=[C.negtri], writes=[C.negtri4])
        C.sellast = P.sb(gst, "sellast", [128, 128], F32)
        P.op("dve", lambda e: e.tensor_copy(C.sellast[:], C.ident[:, 127:128].to_broadcast([128, 128])), reads=[C.ident],
             writes=[C.sellast])
        psum = [P.ps(gst, f"psb{i}", [128, 512], F32) for i in range(8)]
        C.psum = psum

        def load_x0(xb, tok0, n):
            r = tok0
            done = 0
            while done < n:
                rr = r + done
                if rr < N_META:
                    m = min(n - done, N_META - rr)
                    src = I.meta[rr:rr + m, :]
                elif rr < NP:
                    m = min(n - done, NP - rr)
                    src = I.x_prompt[rr - N_META:rr - N_META + m, :]
                else:
                    m = n - done
                    src = I.x_sample[rr - NP:rr - NP + m, :]
                P.load(xb, xb[done:done + m, :], src)
                done += m

        phase_inproj_a(P, C, I, O, S, load_x0)
        P.end_phase()
        if stop_after == "inproj_a":
            return nc
        phase_cache_prep(P, C, I, S)
        P.end_phase()
        if stop_after == "cache":
            return nc
        phase_attention(P, C, I, O, S, seqs)
        P.end_phase()
        if stop_after == "attn":
            return nc
        phase_ssd_conv(P, C, I, O, S, seqs)
        P.end_phase()
        phase_ssd(P, C, I, O, S, seqs)
        P.end_phase()
        if stop_after == "ssd":
            return nc
        phase_outproj(P, C, I.w_out_a, 32, S.mixT, S.mix)
        P.end_phase()
        phase_x0_copy(P, C, S, load_x0)
        P.end_phase()
        phase_ln_T(P, C, S.x0, S.mix, I.ln_mix_g[0:1, :], I.ln_mix_b[0:1, :], S.x1, S.x1T)
        P.end_phase()
        phase_ffn(P, C, S.x1T, I.w_ffn_gate[0], I.w_ffn_up[0], I.w_ffn_down[0], S.mix2)
        P.end_phase()
        if stop_after == "l0":
            phase_final_ln(P, C, S.x1, S.mix2, I.ln_ffn_g[0:1, :], I.ln_ffn_b[0:1, :], O.y)
            P.end_phase()
            return nc
        phase_inproj_c(P, C, I, S, lambda st: make_ln_loader(P, C, st, S.x1, S.mix2, I.ln_ffn_g[0:1, :], I.ln_ffn_b[0:1, :], S.x2))
        P.end_phase()
        phase_mlstm(P, C, I, O, S, seqs)
        P.end_phase()
        phase_outproj(P, C, I.w_out_c, 16, S.mixT[0:2048, :], S.mix)
        P.end_phase()
        phase_ln_T(P, C, S.x2, S.mix, I.ln_mix_g[1:2, :], I.ln_mix_b[1:2, :], S.x1, S.x1T)
        P.end_phase()
        phase_ffn(P, C, S.x1T, I.w_ffn_gate[1], I.w_ffn_up[1], I.w_ffn_down[1], S.mix2)
        P.end_phase()
        phase_final_ln(P, C, S.x1, S.mix2, I.ln_ffn_g[1:2, :], I.ln_ffn_b[1:2, :], O.y)
        P.end_phase()
    return nc


def phase_x0_copy(P, C, S, load_x0):
    with ExitStack() as st:
        r = Ring([P.sb(st, f"x0c{i}", [128, D], F32) for i in range(3)])
        for (t0, n) in _tiles(C.NTOK, 128):
            b = r.next()
            load_x0(b, t0, n)
            P.store(b, S.x0[t0:t0 + n, :], b[0:n, :])


def phase_inproj_a(P, C, I, O, S, load_x0):
    NTOK = C.NTOK
    scale = DH_A ** -0.5
    with ExitStack() as st:
        TT = 1024
        xT = P.sb(st, "xT", [128, 16, TT + 128], BF16)
        P.op("pool", lambda e: e.memset(xT[:], 0.0), writes=[xT])
        xin = Ring([P.sb(st, f"xin{i}", [128, D], F32) for i in range(2)])
        for b_ in xin.items:
            P.op("pool", lambda e, b_=b_: e.memset(b_[:], 0.0), writes=[b_])
        wring = Ring([P.sb(st, f"w{i}", [128, 16, 512], BF16) for i in range(2)])
        sf = Ring([P.sb(st, f"sf{i}", [128, 512], F32) for i in range(3)])
        sh = Ring([P.sb(st, f"sh{i}", [128, 512], BF16) for i in range(3)])
        sm = Ring([P.sb(st, f"sm{i}", [128, 16], F32) for i in range(3)])
        bfb = P.sb(st, "bfb", [128, 16], F32)
        P.load(bfb, bfb[:], I.b_fgate_a[0:1, :].to_broadcast([128, 16]))
        psT = Ring(C.psum[0:2])
        psG = Ring(C.psum[2:8])
        for (T0, NT) in supertiles(NTOK, TT):
            for (t0, n) in _tiles(NT, 128):
                build_xT(P, C, st, xT, T0 + t0, n, t0, load_x0, xin, psT)

            def ev_qk(dst, mul):
                def f(ps, t0, n, col0, ncol):
                    b = sh.next()
                    P.op("act", lambda e: e.activation(b[0:ncol, 0:n], ps[0:ncol, 0:n], AF.Copy, scale=float(mul)),
                         reads=[ps], writes=[b])
                    P.store(b, dst[col0:col0 + ncol, T0 + t0:T0 + t0 + n], b[0:ncol, 0:n])
                return f

            def ev_ktok(ps, t0, n, col0, ncol):
                b = sf.next()
                P.op("act", lambda e: e.copy(b[0:n, 0:ncol], ps[0:n, 0:ncol]), reads=[ps], writes=[b])
                P.store(b, O.fox_k[T0 + t0:T0 + t0 + n, col0 - 2048:col0 - 2048 + ncol], b[0:n, 0:ncol])

            def ev_v(ps, t0, n, col0, ncol):
                b = sf.next()
                P.op("act", lambda e: e.copy(b[0:n, 0:ncol], ps[0:n, 0:ncol]), reads=[ps], writes=[b])
                P.store(b, O.fox_v[T0 + t0:T0 + t0 + n, col0 - 4096:col0 - 4096 + ncol], b[0:n, 0:ncol])
                b2 = sh.next()
                P.op("dve", lambda e: e.tensor_copy(b2[0:n, 0:ncol], b[0:n, 0:ncol]), reads=[b], writes=[b2])
                P.store(b2, S.vbf[T0 + t0:T0 + t0 + n, col0 - 4096:col0 - 4096 + ncol], b2[0:n, 0:ncol])

            def ev_fg(ps, t0, n, col0, ncol):
                b = sm.next()
                P.op("dve", lambda e: e.tensor_tensor(b[0:n, :], ps[0:n, 0:16], bfb[0:n, :], ALU.add),
                     reads=[ps, bfb], writes=[b])
                P.op("act", lambda e: e.activation(b[0:n, :], b[0:n, :], AF.Exp, scale=-1.0), reads=[b], writes=[b])
                P.op("act", lambda e: e.activation(b[0:n, :], b[0:n, :], AF.Ln, bias=1.0), reads=[b], writes=[b])
                P.op("act", lambda e: e.mul(b[0:n, :], b[0:n, :], -1.0), reads=[b], writes=[b])
                P.store(b, O.fox_logf[T0 + t0:T0 + t0 + n, :], b[0:n, :])
                P.store(b, S.logf[T0 + t0:T0 + t0 + n, :], b[0:n, :])

            def ev_z(ps, t0, n, col0, ncol):
                b = sf.next()
                P.op("act", lambda e: e.copy(b[0:n, 0:ncol], ps[0:n, 0:ncol]), reads=[ps], writes=[b])
                P.store(b, S.z[T0 + t0:T0 + t0 + n, col0 - 6160:col0 - 6160 + ncol], b[0:n, 0:ncol])

            def ev_xbc(ps, t0, n, col0, ncol):
                b = sf.next()
                P.op("act", lambda e: e.copy(b[0:ncol, 0:n], ps[0:ncol, 0:n]), reads=[ps], writes=[b])
                P.store(b, S.xbcT[col0 - 8208:col0 - 8208 + ncol, T0 + t0:T0 + t0 + n], b[0:ncol, 0:n])

            def ev_dt(ps, t0, n, col0, ncol):
                b = sf.next()
                P.op("act", lambda e: e.copy(b[0:n, 0:32], ps[0:n, 0:32]), reads=[ps], writes=[b])
                P.store(b, S.dtraw[T0 + t0:T0 + t0 + n, :], b[0:n, 0:32])

            segs = [
                (0, 2048, "feat", ev_qk(S.qT, scale)),
                (2048, 4096, "feat", lambda ps, t0, n, col0, ncol: ev_qk(S.kT, 1.0)(ps, t0, n, col0 - 2048, ncol)),
                (2048, 4096, "tok", ev_ktok),
                (4096, 6144, "tok", ev_v),
                (6144, 6160, "tok", ev_fg),
                (6160, 8208, "tok", ev_z),
                (8208, 11280, "feat", ev_xbc),
                (11280, 11312, "tok", ev_dt),
            ]
            import os
            if os.environ.get("K_SEGS"):
                segs = [segs[int(i)] for i in os.environ["K_SEGS"].split(",")]
            gemm_segments(P, C, xT, 16, NT, I.w_in_a, segs, wring, psG, 512)


def phase_cache_prep(P, C, I, S):
    with ExitStack() as st:
        kin = Ring([P.sb(st, f"kin{i}", [128, 2048], F32) for i in range(2)])
        vin = Ring([P.sb(st, f"vin{i}", [128, 2048], F32) for i in range(2)])
        kts = Ring([P.sb(st, f"kts{i}", [128, 16, 128], BF16) for i in range(2)])
        vbs = Ring([P.sb(st, f"vbs{i}", [128, 2048], BF16) for i in range(2)])
        psT = Ring(C.psum[0:4])
        for s in range(C.NS):
            for (k0, n) in _tiles(C.PAST, 128):
                kb = kin.next()
                P.load(kb, kb[0:n, :], I.cache_k[s, k0:k0 + n, :])
                ko = kts.next()
                for c4 in range(4):
                    ps = psT.next()
                    for j in range(4):
                        c = c4 * 4 + j
                        P.op("pe", lambda e, ps=ps, kb=kb, c=c, j=j, n=n: e.transpose(
                            ps[0:128, j * 128:j * 128 + n], kb[0:n, c * 128:(c + 1) * 128], C.ident[0:n, 0:n]),
                            reads=[kb, C.ident], writes=[ps])
                    eng = "dve" if c4 % 2 == 0 else "act"
                    if eng == "dve":
                        P.op("dve", lambda e, ps=ps, c4=c4, ko=ko, n=n: e.tensor_copy(
                            ko[:, c4 * 4:(c4 + 1) * 4, 0:n], ps[:].rearrange("p (j m) -> p j m", j=4)[:, :, 0:n]),
                            reads=[ps], writes=[ko])
                    else:
                        P.op("act", lambda e, ps=ps, c4=c4, ko=ko, n=n: e.copy(
                            ko[:, c4 * 4:(c4 + 1) * 4, 0:n], ps[:].rearrange("p (j m) -> p j m", j=4)[:, :, 0:n]),
                            reads=[ps], writes=[ko])
                P.store(ko, S.kTp[s, :, k0:k0 + n].rearrange("(c p) m -> p c m", p=128), ko[:, :, 0:n])
                vb = vin.next()
                P.load(vb, vb[0:n, :], I.cache_v[s, k0:k0 + n, :], q="act")
                vo = vbs.next()
                P.op("pool", lambda e, vo=vo, vb=vb, n=n: e.tensor_copy(vo[0:n, :], vb[0:n, :]), reads=[vb], writes=[vo])
                P.store(vo, S.vp[s, k0:k0 + n, :], vo[0:n, :], q="act")


def phase_attention(P, C, I, O, S, seqs):
    with ExitStack() as st:
        Smax = max(past + T for (_, T, past, _) in seqs)
        NKT = (Smax + 127) // 128
        Tmax = max(T for (_, T, _, _) in seqs)
        F_all = P.sb(st, "F_all", [128, NKT, 16], F32)
        lfin = P.sb(st, "lfin", [128, NKT, 16], F32)
        kTh = Ring([P.sb(st, f"kTh{i}", [128, NKT * 128], BF16) for i in range(2)])
        vh = Ring([P.sb(st, f"vh{i}", [128, NKT, 128], BF16) for i in range(2)])
        qTh = Ring([P.sb(st, f"qTh{i}", [128, Tmax], BF16) for i in range(2)])
        pT = Ring([P.sb(st, f"pT{i}", [128, 512], BF16) for i in range(3)])
        bcol = Ring([P.sb(st, f"bcol{i}", [128, 1], F32) for i in range(4)])
        c_b = P.sb(st, "c_b", [128, 16], F32)
        rl = P.sb(st, "rl", [128, 512], F32)
        for b_ in kTh.items + vh.items + [F_all, lfin]:
            P.op("pool", lambda e, b_=b_: e.memset(b_[:], 0.0), writes=[b_])
        osb = Ring([P.sb(st, f"osb{i}", [128, 512], BF16) for i in range(2)])
        ps_s = Ring(C.psum[0:2])
        ps_o = Ring(C.psum[2:4])
        ps_l = Ring(C.psum[4:6])
        ps_m = Ring(C.psum[6:8])
        ones_part = P.sb(st, "ones_part", [128, 128], BF16)
        P.op("dve", lambda e: e.memset(ones_part[:], 0.0), writes=[ones_part])
        P.op("dve", lambda e: e.memset(ones_part[0:16, :], 1.0), writes=[ones_part])
        for (tok0, T, past, sidx) in seqs:
            Sk = past + T
            ktiles = _tiles(Sk, 128)
            assert all(nk in (128, 16) for (_, nk) in ktiles)
            npast_t = past // 128
            for j0 in range(0, npast_t, 16):
                j1 = min(npast_t, j0 + 16)
                P.load(lfin, lfin[:, j0:j1, :], I.cache_logf[sidx - 1, j0 * 128:j1 * 128, :].rearrange("(j p) h -> p j h", p=128))
            nfull = T // 128
            for j0 in range(0, nfull, 16):
                j1 = min(nfull, j0 + 16)
                P.load(lfin, lfin[:, npast_t + j0:npast_t + j1, :],
                       S.logf[tok0 + j0 * 128:tok0 + j1 * 128, :].rearrange("(j p) h -> p j h", p=128))
            rem = T - nfull * 128
            if rem:
                P.load(lfin, lfin[0:rem, npast_t + nfull, :], S.logf[tok0 + nfull * 128:tok0 + T, :])
            for j, (k0, nk) in enumerate(ktiles):
                ps = ps_m.next()
                P.op("pe", lambda e, ps=ps, j=j, nk=nk: e.matmul(ps[0:128, 0:16], C.tri[:, :], lfin[:, j, :],
                                                                  start=True, stop=(j == 0)),
                     reads=[C.tri, lfin], writes=[ps])
                if j > 0:
                    P.op("pe", lambda e, ps=ps, j=j, nk=nk: e.matmul(ps[0:128, 0:16], C.sellast[:, 0:128], F_all[:, j - 1, :],
                                                                      start=False, stop=True),
                         reads=[C.sellast, F_all], writes=[ps])
                P.op("dve", lambda e, ps=ps, j=j, nk=nk: e.tensor_copy(F_all[0:nk, j, :], ps[0:nk, 0:16]),
                     reads=[ps], writes=[F_all])
            if C.dbgF is not None and sidx == 0:
                P.store(F_all, C.dbgF, F_all[:].rearrange("p j h -> p (j h)"))
                P.store(lfin, C.dbgL, lfin[:].rearrange("p j h -> p (j h)"))
            for h in range(H_A):
                kb = kTh.next()
                vb = vh.next()
                qb = qTh.next()
                r0 = h * 128
                if past:
                    P.load(kb, kb[:, 0:past], S.kTp[sidx - 1, r0:r0 + 128, :])
                    for j0 in range(0, npast_t, 16):
                        j1 = min(npast_t, j0 + 16)
                        P.load(vb, vb[:, j0:j1, :], S.vp[sidx - 1, j0 * 128:j1 * 128, r0:r0 + 128].rearrange("(j p) d -> p j d", p=128),
                               q="act")
                P.load(kb, kb[:, past:past + T], S.kT[r0:r0 + 128, tok0:tok0 + T])
                for j0 in range(0, nfull, 16):
                    j1 = min(nfull, j0 + 16)
                    P.load(vb, vb[:, npast_t + j0:npast_t + j1, :],
                           S.vbf[tok0 + j0 * 128:tok0 + j1 * 128, r0:r0 + 128].rearrange("(j p) d -> p j d", p=128), q="act")
                if rem:
                    P.op("pool", lambda e, vb=vb, jj=npast_t + nfull: e.memset(vb[:, jj, :], 0.0), writes=[vb])
                    P.load(vb, vb[0:rem, npast_t + nfull, :], S.vbf[tok0 + nfull * 128:tok0 + T, r0:r0 + 128], q="act")
                P.load(qb, qb[:, 0:T], S.qT[r0:r0 + 128, tok0:tok0 + T])
                for (q0, nq) in _tiles(T, 512):
                    qa = past + q0
                    ja = qa // 128
                    assert qa % 128 == 0
                    if h == 0 or True:
                        psc = ps_m.next()
                        P.op("pe", lambda e, psc=psc, ja=ja: e.matmul(psc[:, 0:16], C.sel0[:, :], F_all[:, ja, :],
                                                                        start=True, stop=True),
                             reads=[C.sel0, F_all], writes=[psc])
                        P.op("dve", lambda e, psc=psc: e.tensor_copy(c_b[:], psc[:, 0:16]), reads=[psc], writes=[c_b])
                    po = ps_o.next()
                    pl = ps_l.next()
                    jl = [j for j, (k0, nk) in enumerate(ktiles) if k0 <= qa + nq - 1]
                    for j in jl:
                        k0, nk = ktiles[j]
                        fs = max(0, k0 - qa)
                        bc = bcol.next()
                        if nk < 128:
                            P.op("dve", lambda e, bc=bc: e.memset(bc[:], -30000.0), writes=[bc])
                        P.op("dve", lambda e, bc=bc, j=j, nk=nk, h=h: e.tensor_tensor(
                            bc[0:nk, :], c_b[0:nk, h:h + 1], F_all[0:nk, j, h:h + 1], ALU.subtract),
                            reads=[c_b, F_all], writes=[bc])
                        pss = ps_s.next()
                        P.op("pe", lambda e, pss=pss, kb=kb, qb=qb, k0=k0, nk=nk, fs=fs, q0=q0, nq=nq: e.matmul(
                            pss[0:128, fs:nq], kb[:, k0:k0 + 128], qb[:, q0 + fs:q0 + nq], start=True, stop=True),
                            reads=[kb, qb], writes=[pss])
                        pt = pT.next()
                        P.op("act", lambda e, pt=pt, pss=pss, bc=bc, nk=nk, fs=fs, nq=nq: e.activation(
                            pt[0:128, fs:nq], pss[0:128, fs:nq], AF.Exp, bias=bc[0:128, :]),
                            reads=[pss, bc], writes=[pt])
                        if k0 >= qa:
                            w = min(nk, nq - fs)
                            P.op("dve", lambda e, pt=pt, nk=nk, fs=fs, w=w: e.tensor_tensor(
                                pt[0:nk, fs:fs + w], pt[0:nk, fs:fs + w], C.tri_bf[0:nk, 0:w], ALU.mult),
                                reads=[pt, C.tri_bf], writes=[pt])
                        first = (j == jl[0])
                        last = (j == jl[-1])
                        P.op("pe", lambda e, po=po, vb=vb, pt=pt, j=j, nk=nk, fs=fs, nq=nq, first=first, last=last: e.matmul(
                            po[:, fs:nq], vb[0:128, j, :], pt[0:128, fs:nq], start=first, stop=last),
                            reads=[vb, pt], writes=[po])
                        P.op("pe", lambda e, pl=pl, pt=pt, nk=nk, fs=fs, nq=nq, first=first, last=last: e.matmul(
                            pl[:, fs:nq], (C.ones_bf if nk == 128 else ones_part)[0:128, :], pt[0:128, fs:nq],
                            start=first, stop=last),
                            reads=[C.ones_bf, ones_part, pt], writes=[pl])
                    P.op("dve", lambda e, pl=pl, nq=nq: e.reciprocal(rl[:, 0:nq], pl[:, 0:nq]), reads=[pl], writes=[rl])
                    ob = osb.next()
                    P.op("dve", lambda e, po=po, ob=ob, nq=nq: e.tensor_tensor(ob[:, 0:nq], po[:, 0:nq], rl[:, 0:nq], ALU.mult),
                         reads=[po, rl], writes=[ob])
                    P.store(ob, S.mixT[r0:r0 + 128, tok0 + q0:tok0 + q0 + nq], ob[:, 0:nq])


def _consts():
    idx = np.arange(128)
    ident = np.eye(128, dtype=np.float32)
    tri = (idx[:, None] <= idx[None, :]).astype(np.float32)
    negtri = np.where(idx[None, :] < idx[:, None], np.float32(NEG), np.float32(0.0)).astype(np.float32)
    return {"c_ident": ident, "c_tri": tri, "c_negtri": negtri}


def make_in_map(inp, core, NS=2):
    f = lambda a: np.ascontiguousarray(a, dtype=np.float32)
    s0, s1 = core * NS, (core + 1) * NS
    past = inp["cache_fox_k"].shape[2]
    m = {
        "x_prompt": f(inp["x_prompt"][core]),
        "x_sample": f(inp["x_sample"][s0:s1].reshape(NS * DEC, D)),
        "meta_tokens": f(inp["meta_tokens"]),
        "cache_fox_k": f(inp["cache_fox_k"][0, s0:s1].reshape(NS, past, 2048)),
        "cache_fox_v": f(inp["cache_fox_v"][0, s0:s1].reshape(NS, past, 2048)),
        "cache_fox_logf": f(inp["cache_fox_logf"][0, s0:s1]),
        "state_ssd_conv": f(inp["state_ssd_conv"][0, s0:s1]),
        "state_ssd": f(inp["state_ssd"][0, s0:s1].reshape(NS, 2048, 128)),
        "state_mlstm_c": f(inp["state_mlstm_c"][0, s0:s1].reshape(NS, 2048, 128)),
        "state_mlstm_n": f(inp["state_mlstm_n"][0, s0:s1]),
        "state_mlstm_m": f(inp["state_mlstm_m"][0, s0:s1]),
        "w_in_a": f(inp["w_in_a"][0]),
        "b_fgate_a": f(inp["b_fgate_a"]),
        "conv_w": f(inp["conv_w"][0]),
        "conv_b": f(inp["conv_b"]),
        "dt_bias": f(inp["dt_bias"]),
        "a_log": f(inp["a_log"]),
        "d_skip": f(inp["d_skip"]),
        "ssd_norm_g": f(inp["ssd_norm_g"]),
        "w_out_a": f(inp["w_out_a"][0]),
        "w_in_c": f(inp["w_in_c"][0]),
        "b_igate_c": f(inp["b_igate_c"]),
        "b_fgate_c": f(inp["b_fgate_c"]),
        "mlstm_norm_g": f(inp["mlstm_norm_g"]),
        "w_out_c": f(inp["w_out_c"][0]),
        "ln_mix_g": f(inp["ln_mix_g"]),
        "ln_mix_b": f(inp["ln_mix_b"]),
        "ln_ffn_g": f(inp["ln_ffn_g"]),
        "ln_ffn_b": f(inp["ln_ffn_b"]),
        "w_ffn_gate": f(inp["w_ffn_gate"]),
        "w_ffn_up": f(inp["w_ffn_up"]),
        "w_ffn_down": f(inp["w_ffn_down"]),
    }
    m.update(_consts())
    return m


def bview(ap, shape):
    return ap.to_broadcast(list(shape))


def phase_ssd_conv(P, C, I, O, S, seqs):
    with ExitStack() as st:
        Tmax = max(T for (_, T, _, _) in seqs)
        Tpad = ((Tmax + 127) // 128) * 128
        cwin = P.sb(st, "cwin", [128, CONV_DIM], F32)
        hin = P.sb(st, "hin", [128, CONV_DIM], F32)
        cw = P.sb(st, "cw", [128, 24, 8], F32)
        hT = P.sb(st, "hT", [128, 24, 4], F32)
        xc = Ring([P.sb(st, f"xc{i}", [128, Tpad + 4], F32) for i in range(2)])
        acc = Ring([P.sb(st, f"acc{i}", [128, Tpad], F32) for i in range(2)])
        accb = Ring([P.sb(st, f"accb{i}", [128, Tpad], BF16) for i in range(2)])
        cst = P.sb(st, "cst", [128, 128], F32)
        convout = P.sb(st, "convout", [128, CONV_DIM], F32)
        stf = Ring([P.sb(st, f"stf{i}", [128, 128], F32) for i in range(3)])
        sth = Ring([P.sb(st, f"sth{i}", [128, 128], BF16) for i in range(3)])
        psT = Ring(C.psum[0:4])
        for b_ in [cwin, hin, cst] + xc.items + acc.items:
            P.op("pool", lambda e, b_=b_: e.memset(b_[:], 0.0), writes=[b_])
        P.load(cwin, cwin[0:4, :], I.conv_w)
        P.load(cwin, cwin[4:5, :], I.conv_b)
        for cc in range(24):
            ps = psT.next()
            P.op("pe", lambda e, ps=ps, cc=cc: e.transpose(ps[:, 0:128], cwin[:, cc * 128:(cc + 1) * 128], C.ident[:, :]),
                 reads=[cwin, C.ident], writes=[ps])
            P.op("dve", lambda e, ps=ps, cc=cc: e.tensor_copy(cw[:, cc, 0:5], ps[:, 0:5]), reads=[ps], writes=[cw])
        for (tok0, T, past, sidx) in seqs:
            if past:
                P.load(hin, hin[0:3, :], I.st_conv[sidx - 1])
                for cc in range(24):
                    ps = psT.next()
                    P.op("pe", lambda e, ps=ps, cc=cc: e.transpose(ps[:, 0:128], hin[:, cc * 128:(cc + 1) * 128], C.ident[:, :]),
                         reads=[hin, C.ident], writes=[ps])
                    P.op("dve", lambda e, ps=ps, cc=cc: e.tensor_copy(hT[:, cc, 0:3], ps[:, 0:3]), reads=[ps], writes=[hT])
            else:
                P.op("dve", lambda e: e.memset(hT[:], 0.0), writes=[hT])
            for cc in range(24):
                xb = xc.next()
                P.load(xb, xb[:, 3:3 + T], S.xbcT[cc * 128:(cc + 1) * 128, tok0:tok0 + T])
                P.op("dve", lambda e, xb=xb, cc=cc: e.tensor_copy(xb[:, 0:3], hT[:, cc, 0:3]), reads=[hT], writes=[xb])
                P.op("dve", lambda e, xb=xb, T=T: e.tensor_copy(cst[:, 0:3], xb[:, T:T + 3]), reads=[xb], writes=[cst])
                ps = psT.next()
                P.op("pe", lambda e, ps=ps: e.transpose(ps[:, 0:128], cst[:, :], C.ident[:, :]), reads=[cst, C.ident], writes=[ps])
                P.op("dve", lambda e, ps=ps, cc=cc: e.tensor_copy(convout[0:3, cc * 128:(cc + 1) * 128], ps[0:3, 0:128]),
                     reads=[ps], writes=[convout])
                a = acc.next()
                P.op("act", lambda e, a=a, xb=xb, cc=cc, T=T: e.activation(a[:, 0:T], xb[:, 0:T], AF.Identity,
                                                                         bias=cw[:, cc, 4:5], scale=cw[:, cc, 0:1]),
                     reads=[xb, cw], writes=[a])
                for i in range(1, 4):
                    P.op("dve", lambda e, a=a, xb=xb, cc=cc, T=T, i=i: e.scalar_tensor_tensor(
                        out=a[:, 0:T], in0=xb[:, i:i + T], scalar=cw[:, cc, i:i + 1], in1=a[:, 0:T], op0=ALU.mult, op1=ALU.add),
                        reads=[a, xb, cw], writes=[a])
                P.op("act", lambda e, a=a, T=T: e.activation(a[:, 0:T], a[:, 0:T], AF.Silu), reads=[a], writes=[a])
                if cc < 20:
                    for (b0, n) in _tiles(T, 128):
                        ps = psT.next()
                        P.op("pe", lambda e, ps=ps, a=a, b0=b0: e.transpose(ps[:, 0:128], a[:, b0:b0 + 128], C.ident[:, :]),
                             reads=[a, C.ident], writes=[ps])
                        if cc < 16:
                            sb_ = stf.next()
                            P.op("act", lambda e, ps=ps, sb_=sb_: e.copy(sb_[:, :], ps[:, 0:128]), reads=[ps], writes=[sb_])
                            P.store(sb_, S.xtok[tok0 + b0:tok0 + b0 + n, cc * 128:(cc + 1) * 128], sb_[0:n, :])
                        else:
                            sb_ = sth.next()
                            P.op("act", lambda e, ps=ps, sb_=sb_: e.copy(sb_[:, :], ps[:, 0:128]), reads=[ps], writes=[sb_])
                            P.store(sb_, S.bmtok[tok0 + b0:tok0 + b0 + n, (cc - 16) * 128:(cc - 15) * 128], sb_[0:n, :])
                if cc >= 16:
                    ab = accb.next()
                    P.op("pool", lambda e, ab=ab, a=a, T=T: e.tensor_copy(ab[:, 0:T], a[:, 0:T]), reads=[a], writes=[ab])
                    P.store(ab, S.bcT[(cc - 16) * 128:(cc - 15) * 128, tok0:tok0 + T], ab[:, 0:T])
            P.store(convout, O.ssd_conv[sidx], convout[0:3, :])


def phase_ssd(P, C, I, O, S, seqs):
    with ExitStack() as st:
        sb = lambda name, shape, dt=F32: P.sb(st, name, shape, dt)
        dtb_b = sb("dtb_b", [128, 32]); a_b = sb("a_b", [128, 32]); D_b = sb("D_b", [128, 32])
        g_b = sb("g_b", [128, 2048])
        P.load(dtb_b, dtb_b[:], bview(I.dt_bias[0:1, :], [128, 32]))
        P.load(a_b, a_b[:], bview(I.a_log[0:1, :], [128, 32]))
        P.load(D_b, D_b[:], bview(I.d_skip[0:1, :], [128, 32]))
        P.load(g_b, g_b[:], bview(I.ssd_norm_g[0:1, :], [128, 2048]))
        P.op("act", lambda e: e.activation(a_b[:], a_b[:], AF.Exp), reads=[a_b], writes=[a_b])
        P.op("act", lambda e: e.mul(a_b[:], a_b[:], -1.0), reads=[a_b], writes=[a_b])
        valid16 = sb("valid16", [128, 1])
        P.op("dve", lambda e: e.memset(valid16[:], 0.0), writes=[valid16])
        P.op("dve", lambda e: e.memset(valid16[0:16, :], 1.0), writes=[valid16])
        S_sb = sb("S_sb", [128, 2048]); S_bf = sb("S_bf", [128, 2048], BF16)
        sio = sb("sio", [128, 16, 128])
        x_c = Ring([sb(f"x_c{i}", [128, 2048]) for i in range(2)])
        z_c = Ring([sb(f"z_c{i}", [128, 2048]) for i in range(2)])
        bm_c = Ring([sb(f"bm_c{i}", [128, 512], BF16) for i in range(2)])
        bcT_c = Ring([sb(f"bcT_c{i}", [128, 8, 128], BF16) for i in range(2)])
        dt_c = Ring([sb(f"dt_c{i}", [128, 32]) for i in range(2)])
        dt = sb("dt", [128, 32]); lndt = sb("lndt", [128, 32]); dta = sb("dta", [128, 32]); A_sb = sb("A_sb", [128, 32])
        bias_all = sb("bias_all", [128, 32]); wend = sb("wend", [128, 32]); cd_b = sb("cd_b", [128, 32]); expA = sb("expA", [128, 32])
        D2 = sb("D2", [128, 32, 128])
        x_bf = sb("x_bf", [128, 2048], BF16); xw = sb("xw", [128, 2048], BF16)
        dec = Ring([sb(f"dec{i}", [128, 128]) for i in range(3)])
        WT = Ring([sb(f"WT{i}", [128, 128], BF16) for i in range(3)])
        y_c = sb("y_c", [128, 2048]); tmpb = sb("tmpb", [128, 2048]); junk = sb("junk", [128, 512])
        ss = sb("ss", [128, 4]); rstd = sb("rstd", [128, 4])
        yT = Ring([sb(f"yT{i}", [128, 16, 128], BF16) for i in range(2)])
        psE = [C.psum[0], C.psum[1]]
        psCB, ps_y, ps_off, ps_st = C.psum[2], C.psum[3], C.psum[4], C.psum[5]
        psm = Ring(C.psum[6:8])
        for b_ in x_c.items + z_c.items + bm_c.items + bcT_c.items + dt_c.items + [sio]:
            P.op("pool", lambda e, b_=b_: e.memset(b_[:], 0.0), writes=[b_])

        def v3(buf, lo, n):
            return buf[:, lo * 64:(lo + n) * 64].rearrange("p (h q) -> p h q", q=64)

        for (tok0, T, past, sidx) in seqs:
            if past:
                P.load(sio, sio[:], I.st_ssd[sidx - 1].rearrange("(c p) n -> p c n", p=128))
                for c in range(16):
                    ps = psm.next()
                    P.op("pe", lambda e, ps=ps, c=c: e.transpose(ps[:, 0:128], sio[:, c, :], C.ident[:, :]),
                         reads=[sio, C.ident], writes=[ps])
                    P.op("dve", lambda e, ps=ps, c=c: e.tensor_copy(S_sb[:, c * 128:(c + 1) * 128], ps[:, 0:128]),
                         reads=[ps], writes=[S_sb])
            else:
                P.op("dve", lambda e: e.memset(S_sb[:], 0.0), writes=[S_sb])
            P.op("act", lambda e: e.copy(S_bf[:], S_sb[:]), reads=[S_sb], writes=[S_bf])
            for (t0, L) in _tiles(T, 128):
                assert L in (128, 16)
                a0 = tok0 + t0
                xb = x_c.next(); zb = z_c.next(); bmb = bm_c.next(); bcb = bcT_c.next(); dtr = dt_c.next()
                if L < 128:
                    for b_ in (xb, zb, bmb, bcb, dtr):
                        P.op("pool", lambda e, b_=b_: e.memset(b_[:], 0.0), writes=[b_])
                P.load(xb, xb[0:L, :], S.xtok[a0:a0 + L, :])
                P.load(zb, zb[0:L, :], S.z[a0:a0 + L, :], q="act")
                P.load(bmb, bmb[0:L, :], S.bmtok[a0:a0 + L, :])
                P.load(bcb, bcb[:, :, 0:L], S.bcT[:, a0:a0 + L].rearrange("(g n) t -> n g t", n=128), q="act")
                P.load(dtr, dtr[0:L, :], S.dtraw[a0:a0 + L, :])
                P.op("dve", lambda e, dtr=dtr: e.tensor_tensor(dt[:], dtr[:], dtb_b[:], ALU.add), reads=[dtr, dtb_b], writes=[dt])
                P.op("act", lambda e: e.activation(dt[:], dt[:], AF.Exp), reads=[dt], writes=[dt])
                P.op("act", lambda e: e.activation(dt[:], dt[:], AF.Ln, bias=1.0), reads=[dt], writes=[dt])
                if L < 128:
                    P.op("dve", lambda e: e.tensor_scalar(dt[:], dt[:], valid16[:, 0:1], None, ALU.mult), reads=[dt, valid16], writes=[dt])
                P.op("dve", lambda e: e.tensor_scalar(lndt[:], dt[:], 1e-30, None, ALU.max), reads=[dt], writes=[lndt])
                P.op("act", lambda e: e.activation(lndt[:], lndt[:], AF.Ln), reads=[lndt], writes=[lndt])
                P.op("dve", lambda e: e.tensor_tensor(dta[:], dt[:], a_b[:], ALU.mult), reads=[dt, a_b], writes=[dta])
                ps = psm.next()
                P.op("pe", lambda e, ps=ps: e.matmul(ps[:, 0:32], C.tri[:, :], dta[:, :], start=True, stop=True),
                     reads=[C.tri, dta], writes=[ps])
                P.op("dve", lambda e, ps=ps: e.tensor_copy(A_sb[:], ps[:, 0:32]), reads=[ps], writes=[A_sb])
                pse = psm.next()
                P.op("pe", lambda e, pse=pse: e.matmul(pse[:, 0:32], C.sellast[:, :], A_sb[:, :], start=True, stop=True),
                     reads=[C.sellast, A_sb], writes=[pse])
                P.op("dve", lambda e: e.tensor_tensor(bias_all[:], lndt[:], A_sb[:], ALU.subtract), reads=[lndt, A_sb], writes=[bias_all])
                P.op("dve", lambda e, pse=pse: e.tensor_tensor(wend[:], pse[:, 0:32], bias_all[:], ALU.add), reads=[pse, bias_all], writes=[wend])
                P.op("act", lambda e: e.activation(wend[:], wend[:], AF.Exp), reads=[wend], writes=[wend])
                P.op("act", lambda e, pse=pse: e.activation(cd_b[:], pse[:, 0:32], AF.Exp), reads=[pse], writes=[cd_b])
                P.op("act", lambda e: e.activation(expA[:], A_sb[:], AF.Exp), reads=[A_sb], writes=[expA])
                P.op("dve", lambda e: e.tensor_tensor(D2[:], bview(C.ident[:, :].unsqueeze(1), [128, 32, 128]),
                                                      bview(A_sb[:, :].unsqueeze(2), [128, 32, 128]), ALU.mult),
                     reads=[C.ident, A_sb], writes=[D2])
                P.op("act", lambda e, xb=xb: e.copy(x_bf[:], xb[:]), reads=[xb], writes=[x_bf])
                P.op("dve", lambda e, xb=xb: e.tensor_tensor(v3(xw, 0, 32), v3(xb, 0, 32), bview(wend[:, :].unsqueeze(2), [128, 32, 64]), ALU.mult),
                     reads=[xb, wend], writes=[xw])
                for g in range(4):
                    for half in range(2):
                        pe_ = psE[half]
                        P.op("pe", lambda e, pe_=pe_, g=g, half=half: e.matmul(
                            pe_[:, 0:512], C.ones_f[:, :], D2[:, g * 8 + half * 4:g * 8 + half * 4 + 4, :].rearrange("p h l -> p (h l)"),
                            start=True, stop=False), reads=[C.ones_f, D2], writes=[pe_])
                        P.op("pe", lambda e, pe_=pe_: e.matmul(
                            pe_[:, 0:512], C.ident[:, :], C.negtri4[:, :], start=False, stop=True),
                            reads=[C.ident, C.negtri4], writes=[pe_])
                    P.op("pe", lambda e, bcb=bcb, g=g: e.matmul(psCB[:, 0:128], bcb[:, g, :], bcb[:, 4 + g, :], start=True, stop=True),
                         reads=[bcb], writes=[psCB])
                    for hh in range(8):
                        h = g * 8 + hh
                        pe_ = psE[hh // 4]
                        d_ = dec.next()
                        P.op("act", lambda e, d_=d_, pe_=pe_, hh=hh, h=h: e.activation(
                            d_[:, :], pe_[:, (hh % 4) * 128:(hh % 4 + 1) * 128], AF.Exp, bias=bias_all[:, h:h + 1]),
                            reads=[pe_, bias_all], writes=[d_])
                        w_ = WT.next()
                        P.op("dve", lambda e, d_=d_, w_=w_: e.tensor_tensor(w_[:, :], d_[:, :], psCB[:, 0:128], ALU.mult),
                             reads=[d_, psCB], writes=[w_])
                        P.op("pe", lambda e, w_=w_, hh=hh, h=h: e.matmul(ps_y[:, hh * 64:(hh + 1) * 64], w_[:, :], x_bf[:, h * 64:(h + 1) * 64],
                                                                          start=True, stop=True), reads=[w_, x_bf], writes=[ps_y])
                    P.op("pe", lambda e, bcb=bcb, g=g: e.matmul(ps_off[:, 0:512], bcb[:, 4 + g, :], S_bf[:, g * 512:(g + 1) * 512],
                                                                  start=True, stop=True), reads=[bcb, S_bf], writes=[ps_off])
                    P.op("dve", lambda e, g=g: e.tensor_tensor(v3(y_c, g * 8, 8), ps_off[:, 0:512].rearrange("p (h q) -> p h q", q=64),
                                                                bview(expA[:, g * 8:(g + 1) * 8].unsqueeze(2), [128, 8, 64]), ALU.mult),
                         reads=[ps_off, expA], writes=[y_c])
                    P.op("dve", lambda e, g=g: e.tensor_tensor(y_c[:, g * 512:(g + 1) * 512], y_c[:, g * 512:(g + 1) * 512], ps_y[:, 0:512], ALU.add),
                         reads=[y_c, ps_y], writes=[y_c])
                    P.op("pe", lambda e, bmb=bmb, g=g: e.matmul(ps_st[:, 0:512], bmb[:, g * 128:(g + 1) * 128], xw[:, g * 512:(g + 1) * 512],
                                                                  start=True, stop=True), reads=[bmb, xw], writes=[ps_st])
                    P.op("dve", lambda e, g=g: e.tensor_tensor(v3(S_sb, g * 8, 8), v3(S_sb, g * 8, 8),
                                                                bview(cd_b[:, g * 8:(g + 1) * 8].unsqueeze(2), [128, 8, 64]), ALU.mult),
                         reads=[S_sb, cd_b], writes=[S_sb])
                    P.op("dve", lambda e, g=g: e.tensor_tensor(S_sb[:, g * 512:(g + 1) * 512], S_sb[:, g * 512:(g + 1) * 512], ps_st[:, 0:512], ALU.add),
                         reads=[S_sb, ps_st], writes=[S_sb])
                    P.op("act", lambda e, g=g: e.copy(S_bf[:, g * 512:(g + 1) * 512], S_sb[:, g * 512:(g + 1) * 512]), reads=[S_sb], writes=[S_bf])
                P.op("pool", lambda e, xb=xb: e.tensor_tensor(v3(tmpb, 0, 32), v3(xb, 0, 32), bview(D_b[:, :].unsqueeze(2), [128, 32, 64]), ALU.mult),
                     reads=[xb, D_b], writes=[tmpb])
                P.op("dve", lambda e: e.tensor_tensor(y_c[:], y_c[:], tmpb[:], ALU.add), reads=[y_c, tmpb], writes=[y_c])
                P.op("act", lambda e, zb=zb: e.activation(tmpb[:], zb[:], AF.Silu), reads=[zb], writes=[tmpb])
                P.op("dve", lambda e: e.tensor_tensor(y_c[:], y_c[:], tmpb[:], ALU.mult), reads=[y_c, tmpb], writes=[y_c])
                for g in range(4):
                    P.op("act", lambda e, g=g: e.activation(junk[:, :], y_c[:, g * 512:(g + 1) * 512], AF.Square, accum_out=ss[:, g:g + 1]),
                         reads=[y_c], writes=[junk, ss])
                P.op("dve", lambda e: e.tensor_scalar(rstd[:], ss[:], 1.0 / 512, 1e-5, ALU.mult, ALU.add), reads=[ss], writes=[rstd])
                P.op("act", lambda e: e.activation(rstd[:], rstd[:], AF.Sqrt), reads=[rstd], writes=[rstd])
                P.op("dve", lambda e: e.reciprocal(rstd[:], rstd[:]), reads=[rstd], writes=[rstd])
                for g in range(4):
                    P.op("dve", lambda e, g=g: e.tensor_scalar(y_c[:, g * 512:(g + 1) * 512], y_c[:, g * 512:(g + 1) * 512], rstd[:, g:g + 1], None, ALU.mult),
                         reads=[y_c, rstd], writes=[y_c])
                P.op("pool", lambda e: e.tensor_tensor(y_c[:], y_c[:], g_b[:], ALU.mult), reads=[y_c, g_b], writes=[y_c])
                yt = yT.next()
                for c4 in range(4):
                    ps = psm.next()
                    for j in range(4):
                        c = c4 * 4 + j
                        P.op("pe", lambda e, ps=ps, c=c, j=j: e.transpose(ps[:, j * 128:(j + 1) * 128], y_c[:, c * 128:(c + 1) * 128], C.ident[:, :]),
                             reads=[y_c, C.ident], writes=[ps])
                    P.op("act", lambda e, ps=ps, c4=c4, yt=yt: e.copy(yt[:, c4 * 4:(c4 + 1) * 4, :], ps[:].rearrange("p (j m) -> p j m", j=4)),
                         reads=[ps], writes=[yt])
                P.store(yt, S.mixT[2048:4096, a0:a0 + L].rearrange("(c p) t -> p c t", p=128), yt[:, :, 0:L])
            for c in range(16):
                ps = psm.next()
                P.op("pe", lambda e, ps=ps, c=c: e.transpose(ps[:, 0:128], S_sb[:, c * 128:(c + 1) * 128], C.ident[:, :]),
                     reads=[S_sb, C.ident], writes=[ps])
                P.op("dve", lambda e, ps=ps, c=c: e.tensor_copy(sio[:, c, :], ps[:, 0:128]), reads=[ps], writes=[sio])
            P.store(sio, O.ssd_state[sidx].rearrange("(c p) n -> p c n", p=128), sio[:])


def phase_outproj(P, C, W, Kc, srcT, dst):
    with ExitStack() as st:
        TT = 1024
        wcols = 512 if Kc <= 16 else 256
        xT = P.sb(st, "xT", [128, Kc, TT + 128], BF16)
        P.op("pool", lambda e: e.memset(xT[:], 0.0), writes=[xT])
        wring = Ring([P.sb(st, f"w{i}", [128, Kc, wcols], BF16) for i in range(2)])
        sf = Ring([P.sb(st, f"sf{i}", [128, 512], F32) for i in range(3)])
        psG = Ring(C.psum[0:8])
        for (T0, NT) in supertiles(C.NTOK, TT):
            for k0 in range(0, Kc, 16):
                P.load(xT, xT[:, k0:k0 + 16, 0:NT], srcT[k0 * 128:(k0 + 16) * 128, T0:T0 + NT].rearrange("(c p) t -> p c t", p=128),
                       q=("sp" if (k0 // 16) % 2 == 0 else "act"))

            def ev(ps, t0, n, col0, ncol, T0=T0):
                b = sf.next()
                P.op("act", lambda e: e.copy(b[0:n, 0:ncol], ps[0:n, 0:ncol]), reads=[ps], writes=[b])
                P.store(b, dst[T0 + t0:T0 + t0 + n, col0:col0 + ncol], b[0:n, 0:ncol])

            gemm_segments(P, C, xT, Kc, NT, W, [(0, 2048, "tok", ev)], wring, psG, wcols)


def make_ln_loader(P, C, st, xa_src, xb_src, g_ap, b_ap, x_dst):
    g_b = P.sb(st, "lng", [128, D], F32)
    b_b = P.sb(st, "lnb", [128, D], F32)
    P.load(g_b, g_b[:], bview(g_ap, [128, D]))
    P.load(b_b, b_b[:], bview(b_ap, [128, D]))
    mring = Ring([P.sb(st, f"lnm{i}", [128, D], F32) for i in range(2)])
    aring = Ring([P.sb(st, f"lna{i}", [128, D], F32) for i in range(2)])
    stats = P.sb(st, "lnstats", [128, 24], F32)
    mv = P.sb(st, "lnmv", [128, 2], F32)
    rstd = P.sb(st, "lnrstd", [128, 1], F32)

    def loader(xb, tok0, n):
        xa = aring.next()
        mb = mring.next()
        P.load(xa, xa[0:n, :], xa_src[tok0:tok0 + n, :])
        P.load(mb, mb[0:n, :], xb_src[tok0:tok0 + n, :], q="act")
        layer_norm_tile(P, C, (stats, mv, rstd), xa, mb, n, g_b, b_b, xb)
        P.store(xb, x_dst[tok0:tok0 + n, :], xb[0:n, :])

    return loader


def phase_ffn(P, C, xT_src, wg, wu, wd, dst):
    with ExitStack() as st:
        TT = 896
        KF = D_FF // 128
        xT = P.sb(st, "xT", [128, 16, TT + 128], BF16)
        hT = P.sb(st, "hT", [128, KF, TT + 128], BF16)
        P.op("pool", lambda e: e.memset(xT[:], 0.0), writes=[xT])
        P.op("pool", lambda e: e.memset(hT[:], 0.0), writes=[hT])
        wring = Ring([P.sb(st, f"w{i}", [128, 16 * 512], BF16) for i in range(3)])
        sf = Ring([P.sb(st, f"sf{i}", [128, 512], F32) for i in range(3)])
        sg = Ring([P.sb(st, f"sg{i}", [128, 512], F32) for i in range(2)])
        psA = Ring(C.psum[0:4])
        psB = Ring(C.psum[4:8])
        wgv = wg.rearrange("(c p) n -> p c n", p=128)
        wuv = wu.rearrange("(c p) n -> p c n", p=128)
        wdv = wd.rearrange("(c p) n -> p c n", p=128)
        for (T0, NT) in supertiles(C.NTOK, TT):
            P.load(xT, xT[:, :, 0:NT], xT_src[:, T0:T0 + NT].rearrange("(c p) t -> p c t", p=128))
            for (f0, nf) in _tiles(D_FF, 512):
                wgb = wring.next()
                wub = wring.next()
                wg3 = wgb[:, :].rearrange("p (c n) -> p c n", n=512)
                wu3 = wub[:, :].rearrange("p (c n) -> p c n", n=512)
                P.load(wgb, wg3[:, :, 0:nf], wgv[:, :, f0:f0 + nf], q="pool")
                P.load(wub, wu3[:, :, 0:nf], wuv[:, :, f0:f0 + nf], q="pool")
                for (s0, ns) in _tiles(nf, 128):
                    j = (f0 + s0) // 128
                    for (t0, n) in _tiles(NT, 512):
                        pg = psA.next()
                        pu = psB.next()
                        for c in range(16):
                            P.op("pe", lambda e, pg=pg, c=c, wg3=wg3, s0=s0, t0=t0, n=n: e.matmul(
                                pg[:, 0:n], wg3[:, c, s0:s0 + 128], xT[:, c, t0:t0 + n], start=(c == 0), stop=(c == 15)),
                                reads=[wgb, xT], writes=[pg])
                        for c in range(16):
                            P.op("pe", lambda e, pu=pu, c=c, wu3=wu3, s0=s0, t0=t0, n=n: e.matmul(
                                pu[:, 0:n], wu3[:, c, s0:s0 + 128], xT[:, c, t0:t0 + n], start=(c == 0), stop=(c == 15)),
                                reads=[wub, xT], writes=[pu])
                        g_ = sg.next()
                        P.op("act", lambda e, g_=g_, pg=pg, n=n: e.activation(g_[:, 0:n], pg[:, 0:n], AF.Silu), reads=[pg], writes=[g_])
                        P.op("dve", lambda e, g_=g_, pu=pu, n=n, j=j, t0=t0: e.tensor_tensor(hT[:, j, t0:t0 + n], g_[:, 0:n], pu[:, 0:n], ALU.mult),
                             reads=[g_, pu], writes=[hT])
            for (c0, ncw) in _tiles(D, 128):
                wb = wring.next()
                w3 = wb[:, 0:KF * 128].rearrange("p (c n) -> p c n", n=128)
                for k0 in range(0, KF, 11):
                    P.load(wb, w3[:, k0:k0 + 11, 0:ncw], wdv[:, k0:k0 + 11, c0:c0 + ncw], q="pool")
                for (t0, n) in _tiles(NT, 128):
                    ps = psA.next()
                    for c in range(KF):
                        P.op("pe", lambda e, ps=ps, c=c, t0=t0, w3=w3, ncw=ncw: e.matmul(
                            ps[0:128, 0:ncw], hT[:, c, t0:t0 + 128], w3[:, c, 0:ncw], start=(c == 0), stop=(c == KF - 1)),
                            reads=[hT, wb], writes=[ps])
                    b = sf.next()
                    P.op("act", lambda e, b=b, ps=ps, n=n, ncw=ncw: e.copy(b[0:n, 0:ncw], ps[0:n, 0:ncw]), reads=[ps], writes=[b])
                    P.store(b, dst[T0 + t0:T0 + t0 + n, c0:c0 + ncw], b[0:n, 0:ncw])


def phase_ln_T(P, C, xa_src, xb_src, g_ap, b_ap, x_dst, xT_dst):
    with ExitStack() as st:
        loader = make_ln_loader(P, C, st, xa_src, xb_src, g_ap, b_ap, x_dst)
        xin = Ring([P.sb(st, f"xin{i}", [128, D], F32) for i in range(2)])
        for b_ in xin.items:
            P.op("pool", lambda e, b_=b_: e.memset(b_[:], 0.0), writes=[b_])
        xTs = Ring([P.sb(st, f"xTs{i}", [128, 16, 128], BF16) for i in range(2)])
        psT = Ring(C.psum[0:4])
        for (t0, n) in _tiles(C.NTOK, 128):
            xt = xTs.next()
            build_xT(P, C, st, xt, t0, n, 0, loader, xin, psT)
            P.store(xt, xT_dst[:, t0:t0 + n].rearrange("(c p) t -> p c t", p=128), xt[:, :, 0:n])


def phase_final_ln(P, C, xa_src, xb_src, g_ap, b_ap, dst):
    with ExitStack() as st:
        loader = make_ln_loader(P, C, st, xa_src, xb_src, g_ap, b_ap, dst)
        outr = Ring([P.sb(st, f"lnout{i}", [128, D], F32) for i in range(2)])
        for (t0, n) in _tiles(C.NTOK, 128):
            loader(outr.next(), t0, n)


def phase_inproj_c(P, C, I, S, loader_factory):
    NTOK = C.NTOK
    kscale = DK_C ** -0.5
    with ExitStack() as st:
        TT = 1024
        loader = loader_factory(st)
        xT = P.sb(st, "xT", [128, 16, TT + 128], BF16)
        P.op("pool", lambda e: e.memset(xT[:], 0.0), writes=[xT])
        xin = Ring([P.sb(st, f"xin{i}", [128, D], F32) for i in range(2)])
        for b_ in xin.items:
            P.op("pool", lambda e, b_=b_: e.memset(b_[:], 0.0), writes=[b_])
        wring = Ring([P.sb(st, f"w{i}", [128, 16, 512], BF16) for i in range(2)])
        sf = Ring([P.sb(st, f"sf{i}", [128, 512], F32) for i in range(3)])
        sh = Ring([P.sb(st, f"sh{i}", [128, 512], BF16) for i in range(3)])
        sm = Ring([P.sb(st, f"sm{i}", [128, 16], F32) for i in range(3)])
        gb = P.sb(st, "gb", [128, 16], F32)
        P.load(gb, gb[:, 0:8], bview(I.b_igate_c[0:1, :], [128, 8]))
        P.load(gb, gb[:, 8:16], bview(I.b_fgate_c[0:1, :], [128, 8]))
        psT = Ring(C.psum[0:2])
        psG = Ring(C.psum[2:8])
        for (T0, NT) in supertiles(NTOK, TT):
            for (t0, n) in _tiles(NT, 128):
                build_xT(P, C, st, xT, T0 + t0, n, t0, loader, xin, psT)

            def ev_feat(dst, mul, cbase, T0=T0):
                def f(ps, t0, n, col0, ncol):
                    b = sh.next()
                    P.op("act", lambda e: e.activation(b[0:ncol, 0:n], ps[0:ncol, 0:n], AF.Copy, scale=float(mul)), reads=[ps], writes=[b])
                    P.store(b, dst[col0 - cbase:col0 - cbase + ncol, T0 + t0:T0 + t0 + n], b[0:ncol, 0:n])
                return f

            def ev_tok_bf(dst, mul, cbase, T0=T0):
                def f(ps, t0, n, col0, ncol):
                    b = sh.next()
                    P.op("act", lambda e: e.activation(b[0:n, 0:ncol], ps[0:n, 0:ncol], AF.Copy, scale=float(mul)), reads=[ps], writes=[b])
                    P.store(b, dst[T0 + t0:T0 + t0 + n, col0 - cbase:col0 - cbase + ncol], b[0:n, 0:ncol])
                return f

            def ev_o(ps, t0, n, col0, ncol, T0=T0):
                b = sf.next()
                P.op("act", lambda e: e.copy(b[0:n, 0:ncol], ps[0:n, 0:ncol]), reads=[ps], writes=[b])
                P.store(b, S.z[T0 + t0:T0 + t0 + n, col0 - 4096:col0 - 4096 + ncol], b[0:n, 0:ncol])

            def ev_gates(ps, t0, n, col0, ncol, T0=T0):
                b = sm.next()
                P.op("dve", lambda e: e.tensor_tensor(b[0:n, :], ps[0:n, 0:16], gb[0:n, :], ALU.add), reads=[ps, gb], writes=[b])
                P.op("act", lambda e: e.activation(b[0:n, 8:16], b[0:n, 8:16], AF.Exp, scale=-1.0), reads=[b], writes=[b])
                P.op("act", lambda e: e.activation(b[0:n, 8:16], b[0:n, 8:16], AF.Ln, bias=1.0), reads=[b], writes=[b])
                P.op("act", lambda e: e.mul(b[0:n, 8:16], b[0:n, 8:16], -1.0), reads=[b], writes=[b])
                P.store(b, S.gates[T0 + t0:T0 + t0 + n, :], b[0:n, :])

            segs = [
                (0, 1024, "feat", ev_feat(S.qT, 1.0, 0)),
                (1024, 2048, "feat", ev_feat(S.kT, kscale, 1024)),
                (1024, 2048, "tok", ev_tok_bf(S.ktok, kscale, 1024)),
                (2048, 4096, "tok", ev_tok_bf(S.vbf, 1.0, 2048)),
                (4096, 6144, "tok", ev_o),
                (6144, 6160, "tok", ev_gates),
            ]
            gemm_segments(P, C, xT, 16, NT, I.w_in_c, segs, wring, psG, 512)


def phase_mlstm(P, C, I, O, S, seqs):
    with ExitStack() as st:
        sb = lambda name, shape, dt=F32: P.sb(st, name, shape, dt)
        g_b = sb("g_b", [128, 2048])
        P.load(g_b, g_b[:], bview(I.mlstm_norm_g[0:1, :], [128, 2048]))
        Cs = sb("Cs", [128, 8, 257]); Cs_bf = sb("Cs_bf", [128, 8, 257], BF16)
        m_b = sb("m_b", [128, 8]); m_col = sb("m_col", [128, 1])
        cio = sb("cio", [128, 16, 128])
        npad = sb("npad", [128, 128])
        gt = Ring([sb(f"gt{i}", [128, 16]) for i in range(2)])
        qT_c = Ring([sb(f"qT_c{i}", [128, 8, 128], BF16) for i in range(2)])
        kT_c = Ring([sb(f"kT_c{i}", [128, 8, 128], BF16) for i in range(2)])
        kt_c = Ring([sb(f"kt_c{i}", [128, 1024], BF16) for i in range(2)])
        va_c = Ring([sb(f"va_c{i}", [128, 8, 257], BF16) for i in range(2)])
        o_c = Ring([sb(f"o_c{i}", [128, 2048]) for i in range(2)])
        bcum = sb("bcum", [128, 8]); u = sb("u", [128, 8]); upad = sb("upad", [128, 128]); bpad = sb("bpad", [128, 128])
        uT = sb("uT", [128, 128]); bT = sb("bT", [128, 128]); MT = sb("MT", [128, 128]); ones8 = sb("ones8", [128, 128])
        M_sb = sb("M_sb", [128, 8]); negM = sb("negM", [128, 8]); Mlast = sb("Mlast", [128, 8]); btot = sb("btot", [128, 8])
        ginter = sb("ginter", [128, 8]); wend = sb("wend", [128, 8]); gold = sb("gold", [128, 8]); emr = sb("emr", [128, 8])
        D2m = sb("D2m", [128, 8, 128])
        dec = Ring([sb(f"dec{i}", [128, 128]) for i in range(3)])
        WT = Ring([sb(f"WT{i}", [128, 128], BF16) for i in range(3)])
        numsb = Ring([sb(f"numsb{i}", [128, 257]) for i in range(2)])
        comb = Ring([sb(f"comb{i}", [128, 257]) for i in range(2)])
        rdn = Ring([sb(f"rdn{i}", [128, 1]) for i in range(2)])
        vw = Ring([sb(f"vw{i}", [128, 257], BF16) for i in range(2)])
        h_c = sb("h_c", [128, 2048]); tmpb = sb("tmpb", [128, 2048]); junk = sb("junk", [128, 256])
        ss = sb("ss", [128, 8]); rstd = sb("rstd", [128, 8])
        hTs = Ring([sb(f"hTs{i}", [128, 16, 128], BF16) for i in range(2)])
        psE = [C.psum[0], C.psum[1]]
        ps_s, ps_num, ps_int, ps_c = C.psum[2], C.psum[3], C.psum[4], C.psum[5]
        psm = Ring(C.psum[6:8])
        for b_ in [upad, bpad, MT, npad, cio, m_col] + qT_c.items + kT_c.items + kt_c.items + va_c.items + o_c.items:
            P.op("pool", lambda e, b_=b_: e.memset(b_[:], 0.0), writes=[b_])
        P.op("pool", lambda e: e.memset(ones8[:], 1.0), writes=[ones8])

        for (tok0, T, past, sidx) in seqs:
            if past:
                P.load(cio, cio[:], I.st_c[sidx - 1].rearrange("(c p) d -> p c d", p=128))
                for c in range(16):
                    ps = psm.next()
                    P.op("pe", lambda e, ps=ps, c=c: e.transpose(ps[:, 0:128], cio[:, c, :], C.ident[:, :]), reads=[cio, C.ident], writes=[ps])
                    P.op("dve", lambda e, ps=ps, c=c: e.tensor_copy(Cs[:, c // 2, (c % 2) * 128:(c % 2 + 1) * 128], ps[:, 0:128]),
                         reads=[ps], writes=[Cs])
                P.load(npad, npad[0:8, :], I.st_n[sidx - 1])
                ps = psm.next()
                P.op("pe", lambda e, ps=ps: e.transpose(ps[:, 0:128], npad[:, :], C.ident[:, :]), reads=[npad, C.ident], writes=[ps])
                P.op("dve", lambda e, ps=ps: e.tensor_copy(Cs[:, :, 256], ps[:, 0:8]), reads=[ps], writes=[Cs])
                P.load(m_b, m_b[:], bview(I.st_m[sidx - 1:sidx, :], [128, 8]))
                P.load(m_col, m_col[0:8, :], I.st_m[sidx - 1:sidx, :].rearrange("o h -> h o"))
            else:
                P.op("dve", lambda e: e.memset(Cs[:], 0.0), writes=[Cs])
                P.op("dve", lambda e: e.memset(m_b[:], 0.0), writes=[m_b])
                P.op("dve", lambda e: e.memset(m_col[:], 0.0), writes=[m_col])
            P.op("act", lambda e: e.copy(Cs_bf[:], Cs[:]), reads=[Cs], writes=[Cs_bf])
            for (t0, L) in _tiles(T, 128):
                assert L in (128, 16)
                a0 = tok0 + t0
                g_ = gt.next(); qb = qT_c.next(); kb = kT_c.next(); ktb = kt_c.next(); vab = va_c.next(); ob = o_c.next()
                if L < 128:
                    for b_ in (qb, kb, ktb, vab, ob):
                        P.op("pool", lambda e, b_=b_: e.memset(b_[:], 0.0), writes=[b_])
                P.op("pool", lambda e, g_=g_: e.memset(g_[:, 0:8], NEG), writes=[g_])
                P.op("pool", lambda e, g_=g_: e.memset(g_[:, 8:16], 0.0), writes=[g_])
                P.load(g_, g_[0:L, :], S.gates[a0:a0 + L, :])
                P.load(qb, qb[:, :, 0:L], S.qT[0:1024, a0:a0 + L].rearrange("(h d) t -> d h t", d=128))
                P.load(kb, kb[:, :, 0:L], S.kT[0:1024, a0:a0 + L].rearrange("(h d) t -> d h t", d=128), q="act")
                P.load(ktb, ktb[0:L, :], S.ktok[a0:a0 + L, :])
                P.load(vab, vab[0:L, :, 0:256], S.vbf[a0:a0 + L, :].rearrange("t (h v) -> t h v", v=256), q="act")
                P.op("pool", lambda e, vab=vab: e.memset(vab[:, :, 256:257], 1.0), writes=[vab])
                P.load(ob, ob[0:L, :], S.z[a0:a0 + L, :])
                ps = psm.next()
                P.op("pe", lambda e, ps=ps, g_=g_: e.matmul(ps[:, 0:8], C.tri[:, :], g_[:, 8:16], start=True, stop=True), reads=[C.tri, g_], writes=[ps])
                P.op("dve", lambda e, ps=ps: e.tensor_copy(bcum[:], ps[:, 0:8]), reads=[ps], writes=[bcum])
                P.op("dve", lambda e, g_=g_: e.tensor_tensor(u[:], g_[:, 0:8], bcum[:], ALU.subtract), reads=[g_, bcum], writes=[u])
                P.op("dve", lambda e: e.tensor_copy(upad[:, 0:8], u[:]), reads=[u], writes=[upad])
                P.op("dve", lambda e: e.tensor_copy(bpad[:, 0:8], bcum[:]), reads=[bcum], writes=[bpad])
                ps = psm.next()
                P.op("pe", lambda e, ps=ps: e.transpose(ps[:, 0:128], upad[:, :], C.ident[:, :]), reads=[upad, C.ident], writes=[ps])
                P.op("dve", lambda e, ps=ps: e.tensor_copy(uT[0:8, :], ps[0:8, 0:128]), reads=[ps], writes=[uT])
                ps = psm.next()
                P.op("pe", lambda e, ps=ps: e.transpose(ps[:, 0:128], bpad[:, :], C.ident[:, :]), reads=[bpad, C.ident], writes=[ps])
                P.op("dve", lambda e, ps=ps: e.tensor_copy(bT[0:8, :], ps[0:8, 0:128]), reads=[ps], writes=[bT])
                P.op("dve", lambda e: e.tensor_tensor_scan(MT[0:8, :], ones8[0:8, :], uT[0:8, :], m_col[0:8, 0:1], ALU.mult, ALU.max),
                     reads=[ones8, uT, m_col], writes=[MT])
                ps = psm.next()
                P.op("pe", lambda e, ps=ps: e.transpose(ps[:, 0:128], MT[:, :], C.ident[:, :]), reads=[MT, C.ident], writes=[ps])
                P.op("dve", lambda e, ps=ps: e.tensor_copy(M_sb[:], ps[:, 0:8]), reads=[ps], writes=[M_sb])
                P.op("act", lambda e: e.mul(negM[:], M_sb[:], -1.0), reads=[M_sb], writes=[negM])
                ps = psm.next()
                P.op("pe", lambda e, ps=ps: e.matmul(ps[:, 0:8], C.sellast[:, :], M_sb[:, :], start=True, stop=True), reads=[C.sellast, M_sb], writes=[ps])
                P.op("dve", lambda e, ps=ps: e.tensor_copy(Mlast[:], ps[:, 0:8]), reads=[ps], writes=[Mlast])
                ps = psm.next()
                P.op("pe", lambda e, ps=ps: e.matmul(ps[:, 0:8], C.sellast[:, :], bcum[:, :], start=True, stop=True), reads=[C.sellast, bcum], writes=[ps])
                P.op("dve", lambda e, ps=ps: e.tensor_copy(btot[:], ps[:, 0:8]), reads=[ps], writes=[btot])
                P.op("dve", lambda e: e.tensor_tensor(ginter[:], m_b[:], M_sb[:], ALU.subtract), reads=[m_b, M_sb], writes=[ginter])
                P.op("act", lambda e: e.activation(ginter[:], ginter[:], AF.Exp), reads=[ginter], writes=[ginter])
                P.op("dve", lambda e: e.tensor_tensor(wend[:], u[:], Mlast[:], ALU.subtract), reads=[u, Mlast], writes=[wend])
                P.op("act", lambda e: e.activation(wend[:], wend[:], AF.Exp), reads=[wend], writes=[wend])
                P.op("dve", lambda e: e.tensor_tensor(gold[:], m_b[:], Mlast[:], ALU.subtract), reads=[m_b, Mlast], writes=[gold])
                P.op("act", lambda e: e.activation(gold[:], gold[:], AF.Exp), reads=[gold], writes=[gold])
                P.op("dve", lambda e: e.tensor_tensor(emr[:], bcum[:], M_sb[:], ALU.add), reads=[bcum, M_sb], writes=[emr])
                P.op("act", lambda e: e.activation(emr[:], emr[:], AF.Exp, scale=-1.0), reads=[emr], writes=[emr])
                P.op("dve", lambda e: e.tensor_tensor(D2m[:], bview(C.ident[:, :].unsqueeze(1), [128, 8, 128]),
                                                      bview(negM[:, :].unsqueeze(2), [128, 8, 128]), ALU.mult),
                     reads=[C.ident, negM], writes=[D2m])
                for half in range(2):
                    pe_ = psE[half]
                    P.op("pe", lambda e, pe_=pe_, half=half: e.matmul(pe_[:, 0:512], C.ones_f[:, :],
                                                                        D2m[:, half * 4:half * 4 + 4, :].rearrange("p h l -> p (h l)"),
                                                                        start=True, stop=False), reads=[C.ones_f, D2m], writes=[pe_])
                    P.op("pe", lambda e, pe_=pe_: e.matmul(pe_[:, 0:512], C.ident[:, :], C.negtri4[:, :], start=False, stop=True),
                         reads=[C.ident, C.negtri4], writes=[pe_])
                for h in range(8):
                    pe_ = psE[h // 4]
                    P.op("pe", lambda e, kb=kb, qb=qb, h=h: e.matmul(ps_s[:, 0:128], kb[:, h, :], qb[:, h, :], start=True, stop=True),
                         reads=[kb, qb], writes=[ps_s])
                    d_ = dec.next()
                    P.op("act", lambda e, d_=d_, pe_=pe_, h=h: e.activation(d_[:, :], pe_[:, (h % 4) * 128:(h % 4 + 1) * 128], AF.Exp, bias=u[:, h:h + 1]),
                         reads=[pe_, u], writes=[d_])
                    w_ = WT.next()
                    P.op("dve", lambda e, d_=d_, w_=w_: e.tensor_tensor(w_[:, :], d_[:, :], ps_s[:, 0:128], ALU.mult), reads=[d_, ps_s], writes=[w_])
                    P.op("pe", lambda e, w_=w_, vab=vab, h=h: e.matmul(ps_num[:, 0:257], w_[:, :], vab[:, h, :], start=True, stop=True),
                         reads=[w_, vab], writes=[ps_num])
                    P.op("pe", lambda e, qb=qb, h=h: e.matmul(ps_int[:, 0:257], qb[:, h, :], Cs_bf[:, h, :], start=True, stop=True),
                         reads=[qb, Cs_bf], writes=[ps_int])
                    ns_ = numsb.next()
                    P.op("act", lambda e, ns_=ns_: e.copy(ns_[:, :], ps_num[:, 0:257]), reads=[ps_num], writes=[ns_])
                    cb_ = comb.next()
                    P.op("dve", lambda e, cb_=cb_, ns_=ns_, h=h: e.scalar_tensor_tensor(out=cb_[:, :], in0=ps_int[:, 0:257], scalar=ginter[:, h:h + 1],
                                                                                      in1=ns_[:, :], op0=ALU.mult, op1=ALU.add),
                         reads=[ps_int, ginter, ns_], writes=[cb_])
                    r_ = rdn.next()
                    P.op("act", lambda e, r_=r_, cb_=cb_: e.activation(r_[:, :], cb_[:, 256:257], AF.Abs), reads=[cb_], writes=[r_])
                    P.op("dve", lambda e, r_=r_, h=h: e.tensor_tensor(r_[:, :], r_[:, :], emr[:, h:h + 1], ALU.max), reads=[r_, emr], writes=[r_])
                    P.op("dve", lambda e, r_=r_: e.reciprocal(r_[:, :], r_[:, :]), reads=[r_], writes=[r_])
                    P.op("dve", lambda e, r_=r_, cb_=cb_, h=h: e.tensor_scalar(h_c[:, h * 256:(h + 1) * 256], cb_[:, 0:256], r_[:, 0:1], None, ALU.mult),
                         reads=[cb_, r_], writes=[h_c])
                    v_ = vw.next()
                    P.op("pool", lambda e, v_=v_, vab=vab, h=h: e.tensor_scalar(v_[:, :], vab[:, h, :], wend[:, h:h + 1], None, ALU.mult),
                         reads=[vab, wend], writes=[v_])
                    P.op("pe", lambda e, ktb=ktb, v_=v_, h=h: e.matmul(ps_c[:, 0:257], ktb[:, h * 128:(h + 1) * 128], v_[:, :], start=True, stop=True),
                         reads=[ktb, v_], writes=[ps_c])
                    P.op("dve", lambda e, h=h: e.scalar_tensor_tensor(out=Cs[:, h, :], in0=Cs[:, h, :], scalar=gold[:, h:h + 1], in1=ps_c[:, 0:257],
                                                                       op0=ALU.mult, op1=ALU.add), reads=[Cs, gold, ps_c], writes=[Cs])
                    P.op("act", lambda e, h=h: e.copy(Cs_bf[:, h, :], Cs[:, h, :]), reads=[Cs], writes=[Cs_bf])
                P.op("dve", lambda e: e.tensor_tensor(m_b[:], btot[:], Mlast[:], ALU.add), reads=[btot, Mlast], writes=[m_b])
                P.op("dve", lambda e: e.tensor_tensor(m_col[0:8, :], bT[0:8, 127:128], MT[0:8, 127:128], ALU.add), reads=[bT, MT], writes=[m_col])
                for h in range(8):
                    P.op("act", lambda e, h=h: e.activation(junk[:, :], h_c[:, h * 256:(h + 1) * 256], AF.Square, accum_out=ss[:, h:h + 1]),
                         reads=[h_c], writes=[junk, ss])
                P.op("dve", lambda e: e.tensor_scalar(rstd[:], ss[:], 1.0 / 256, 1e-5, ALU.mult, ALU.add), reads=[ss], writes=[rstd])
                P.op("act", lambda e: e.activation(rstd[:], rstd[:], AF.Sqrt), reads=[rstd], writes=[rstd])
                P.op("dve", lambda e: e.reciprocal(rstd[:], rstd[:]), reads=[rstd], writes=[rstd])
                for h in range(8):
                    P.op("dve", lambda e, h=h: e.tensor_scalar(h_c[:, h * 256:(h + 1) * 256], h_c[:, h * 256:(h + 1) * 256], rstd[:, h:h + 1], None, ALU.mult),
                         reads=[h_c, rstd], writes=[h_c])
                P.op("pool", lambda e: e.tensor_tensor(h_c[:], h_c[:], g_b[:], ALU.mult), reads=[h_c, g_b], writes=[h_c])
                P.op("act", lambda e, ob=ob: e.activation(tmpb[:], ob[:], AF.Sigmoid), reads=[ob], writes=[tmpb])
                P.op("dve", lambda e: e.tensor_tensor(h_c[:], h_c[:], tmpb[:], ALU.mult), reads=[h_c, tmpb], writes=[h_c])
                ht = hTs.next()
                for c4 in range(4):
                    ps = psm.next()
                    for j in range(4):
                        c = c4 * 4 + j
                        P.op("pe", lambda e, ps=ps, c=c, j=j: e.transpose(ps[:, j * 128:(j + 1) * 128], h_c[:, c * 128:(c + 1) * 128], C.ident[:, :]),
                             reads=[h_c, C.ident], writes=[ps])
                    P.op("act", lambda e, ps=ps, c4=c4, ht=ht: e.copy(ht[:, c4 * 4:(c4 + 1) * 4, :], ps[:].rearrange("p (j m) -> p j m", j=4)),
                         reads=[ps], writes=[ht])
                P.store(ht, S.mixT[0:2048, a0:a0 + L].rearrange("(c p) t -> p c t", p=128), ht[:, :, 0:L])
            for c in range(16):
                ps = psm.next()
                P.op("pe", lambda e, ps=ps, c=c: e.transpose(ps[:, 0:128], Cs[:, c // 2, (c % 2) * 128:(c % 2 + 1) * 128], C.ident[:, :]),
                     reads=[Cs, C.ident], writes=[ps])
                P.op("dve", lambda e, ps=ps, c=c: e.tensor_copy(cio[:, c, :], ps[:, 0:128]), reads=[ps], writes=[cio])
            P.store(cio, O.mlstm_c[sidx].rearrange("(c p) d -> p c d", p=128), cio[:])
            P.op("dve", lambda e: e.tensor_copy(npad[:, 0:8], Cs[:, :, 256]), reads=[Cs], writes=[npad])
            ps = psm.next()
            P.op("pe", lambda e, ps=ps: e.transpose(ps[:, 0:128], npad[:, :], C.ident[:, :]), reads=[npad, C.ident], writes=[ps])
            P.op("dve", lambda e, ps=ps: e.tensor_copy(bpad[0:8, :], ps[0:8, 0:128]), reads=[ps], writes=[bpad])
            P.store(bpad, O.mlstm_n[sidx], bpad[0:8, :])
            P.op("dve", lambda e: e.memset(bpad[:], 0.0), writes=[bpad])
            P.op("dve", lambda e: e.memset(npad[:], 0.0), writes=[npad])
            P.store(m_col, O.mlstm_m[sidx:sidx + 1, :].rearrange("o h -> h o"), m_col[0:8, :])


_PROG_CACHE = {}


def kernel(**inputs):
    NCORES = 8
    SEQ = inputs["x_prompt"].shape[1]
    PAST = inputs["cache_fox_k"].shape[2]
    NS = inputs["x_sample"].shape[0] // NCORES
    key = (SEQ, PAST, NS)
    if key not in _PROG_CACHE:
        _PROG_CACHE[key] = build_program(SEQ=SEQ, PAST=PAST, NS=NS)
    nc = _PROG_CACHE[key]
    in_maps = [make_in_map(inputs, c, NS=NS) for c in range(NCORES)]
    res = run_bass_kernel_spmd(nc, in_maps, core_ids=list(range(NCORES)))
    R = res.results
    NP = N_META + SEQ
    B = NCORES
    DB = NCORES * NS
    f32 = np.float32
    y_p = np.empty((B, SEQ, D), f32); y_s = np.empty((DB, DEC, D), f32)
    kp = np.empty((1, B, NP, H_A, DH_A), f32); vp = np.empty((1, B, NP, H_A, DH_A), f32); fp = np.empty((1, B, NP, H_A), f32)
    cvp = np.empty((1, B, 3, CONV_DIM), f32); hp = np.empty((1, B, H_B, P_B, N_B), f32)
    cp = np.empty((1, B, H_C, DV_C, DK_C), f32); np_ = np.empty((1, B, H_C, DK_C), f32); mp = np.empty((1, B, H_C), f32)
    ks = np.empty((1, DB, DEC, H_A, DH_A), f32); vs = np.empty((1, DB, DEC, H_A, DH_A), f32); fs = np.empty((1, DB, DEC, H_A), f32)
    cvs = np.empty((1, DB, 3, CONV_DIM), f32); hs = np.empty((1, DB, H_B, P_B, N_B), f32)
    cs = np.empty((1, DB, H_C, DV_C, DK_C), f32); ns_ = np.empty((1, DB, H_C, DK_C), f32); ms = np.empty((1, DB, H_C), f32)
    for c in range(NCORES):
        r = R[c]
        y = np.asarray(r["o_y"]); y_p[c] = y[N_META:NP]; y_s[c * NS:(c + 1) * NS] = y[NP:].reshape(NS, DEC, D)
        k = np.asarray(r["o_fox_k"]); kp[0, c] = k[:NP].reshape(NP, H_A, DH_A); ks[0, c * NS:(c + 1) * NS] = k[NP:].reshape(NS, DEC, H_A, DH_A)
        v = np.asarray(r["o_fox_v"]); vp[0, c] = v[:NP].reshape(NP, H_A, DH_A); vs[0, c * NS:(c + 1) * NS] = v[NP:].reshape(NS, DEC, H_A, DH_A)
        lf = np.asarray(r["o_fox_logf"]); fp[0, c] = lf[:NP]; fs[0, c * NS:(c + 1) * NS] = lf[NP:].reshape(NS, DEC, H_A)
        cv = np.asarray(r["o_ssd_conv"]); cvp[0, c] = cv[0]; cvs[0, c * NS:(c + 1) * NS] = cv[1:]
        hh = np.asarray(r["o_ssd_state"]).reshape(1 + NS, H_B, P_B, N_B); hp[0, c] = hh[0]; hs[0, c * NS:(c + 1) * NS] = hh[1:]
        cc = np.asarray(r["o_mlstm_c"]).reshape(1 + NS, H_C, DV_C, DK_C); cp[0, c] = cc[0]; cs[0, c * NS:(c + 1) * NS] = cc[1:]
        nn = np.asarray(r["o_mlstm_n"]); np_[0, c] = nn[0]; ns_[0, c * NS:(c + 1) * NS] = nn[1:]
        mm = np.asarray(r["o_mlstm_m"]); mp[0, c] = mm[0]; ms[0, c * NS:(c + 1) * NS] = mm[1:]
    return (y_p, y_s, kp, vp, fp, cvp, hp, cp, np_, mp, ks, vs, fs, cvs, hs, cs, ns_, ms)
```

```python
import math
import numpy as np
from contextlib import ExitStack
import concourse.bass as bass
import concourse.mybir as mybir
from concourse.bass_utils import run_bass_kernel_spmd

F32 = mybir.dt.float32
BF16 = mybir.dt.bfloat16
AF = mybir.ActivationFunctionType
ALU = mybir.AluOpType
AX = mybir.AxisListType

D = 2048
N_META = 16
DEC = 16
H_A, DH_A = 16, 128
H_B, P_B, G_B, N_B = 32, 64, 4, 128
D_SSM = 2048
CONV_DIM = 3072
H_C, DK_C, DV_C = 8, 128, 256
D_FF = 5632
E_A = 11312
E_C = 6160
DEPTH = 2
ALPHA = (2 * DEPTH) ** 0.25
NEG = -1e30

ENGS = ("pe", "act", "dve", "pool", "sp")


class Buf:
    __slots__ = ("name", "t", "last_w", "readers", "is_psum")

    def __init__(self, name, t, is_psum=False):
        self.name = name
        self.t = t
        self.last_w = None
        self.readers = []
        self.is_psum = is_psum

    def __getitem__(self, idx):
        return self.t[idx]


class Op:
    __slots__ = ("eng", "fn", "deps", "is_dma", "key", "sig", "tick", "dcount")

    def __init__(self, eng, fn, is_dma=False, key=None):
        self.eng = eng
        self.fn = fn
        self.deps = []
        self.is_dma = is_dma
        self.key = key
        self.sig = False
        self.tick = 0
        self.dcount = 0


class Prog:
    def __init__(self, nc, stack):
        self.nc = nc
        self.stack = stack
        self.esem = {e: stack.enter_context(nc.semaphore("sem_" + e)) for e in ENGS}
        self.ecnt = {e: 0 for e in ENGS}
        self.ksem = {}
        self.kcnt = {}
        self.ops = {e: [] for e in ENGS}
        self.known = {e: {} for e in ENGS}
        self.bufs = []
        self.nops = 0
        self.uid = 0
        self.dmaq = 0
        self.free_dsems = {"sw": [], "hw": []}
        self.kkind = {}
        self.all_dsems = []
        self.keep_names = {"ident", "tri", "negtri", "negtri4", "tri", "ones", "sel0", "sellast"}

    def sb(self, stack, name, shape, dt):
        self.uid += 1
        nm = f"{name}_{self.uid}"
        t = stack.enter_context(self.nc.sbuf_tensor(nm, list(shape), dt))
        b = Buf(nm, t)
        self.bufs.append(b)
        return b

    def ps(self, stack, name, shape, dt=F32):
        self.uid += 1
        nm = f"{name}_{self.uid}"
        t = stack.enter_context(self.nc.psum_tensor(nm, list(shape), dt))
        b = Buf(nm, t, True)
        self.bufs.append(b)
        return b

    def _deps(self, op, reads, writes):
        for b in reads:
            if b.last_w is not None:
                op.deps.append(b.last_w)
            if b.is_psum:
                for r in b.readers:
                    if r.eng != op.eng:
                        op.deps.append(r)
        for b in writes:
            lw = b.last_w
            if lw is not None:
                if op.is_dma and lw.is_dma and lw.key is op.key:
                    op.deps.extend(lw.deps)
                else:
                    op.deps.append(lw)
            for r in b.readers:
                op.deps.append(r)
        for b in reads:
            b.readers.append(op)
        for b in writes:
            b.last_w = op
            b.readers = []

    def op(self, eng, fn, reads=(), writes=()):
        o = Op(eng, fn)
        self._deps(o, reads, writes)
        self.ops[eng].append(o)
        self.nops += 1
        return o

    def dma(self, out, in_, key, reads=(), writes=(), q=None):
        if q is None:
            q = "sp"
        kind = "sw" if q == "pool" else "hw"
        if key in self.kkind:
            assert self.kkind[key] == kind, key.name
        if key not in self.ksem:
            self.kkind[key] = kind
            if self.free_dsems[kind]:
                self.ksem[key], self.kcnt[key] = self.free_dsems[kind].pop()
            else:
                self.ksem[key] = self.stack.enter_context(self.nc.semaphore(f"dk{len(self.all_dsems)}"))
                self.kcnt[key] = 0
                self.all_dsems.append(self.ksem[key])
        o = Op(q, lambda e, out=out, in_=in_: e.dma_start(out=out, in_=in_), True, key)
        self.kcnt[key] += 16
        o.dcount = self.kcnt[key]
        self._deps(o, reads, writes)
        self.ops[q].append(o)
        self.nops += 1
        return o

    def load(self, buf, dst, src, q=None):
        return self.dma(dst, src, buf, writes=[buf], q=q)

    def store(self, buf, dst, src, q=None):
        return self.dma(dst, src, buf, reads=[buf], q=q)

    def end_phase(self):
        nc = self.nc
        for e in ENGS:
            for o in self.ops[e]:
                for d in o.deps:
                    if not d.is_dma and (d.eng != o.eng or o.eng != "pe"):
                        d.sig = True
        for e in ENGS:
            c = self.ecnt[e]
            for o in self.ops[e]:
                if o.sig:
                    c += 1
                    o.tick = c
            self.ecnt[e] = c
        ops, esem, ksem, known, kcnt = self.ops, self.esem, self.ksem, self.known, self.kcnt

        def replay(eng_obj, name):
            kn = known[name]
            for o in ops[name]:
                need = {}
                for d in o.deps:
                    if d.is_dma:
                        s, v = ksem[d.key], d.dcount
                    else:
                        if d.eng == name and name == "pe":
                            continue
                        s, v = esem[d.eng], d.tick
                    if kn.get(id(s), 0) >= v:
                        continue
                    if need.get(id(s), (None, 0))[1] < v:
                        need[id(s)] = (s, v)
                for sid, (s, v) in need.items():
                    eng_obj.wait_ge(s, v)
                    kn[sid] = v
                ins = o.fn(eng_obj)
                if o.is_dma:
                    ins.then_inc(ksem[o.key], 16)
                elif o.sig:
                    ins.then_inc(esem[name], 1)
            if name == "sp":
                for k, c in kcnt.items():
                    s = ksem[k]
                    if kn.get(id(s), 0) < c:
                        eng_obj.wait_ge(s, c)
                        kn[id(s)] = c

        with nc.Block() as block:
            @block.tensor
            def _(e):
                replay(e, "pe")

            @block.scalar
            def _(e):
                replay(e, "act")

            @block.vector
            def _(e):
                replay(e, "dve")

            @block.gpsimd
            def _(e):
                replay(e, "pool")

            @block.sync
            def _(e):
                replay(e, "sp")

        for e in ENGS:
            self.ops[e] = []
            for e2 in ENGS:
                self.known[e][id(self.esem[e2])] = self.ecnt[e2]
            for k, c in self.kcnt.items():
                self.known[e][id(self.ksem[k])] = c
        for k, c in self.kcnt.items():
            self.free_dsems[self.kkind[k]].append((self.ksem[k], c))
        self.ksem = {}
        self.kcnt = {}
        self.kkind = {}
        for b in self.bufs:
            b.last_w = None
            b.readers = []
        self.bufs = [b for b in self.bufs if b.is_psum or b.name.split("_")[0] in self.keep_names]


class Ring:
    def __init__(self, items):
        self.items = items
        self.i = 0

    def next(self):
        b = self.items[self.i % len(self.items)]
        self.i += 1
        return b


def _tiles(n, step):
    return [(i, min(step, n - i)) for i in range(0, n, step)]


class Ctx:
    pass


def build_xT(P, C, pst, xT, tok0, n, off, loader, xin_ring, psT_ring):
    xb = xin_ring.next()
    loader(xb, tok0, n)
    for c4 in range(4):
        ps = psT_ring.next()
        for j in range(4):
            c = c4 * 4 + j
            P.op("pe", lambda e, ps=ps, xb=xb, c=c, j=j, n=n: e.transpose(
                ps[0:128, j * 128:j * 128 + 128], xb[0:128, c * 128:(c + 1) * 128], C.ident[:, :]),
                reads=[xb, C.ident], writes=[ps])
        P.op("dve", lambda e, ps=ps, c4=c4, n=n, off=off: e.tensor_copy(
            xT[:, c4 * 4:(c4 + 1) * 4, off:off + n],
            ps[:].rearrange("p (j m) -> p j m", j=4)[:, :, 0:n]), reads=[ps], writes=[xT])


def gemm_segments(P, C, xT, Kc, ntok, W, segs, wring, ps_ring, wcols):
    Wv = W.rearrange("(c p) n -> p c n", p=128)
    for (c0, c1, orient, evac) in segs:
        for (cc, ncw) in _tiles(c1 - c0, wcols):
            wb = wring.next()
            P.load(wb, wb[:, 0:Kc, 0:ncw], Wv[:, :, c0 + cc:c0 + cc + ncw], q="pool")
            if orient == "tok":
                for (t0, n) in _tiles(ntok, 128):
                    ps = ps_ring.next()
                    for c in range(Kc):
                        P.op("pe", lambda e, ps=ps, c=c, t0=t0, n=n, wb=wb, ncw=ncw: e.matmul(
                            ps[0:128, 0:ncw], xT[:, c, t0:t0 + 128], wb[:, c, 0:ncw],
                            start=(c == 0), stop=(c == Kc - 1)), reads=[xT, wb], writes=[ps])
                    evac(ps, t0, n, c0 + cc, ncw)
            else:
                for (s0, ns) in _tiles(ncw, 128):
                    for (t0, n) in _tiles(ntok, 512):
                        ps = ps_ring.next()
                        for c in range(Kc):
                            P.op("pe", lambda e, ps=ps, c=c, t0=t0, n=n, wb=wb, s0=s0, ns=ns: e.matmul(
                                ps[0:ns, 0:n], wb[:, c, s0:s0 + ns], xT[:, c, t0:t0 + n],
                                start=(c == 0), stop=(c == Kc - 1)), reads=[xT, wb], writes=[ps])
                        evac(ps, t0, n, c0 + cc + s0, ns)


def supertiles(ntok, tt):
    n128 = (ntok + 127) // 128
    per = tt // 128
    nsup = (n128 + per - 1) // per
    base, extra = divmod(n128, nsup)
    out = []
    a = 0
    for i in range(nsup):
        k = base + (1 if i < extra else 0)
        b = min(ntok, a + k * 128)
        out.append((a, b - a))
        a = b
    return out


def layer_norm_tile(P, C, st_bufs, xa, xb_, n, g_b, b_b, out):
    stats, mv, rstd = st_bufs
    P.op("dve", lambda e: e.scalar_tensor_tensor(out=xa[0:n, :], in0=xa[0:n, :], scalar=float(ALPHA), in1=xb_[0:n, :],
                                                 op0=ALU.mult, op1=ALU.add), reads=[xa, xb_], writes=[xa])
    for k in range(4):
        P.op("dve", lambda e, k=k: e.bn_stats(stats[0:n, k * 6:(k + 1) * 6], xa[0:n, k * 512:(k + 1) * 512]),
             reads=[xa], writes=[stats])
    P.op("dve", lambda e: e.bn_aggr(mv[0:n, :], stats[0:n, :]), reads=[stats], writes=[mv])
    P.op("dve", lambda e: e.tensor_scalar(rstd[0:n, :], mv[0:n, 1:2], 1e-5, None, ALU.add), reads=[mv], writes=[rstd])
    P.op("act", lambda e: e.activation(rstd[0:n, :], rstd[0:n, :], AF.Sqrt), reads=[rstd], writes=[rstd])
    P.op("dve", lambda e: e.reciprocal(rstd[0:n, :], rstd[0:n, :]), reads=[rstd], writes=[rstd])
    P.op("dve", lambda e: e.tensor_scalar(xa[0:n, :], xa[0:n, :], mv[0:n, 0:1], rstd[0:n, 0:1], ALU.subtract, ALU.mult),
         reads=[xa, mv, rstd], writes=[xa])
    P.op("pool", lambda e: e.tensor_tensor(xa[0:n, :], xa[0:n, :], g_b[0:n, :], ALU.mult), reads=[xa, g_b], writes=[xa])
    P.op("pool", lambda e: e.tensor_tensor(out[0:n, :], xa[0:n, :], b_b[0:n, :], ALU.add), reads=[xa, b_b], writes=[out])


def build_program(SEQ=4096, PAST=4096, NS=2, stop_after=None, debug=False):
    nc = bass.Bass("TRN2", target_bir_lowering=False)
    C = Ctx()
    NP = N_META + SEQ
    NTOK = NP + NS * DEC
    C.NP, C.NTOK, C.SEQ, C.PAST, C.NS = NP, NTOK, SEQ, PAST, NS

    early = {"x_prompt", "x_sample", "meta_tokens", "cache_fox_k", "cache_fox_v", "cache_fox_logf", "w_in_a",
             "b_fgate_a", "c_ident", "c_tri", "c_negtri"}
    if stop_after == "l0":
        early |= {"w_out_a", "ln_mix_g", "ln_mix_b", "ln_ffn_g", "ln_ffn_b", "w_ffn_gate", "w_ffn_up", "w_ffn_down"}
    if stop_after in ("ssd", "l0"):
        early |= {"state_ssd_conv", "state_ssd", "conv_w", "conv_b", "dt_bias", "a_log", "d_skip", "ssd_norm_g"}
    C.early = early

    def din(name, shape, dt=F32):
        if stop_after in ("inproj_a", "attn", "cache", "ssd", "l0") and name not in early:
            return None
        return nc.dram_tensor(name, list(shape), dt, kind="ExternalInput").ap()

    def dout(name, shape, dt=F32):
        return nc.dram_tensor(name, list(shape), dt, kind="ExternalOutput").ap()

    def dscr(name, shape, dt=F32):
        if debug:
            return nc.dram_tensor(name, list(shape), dt, kind="ExternalOutput").ap()
        return nc.dram_tensor(name, list(shape), dt, kind="Internal").ap()

    I = Ctx()
    I.x_prompt = din("x_prompt", [SEQ, D])
    I.x_sample = din("x_sample", [NS * DEC, D])
    I.meta = din("meta_tokens", [N_META, D])
    I.cache_k = din("cache_fox_k", [NS, PAST, 2048])
    I.cache_v = din("cache_fox_v", [NS, PAST, 2048])
    I.cache_logf = din("cache_fox_logf", [NS, PAST, 16])
    I.st_conv = din("state_ssd_conv", [NS, 3, CONV_DIM])
    I.st_ssd = din("state_ssd", [NS, 2048, 128])
    I.st_c = din("state_mlstm_c", [NS, 2048, 128])
    I.st_n = din("state_mlstm_n", [NS, 8, 128])
    I.st_m = din("state_mlstm_m", [NS, 8])
    I.w_in_a = din("w_in_a", [D, E_A])
    I.b_fgate_a = din("b_fgate_a", [1, 16])
    I.conv_w = din("conv_w", [4, CONV_DIM])
    I.conv_b = din("conv_b", [1, CONV_DIM])
    I.dt_bias = din("dt_bias", [1, 32])
    I.a_log = din("a_log", [1, 32])
    I.d_skip = din("d_skip", [1, 32])
    I.ssd_norm_g = din("ssd_norm_g", [1, 2048])
    I.w_out_a = din("w_out_a", [4096, D])
    I.w_in_c = din("w_in_c", [D, E_C])
    I.b_igate_c = din("b_igate_c", [1, 8])
    I.b_fgate_c = din("b_fgate_c", [1, 8])
    I.mlstm_norm_g = din("mlstm_norm_g", [1, 2048])
    I.w_out_c = din("w_out_c", [2048, D])
    I.ln_mix_g = din("ln_mix_g", [2, D])
    I.ln_mix_b = din("ln_mix_b", [2, D])
    I.ln_ffn_g = din("ln_ffn_g", [2, D])
    I.ln_ffn_b = din("ln_ffn_b", [2, D])
    I.w_ffn_gate = din("w_ffn_gate", [2, D, D_FF])
    I.w_ffn_up = din("w_ffn_up", [2, D, D_FF])
    I.w_ffn_down = din("w_ffn_down", [2, D_FF, D])
    I.c_ident = din("c_ident", [128, 128])
    I.c_tri = din("c_tri", [128, 128])
    I.c_negtri = din("c_negtri", [128, 128])

    O = Ctx()
    O.y = dout("o_y", [NTOK, D])
    O.fox_k = dout("o_fox_k", [NTOK, 2048])
    O.fox_v = dout("o_fox_v", [NTOK, 2048])
    O.fox_logf = dout("o_fox_logf", [NTOK, 16])
    O.ssd_conv = dout("o_ssd_conv", [1 + NS, 3, CONV_DIM])
    O.ssd_state = dout("o_ssd_state", [1 + NS, 2048, 128])
    O.mlstm_c = dout("o_mlstm_c", [1 + NS, 2048, 128])
    O.mlstm_n = dout("o_mlstm_n", [1 + NS, 8, 128])
    O.mlstm_m = dout("o_mlstm_m", [1 + NS, 8])

    S = Ctx()
    S.qT = dscr("s_qT", [2048, NTOK], BF16)
    S.kT = dscr("s_kT", [2048, NTOK], BF16)
    S.vbf = dscr("s_vbf", [NTOK, 2048], BF16)
    S.kTp = dscr("s_kTp", [NS, 2048, PAST], BF16)
    S.vp = dscr("s_vp", [NS, PAST, 2048], BF16)
    S.logf = dscr("s_logf", [NTOK, 16])
    S.z = dscr("s_z", [NTOK, 2048])
    S.xbcT = dscr("s_xbcT", [CONV_DIM, NTOK])
    S.dtraw = dscr("s_dtraw", [NTOK, 32])
    S.xtok = dscr("s_xtok", [NTOK, 2048])
    S.bmtok = dscr("s_bmtok", [NTOK, 512], BF16)
    S.bcT = dscr("s_bcT", [1024, NTOK], BF16)
    S.mixT = dscr("s_mixT", [4096, NTOK], BF16)
    S.mix = dscr("s_mix", [NTOK, D])
    S.mix2 = dscr("s_mix2", [NTOK, D])
    S.x0 = dscr("s_x0", [NTOK, D])
    S.x1T = dscr("s_x1T", [D, NTOK], BF16)
    S.ktok = dscr("s_ktok", [NTOK, 1024], BF16)
    S.gates = dscr("s_gates", [NTOK, 16])
    S.x1 = dscr("s_x1", [NTOK, D])
    S.x2 = dscr("s_x2", [NTOK, D])

    NKT_ = (max(NP, PAST + DEC) + 127) // 128
    C.dbgF = dscr("s_dbgF", [128, NKT_ * 16]) if debug else None
    C.dbgL = dscr("s_dbgL", [128, NKT_ * 16]) if debug else None
    seqs = [(0, NP, 0, 0)] + [(NP + s * DEC, DEC, PAST, 1 + s) for s in range(NS)]

    with ExitStack() as gst:
        P = Prog(nc, gst)
        C.ident = P.sb(gst, "ident", [128, 128], F32)
        C.tri = P.sb(gst, "tri", [128, 128], F32)
        C.negtri = P.sb(gst, "negtri", [128, 128], F32)
        C.tri_bf = P.sb(gst, "tri_bf", [128, 128], BF16)
        C.ones_bf = P.sb(gst, "ones_bf", [128, 128], BF16)
        C.ones_f = P.sb(gst, "ones_f", [128, 128], F32)
        C.sel0 = P.sb(gst, "sel0", [128, 128], F32)
        P.load(C.ident, C.ident[:], I.c_ident)
        P.load(C.tri, C.tri[:], I.c_tri)
        P.load(C.negtri, C.negtri[:], I.c_negtri)
        P.op("dve", lambda e: e.tensor_copy(C.tri_bf[:], C.tri[:]), reads=[C.tri], writes=[C.tri_bf])
        P.op("dve", lambda e: e.memset(C.ones_bf[:], 1.0), writes=[C.ones_bf])
        P.op("dve", lambda e: e.memset(C.ones_f[:], 1.0), writes=[C.ones_f])
        P.op("dve", lambda e: e.tensor_copy(C.sel0[:], C.ident[:, 0:1].to_broadcast([128, 128])), reads=[C.ident],
             writes=[C.sel0])
        C.negtri4 = P.sb(gst, "negtri4", [128, 512], F32)
        P.op("dve", lambda e: e.tensor_copy(C.negtri4[:].rearrange("p (h l) -> p h l", h=4),
                                            C.negtri[:, :].unsqueeze(1).to_broadcast([128, 4, 128])),
             reads=[C.negtri], writes=[C.negtri4])
        C.sellast = P.sb(gst, "sellast", [128, 128], F32)
        P.op("dve", lambda e: e.tensor_copy(C.sellast[:], C.ident[:, 127:128].to_broadcast([128, 128])), reads=[C.ident],
             writes=[C.sellast])
        psum = [P.ps(gst, f"psb{i}", [128, 512], F32) for i in range(8)]
        C.psum = psum

        def load_x0(xb, tok0, n):
            r = tok0
            done = 0
            while done < n:
                rr = r + done
                if rr < N_META:
                    m = min(n - done, N_META - rr)
                    src = I.meta[rr:rr + m, :]
                elif rr < NP:
                    m = min(n - done, NP - rr)
                    src = I.x_prompt[rr - N_META:rr - N_META + m, :]
                else:
                    m = n - done
                    src = I.x_sample[rr - NP:rr - NP + m, :]
                P.load(xb, xb[done:done + m, :], src)
                done += m

        phase_inproj_a(P, C, I, O, S, load_x0)
        P.end_phase()
        if stop_after == "inproj_a":
            return nc
        phase_cache_prep(P, C, I, S)
        P.end_phase()
        if stop_after == "cache":
            return nc
        phase_attention(P, C, I, O, S, seqs)
        P.end_phase()
        if stop_after == "attn":
            return nc
        phase_ssd_conv(P, C, I, O, S, seqs)
        P.end_phase()
        phase_ssd(P, C, I, O, S, seqs)
        P.end_phase()
        if stop_after == "ssd":
            return nc
        phase_outproj(P, C, I.w_out_a, 32, S.mixT, S.mix)
        P.end_phase()
        phase_ln_T(P, C, load_x0, S.mix, I.ln_mix_g[0:1, :], I.ln_mix_b[0:1, :], S.x1, S.x1T)
        P.end_phase()
        phase_ffn(P, C, S.x1T, I.w_ffn_gate[0], I.w_ffn_up[0], I.w_ffn_down[0], S.mix2)
        P.end_phase()
        if stop_after == "l0":
            phase_final_ln(P, C, S.x1, S.mix2, I.ln_ffn_g[0:1, :], I.ln_ffn_b[0:1, :], O.y)
            P.end_phase()
            return nc
        phase_inproj_c(P, C, I, S, lambda st: make_ln_loader(P, C, st, S.x1, S.mix2, I.ln_ffn_g[0:1, :], I.ln_ffn_b[0:1, :], S.x2))
        P.end_phase()
        phase_mlstm(P, C, I, O, S, seqs)
        P.end_phase()
        phase_outproj(P, C, I.w_out_c, 16, S.mixT[0:2048, :], S.mix)
        P.end_phase()
        phase_ln_T(P, C, S.x2, S.mix, I.ln_mix_g[1:2, :], I.ln_mix_b[1:2, :], S.x1, S.x1T)
        P.end_phase()
        phase_ffn(P, C, S.x1T, I.w_ffn_gate[1], I.w_ffn_up[1], I.w_ffn_down[1], S.mix2)
        P.end_phase()
        phase_final_ln(P, C, S.x1, S.mix2, I.ln_ffn_g[1:2, :], I.ln_ffn_b[1:2, :], O.y)
        P.end_phase()
    return nc


def phase_x0_copy(P, C, S, load_x0):
    with ExitStack() as st:
        r = Ring([P.sb(st, f"x0c{i}", [128, D], F32) for i in range(3)])
        for (t0, n) in _tiles(C.NTOK, 128):
            b = r.next()
            load_x0(b, t0, n)
            P.store(b, S.x0[t0:t0 + n, :], b[0:n, :])


def phase_inproj_a(P, C, I, O, S, load_x0):
    NTOK = C.NTOK
    scale = DH_A ** -0.5
    with ExitStack() as st:
        TT = 1152
        xT = P.sb(st, "xT", [128, 16, TT + 128], BF16)
        P.op("pool", lambda e: e.memset(xT[:], 0.0), writes=[xT])
        xin = Ring([P.sb(st, f"xin{i}", [128, D], F32) for i in range(2)])
        for b_ in xin.items:
            P.op("pool", lambda e, b_=b_: e.memset(b_[:], 0.0), writes=[b_])
        wring = Ring([P.sb(st, f"w{i}", [128, 16, 512], BF16) for i in range(2)])
        sf = Ring([P.sb(st, f"sf{i}", [128, 512], F32) for i in range(3)])
        sh = Ring([P.sb(st, f"sh{i}", [128, 512], BF16) for i in range(3)])
        sm = Ring([P.sb(st, f"sm{i}", [128, 16], F32) for i in range(3)])
        bfb = P.sb(st, "bfb", [128, 16], F32)
        P.load(bfb, bfb[:], I.b_fgate_a[0:1, :].to_broadcast([128, 16]))
        psT = Ring(C.psum[0:2])
        psG = Ring(C.psum[2:8])
        for (T0, NT) in supertiles(NTOK, TT):
            for (t0, n) in _tiles(NT, 128):
                build_xT(P, C, st, xT, T0 + t0, n, t0, load_x0, xin, psT)

            def ev_qk(dst, mul):
                def f(ps, t0, n, col0, ncol):
                    b = sh.next()
                    P.op("act", lambda e: e.activation(b[0:ncol, 0:n], ps[0:ncol, 0:n], AF.Copy, scale=float(mul)),
                         reads=[ps], writes=[b])
                    P.store(b, dst[col0:col0 + ncol, T0 + t0:T0 + t0 + n], b[0:ncol, 0:n])
                return f

            def ev_ktok(ps, t0, n, col0, ncol):
                b = sf.next()
                P.op("act", lambda e: e.copy(b[0:n, 0:ncol], ps[0:n, 0:ncol]), reads=[ps], writes=[b])
                P.store(b, O.fox_k[T0 + t0:T0 + t0 + n, col0 - 2048:col0 - 2048 + ncol], b[0:n, 0:ncol])

            def ev_v(ps, t0, n, col0, ncol):
                b = sf.next()
                P.op("act", lambda e: e.copy(b[0:n, 0:ncol], ps[0:n, 0:ncol]), reads=[ps], writes=[b])
                P.store(b, O.fox_v[T0 + t0:T0 + t0 + n, col0 - 4096:col0 - 4096 + ncol], b[0:n, 0:ncol])
                b2 = sh.next()
                P.op("dve", lambda e: e.tensor_copy(b2[0:n, 0:ncol], b[0:n, 0:ncol]), reads=[b], writes=[b2])
                P.store(b2, S.vbf[T0 + t0:T0 + t0 + n, col0 - 4096:col0 - 4096 + ncol], b2[0:n, 0:ncol])

            def ev_fg(ps, t0, n, col0, ncol):
                b = sm.next()
                P.op("dve", lambda e: e.tensor_tensor(b[0:n, :], ps[0:n, 0:16], bfb[0:n, :], ALU.add),
                     reads=[ps, bfb], writes=[b])
                P.op("act", lambda e: e.activation(b[0:n, :], b[0:n, :], AF.Exp, scale=-1.0), reads=[b], writes=[b])
                P.op("act", lambda e: e.activation(b[0:n, :], b[0:n, :], AF.Ln, bias=1.0), reads=[b], writes=[b])
                P.op("act", lambda e: e.mul(b[0:n, :], b[0:n, :], -1.0), reads=[b], writes=[b])
                P.store(b, O.fox_logf[T0 + t0:T0 + t0 + n, :], b[0:n, :])
                P.store(b, S.logf[T0 + t0:T0 + t0 + n, :], b[0:n, :])

            def ev_z(ps, t0, n, col0, ncol):
                b = sf.next()
                P.op("act", lambda e: e.copy(b[0:n, 0:ncol], ps[0:n, 0:ncol]), reads=[ps], writes=[b])
                P.store(b, S.z[T0 + t0:T0 + t0 + n, col0 - 6160:col0 - 6160 + ncol], b[0:n, 0:ncol])

            def ev_xbc(ps, t0, n, col0, ncol):
                b = sf.next()
                P.op("act", lambda e: e.copy(b[0:ncol, 0:n], ps[0:ncol, 0:n]), reads=[ps], writes=[b])
                P.store(b, S.xbcT[col0 - 8208:col0 - 8208 + ncol, T0 + t0:T0 + t0 + n], b[0:ncol, 0:n])

            def ev_dt(ps, t0, n, col0, ncol):
                b = sf.next()
                P.op("act", lambda e: e.copy(b[0:n, 0:32], ps[0:n, 0:32]), reads=[ps], writes=[b])
                P.store(b, S.dtraw[T0 + t0:T0 + t0 + n, :], b[0:n, 0:32])

            segs = [
                (0, 2048, "feat", ev_qk(S.qT, scale)),
                (2048, 4096, "feat", lambda ps, t0, n, col0, ncol: ev_qk(S.kT, 1.0)(ps, t0, n, col0 - 2048, ncol)),
                (2048, 4096, "tok", ev_ktok),
                (4096, 6144, "tok", ev_v),
                (6144, 6160, "tok", ev_fg),
                (6160, 8208, "tok", ev_z),
                (8208, 11280, "feat", ev_xbc),
                (11280, 11312, "tok", ev_dt),
            ]
            import os
            if os.environ.get("K_SEGS"):
                segs = [segs[int(i)] for i in os.environ["K_SEGS"].split(",")]
            gemm_segments(P, C, xT, 16, NT, I.w_in_a, segs, wring, psG, 512)


def phase_cache_prep(P, C, I, S):
    with ExitStack() as st:
        kin = Ring([P.sb(st, f"kin{i}", [128, 2048], F32) for i in range(2)])
        vin = Ring([P.sb(st, f"vin{i}", [128, 2048], F32) for i in range(2)])
        kts = Ring([P.sb(st, f"kts{i}", [128, 16, 128], BF16) for i in range(2)])
        vbs = Ring([P.sb(st, f"vbs{i}", [128, 2048], BF16) for i in range(2)])
        psT = Ring(C.psum[0:4])
        for s in range(C.NS):
            for (k0, n) in _tiles(C.PAST, 128):
                kb = kin.next()
                P.load(kb, kb[0:n, :], I.cache_k[s, k0:k0 + n, :])
                ko = kts.next()
                for c4 in range(4):
                    ps = psT.next()
                    for j in range(4):
                        c = c4 * 4 + j
                        P.op("pe", lambda e, ps=ps, kb=kb, c=c, j=j, n=n: e.transpose(
                            ps[0:128, j * 128:j * 128 + n], kb[0:n, c * 128:(c + 1) * 128], C.ident[0:n, 0:n]),
                            reads=[kb, C.ident], writes=[ps])
                    eng = "dve" if c4 % 2 == 0 else "act"
                    if eng == "dve":
                        P.op("dve", lambda e, ps=ps, c4=c4, ko=ko, n=n: e.tensor_copy(
                            ko[:, c4 * 4:(c4 + 1) * 4, 0:n], ps[:].rearrange("p (j m) -> p j m", j=4)[:, :, 0:n]),
                            reads=[ps], writes=[ko])
                    else:
                        P.op("act", lambda e, ps=ps, c4=c4, ko=ko, n=n: e.copy(
                            ko[:, c4 * 4:(c4 + 1) * 4, 0:n], ps[:].rearrange("p (j m) -> p j m", j=4)[:, :, 0:n]),
                            reads=[ps], writes=[ko])
                P.store(ko, S.kTp[s, :, k0:k0 + n].rearrange("(c p) m -> p c m", p=128), ko[:, :, 0:n])
                vb = vin.next()
                P.load(vb, vb[0:n, :], I.cache_v[s, k0:k0 + n, :], q="act")
                vo = vbs.next()
                P.op("pool", lambda e, vo=vo, vb=vb, n=n: e.tensor_copy(vo[0:n, :], vb[0:n, :]), reads=[vb], writes=[vo])
                P.store(vo, S.vp[s, k0:k0 + n, :], vo[0:n, :], q="act")


def phase_attention(P, C, I, O, S, seqs):
    with ExitStack() as st:
        Smax = max(past + T for (_, T, past, _) in seqs)
        NKT = (Smax + 127) // 128
        Tmax = max(T for (_, T, _, _) in seqs)
        F_all = P.sb(st, "F_all", [128, NKT, 16], F32)
        lfin = P.sb(st, "lfin", [128, NKT, 16], F32)
        kTh = Ring([P.sb(st, f"kTh{i}", [128, NKT * 128], BF16) for i in range(2)])
        vh = Ring([P.sb(st, f"vh{i}", [128, NKT, 128], BF16) for i in range(2)])
        qTh = Ring([P.sb(st, f"qTh{i}", [128, Tmax], BF16) for i in range(2)])
        pT = Ring([P.sb(st, f"pT{i}", [128, 512], BF16) for i in range(3)])
        bcol = Ring([P.sb(st, f"bcol{i}", [128, 1], F32) for i in range(4)])
        c_b = P.sb(st, "c_b", [128, 16], F32)
        rl = P.sb(st, "rl", [128, 512], F32)
        for b_ in kTh.items + vh.items + [F_all, lfin]:
            P.op("pool", lambda e, b_=b_: e.memset(b_[:], 0.0), writes=[b_])
        osb = Ring([P.sb(st, f"osb{i}", [128, 512], BF16) for i in range(2)])
        ps_s = Ring(C.psum[0:2])
        ps_o = Ring(C.psum[2:4])
        ps_l = Ring(C.psum[4:6])
        ps_m = Ring(C.psum[6:8])
        ones_part = P.sb(st, "ones_part", [128, 128], BF16)
        P.op("dve", lambda e: e.memset(ones_part[:], 0.0), writes=[ones_part])
        P.op("dve", lambda e: e.memset(ones_part[0:16, :], 1.0), writes=[ones_part])
        for (tok0, T, past, sidx) in seqs:
            Sk = past + T
            ktiles = _tiles(Sk, 128)
            assert all(nk in (128, 16) for (_, nk) in ktiles)
            npast_t = past // 128
            for j0 in range(0, npast_t, 16):
                j1 = min(npast_t, j0 + 16)
                P.load(lfin, lfin[:, j0:j1, :], I.cache_logf[sidx - 1, j0 * 128:j1 * 128, :].rearrange("(j p) h -> p j h", p=128))
            nfull = T // 128
            for j0 in range(0, nfull, 16):
                j1 = min(nfull, j0 + 16)
                P.load(lfin, lfin[:, npast_t + j0:npast_t + j1, :],
                       S.logf[tok0 + j0 * 128:tok0 + j1 * 128, :].rearrange("(j p) h -> p j h", p=128))
            rem = T - nfull * 128
            if rem:
                P.load(lfin, lfin[0:rem, npast_t + nfull, :], S.logf[tok0 + nfull * 128:tok0 + T, :])
            for j, (k0, nk) in enumerate(ktiles):
                ps = ps_m.next()
                P.op("pe", lambda e, ps=ps, j=j, nk=nk: e.matmul(ps[0:128, 0:16], C.tri[:, :], lfin[:, j, :],
                                                                  start=True, stop=(j == 0)),
                     reads=[C.tri, lfin], writes=[ps])
                if j > 0:
                    P.op("pe", lambda e, ps=ps, j=j, nk=nk: e.matmul(ps[0:128, 0:16], C.sellast[:, 0:128], F_all[:, j - 1, :],
                                                                      start=False, stop=True),
                         reads=[C.sellast, F_all], writes=[ps])
                P.op("dve", lambda e, ps=ps, j=j, nk=nk: e.tensor_copy(F_all[0:nk, j, :], ps[0:nk, 0:16]),
                     reads=[ps], writes=[F_all])
            if C.dbgF is not None and sidx == 0:
                P.store(F_all, C.dbgF, F_all[:].rearrange("p j h -> p (j h)"))
                P.store(lfin, C.dbgL, lfin[:].rearrange("p j h -> p (j h)"))
            for h in range(H_A):
                kb = kTh.next()
                vb = vh.next()
                qb = qTh.next()
                r0 = h * 128
                if past:
                    P.load(kb, kb[:, 0:past], S.kTp[sidx - 1, r0:r0 + 128, :])
                    for j0 in range(0, npast_t, 16):
                        j1 = min(npast_t, j0 + 16)
                        P.load(vb, vb[:, j0:j1, :], S.vp[sidx - 1, j0 * 128:j1 * 128, r0:r0 + 128].rearrange("(j p) d -> p j d", p=128),
                               q="act")
                P.load(kb, kb[:, past:past + T], S.kT[r0:r0 + 128, tok0:tok0 + T])
                for j0 in range(0, nfull, 16):
                    j1 = min(nfull, j0 + 16)
                    P.load(vb, vb[:, npast_t + j0:npast_t + j1, :],
                           S.vbf[tok0 + j0 * 128:tok0 + j1 * 128, r0:r0 + 128].rearrange("(j p) d -> p j d", p=128), q="act")
                if rem:
                    P.op("pool", lambda e, vb=vb, jj=npast_t + nfull: e.memset(vb[:, jj, :], 0.0), writes=[vb])
                    P.load(vb, vb[0:rem, npast_t + nfull, :], S.vbf[tok0 + nfull * 128:tok0 + T, r0:r0 + 128], q="act")
                P.load(qb, qb[:, 0:T], S.qT[r0:r0 + 128, tok0:tok0 + T])
                for (q0, nq) in _tiles(T, 512):
                    qa = past + q0
                    ja = qa // 128
                    assert qa % 128 == 0
                    if h == 0 or True:
                        psc = ps_m.next()
                        P.op("pe", lambda e, psc=psc, ja=ja: e.matmul(psc[:, 0:16], C.sel0[:, :], F_all[:, ja, :],
                                                                        start=True, stop=True),
                             reads=[C.sel0, F_all], writes=[psc])
                        P.op("dve", lambda e, psc=psc: e.tensor_copy(c_b[:], psc[:, 0:16]), reads=[psc], writes=[c_b])
                    po = ps_o.next()
                    pl = ps_l.next()
                    jl = [j for j, (k0, nk) in enumerate(ktiles) if k0 <= qa + nq - 1]
                    for j in jl:
                        k0, nk = ktiles[j]
                        fs = max(0, k0 - qa)
                        bc = bcol.next()
                        if nk < 128:
                            P.op("dve", lambda e, bc=bc: e.memset(bc[:], -30000.0), writes=[bc])
                        P.op("dve", lambda e, bc=bc, j=j, nk=nk, h=h: e.tensor_tensor(
                            bc[0:nk, :], c_b[0:nk, h:h + 1], F_all[0:nk, j, h:h + 1], ALU.subtract),
                            reads=[c_b, F_all], writes=[bc])
                        pss = ps_s.next()
                        P.op("pe", lambda e, pss=pss, kb=kb, qb=qb, k0=k0, nk=nk, fs=fs, q0=q0, nq=nq: e.matmul(
                            pss[0:128, fs:nq], kb[:, k0:k0 + 128], qb[:, q0 + fs:q0 + nq], start=True, stop=True),
                            reads=[kb, qb], writes=[pss])
                        pt = pT.next()
                        P.op("act", lambda e, pt=pt, pss=pss, bc=bc, nk=nk, fs=fs, nq=nq: e.activation(
                            pt[0:128, fs:nq], pss[0:128, fs:nq], AF.Exp, bias=bc[0:128, :]),
                            reads=[pss, bc], writes=[pt])
                        if k0 >= qa:
                            w = min(nk, nq - fs)
                            P.op("dve", lambda e, pt=pt, nk=nk, fs=fs, w=w: e.tensor_tensor(
                                pt[0:nk, fs:fs + w], pt[0:nk, fs:fs + w], C.tri_bf[0:nk, 0:w], ALU.mult),
                                reads=[pt, C.tri_bf], writes=[pt])
                        first = (j == jl[0])
                        last = (j == jl[-1])
                        P.op("pe", lambda e, po=po, vb=vb, pt=pt, j=j, nk=nk, fs=fs, nq=nq, first=first, last=last: e.matmul(
                            po[:, fs:nq], vb[0:128, j, :], pt[0:128, fs:nq], start=first, stop=last),
                            reads=[vb, pt], writes=[po])
                        P.op("pe", lambda e, pl=pl, pt=pt, nk=nk, fs=fs, nq=nq, first=first, last=last: e.matmul(
                            pl[:, fs:nq], (C.ones_bf if nk == 128 else ones_part)[0:128, :], pt[0:128, fs:nq],
                            start=first, stop=last),
                            reads=[C.ones_bf, ones_part, pt], writes=[pl])
                    P.op("dve", lambda e, pl=pl, nq=nq: e.reciprocal(rl[:, 0:nq], pl[:, 0:nq]), reads=[pl], writes=[rl])
                    ob = osb.next()
                    P.op("dve", lambda e, po=po, ob=ob, nq=nq: e.tensor_tensor(ob[:, 0:nq], po[:, 0:nq], rl[:, 0:nq], ALU.mult),
                         reads=[po, rl], writes=[ob])
                    P.store(ob, S.mixT[r0:r0 + 128, tok0 + q0:tok0 + q0 + nq], ob[:, 0:nq])


def _consts():
    idx = np.arange(128)
    ident = np.eye(128, dtype=np.float32)
    tri = (idx[:, None] <= idx[None, :]).astype(np.float32)
    negtri = np.where(idx[None, :] < idx[:, None], np.float32(NEG), np.float32(0.0)).astype(np.float32)
    return {"c_ident": ident, "c_tri": tri, "c_negtri": negtri}


def make_in_map(inp, core, NS=2):
    f = lambda a: np.ascontiguousarray(a, dtype=np.float32)
    s0, s1 = core * NS, (core + 1) * NS
    past = inp["cache_fox_k"].shape[2]
    m = {
        "x_prompt": f(inp["x_prompt"][core]),
        "x_sample": f(inp["x_sample"][s0:s1].reshape(NS * DEC, D)),
        "meta_tokens": f(inp["meta_tokens"]),
        "cache_fox_k": f(inp["cache_fox_k"][0, s0:s1].reshape(NS, past, 2048)),
        "cache_fox_v": f(inp["cache_fox_v"][0, s0:s1].reshape(NS, past, 2048)),
        "cache_fox_logf": f(inp["cache_fox_logf"][0, s0:s1]),
        "state_ssd_conv": f(inp["state_ssd_conv"][0, s0:s1]),
        "state_ssd": f(inp["state_ssd"][0, s0:s1].reshape(NS, 2048, 128)),
        "state_mlstm_c": f(inp["state_mlstm_c"][0, s0:s1].reshape(NS, 2048, 128)),
        "state_mlstm_n": f(inp["state_mlstm_n"][0, s0:s1]),
        "state_mlstm_m": f(inp["state_mlstm_m"][0, s0:s1]),
        "w_in_a": f(inp["w_in_a"][0]),
        "b_fgate_a": f(inp["b_fgate_a"]),
        "conv_w": f(inp["conv_w"][0]),
        "conv_b": f(inp["conv_b"]),
        "dt_bias": f(inp["dt_bias"]),
        "a_log": f(inp["a_log"]),
        "d_skip": f(inp["d_skip"]),
        "ssd_norm_g": f(inp["ssd_norm_g"]),
        "w_out_a": f(inp["w_out_a"][0]),
        "w_in_c": f(inp["w_in_c"][0]),
        "b_igate_c": f(inp["b_igate_c"]),
        "b_fgate_c": f(inp["b_fgate_c"]),
        "mlstm_norm_g": f(inp["mlstm_norm_g"]),
        "w_out_c": f(inp["w_out_c"][0]),
        "ln_mix_g": f(inp["ln_mix_g"]),
        "ln_mix_b": f(inp["ln_mix_b"]),
        "ln_ffn_g": f(inp["ln_ffn_g"]),
        "ln_ffn_b": f(inp["ln_ffn_b"]),
        "w_ffn_gate": f(inp["w_ffn_gate"]),
        "w_ffn_up": f(inp["w_ffn_up"]),
        "w_ffn_down": f(inp["w_ffn_down"]),
    }
    m.update(_consts())
    return m


def bview(ap, shape):
    return ap.to_broadcast(list(shape))


def phase_ssd_conv(P, C, I, O, S, seqs):
    with ExitStack() as st:
        Tmax = max(T for (_, T, _, _) in seqs)
        Tpad = ((Tmax + 127) // 128) * 128
        cwin = P.sb(st, "cwin", [128, CONV_DIM], F32)
        hin = P.sb(st, "hin", [128, CONV_DIM], F32)
        cw = P.sb(st, "cw", [128, 24, 8], F32)
        hT = P.sb(st, "hT", [128, 24, 4], F32)
        xc = Ring([P.sb(st, f"xc{i}", [128, Tpad + 4], F32) for i in range(2)])
        acc = Ring([P.sb(st, f"acc{i}", [128, Tpad], F32) for i in range(2)])
        accb = Ring([P.sb(st, f"accb{i}", [128, Tpad], BF16) for i in range(2)])
        cst = P.sb(st, "cst", [128, 128], F32)
        convout = P.sb(st, "convout", [128, CONV_DIM], F32)
        stf = Ring([P.sb(st, f"stf{i}", [128, 128], F32) for i in range(3)])
        sth = Ring([P.sb(st, f"sth{i}", [128, 128], BF16) for i in range(3)])
        psT = Ring(C.psum[0:4])
        for b_ in [cwin, hin, cst] + xc.items + acc.items:
            P.op("pool", lambda e, b_=b_: e.memset(b_[:], 0.0), writes=[b_])
        P.load(cwin, cwin[0:4, :], I.conv_w)
        P.load(cwin, cwin[4:5, :], I.conv_b)
        for cc in range(24):
            ps = psT.next()
            P.op("pe", lambda e, ps=ps, cc=cc: e.transpose(ps[:, 0:128], cwin[:, cc * 128:(cc + 1) * 128], C.ident[:, :]),
                 reads=[cwin, C.ident], writes=[ps])
            P.op("dve", lambda e, ps=ps, cc=cc: e.tensor_copy(cw[:, cc, 0:5], ps[:, 0:5]), reads=[ps], writes=[cw])
        for (tok0, T, past, sidx) in seqs:
            if past:
                P.load(hin, hin[0:3, :], I.st_conv[sidx - 1])
                for cc in range(24):
                    ps = psT.next()
                    P.op("pe", lambda e, ps=ps, cc=cc: e.transpose(ps[:, 0:128], hin[:, cc * 128:(cc + 1) * 128], C.ident[:, :]),
                         reads=[hin, C.ident], writes=[ps])
                    P.op("dve", lambda e, ps=ps, cc=cc: e.tensor_copy(hT[:, cc, 0:3], ps[:, 0:3]), reads=[ps], writes=[hT])
            else:
                P.op("dve", lambda e: e.memset(hT[:], 0.0), writes=[hT])
            for cc in range(24):
                xb = xc.next()
                P.load(xb, xb[:, 3:3 + T], S.xbcT[cc * 128:(cc + 1) * 128, tok0:tok0 + T])
                P.op("dve", lambda e, xb=xb, cc=cc: e.tensor_copy(xb[:, 0:3], hT[:, cc, 0:3]), reads=[hT], writes=[xb])
                P.op("dve", lambda e, xb=xb, T=T: e.tensor_copy(cst[:, 0:3], xb[:, T:T + 3]), reads=[xb], writes=[cst])
                ps = psT.next()
                P.op("pe", lambda e, ps=ps: e.transpose(ps[:, 0:128], cst[:, :], C.ident[:, :]), reads=[cst, C.ident], writes=[ps])
                P.op("dve", lambda e, ps=ps, cc=cc: e.tensor_copy(convout[0:3, cc * 128:(cc + 1) * 128], ps[0:3, 0:128]),
                     reads=[ps], writes=[convout])
                a = acc.next()
                P.op("act", lambda e, a=a, xb=xb, cc=cc, T=T: e.activation(a[:, 0:T], xb[:, 0:T], AF.Identity,
                                                                         bias=cw[:, cc, 4:5], scale=cw[:, cc, 0:1]),
                     reads=[xb, cw], writes=[a])
                for i in range(1, 4):
                    P.op("dve", lambda e, a=a, xb=xb, cc=cc, T=T, i=i: e.scalar_tensor_tensor(
                        out=a[:, 0:T], in0=xb[:, i:i + T], scalar=cw[:, cc, i:i + 1], in1=a[:, 0:T], op0=ALU.mult, op1=ALU.add),
                        reads=[a, xb, cw], writes=[a])
                P.op("act", lambda e, a=a, T=T: e.activation(a[:, 0:T], a[:, 0:T], AF.Silu), reads=[a], writes=[a])
                if cc < 20:
                    for (b0, n) in _tiles(T, 128):
                        ps = psT.next()
                        P.op("pe", lambda e, ps=ps, a=a, b0=b0: e.transpose(ps[:, 0:128], a[:, b0:b0 + 128], C.ident[:, :]),
                             reads=[a, C.ident], writes=[ps])
                        if cc < 16:
                            sb_ = stf.next()
                            P.op("act", lambda e, ps=ps, sb_=sb_: e.copy(sb_[:, :], ps[:, 0:128]), reads=[ps], writes=[sb_])
                            P.store(sb_, S.xtok[tok0 + b0:tok0 + b0 + n, cc * 128:(cc + 1) * 128], sb_[0:n, :])
                        else:
                            sb_ = sth.next()
                            P.op("act", lambda e, ps=ps, sb_=sb_: e.copy(sb_[:, :], ps[:, 0:128]), reads=[ps], writes=[sb_])
                            P.store(sb_, S.bmtok[tok0 + b0:tok0 + b0 + n, (cc - 16) * 128:(cc - 15) * 128], sb_[0:n, :])
                if cc >= 16:
                    ab = accb.next()
                    P.op("pool", lambda e, ab=ab, a=a, T=T: e.tensor_copy(ab[:, 0:T], a[:, 0:T]), reads=[a], writes=[ab])
                    P.store(ab, S.bcT[(cc - 16) * 128:(cc - 15) * 128, tok0:tok0 + T], ab[:, 0:T])
            P.store(convout, O.ssd_conv[sidx], convout[0:3, :])


def phase_ssd(P, C, I, O, S, seqs):
    with ExitStack() as st:
        sb = lambda name, shape, dt=F32: P.sb(st, name, shape, dt)
        dtb_b = sb("dtb_b", [128, 32]); a_b = sb("a_b", [128, 32]); D_b = sb("D_b", [128, 32])
        g_b = sb("g_b", [128, 2048])
        P.load(dtb_b, dtb_b[:], bview(I.dt_bias[0:1, :], [128, 32]))
        P.load(a_b, a_b[:], bview(I.a_log[0:1, :], [128, 32]))
        P.load(D_b, D_b[:], bview(I.d_skip[0:1, :], [128, 32]))
        P.load(g_b, g_b[:], bview(I.ssd_norm_g[0:1, :], [128, 2048]))
        P.op("act", lambda e: e.activation(a_b[:], a_b[:], AF.Exp), reads=[a_b], writes=[a_b])
        P.op("act", lambda e: e.mul(a_b[:], a_b[:], -1.0), reads=[a_b], writes=[a_b])
        valid16 = sb("valid16", [128, 1])
        P.op("dve", lambda e: e.memset(valid16[:], 0.0), writes=[valid16])
        P.op("dve", lambda e: e.memset(valid16[0:16, :], 1.0), writes=[valid16])
        S_sb = sb("S_sb", [128, 2048]); S_bf = sb("S_bf", [128, 2048], BF16)
        sio = sb("sio", [128, 16, 128])
        x_c = Ring([sb(f"x_c{i}", [128, 2048]) for i in range(2)])
        z_c = Ring([sb(f"z_c{i}", [128, 2048]) for i in range(2)])
        bm_c = Ring([sb(f"bm_c{i}", [128, 512], BF16) for i in range(2)])
        bcT_c = Ring([sb(f"bcT_c{i}", [128, 8, 128], BF16) for i in range(2)])
        dt_c = Ring([sb(f"dt_c{i}", [128, 32]) for i in range(2)])
        dt = sb("dt", [128, 32]); lndt = sb("lndt", [128, 32]); dta = sb("dta", [128, 32]); A_sb = sb("A_sb", [128, 32])
        bias_all = sb("bias_all", [128, 32]); wend = sb("wend", [128, 32]); cd_b = sb("cd_b", [128, 32]); expA = sb("expA", [128, 32])
        D2 = sb("D2", [128, 32, 128])
        x_bf = sb("x_bf", [128, 2048], BF16); xw = sb("xw", [128, 2048], BF16)
        dec = Ring([sb(f"dec{i}", [128, 128]) for i in range(3)])
        WT = Ring([sb(f"WT{i}", [128, 128], BF16) for i in range(3)])
        y_c = sb("y_c", [128, 2048]); tmpb = sb("tmpb", [128, 2048]); junk = sb("junk", [128, 512])
        ss = sb("ss", [128, 4]); rstd = sb("rstd", [128, 4])
        yT = Ring([sb(f"yT{i}", [128, 16, 128], BF16) for i in range(2)])
        psE = [C.psum[0], C.psum[1]]
        psCB, ps_y, ps_off, ps_st = C.psum[2], C.psum[3], C.psum[4], C.psum[5]
        psm = Ring(C.psum[6:8])
        for b_ in x_c.items + z_c.items + bm_c.items + bcT_c.items + dt_c.items + [sio]:
            P.op("pool", lambda e, b_=b_: e.memset(b_[:], 0.0), writes=[b_])

        def v3(buf, lo, n):
            return buf[:, lo * 64:(lo + n) * 64].rearrange("p (h q) -> p h q", q=64)

        for (tok0, T, past, sidx) in seqs:
            if past:
                P.load(sio, sio[:], I.st_ssd[sidx - 1].rearrange("(c p) n -> p c n", p=128))
                for c in range(16):
                    ps = psm.next()
                    P.op("pe", lambda e, ps=ps, c=c: e.transpose(ps[:, 0:128], sio[:, c, :], C.ident[:, :]),
                         reads=[sio, C.ident], writes=[ps])
                    P.op("dve", lambda e, ps=ps, c=c: e.tensor_copy(S_sb[:, c * 128:(c + 1) * 128], ps[:, 0:128]),
                         reads=[ps], writes=[S_sb])
            else:
                P.op("dve", lambda e: e.memset(S_sb[:], 0.0), writes=[S_sb])
            P.op("act", lambda e: e.copy(S_bf[:], S_sb[:]), reads=[S_sb], writes=[S_bf])
            for (t0, L) in _tiles(T, 128):
                assert L in (128, 16)
                a0 = tok0 + t0
                xb = x_c.next(); zb = z_c.next(); bmb = bm_c.next(); bcb = bcT_c.next(); dtr = dt_c.next()
                if L < 128:
                    for b_ in (xb, zb, bmb, bcb, dtr):
                        P.op("pool", lambda e, b_=b_: e.memset(b_[:], 0.0), writes=[b_])
                P.load(xb, xb[0:L, :], S.xtok[a0:a0 + L, :])
                P.load(zb, zb[0:L, :], S.z[a0:a0 + L, :], q="act")
                P.load(bmb, bmb[0:L, :], S.bmtok[a0:a0 + L, :])
                P.load(bcb, bcb[:, :, 0:L], S.bcT[:, a0:a0 + L].rearrange("(g n) t -> n g t", n=128), q="act")
                P.load(dtr, dtr[0:L, :], S.dtraw[a0:a0 + L, :])
                P.op("dve", lambda e, dtr=dtr: e.tensor_tensor(dt[:], dtr[:], dtb_b[:], ALU.add), reads=[dtr, dtb_b], writes=[dt])
                P.op("act", lambda e: e.activation(dt[:], dt[:], AF.Exp), reads=[dt], writes=[dt])
                P.op("act", lambda e: e.activation(dt[:], dt[:], AF.Ln, bias=1.0), reads=[dt], writes=[dt])
                if L < 128:
                    P.op("dve", lambda e: e.tensor_scalar(dt[:], dt[:], valid16[:, 0:1], None, ALU.mult), reads=[dt, valid16], writes=[dt])
                P.op("dve", lambda e: e.tensor_scalar(lndt[:], dt[:], 1e-30, None, ALU.max), reads=[dt], writes=[lndt])
                P.op("act", lambda e: e.activation(lndt[:], lndt[:], AF.Ln), reads=[lndt], writes=[lndt])
                P.op("dve", lambda e: e.tensor_tensor(dta[:], dt[:], a_b[:], ALU.mult), reads=[dt, a_b], writes=[dta])
                ps = psm.next()
                P.op("pe", lambda e, ps=ps: e.matmul(ps[:, 0:32], C.tri[:, :], dta[:, :], start=True, stop=True),
                     reads=[C.tri, dta], writes=[ps])
                P.op("dve", lambda e, ps=ps: e.tensor_copy(A_sb[:], ps[:, 0:32]), reads=[ps], writes=[A_sb])
                pse = psm.next()
                P.op("pe", lambda e, pse=pse: e.matmul(pse[:, 0:32], C.sellast[:, :], A_sb[:, :], start=True, stop=True),
                     reads=[C.sellast, A_sb], writes=[pse])
                P.op("dve", lambda e: e.tensor_tensor(bias_all[:], lndt[:], A_sb[:], ALU.subtract), reads=[lndt, A_sb], writes=[bias_all])
                P.op("dve", lambda e, pse=pse: e.tensor_tensor(wend[:], pse[:, 0:32], bias_all[:], ALU.add), reads=[pse, bias_all], writes=[wend])
                P.op("act", lambda e: e.activation(wend[:], wend[:], AF.Exp), reads=[wend], writes=[wend])
                P.op("act", lambda e, pse=pse: e.activation(cd_b[:], pse[:, 0:32], AF.Exp), reads=[pse], writes=[cd_b])
                P.op("act", lambda e: e.activation(expA[:], A_sb[:], AF.Exp), reads=[A_sb], writes=[expA])
                P.op("dve", lambda e: e.tensor_tensor(D2[:], bview(C.ident[:, :].unsqueeze(1), [128, 32, 128]),
                                                      bview(A_sb[:, :].unsqueeze(2), [128, 32, 128]), ALU.mult),
                     reads=[C.ident, A_sb], writes=[D2])
                P.op("act", lambda e, xb=xb: e.copy(x_bf[:], xb[:]), reads=[xb], writes=[x_bf])
                P.op("dve", lambda e, xb=xb: e.tensor_tensor(v3(xw, 0, 32), v3(xb, 0, 32), bview(wend[:, :].unsqueeze(2), [128, 32, 64]), ALU.mult),
                     reads=[xb, wend], writes=[xw])
                for g in range(4):
                    for half in range(2):
                        pe_ = psE[half]
                        P.op("pe", lambda e, pe_=pe_, g=g, half=half: e.matmul(
                            pe_[:, 0:512], C.ones_f[:, :], D2[:, g * 8 + half * 4:g * 8 + half * 4 + 4, :].rearrange("p h l -> p (h l)"),
                            start=True, stop=False), reads=[C.ones_f, D2], writes=[pe_])
                        P.op("pe", lambda e, pe_=pe_: e.matmul(
                            pe_[:, 0:512], C.ident[:, :], C.negtri4[:, :], start=False, stop=True),
                            reads=[C.ident, C.negtri4], writes=[pe_])
                    P.op("pe", lambda e, bcb=bcb, g=g: e.matmul(psCB[:, 0:128], bcb[:, g, :], bcb[:, 4 + g, :], start=True, stop=True),
                         reads=[bcb], writes=[psCB])
                    for hh in range(8):
                        h = g * 8 + hh
                        pe_ = psE[hh // 4]
                        d_ = dec.next()
                        P.op("act", lambda e, d_=d_, pe_=pe_, hh=hh, h=h: e.activation(
                            d_[:, :], pe_[:, (hh % 4) * 128:(hh % 4 + 1) * 128], AF.Exp, bias=bias_all[:, h:h + 1]),
                            reads=[pe_, bias_all], writes=[d_])
                        w_ = WT.next()
                        P.op("dve", lambda e, d_=d_, w_=w_: e.tensor_tensor(w_[:, :], d_[:, :], psCB[:, 0:128], ALU.mult),
                             reads=[d_, psCB], writes=[w_])
                        P.op("pe", lambda e, w_=w_, hh=hh, h=h: e.matmul(ps_y[:, hh * 64:(hh + 1) * 64], w_[:, :], x_bf[:, h * 64:(h + 1) * 64],
                                                                          start=True, stop=True), reads=[w_, x_bf], writes=[ps_y])
                    P.op("pe", lambda e, bcb=bcb, g=g: e.matmul(ps_off[:, 0:512], bcb[:, 4 + g, :], S_bf[:, g * 512:(g + 1) * 512],
                                                                  start=True, stop=True), reads=[bcb, S_bf], writes=[ps_off])
                    P.op("dve", lambda e, g=g: e.tensor_tensor(v3(y_c, g * 8, 8), ps_off[:, 0:512].rearrange("p (h q) -> p h q", q=64),
                                                                bview(expA[:, g * 8:(g + 1) * 8].unsqueeze(2), [128, 8, 64]), ALU.mult),
                         reads=[ps_off, expA], writes=[y_c])
                    P.op("dve", lambda e, g=g: e.tensor_tensor(y_c[:, g * 512:(g + 1) * 512], y_c[:, g * 512:(g + 1) * 512], ps_y[:, 0:512], ALU.add),
                         reads=[y_c, ps_y], writes=[y_c])
                    P.op("pe", lambda e, bmb=bmb, g=g: e.matmul(ps_st[:, 0:512], bmb[:, g * 128:(g + 1) * 128], xw[:, g * 512:(g + 1) * 512],
                                                                  start=True, stop=True), reads=[bmb, xw], writes=[ps_st])
                    P.op("dve", lambda e, g=g: e.tensor_tensor(v3(S_sb, g * 8, 8), v3(S_sb, g * 8, 8),
                                                                bview(cd_b[:, g * 8:(g + 1) * 8].unsqueeze(2), [128, 8, 64]), ALU.mult),
                         reads=[S_sb, cd_b], writes=[S_sb])
                    P.op("dve", lambda e, g=g: e.tensor_tensor(S_sb[:, g * 512:(g + 1) * 512], S_sb[:, g * 512:(g + 1) * 512], ps_st[:, 0:512], ALU.add),
                         reads=[S_sb, ps_st], writes=[S_sb])
                    P.op("act", lambda e, g=g: e.copy(S_bf[:, g * 512:(g + 1) * 512], S_sb[:, g * 512:(g + 1) * 512]), reads=[S_sb], writes=[S_bf])
                P.op("pool", lambda e, xb=xb: e.tensor_tensor(v3(tmpb, 0, 32), v3(xb, 0, 32), bview(D_b[:, :].unsqueeze(2), [128, 32, 64]), ALU.mult),
                     reads=[xb, D_b], writes=[tmpb])
                P.op("dve", lambda e: e.tensor_tensor(y_c[:], y_c[:], tmpb[:], ALU.add), reads=[y_c, tmpb], writes=[y_c])
                P.op("act", lambda e, zb=zb: e.activation(tmpb[:], zb[:], AF.Silu), reads=[zb], writes=[tmpb])
                P.op("dve", lambda e: e.tensor_tensor(y_c[:], y_c[:], tmpb[:], ALU.mult), reads=[y_c, tmpb], writes=[y_c])
                for g in range(4):
                    P.op("act", lambda e, g=g: e.activation(junk[:, :], y_c[:, g * 512:(g + 1) * 512], AF.Square, accum_out=ss[:, g:g + 1]),
                         reads=[y_c], writes=[junk, ss])
                P.op("dve", lambda e: e.tensor_scalar(rstd[:], ss[:], 1.0 / 512, 1e-5, ALU.mult, ALU.add), reads=[ss], writes=[rstd])
                P.op("act", lambda e: e.activation(rstd[:], rstd[:], AF.Sqrt), reads=[rstd], writes=[rstd])
                P.op("dve", lambda e: e.reciprocal(rstd[:], rstd[:]), reads=[rstd], writes=[rstd])
                for g in range(4):
                    P.op("dve", lambda e, g=g: e.tensor_scalar(y_c[:, g * 512:(g + 1) * 512], y_c[:, g * 512:(g + 1) * 512], rstd[:, g:g + 1], None, ALU.mult),
                         reads=[y_c, rstd], writes=[y_c])
                P.op("pool", lambda e: e.tensor_tensor(y_c[:], y_c[:], g_b[:], ALU.mult), reads=[y_c, g_b], writes=[y_c])
                yt = yT.next()
                for c4 in range(4):
                    ps = psm.next()
                    for j in range(4):
                        c = c4 * 4 + j
                        P.op("pe", lambda e, ps=ps, c=c, j=j: e.transpose(ps[:, j * 128:(j + 1) * 128], y_c[:, c * 128:(c + 1) * 128], C.ident[:, :]),
                             reads=[y_c, C.ident], writes=[ps])
                    P.op("act", lambda e, ps=ps, c4=c4, yt=yt: e.copy(yt[:, c4 * 4:(c4 + 1) * 4, :], ps[:].rearrange("p (j m) -> p j m", j=4)),
                         reads=[ps], writes=[yt])
                P.store(yt, S.mixT[2048:4096, a0:a0 + L].rearrange("(c p) t -> p c t", p=128), yt[:, :, 0:L])
            for c in range(16):
                ps = psm.next()
                P.op("pe", lambda e, ps=ps, c=c: e.transpose(ps[:, 0:128], S_sb[:, c * 128:(c + 1) * 128], C.ident[:, :]),
                     reads=[S_sb, C.ident], writes=[ps])
                P.op("dve", lambda e, ps=ps, c=c: e.tensor_copy(sio[:, c, :], ps[:, 0:128]), reads=[ps], writes=[sio])
            P.store(sio, O.ssd_state[sidx].rearrange("(c p) n -> p c n", p=128), sio[:])


def phase_outproj(P, C, W, Kc, srcT, dst):
    with ExitStack() as st:
        TT = 1152
        wcols = 512 if Kc <= 16 else 256
        xT = P.sb(st, "xT", [128, Kc, TT + 128], BF16)
        P.op("pool", lambda e: e.memset(xT[:], 0.0), writes=[xT])
        wring = Ring([P.sb(st, f"w{i}", [128, Kc, wcols], BF16) for i in range(2)])
        sf = Ring([P.sb(st, f"sf{i}", [128, 512], F32) for i in range(3)])
        psG = Ring(C.psum[0:8])
        for (T0, NT) in supertiles(C.NTOK, TT):
            for k0 in range(0, Kc, 16):
                P.load(xT, xT[:, k0:k0 + 16, 0:NT], srcT[k0 * 128:(k0 + 16) * 128, T0:T0 + NT].rearrange("(c p) t -> p c t", p=128),
                       q=("sp" if (k0 // 16) % 2 == 0 else "act"))

            def ev(ps, t0, n, col0, ncol, T0=T0):
                b = sf.next()
                P.op("act", lambda e: e.copy(b[0:n, 0:ncol], ps[0:n, 0:ncol]), reads=[ps], writes=[b])
                P.store(b, dst[T0 + t0:T0 + t0 + n, col0:col0 + ncol], b[0:n, 0:ncol])

            gemm_segments(P, C, xT, Kc, NT, W, [(0, 2048, "tok", ev)], wring, psG, wcols)


def make_ln_loader(P, C, st, xa_src, xb_src, g_ap, b_ap, x_dst):
    g_b = P.sb(st, "lng", [128, D], F32)
    b_b = P.sb(st, "lnb", [128, D], F32)
    P.load(g_b, g_b[:], bview(g_ap, [128, D]))
    P.load(b_b, b_b[:], bview(b_ap, [128, D]))
    mring = Ring([P.sb(st, f"lnm{i}", [128, D], F32) for i in range(2)])
    aring = Ring([P.sb(st, f"lna{i}", [128, D], F32) for i in range(2)])
    stats = P.sb(st, "lnstats", [128, 24], F32)
    mv = P.sb(st, "lnmv", [128, 2], F32)
    rstd = P.sb(st, "lnrstd", [128, 1], F32)

    def loader(xb, tok0, n):
        xa = aring.next()
        mb = mring.next()
        if callable(xa_src):
            xa_src(xa, tok0, n)
        else:
            P.load(xa, xa[0:n, :], xa_src[tok0:tok0 + n, :])
        P.load(mb, mb[0:n, :], xb_src[tok0:tok0 + n, :], q="act")
        layer_norm_tile(P, C, (stats, mv, rstd), xa, mb, n, g_b, b_b, xb)
        P.store(xb, x_dst[tok0:tok0 + n, :], xb[0:n, :])

    return loader


def phase_ffn(P, C, xT_src, wg, wu, wd, dst):
    with ExitStack() as st:
        TT = 896
        KF = D_FF // 128
        xT = P.sb(st, "xT", [128, 16, TT], BF16)
        hT = P.sb(st, "hT", [128, KF, TT], BF16)
        P.op("pool", lambda e: e.memset(xT[:], 0.0), writes=[xT])
        P.op("pool", lambda e: e.memset(hT[:], 0.0), writes=[hT])
        wring = Ring([P.sb(st, f"w{i}", [128, KF * 256], BF16) for i in range(3)])
        assert all(((nt + 127) // 128) * 128 <= TT for (_, nt) in supertiles(C.NTOK, TT))
        sf = Ring([P.sb(st, f"sf{i}", [128, 512], F32) for i in range(3)])
        sg = Ring([P.sb(st, f"sg{i}", [128, 512], F32) for i in range(2)])
        psA = Ring(C.psum[0:4])
        psB = Ring(C.psum[4:8])
        wgv = wg.rearrange("(c p) n -> p c n", p=128)
        wuv = wu.rearrange("(c p) n -> p c n", p=128)
        wdv = wd.rearrange("(c p) n -> p c n", p=128)
        for (T0, NT) in supertiles(C.NTOK, TT):
            P.load(xT, xT[:, :, 0:NT], xT_src[:, T0:T0 + NT].rearrange("(c p) t -> p c t", p=128))
            for (f0, nf) in _tiles(D_FF, 512):
                wgb = wring.next()
                wub = wring.next()
                wg3 = wgb[:, 0:16 * 512].rearrange("p (c n) -> p c n", n=512)
                wu3 = wub[:, 0:16 * 512].rearrange("p (c n) -> p c n", n=512)
                P.load(wgb, wg3[:, :, 0:nf], wgv[:, :, f0:f0 + nf], q="pool")
                P.load(wub, wu3[:, :, 0:nf], wuv[:, :, f0:f0 + nf], q="pool")
                for (s0, ns) in _tiles(nf, 128):
                    j = (f0 + s0) // 128
                    for (t0, n) in _tiles(NT, 512):
                        pg = psA.next()
                        pu = psB.next()
                        for c in range(16):
                            P.op("pe", lambda e, pg=pg, c=c, wg3=wg3, s0=s0, t0=t0, n=n: e.matmul(
                                pg[:, 0:n], wg3[:, c, s0:s0 + 128], xT[:, c, t0:t0 + n], start=(c == 0), stop=(c == 15)),
                                reads=[wgb, xT], writes=[pg])
                        for c in range(16):
                            P.op("pe", lambda e, pu=pu, c=c, wu3=wu3, s0=s0, t0=t0, n=n: e.matmul(
                                pu[:, 0:n], wu3[:, c, s0:s0 + 128], xT[:, c, t0:t0 + n], start=(c == 0), stop=(c == 15)),
                                reads=[wub, xT], writes=[pu])
                        g_ = sg.next()
                        P.op("act", lambda e, g_=g_, pg=pg, n=n: e.activation(g_[:, 0:n], pg[:, 0:n], AF.Silu), reads=[pg], writes=[g_])
                        P.op("dve", lambda e, g_=g_, pu=pu, n=n, j=j, t0=t0: e.tensor_tensor(hT[:, j, t0:t0 + n], g_[:, 0:n], pu[:, 0:n], ALU.mult),
                             reads=[g_, pu], writes=[hT])
            for (c0, ncw) in _tiles(D, 256):
                wb = wring.next()
                w3 = wb[:, 0:KF * 256].rearrange("p (c n) -> p c n", n=256)
                for k0 in range(0, KF, 11):
                    P.load(wb, w3[:, k0:k0 + 11, 0:ncw], wdv[:, k0:k0 + 11, c0:c0 + ncw], q="pool")
                for (t0, n) in _tiles(NT, 128):
                    ps = psA.next()
                    for c in range(KF):
                        P.op("pe", lambda e, ps=ps, c=c, t0=t0, w3=w3, ncw=ncw: e.matmul(
                            ps[0:128, 0:ncw], hT[:, c, t0:t0 + 128], w3[:, c, 0:ncw], start=(c == 0), stop=(c == KF - 1)),
                            reads=[hT, wb], writes=[ps])
                    b = sf.next()
                    P.op("act", lambda e, b=b, ps=ps, n=n, ncw=ncw: e.copy(b[0:n, 0:ncw], ps[0:n, 0:ncw]), reads=[ps], writes=[b])
                    P.store(b, dst[T0 + t0:T0 + t0 + n, c0:c0 + ncw], b[0:n, 0:ncw])


def phase_ln_T(P, C, xa_src, xb_src, g_ap, b_ap, x_dst, xT_dst):
    with ExitStack() as st:
        loader = make_ln_loader(P, C, st, xa_src, xb_src, g_ap, b_ap, x_dst)
        xin = Ring([P.sb(st, f"xin{i}", [128, D], F32) for i in range(2)])
        for b_ in xin.items:
            P.op("pool", lambda e, b_=b_: e.memset(b_[:], 0.0), writes=[b_])
        xTs = Ring([P.sb(st, f"xTs{i}", [128, 16, 128], BF16) for i in range(2)])
        psT = Ring(C.psum[0:4])
        for (t0, n) in _tiles(C.NTOK, 128):
            xt = xTs.next()
            build_xT(P, C, st, xt, t0, n, 0, loader, xin, psT)
            P.store(xt, xT_dst[:, t0:t0 + n].rearrange("(c p) t -> p c t", p=128), xt[:, :, 0:n])


def phase_final_ln(P, C, xa_src, xb_src, g_ap, b_ap, dst):
    with ExitStack() as st:
        loader = make_ln_loader(P, C, st, xa_src, xb_src, g_ap, b_ap, dst)
        outr = Ring([P.sb(st, f"lnout{i}", [128, D], F32) for i in range(2)])
        for (t0, n) in _tiles(C.NTOK, 128):
            loader(outr.next(), t0, n)


def phase_inproj_c(P, C, I, S, loader_factory):
    NTOK = C.NTOK
    kscale = DK_C ** -0.5
    with ExitStack() as st:
        TT = 1152
        loader = loader_factory(st)
        xT = P.sb(st, "xT", [128, 16, TT + 128], BF16)
        P.op("pool", lambda e: e.memset(xT[:], 0.0), writes=[xT])
        xin = Ring([P.sb(st, f"xin{i}", [128, D], F32) for i in range(2)])
        for b_ in xin.items:
            P.op("pool", lambda e, b_=b_: e.memset(b_[:], 0.0), writes=[b_])
        wring = Ring([P.sb(st, f"w{i}", [128, 16, 512], BF16) for i in range(2)])
        sf = Ring([P.sb(st, f"sf{i}", [128, 512], F32) for i in range(3)])
        sh = Ring([P.sb(st, f"sh{i}", [128, 512], BF16) for i in range(3)])
        sm = Ring([P.sb(st, f"sm{i}", [128, 16], F32) for i in range(3)])
        gb = P.sb(st, "gb", [128, 16], F32)
        P.load(gb, gb[:, 0:8], bview(I.b_igate_c[0:1, :], [128, 8]))
        P.load(gb, gb[:, 8:16], bview(I.b_fgate_c[0:1, :], [128, 8]))
        psT = Ring(C.psum[0:2])
        psG = Ring(C.psum[2:8])
        for (T0, NT) in supertiles(NTOK, TT):
            for (t0, n) in _tiles(NT, 128):
                build_xT(P, C, st, xT, T0 + t0, n, t0, loader, xin, psT)

            def ev_feat(dst, mul, cbase, T0=T0):
                def f(ps, t0, n, col0, ncol):
                    b = sh.next()
                    P.op("act", lambda e: e.activation(b[0:ncol, 0:n], ps[0:ncol, 0:n], AF.Copy, scale=float(mul)), reads=[ps], writes=[b])
                    P.store(b, dst[col0 - cbase:col0 - cbase + ncol, T0 + t0:T0 + t0 + n], b[0:ncol, 0:n])
                return f

            def ev_tok_bf(dst, mul, cbase, T0=T0):
                def f(ps, t0, n, col0, ncol):
                    b = sh.next()
                    P.op("act", lambda e: e.activation(b[0:n, 0:ncol], ps[0:n, 0:ncol], AF.Copy, scale=float(mul)), reads=[ps], writes=[b])
                    P.store(b, dst[T0 + t0:T0 + t0 + n, col0 - cbase:col0 - cbase + ncol], b[0:n, 0:ncol])
                return f

            def ev_o(ps, t0, n, col0, ncol, T0=T0):
                b = sf.next()
                P.op("act", lambda e: e.copy(b[0:n, 0:ncol], ps[0:n, 0:ncol]), reads=[ps], writes=[b])
                P.store(b, S.z[T0 + t0:T0 + t0 + n, col0 - 4096:col0 - 4096 + ncol], b[0:n, 0:ncol])

            def ev_gates(ps, t0, n, col0, ncol, T0=T0):
                b = sm.next()
                P.op("dve", lambda e: e.tensor_tensor(b[0:n, :], ps[0:n, 0:16], gb[0:n, :], ALU.add), reads=[ps, gb], writes=[b])
                P.op("act", lambda e: e.activation(b[0:n, 8:16], b[0:n, 8:16], AF.Exp, scale=-1.0), reads=[b], writes=[b])
                P.op("act", lambda e: e.activation(b[0:n, 8:16], b[0:n, 8:16], AF.Ln, bias=1.0), reads=[b], writes=[b])
                P.op("act", lambda e: e.mul(b[0:n, 8:16], b[0:n, 8:16], -1.0), reads=[b], writes=[b])
                P.store(b, S.gates[T0 + t0:T0 + t0 + n, :], b[0:n, :])

            segs = [
                (0, 1024, "feat", ev_feat(S.qT, 1.0, 0)),
                (1024, 2048, "feat", ev_feat(S.kT, kscale, 1024)),
                (1024, 2048, "tok", ev_tok_bf(S.ktok, kscale, 1024)),
                (2048, 4096, "tok", ev_tok_bf(S.vbf, 1.0, 2048)),
                (4096, 6144, "tok", ev_o),
                (6144, 6160, "tok", ev_gates),
            ]
            gemm_segments(P, C, xT, 16, NT, I.w_in_c, segs, wring, psG, 512)


def phase_mlstm(P, C, I, O, S, seqs):
    with ExitStack() as st:
        sb = lambda name, shape, dt=F32: P.sb(st, name, shape, dt)
        g_b = sb("g_b", [128, 2048])
        P.load(g_b, g_b[:], bview(I.mlstm_norm_g[0:1, :], [128, 2048]))
        Cs = sb("Cs", [128, 8, 257]); Cs_bf = sb("Cs_bf", [128, 8, 257], BF16)
        m_b = sb("m_b", [128, 8]); m_col = sb("m_col", [128, 1])
        cio = sb("cio", [128, 16, 128])
        npad = sb("npad", [128, 128])
        gt = Ring([sb(f"gt{i}", [128, 16]) for i in range(2)])
        qT_c = Ring([sb(f"qT_c{i}", [128, 8, 128], BF16) for i in range(2)])
        kT_c = Ring([sb(f"kT_c{i}", [128, 8, 128], BF16) for i in range(2)])
        kt_c = Ring([sb(f"kt_c{i}", [128, 1024], BF16) for i in range(2)])
        va_c = Ring([sb(f"va_c{i}", [128, 8, 257], BF16) for i in range(2)])
        o_c = Ring([sb(f"o_c{i}", [128, 2048]) for i in range(2)])
        bcum = sb("bcum", [128, 8]); u = sb("u", [128, 8]); upad = sb("upad", [128, 128]); bpad = sb("bpad", [128, 128])
        uT = sb("uT", [128, 128]); bT = sb("bT", [128, 128]); MT = sb("MT", [128, 128]); ones8 = sb("ones8", [128, 128])
        M_sb = sb("M_sb", [128, 8]); negM = sb("negM", [128, 8]); Mlast = sb("Mlast", [128, 8]); btot = sb("btot", [128, 8])
        ginter = sb("ginter", [128, 8]); wend = sb("wend", [128, 8]); gold = sb("gold", [128, 8]); emr = sb("emr", [128, 8])
        D2m = sb("D2m", [128, 8, 128])
        dec = Ring([sb(f"dec{i}", [128, 128]) for i in range(3)])
        WT = Ring([sb(f"WT{i}", [128, 128], BF16) for i in range(3)])
        numsb = Ring([sb(f"numsb{i}", [128, 257]) for i in range(2)])
        comb = Ring([sb(f"comb{i}", [128, 257]) for i in range(2)])
        rdn = Ring([sb(f"rdn{i}", [128, 1]) for i in range(2)])
        vw = Ring([sb(f"vw{i}", [128, 257], BF16) for i in range(2)])
        h_c = sb("h_c", [128, 2048]); tmpb = sb("tmpb", [128, 2048]); junk = sb("junk", [128, 256])
        ss = sb("ss", [128, 8]); rstd = sb("rstd", [128, 8])
        hTs = Ring([sb(f"hTs{i}", [128, 16, 128], BF16) for i in range(2)])
        psE = [C.psum[0], C.psum[1]]
        ps_s, ps_num, ps_int, ps_c = C.psum[2], C.psum[3], C.psum[4], C.psum[5]
        psm = Ring(C.psum[6:8])
        for b_ in [upad, bpad, MT, npad, cio, m_col] + qT_c.items + kT_c.items + kt_c.items + va_c.items + o_c.items:
            P.op("pool", lambda e, b_=b_: e.memset(b_[:], 0.0), writes=[b_])
        P.op("pool", lambda e: e.memset(ones8[:], 1.0), writes=[ones8])

        for (tok0, T, past, sidx) in seqs:
            if past:
                P.load(cio, cio[:], I.st_c[sidx - 1].rearrange("(c p) d -> p c d", p=128))
                for c in range(16):
                    ps = psm.next()
                    P.op("pe", lambda e, ps=ps, c=c: e.transpose(ps[:, 0:128], cio[:, c, :], C.ident[:, :]), reads=[cio, C.ident], writes=[ps])
                    P.op("dve", lambda e, ps=ps, c=c: e.tensor_copy(Cs[:, c // 2, (c % 2) * 128:(c % 2 + 1) * 128], ps[:, 0:128]),
                         reads=[ps], writes=[Cs])
                P.load(npad, npad[0:8, :], I.st_n[sidx - 1])
                ps = psm.next()
                P.op("pe", lambda e, ps=ps: e.transpose(ps[:, 0:128], npad[:, :], C.ident[:, :]), reads=[npad, C.ident], writes=[ps])
                P.op("dve", lambda e, ps=ps: e.tensor_copy(Cs[:, :, 256], ps[:, 0:8]), reads=[ps], writes=[Cs])
                P.load(m_b, m_b[:], bview(I.st_m[sidx - 1:sidx, :], [128, 8]))
                P.load(m_col, m_col[0:8, :], I.st_m[sidx - 1:sidx, :].rearrange("o h -> h o"))
            else:
                P.op("dve", lambda e: e.memset(Cs[:], 0.0), writes=[Cs])
                P.op("dve", lambda e: e.memset(m_b[:], 0.0), writes=[m_b])
                P.op("dve", lambda e: e.memset(m_col[:], 0.0), writes=[m_col])
            P.op("act", lambda e: e.copy(Cs_bf[:], Cs[:]), reads=[Cs], writes=[Cs_bf])
            for (t0, L) in _tiles(T, 128):
                assert L in (128, 16)
                a0 = tok0 + t0
                g_ = gt.next(); qb = qT_c.next(); kb = kT_c.next(); ktb = kt_c.next(); vab = va_c.next(); ob = o_c.next()
                if L < 128:
                    for b_ in (qb, kb, ktb, vab, ob):
                        P.op("pool", lambda e, b_=b_: e.memset(b_[:], 0.0), writes=[b_])
                P.op("pool", lambda e, g_=g_: e.memset(g_[:, 0:8], NEG), writes=[g_])
                P.op("pool", lambda e, g_=g_: e.memset(g_[:, 8:16], 0.0), writes=[g_])
                P.load(g_, g_[0:L, :], S.gates[a0:a0 + L, :])
                P.load(qb, qb[:, :, 0:L], S.qT[0:1024, a0:a0 + L].rearrange("(h d) t -> d h t", d=128))
                P.load(kb, kb[:, :, 0:L], S.kT[0:1024, a0:a0 + L].rearrange("(h d) t -> d h t", d=128), q="act")
                P.load(ktb, ktb[0:L, :], S.ktok[a0:a0 + L, :])
                P.load(vab, vab[0:L, :, 0:256], S.vbf[a0:a0 + L, :].rearrange("t (h v) -> t h v", v=256), q="act")
                P.op("pool", lambda e, vab=vab: e.memset(vab[:, :, 256:257], 1.0), writes=[vab])
                P.load(ob, ob[0:L, :], S.z[a0:a0 + L, :])
                ps = psm.next()
                P.op("pe", lambda e, ps=ps, g_=g_: e.matmul(ps[:, 0:8], C.tri[:, :], g_[:, 8:16], start=True, stop=True), reads=[C.tri, g_], writes=[ps])
                P.op("dve", lambda e, ps=ps: e.tensor_copy(bcum[:], ps[:, 0:8]), reads=[ps], writes=[bcum])
                P.op("dve", lambda e, g_=g_: e.tensor_tensor(u[:], g_[:, 0:8], bcum[:], ALU.subtract), reads=[g_, bcum], writes=[u])
                P.op("dve", lambda e: e.tensor_copy(upad[:, 0:8], u[:]), reads=[u], writes=[upad])
                P.op("dve", lambda e: e.tensor_copy(bpad[:, 0:8], bcum[:]), reads=[bcum], writes=[bpad])
                ps = psm.next()
                P.op("pe", lambda e, ps=ps: e.transpose(ps[:, 0:128], upad[:, :], C.ident[:, :]), reads=[upad, C.ident], writes=[ps])
                P.op("dve", lambda e, ps=ps: e.tensor_copy(uT[0:8, :], ps[0:8, 0:128]), reads=[ps], writes=[uT])
                ps = psm.next()
                P.op("pe", lambda e, ps=ps: e.transpose(ps[:, 0:128], bpad[:, :], C.ident[:, :]), reads=[bpad, C.ident], writes=[ps])
                P.op("dve", lambda e, ps=ps: e.tensor_copy(bT[0:8, :], ps[0:8, 0:128]), reads=[ps], writes=[bT])
                P.op("dve", lambda e: e.tensor_tensor_scan(MT[0:8, :], ones8[0:8, :], uT[0:8, :], m_col[0:8, 0:1], ALU.mult, ALU.max),
                     reads=[ones8, uT, m_col], writes=[MT])
                ps = psm.next()
                P.op("pe", lambda e, ps=ps: e.transpose(ps[:, 0:128], MT[:, :], C.ident[:, :]), reads=[MT, C.ident], writes=[ps])
                P.op("dve", lambda e, ps=ps: e.tensor_copy(M_sb[:], ps[:, 0:8]), reads=[ps], writes=[M_sb])
                P.op("act", lambda e: e.mul(negM[:], M_sb[:], -1.0), reads=[M_sb], writes=[negM])
                ps = psm.next()
                P.op("pe", lambda e, ps=ps: e.matmul(ps[:, 0:8], C.sellast[:, :], M_sb[:, :], start=True, stop=True), reads=[C.sellast, M_sb], writes=[ps])
                P.op("dve", lambda e, ps=ps: e.tensor_copy(Mlast[:], ps[:, 0:8]), reads=[ps], writes=[Mlast])
                ps = psm.next()
                P.op("pe", lambda e, ps=ps: e.matmul(ps[:, 0:8], C.sellast[:, :], bcum[:, :], start=True, stop=True), reads=[C.sellast, bcum], writes=[ps])
                P.op("dve", lambda e, ps=ps: e.tensor_copy(btot[:], ps[:, 0:8]), reads=[ps], writes=[btot])
                P.op("dve", lambda e: e.tensor_tensor(ginter[:], m_b[:], M_sb[:], ALU.subtract), reads=[m_b, M_sb], writes=[ginter])
                P.op("act", lambda e: e.activation(ginter[:], ginter[:], AF.Exp), reads=[ginter], writes=[ginter])
                P.op("dve", lambda e: e.tensor_tensor(wend[:], u[:], Mlast[:], ALU.subtract), reads=[u, Mlast], writes=[wend])
                P.op("act", lambda e: e.activation(wend[:], wend[:], AF.Exp), reads=[wend], writes=[wend])
                P.op("dve", lambda e: e.tensor_tensor(gold[:], m_b[:], Mlast[:], ALU.subtract), reads=[m_b, Mlast], writes=[gold])
                P.op("act", lambda e: e.activation(gold[:], gold[:], AF.Exp), reads=[gold], writes=[gold])
                P.op("dve", lambda e: e.tensor_tensor(emr[:], bcum[:], M_sb[:], ALU.add), reads=[bcum, M_sb], writes=[emr])
                P.op("act", lambda e: e.activation(emr[:], emr[:], AF.Exp, scale=-1.0), reads=[emr], writes=[emr])
                P.op("dve", lambda e: e.tensor_tensor(D2m[:], bview(C.ident[:, :].unsqueeze(1), [128, 8, 128]),
                                                      bview(negM[:, :].unsqueeze(2), [128, 8, 128]), ALU.mult),
                     reads=[C.ident, negM], writes=[D2m])
                for half in range(2):
                    pe_ = psE[half]
                    P.op("pe", lambda e, pe_=pe_, half=half: e.matmul(pe_[:, 0:512], C.ones_f[:, :],
                                                                        D2m[:, half * 4:half * 4 + 4, :].rearrange("p h l -> p (h l)"),
                                                                        start=True, stop=False), reads=[C.ones_f, D2m], writes=[pe_])
                    P.op("pe", lambda e, pe_=pe_: e.matmul(pe_[:, 0:512], C.ident[:, :], C.negtri4[:, :], start=False, stop=True),
                         reads=[C.ident, C.negtri4], writes=[pe_])
                for h in range(8):
                    pe_ = psE[h // 4]
                    P.op("pe", lambda e, kb=kb, qb=qb, h=h: e.matmul(ps_s[:, 0:128], kb[:, h, :], qb[:, h, :], start=True, stop=True),
                         reads=[kb, qb], writes=[ps_s])
                    d_ = dec.next()
                    P.op("act", lambda e, d_=d_, pe_=pe_, h=h: e.activation(d_[:, :], pe_[:, (h % 4) * 128:(h % 4 + 1) * 128], AF.Exp, bias=u[:, h:h + 1]),
                         reads=[pe_, u], writes=[d_])
                    w_ = WT.next()
                    P.op("dve", lambda e, d_=d_, w_=w_: e.tensor_tensor(w_[:, :], d_[:, :], ps_s[:, 0:128], ALU.mult), reads=[d_, ps_s], writes=[w_])
                    P.op("pe", lambda e, w_=w_, vab=vab, h=h: e.matmul(ps_num[:, 0:257], w_[:, :], vab[:, h, :], start=True, stop=True),
                         reads=[w_, vab], writes=[ps_num])
                    P.op("pe", lambda e, qb=qb, h=h: e.matmul(ps_int[:, 0:257], qb[:, h, :], Cs_bf[:, h, :], start=True, stop=True),
                         reads=[qb, Cs_bf], writes=[ps_int])
                    ns_ = numsb.next()
                    P.op("act", lambda e, ns_=ns_: e.copy(ns_[:, :], ps_num[:, 0:257]), reads=[ps_num], writes=[ns_])
                    cb_ = comb.next()
                    P.op("dve", lambda e, cb_=cb_, ns_=ns_, h=h: e.scalar_tensor_tensor(out=cb_[:, :], in0=ps_int[:, 0:257], scalar=ginter[:, h:h + 1],
                                                                                      in1=ns_[:, :], op0=ALU.mult, op1=ALU.add),
                         reads=[ps_int, ginter, ns_], writes=[cb_])
                    r_ = rdn.next()
                    P.op("act", lambda e, r_=r_, cb_=cb_: e.activation(r_[:, :], cb_[:, 256:257], AF.Abs), reads=[cb_], writes=[r_])
                    P.op("dve", lambda e, r_=r_, h=h: e.tensor_tensor(r_[:, :], r_[:, :], emr[:, h:h + 1], ALU.max), reads=[r_, emr], writes=[r_])
                    P.op("dve", lambda e, r_=r_: e.reciprocal(r_[:, :], r_[:, :]), reads=[r_], writes=[r_])
                    P.op("dve", lambda e, r_=r_, cb_=cb_, h=h: e.tensor_scalar(h_c[:, h * 256:(h + 1) * 256], cb_[:, 0:256], r_[:, 0:1], None, ALU.mult),
                         reads=[cb_, r_], writes=[h_c])
                    v_ = vw.next()
                    P.op("pool", lambda e, v_=v_, vab=vab, h=h: e.tensor_scalar(v_[:, :], vab[:, h, :], wend[:, h:h + 1], None, ALU.mult),
                         reads=[vab, wend], writes=[v_])
                    P.op("pe", lambda e, ktb=ktb, v_=v_, h=h: e.matmul(ps_c[:, 0:257], ktb[:, h * 128:(h + 1) * 128], v_[:, :], start=True, stop=True),
                         reads=[ktb, v_], writes=[ps_c])
                    P.op("dve", lambda e, h=h: e.scalar_tensor_tensor(out=Cs[:, h, :], in0=Cs[:, h, :], scalar=gold[:, h:h + 1], in1=ps_c[:, 0:257],
                                                                       op0=ALU.mult, op1=ALU.add), reads=[Cs, gold, ps_c], writes=[Cs])
                    P.op("act", lambda e, h=h: e.copy(Cs_bf[:, h, :], Cs[:, h, :]), reads=[Cs], writes=[Cs_bf])
                P.op("dve", lambda e: e.tensor_tensor(m_b[:], btot[:], Mlast[:], ALU.add), reads=[btot, Mlast], writes=[m_b])
                P.op("dve", lambda e: e.tensor_tensor(m_col[0:8, :], bT[0:8, 127:128], MT[0:8, 127:128], ALU.add), reads=[bT, MT], writes=[m_col])
                for h in range(8):
                    P.op("act", lambda e, h=h: e.activation(junk[:, :], h_c[:, h * 256:(h + 1) * 256], AF.Square, accum_out=ss[:, h:h + 1]),
                         reads=[h_c], writes=[junk, ss])
                P.op("dve", lambda e: e.tensor_scalar(rstd[:], ss[:], 1.0 / 256, 1e-5, ALU.mult, ALU.add), reads=[ss], writes=[rstd])
                P.op("act", lambda e: e.activation(rstd[:], rstd[:], AF.Sqrt), reads=[rstd], writes=[rstd])
                P.op("dve", lambda e: e.reciprocal(rstd[:], rstd[:]), reads=[rstd], writes=[rstd])
                for h in range(8):
                    P.op("dve", lambda e, h=h: e.tensor_scalar(h_c[:, h * 256:(h + 1) * 256], h_c[:, h * 256:(h + 1) * 256], rstd[:, h:h + 1], None, ALU.mult),
                         reads=[h_c, rstd], writes=[h_c])
                P.op("pool", lambda e: e.tensor_tensor(h_c[:], h_c[:], g_b[:], ALU.mult), reads=[h_c, g_b], writes=[h_c])
                P.op("act", lambda e, ob=ob: e.activation(tmpb[:], ob[:], AF.Sigmoid), reads=[ob], writes=[tmpb])
                P.op("dve", lambda e: e.tensor_tensor(h_c[:], h_c[:], tmpb[:], ALU.mult), reads=[h_c, tmpb], writes=[h_c])
                ht = hTs.next()
                for c4 in range(4):
                    ps = psm.next()
                    for j in range(4):
                        c = c4 * 4 + j
                        P.op("pe", lambda e, ps=ps, c=c, j=j: e.transpose(ps[:, j * 128:(j + 1) * 128], h_c[:, c * 128:(c + 1) * 128], C.ident[:, :]),
                             reads=[h_c, C.ident], writes=[ps])
                    P.op("act", lambda e, ps=ps, c4=c4, ht=ht: e.copy(ht[:, c4 * 4:(c4 + 1) * 4, :], ps[:].rearrange("p (j m) -> p j m", j=4)),
                         reads=[ps], writes=[ht])
                P.store(ht, S.mixT[0:2048, a0:a0 + L].rearrange("(c p) t -> p c t", p=128), ht[:, :, 0:L])
            for c in range(16):
                ps = psm.next()
                P.op("pe", lambda e, ps=ps, c=c: e.transpose(ps[:, 0:128], Cs[:, c // 2, (c % 2) * 128:(c % 2 + 1) * 128], C.ident[:, :]),
                     reads=[Cs, C.ident], writes=[ps])
                P.op("dve", lambda e, ps=ps, c=c: e.tensor_copy(cio[:, c, :], ps[:, 0:128]), reads=[ps], writes=[cio])
            P.store(cio, O.mlstm_c[sidx].rearrange("(c p) d -> p c d", p=128), cio[:])
            P.op("dve", lambda e: e.tensor_copy(npad[:, 0:8], Cs[:, :, 256]), reads=[Cs], writes=[npad])
            ps = psm.next()
            P.op("pe", lambda e, ps=ps: e.transpose(ps[:, 0:128], npad[:, :], C.ident[:, :]), reads=[npad, C.ident], writes=[ps])
            P.op("dve", lambda e, ps=ps: e.tensor_copy(bpad[0:8, :], ps[0:8, 0:128]), reads=[ps], writes=[bpad])
            P.store(bpad, O.mlstm_n[sidx], bpad[0:8, :])
            P.op("dve", lambda e: e.memset(bpad[:], 0.0), writes=[bpad])
            P.op("dve", lambda e: e.memset(npad[:], 0.0), writes=[npad])
            P.store(m_col, O.mlstm_m[sidx:sidx + 1, :].rearrange("o h -> h o"), m_col[0:8, :])


_PROG_CACHE = {}


def kernel(**inputs):
    NCORES = 8
    SEQ = inputs["x_prompt"].shape[1]
    PAST = inputs["cache_fox_k"].shape[2]
    NS = inputs["x_sample"].shape[0] // NCORES
    key = (SEQ, PAST, NS)
    if key not in _PROG_CACHE:
        _PROG_CACHE[key] = build_program(SEQ=SEQ, PAST=PAST, NS=NS)
    nc = _PROG_CACHE[key]
    in_maps = [make_in_map(inputs, c, NS=NS) for c in range(NCORES)]
    res = run_bass_kernel_spmd(nc, in_maps, core_ids=list(range(NCORES)))
    R = res.results
    NP = N_META + SEQ
    B = NCORES
    DB = NCORES * NS
    f32 = np.float32
    y_p = np.empty((B, SEQ, D), f32); y_s = np.empty((DB, DEC, D), f32)
    kp = np.empty((1, B, NP, H_A, DH_A), f32); vp = np.empty((1, B, NP, H_A, DH_A), f32); fp = np.empty((1, B, NP, H_A), f32)
    cvp = np.empty((1, B, 3, CONV_DIM), f32); hp = np.empty((1, B, H_B, P_B, N_B), f32)
    cp = np.empty((1, B, H_C, DV_C, DK_C), f32); np_ = np.empty((1, B, H_C, DK_C), f32); mp = np.empty((1, B, H_C), f32)
    ks = np.empty((1, DB, DEC, H_A, DH_A), f32); vs = np.empty((1, DB, DEC, H_A, DH_A), f32); fs = np.empty((1, DB, DEC, H_A), f32)
    cvs = np.empty((1, DB, 3, CONV_DIM), f32); hs = np.empty((1, DB, H_B, P_B, N_B), f32)
    cs = np.empty((1, DB, H_C, DV_C, DK_C), f32); ns_ = np.empty((1, DB, H_C, DK_C), f32); ms = np.empty((1, DB, H_C), f32)
    for c in range(NCORES):
        r = R[c]
        y = np.asarray(r["o_y"]); y_p[c] = y[N_META:NP]; y_s[c * NS:(c + 1) * NS] = y[NP:].reshape(NS, DEC, D)
        k = np.asarray(r["o_fox_k"]); kp[0, c] = k[:NP].reshape(NP, H_A, DH_A); ks[0, c * NS:(c + 1) * NS] = k[NP:].reshape(NS, DEC, H_A, DH_A)
        v = np.asarray(r["o_fox_v"]); vp[0, c] = v[:NP].reshape(NP, H_A, DH_A); vs[0, c * NS:(c + 1) * NS] = v[NP:].reshape(NS, DEC, H_A, DH_A)
        lf = np.asarray(r["o_fox_logf"]); fp[0, c] = lf[:NP]; fs[0, c * NS:(c + 1) * NS] = lf[NP:].reshape(NS, DEC, H_A)
        cv = np.asarray(r["o_ssd_conv"]); cvp[0, c] = cv[0]; cvs[0, c * NS:(c + 1) * NS] = cv[1:]
        hh = np.asarray(r["o_ssd_state"]).reshape(1 + NS, H_B, P_B, N_B); hp[0, c] = hh[0]; hs[0, c * NS:(c + 1) * NS] = hh[1:]
        cc = np.asarray(r["o_mlstm_c"]).reshape(1 + NS, H_C, DV_C, DK_C); cp[0, c] = cc[0]; cs[0, c * NS:(c + 1) * NS] = cc[1:]
        nn = np.asarray(r["o_mlstm_n"]); np_[0, c] = nn[0]; ns_[0, c * NS:(c + 1) * NS] = nn[1:]
        mm = np.asarray(r["o_mlstm_m"]); mp[0, c] = mm[0]; ms[0, c * NS:(c + 1) * NS] = mm[1:]
    return (y_p, y_s, kp, vp, fp, cvp, hp, cp, np_, mp, ks, vs, fs, cvs, hs, cs, ns_, ms)
```

```python
import math
import numpy as np
from contextlib import ExitStack
import concourse.bass as bass
import concourse.mybir as mybir
from concourse.bass_utils import run_bass_kernel_spmd

F32 = mybir.dt.float32
BF16 = mybir.dt.bfloat16
AF = mybir.ActivationFunctionType
ALU = mybir.AluOpType
AX = mybir.AxisListType

D = 2048
N_META = 16
DEC = 16
H_A, DH_A = 16, 128
H_B, P_B, G_B, N_B = 32, 64, 4, 128
D_SSM = 2048
CONV_DIM = 3072
H_C, DK_C, DV_C = 8, 128, 256
D_FF = 5632
E_A = 11312
E_C = 6160
DEPTH = 2
ALPHA = (2 * DEPTH) ** 0.25
NEG = -1e30

ENGS = ("pe", "act", "dve", "pool", "sp")


class Buf:
    __slots__ = ("name", "t", "last_w", "readers", "is_psum")

    def __init__(self, name, t, is_psum=False):
        self.name = name
        self.t = t
        self.last_w = None
        self.readers = []
        self.is_psum = is_psum

    def __getitem__(self, idx):
        return self.t[idx]


class Op:
    __slots__ = ("eng", "fn", "deps", "is_dma", "key", "sig", "tick", "dcount")

    def __init__(self, eng, fn, is_dma=False, key=None):
        self.eng = eng
        self.fn = fn
        self.deps = []
        self.is_dma = is_dma
        self.key = key
        self.sig = False
        self.tick = 0
        self.dcount = 0


class Prog:
    def __init__(self, nc, stack):
        self.nc = nc
        self.stack = stack
        self.esem = {e: stack.enter_context(nc.semaphore("sem_" + e)) for e in ENGS}
        self.ecnt = {e: 0 for e in ENGS}
        self.ksem = {}
        self.kcnt = {}
        self.ops = {e: [] for e in ENGS}
        self.known = {e: {} for e in ENGS}
        self.bufs = []
        self.nops = 0
        self.uid = 0
        self.dmaq = 0
        self.free_dsems = {"sw": [], "hw": []}
        self.kkind = {}
        self.all_dsems = []
        self.keep_names = {"ident", "tri", "negtri", "negtri4", "tri", "ones", "sel0", "sellast"}

    def sb(self, stack, name, shape, dt):
        self.uid += 1
        nm = f"{name}_{self.uid}"
        t = stack.enter_context(self.nc.sbuf_tensor(nm, list(shape), dt))
        b = Buf(nm, t)
        self.bufs.append(b)
        return b

    def ps(self, stack, name, shape, dt=F32):
        self.uid += 1
        nm = f"{name}_{self.uid}"
        t = stack.enter_context(self.nc.psum_tensor(nm, list(shape), dt))
        b = Buf(nm, t, True)
        self.bufs.append(b)
        return b

    def _deps(self, op, reads, writes):
        for b in reads:
            if b.last_w is not None:
                op.deps.append(b.last_w)
            if b.is_psum:
                for r in b.readers:
                    if r.eng != op.eng:
                        op.deps.append(r)
        for b in writes:
            lw = b.last_w
            if lw is not None:
                if op.is_dma and lw.is_dma and lw.key is op.key:
                    op.deps.extend(lw.deps)
                else:
                    op.deps.append(lw)
            for r in b.readers:
                op.deps.append(r)
        for b in reads:
            b.readers.append(op)
        for b in writes:
            b.last_w = op
            b.readers = []

    def op(self, eng, fn, reads=(), writes=()):
        o = Op(eng, fn)
        self._deps(o, reads, writes)
        self.ops[eng].append(o)
        self.nops += 1
        return o

    def dma(self, out, in_, key, reads=(), writes=(), q=None):
        if q is None:
            q = "sp"
        kind = "sw" if q == "pool" else "hw"
        if key in self.kkind:
            assert self.kkind[key] == kind, key.name
        if key not in self.ksem:
            self.kkind[key] = kind
            if self.free_dsems[kind]:
                self.ksem[key], self.kcnt[key] = self.free_dsems[kind].pop()
            else:
                self.ksem[key] = self.stack.enter_context(self.nc.semaphore(f"dk{len(self.all_dsems)}"))
                self.kcnt[key] = 0
                self.all_dsems.append(self.ksem[key])
        o = Op(q, lambda e, out=out, in_=in_: e.dma_start(out=out, in_=in_), True, key)
        self.kcnt[key] += 16
        o.dcount = self.kcnt[key]
        self._deps(o, reads, writes)
        self.ops[q].append(o)
        self.nops += 1
        return o

    def load(self, buf, dst, src, q=None):
        return self.dma(dst, src, buf, writes=[buf], q=q)

    def store(self, buf, dst, src, q=None):
        return self.dma(dst, src, buf, reads=[buf], q=q)

    def end_phase(self):
        nc = self.nc
        for e in ENGS:
            for o in self.ops[e]:
                for d in o.deps:
                    if not d.is_dma and (d.eng != o.eng or o.eng != "pe"):
                        d.sig = True
        for e in ENGS:
            c = self.ecnt[e]
            for o in self.ops[e]:
                if o.sig:
                    c += 1
                    o.tick = c
            self.ecnt[e] = c
        ops, esem, ksem, known, kcnt = self.ops, self.esem, self.ksem, self.known, self.kcnt

        def replay(eng_obj, name):
            kn = known[name]
            for o in ops[name]:
                need = {}
                for d in o.deps:
                    if d.is_dma:
                        s, v = ksem[d.key], d.dcount
                    else:
                        if d.eng == name and name == "pe":
                            continue
                        s, v = esem[d.eng], d.tick
                    if kn.get(id(s), 0) >= v:
                        continue
                    if need.get(id(s), (None, 0))[1] < v:
                        need[id(s)] = (s, v)
                for sid, (s, v) in need.items():
                    eng_obj.wait_ge(s, v)
                    kn[sid] = v
                ins = o.fn(eng_obj)
                if o.is_dma:
                    ins.then_inc(ksem[o.key], 16)
                elif o.sig:
                    ins.then_inc(esem[name], 1)
            if name == "sp":
                for k, c in kcnt.items():
                    s = ksem[k]
                    if kn.get(id(s), 0) < c:
                        eng_obj.wait_ge(s, c)
                        kn[id(s)] = c

        with nc.Block() as block:
            @block.tensor
            def _(e):
                replay(e, "pe")

            @block.scalar
            def _(e):
                replay(e, "act")

            @block.vector
            def _(e):
                replay(e, "dve")

            @block.gpsimd
            def _(e):
                replay(e, "pool")

            @block.sync
            def _(e):
                replay(e, "sp")

        for e in ENGS:
            self.ops[e] = []
            for e2 in ENGS:
                self.known[e][id(self.esem[e2])] = self.ecnt[e2]
            for k, c in self.kcnt.items():
                self.known[e][id(self.ksem[k])] = c
        for k, c in self.kcnt.items():
            self.free_dsems[self.kkind[k]].append((self.ksem[k], c))
        self.ksem = {}
        self.kcnt = {}
        self.kkind = {}
        for b in self.bufs:
            b.last_w = None
            b.readers = []
        self.bufs = [b for b in self.bufs if b.is_psum or b.name.split("_")[0] in self.keep_names]


class Ring:
    def __init__(self, items):
        self.items = items
        self.i = 0

    def next(self):
        b = self.items[self.i % len(self.items)]
        self.i += 1
        return b


def _tiles(n, step):
    return [(i, min(step, n - i)) for i in range(0, n, step)]


class Ctx:
    pass


def build_xT(P, C, pst, xT, tok0, n, off, loader, xin_ring, psT_ring):
    xb = xin_ring.next()
    loader(xb, tok0, n)
    for c4 in range(4):
        ps = psT_ring.next()
        for j in range(4):
            c = c4 * 4 + j
            P.op("pe", lambda e, ps=ps, xb=xb, c=c, j=j, n=n: e.transpose(
                ps[0:128, j * 128:j * 128 + 128], xb[0:128, c * 128:(c + 1) * 128], C.ident[:, :]),
                reads=[xb, C.ident], writes=[ps])
        P.op("dve", lambda e, ps=ps, c4=c4, n=n, off=off: e.tensor_copy(
            xT[:, c4 * 4:(c4 + 1) * 4, off:off + n],
            ps[:].rearrange("p (j m) -> p j m", j=4)[:, :, 0:n]), reads=[ps], writes=[xT])


def gemm_segments(P, C, xT, Kc, ntok, W, segs, wring, ps_ring, wcols):
    Wv = W.rearrange("(c p) n -> p c n", p=128)
    for (c0, c1, orient, evac) in segs:
        for (cc, ncw) in _tiles(c1 - c0, wcols):
            wb = wring.next()
            P.load(wb, wb[:, 0:Kc, 0:ncw], Wv[:, :, c0 + cc:c0 + cc + ncw], q="pool")
            if orient == "tok":
                for (t0, n) in _tiles(ntok, 128):
                    ps = ps_ring.next()
                    for c in range(Kc):
                        P.op("pe", lambda e, ps=ps, c=c, t0=t0, n=n, wb=wb, ncw=ncw: e.matmul(
                            ps[0:128, 0:ncw], xT[:, c, t0:t0 + 128], wb[:, c, 0:ncw],
                            start=(c == 0), stop=(c == Kc - 1)), reads=[xT, wb], writes=[ps])
                    evac(ps, t0, n, c0 + cc, ncw)
            else:
                for (s0, ns) in _tiles(ncw, 128):
                    for (t0, n) in _tiles(ntok, 512):
                        ps = ps_ring.next()
                        for c in range(Kc):
                            P.op("pe", lambda e, ps=ps, c=c, t0=t0, n=n, wb=wb, s0=s0, ns=ns: e.matmul(
                                ps[0:ns, 0:n], wb[:, c, s0:s0 + ns], xT[:, c, t0:t0 + n],
                                start=(c == 0), stop=(c == Kc - 1)), reads=[xT, wb], writes=[ps])
                        evac(ps, t0, n, c0 + cc + s0, ns)


def supertiles(ntok, tt):
    n128 = (ntok + 127) // 128
    per = tt // 128
    nsup = (n128 + per - 1) // per
    base, extra = divmod(n128, nsup)
    out = []
    a = 0
    for i in range(nsup):
        k = base + (1 if i < extra else 0)
        b = min(ntok, a + k * 128)
        out.append((a, b - a))
        a = b
    return out


def layer_norm_tile(P, C, st_bufs, xa, xb_, n, g_b, b_b, out):
    stats, mv, rstd = st_bufs
    P.op("dve", lambda e: e.scalar_tensor_tensor(out=xa[0:n, :], in0=xa[0:n, :], scalar=float(ALPHA), in1=xb_[0:n, :],
                                                 op0=ALU.mult, op1=ALU.add), reads=[xa, xb_], writes=[xa])
    for k in range(4):
        P.op("dve", lambda e, k=k: e.bn_stats(stats[0:n, k * 6:(k + 1) * 6], xa[0:n, k * 512:(k + 1) * 512]),
             reads=[xa], writes=[stats])
    P.op("dve", lambda e: e.bn_aggr(mv[0:n, :], stats[0:n, :]), reads=[stats], writes=[mv])
    P.op("dve", lambda e: e.tensor_scalar(rstd[0:n, :], mv[0:n, 1:2], 1e-5, None, ALU.add), reads=[mv], writes=[rstd])
    P.op("act", lambda e: e.activation(rstd[0:n, :], rstd[0:n, :], AF.Sqrt), reads=[rstd], writes=[rstd])
    P.op("dve", lambda e: e.reciprocal(rstd[0:n, :], rstd[0:n, :]), reads=[rstd], writes=[rstd])
    P.op("dve", lambda e: e.tensor_scalar(xa[0:n, :], xa[0:n, :], mv[0:n, 0:1], rstd[0:n, 0:1], ALU.subtract, ALU.mult),
         reads=[xa, mv, rstd], writes=[xa])
    P.op("pool", lambda e: e.tensor_tensor(xa[0:n, :], xa[0:n, :], g_b[0:n, :], ALU.mult), reads=[xa, g_b], writes=[xa])
    P.op("pool", lambda e: e.tensor_tensor(out[0:n, :], xa[0:n, :], b_b[0:n, :], ALU.add), reads=[xa, b_b], writes=[out])


def build_program(SEQ=4096, PAST=4096, NS=2, stop_after=None, debug=False):
    nc = bass.Bass("TRN2", target_bir_lowering=False)
    C = Ctx()
    NP = N_META + SEQ
    NTOK = NP + NS * DEC
    C.NP, C.NTOK, C.SEQ, C.PAST, C.NS = NP, NTOK, SEQ, PAST, NS

    early = {"x_prompt", "x_sample", "meta_tokens", "cache_fox_k", "cache_fox_v", "cache_fox_logf", "w_in_a",
             "b_fgate_a", "c_ident", "c_tri", "c_negtri"}
    if stop_after == "l0":
        early |= {"w_out_a", "ln_mix_g", "ln_mix_b", "ln_ffn_g", "ln_ffn_b", "w_ffn_gate", "w_ffn_up", "w_ffn_down"}
    if stop_after in ("ssd", "l0"):
        early |= {"state_ssd_conv", "state_ssd", "conv_w", "conv_b", "dt_bias", "a_log", "d_skip", "ssd_norm_g"}
    C.early = early

    def din(name, shape, dt=F32):
        if stop_after in ("inproj_a", "attn", "cache", "ssd", "l0") and name not in early:
            return None
        return nc.dram_tensor(name, list(shape), dt, kind="ExternalInput").ap()

    def dout(name, shape, dt=F32):
        return nc.dram_tensor(name, list(shape), dt, kind="ExternalOutput").ap()

    def dscr(name, shape, dt=F32):
        if debug:
            return nc.dram_tensor(name, list(shape), dt, kind="ExternalOutput").ap()
        return nc.dram_tensor(name, list(shape), dt, kind="Internal").ap()

    I = Ctx()
    I.x_prompt = din("x_prompt", [SEQ, D])
    I.x_sample = din("x_sample", [NS * DEC, D])
    I.meta = din("meta_tokens", [N_META, D])
    I.cache_k = din("cache_fox_k", [NS, PAST, 2048])
    I.cache_v = din("cache_fox_v", [NS, PAST, 2048])
    I.cache_logf = din("cache_fox_logf", [NS, PAST, 16])
    I.st_conv = din("state_ssd_conv", [NS, 3, CONV_DIM])
    I.st_ssd = din("state_ssd", [NS, 2048, 128])
    I.st_c = din("state_mlstm_c", [NS, 2048, 128])
    I.st_n = din("state_mlstm_n", [NS, 8, 128])
    I.st_m = din("state_mlstm_m", [NS, 8])
    I.w_in_a = din("w_in_a", [D, E_A])
    I.b_fgate_a = din("b_fgate_a", [1, 16])
    I.conv_w = din("conv_w", [4, CONV_DIM])
    I.conv_b = din("conv_b", [1, CONV_DIM])
    I.dt_bias = din("dt_bias", [1, 32])
    I.a_log = din("a_log", [1, 32])
    I.d_skip = din("d_skip", [1, 32])
    I.ssd_norm_g = din("ssd_norm_g", [1, 2048])
    I.w_out_a = din("w_out_a", [4096, D])
    I.w_in_c = din("w_in_c", [D, E_C])
    I.b_igate_c = din("b_igate_c", [1, 8])
    I.b_fgate_c = din("b_fgate_c", [1, 8])
    I.mlstm_norm_g = din("mlstm_norm_g", [1, 2048])
    I.w_out_c = din("w_out_c", [2048, D])
    I.ln_mix_g = din("ln_mix_g", [2, D])
    I.ln_mix_b = din("ln_mix_b", [2, D])
    I.ln_ffn_g = din("ln_ffn_g", [2, D])
    I.ln_ffn_b = din("ln_ffn_b", [2, D])
    I.w_ffn_gate = din("w_ffn_gate", [2, D, D_FF])
    I.w_ffn_up = din("w_ffn_up", [2, D, D_FF])
    I.w_ffn_down = din("w_ffn_down", [2, D_FF, D])
    I.c_ident = din("c_ident", [128, 128])
    I.c_tri = din("c_tri", [128, 128])
    I.c_negtri = din("c_negtri", [128, 128])

    O = Ctx()
    O.y = dout("o_y", [NTOK, D])
    O.fox_k = dout("o_fox_k", [NTOK, 2048])
    O.fox_v = dout("o_fox_v", [NTOK, 2048])
    O.fox_logf = dout("o_fox_logf", [NTOK, 16])
    O.ssd_conv = dout("o_ssd_conv", [1 + NS, 3, CONV_DIM])
    O.ssd_state = dout("o_ssd_state", [1 + NS, 2048, 128])
    O.mlstm_c = dout("o_mlstm_c", [1 + NS, 2048, 128])
    O.mlstm_n = dout("o_mlstm_n", [1 + NS, 8, 128])
    O.mlstm_m = dout("o_mlstm_m", [1 + NS, 8])

    S = Ctx()
    S.qT = dscr("s_qT", [2048, NTOK], BF16)
    S.kT = dscr("s_kT", [2048, NTOK], BF16)
    S.vbf = dscr("s_vbf", [NTOK, 2048], BF16)
    S.kTp = dscr("s_kTp", [NS, 2048, PAST], BF16)
    S.vp = dscr("s_vp", [NS, PAST, 2048], BF16)
    S.logf = dscr("s_logf", [NTOK, 16])
    S.z = dscr("s_z", [NTOK, 2048])
    S.xbcT = dscr("s_xbcT", [CONV_DIM, NTOK])
    S.dtraw = dscr("s_dtraw", [NTOK, 32])
    S.xtok = dscr("s_xtok", [NTOK, 2048])
    S.bmtok = dscr("s_bmtok", [NTOK, 512], BF16)
    S.bcT = dscr("s_bcT", [1024, NTOK], BF16)
    S.mixT = dscr("s_mixT", [4096, NTOK], BF16)
    S.mix = dscr("s_mix", [NTOK, D])
    S.mix2 = dscr("s_mix2", [NTOK, D])
    S.x0 = dscr("s_x0", [NTOK, D])
    S.x1T = dscr("s_x1T", [D, NTOK], BF16)
    S.ktok = dscr("s_ktok", [NTOK, 1024], BF16)
    S.gates = dscr("s_gates", [NTOK, 16])
    S.x1 = dscr("s_x1", [NTOK, D])
    S.x2 = dscr("s_x2", [NTOK, D])

    NKT_ = (max(NP, PAST + DEC) + 127) // 128
    C.dbgF = dscr("s_dbgF", [128, NKT_ * 16]) if debug else None
    C.dbgL = dscr("s_dbgL", [128, NKT_ * 16]) if debug else None
    seqs = [(0, NP, 0, 0)] + [(NP + s * DEC, DEC, PAST, 1 + s) for s in range(NS)]

    with ExitStack() as gst:
        P = Prog(nc, gst)
        C.ident = P.sb(gst, "ident", [128, 128], F32)
        C.tri = P.sb(gst, "tri", [128, 128], F32)
        C.negtri = P.sb(gst, "negtri", [128, 128], F32)
        C.tri_bf = P.sb(gst, "tri_bf", [128, 128], BF16)
        C.ones_bf = P.sb(gst, "ones_bf", [128, 128], BF16)
        C.ones_f = P.sb(gst, "ones_f", [128, 128], F32)
        C.sel0 = P.sb(gst, "sel0", [128, 128], F32)
        P.load(C.ident, C.ident[:], I.c_ident)
        P.load(C.tri, C.tri[:], I.c_tri)
        P.load(C.negtri, C.negtri[:], I.c_negtri)
        P.op("dve", lambda e: e.tensor_copy(C.tri_bf[:], C.tri[:]), reads=[C.tri], writes=[C.tri_bf])
        P.op("dve", lambda e: e.memset(C.ones_bf[:], 1.0), writes=[C.ones_bf])
        P.op("dve", lambda e: e.memset(C.ones_f[:], 1.0), writes=[C.ones_f])
        P.op("dve", lambda e: e.tensor_copy(C.sel0[:], C.ident[:, 0:1].to_broadcast([128, 128])), reads=[C.ident],
             writes=[C.sel0])
        C.negtri4 = P.sb(gst, "negtri4", [128, 512], F32)
        P.op("dve", lambda e: e.tensor_copy(C.negtri4[:].rearrange("p (h l) -> p h l", h=4),
                                            C.negtri[:, :].unsqueeze(1).to_broadcast([128, 4, 128])),
             reads=[C.negtri], writes=[C.negtri4])
        C.sellast = P.sb(gst, "sellast", [128, 128], F32)
        P.op("dve", lambda e: e.tensor_copy(C.sellast[:], C.ident[:, 127:128].to_broadcast([128, 128])), reads=[C.ident],
             writes=[C.sellast])
        psum = [P.ps(gst, f"psb{i}", [128, 512], F32) for i in range(8)]
        C.psum = psum

        def load_x0(xb, tok0, n):
            r = tok0
            done = 0
            while done < n:
                rr = r + done
                if rr < N_META:
                    m = min(n - done, N_META - rr)
                    src = I.meta[rr:rr + m, :]
                elif rr < NP:
                    m = min(n - done, NP - rr)
                    src = I.x_prompt[rr - N_META:rr - N_META + m, :]
                else:
                    m = n - done
                    src = I.x_sample[rr - NP:rr - NP + m, :]
                P.load(xb, xb[done:done + m, :], src)
                done += m

        phase_inproj_a(P, C, I, O, S, load_x0)
        P.end_phase()
        if stop_after == "inproj_a":
            return nc
        phase_cache_prep(P, C, I, S)
        P.end_phase()
        if stop_after == "cache":
            return nc
        phase_attention(P, C, I, O, S, seqs)
        P.end_phase()
        if stop_after == "attn":
            return nc
        phase_ssd_conv(P, C, I, O, S, seqs)
        P.end_phase()
        phase_ssd(P, C, I, O, S, seqs)
        P.end_phase()
        if stop_after == "ssd":
            return nc
        phase_outproj(P, C, I.w_out_a, 32, S.mixT, S.mix)
        P.end_phase()
        phase_ln_T(P, C, load_x0, S.mix, I.ln_mix_g[0:1, :], I.ln_mix_b[0:1, :], S.x1, S.x1T)
        P.end_phase()
        phase_ffn(P, C, S.x1T, I.w_ffn_gate[0], I.w_ffn_up[0], I.w_ffn_down[0], S.mix2)
        P.end_phase()
        if stop_after == "l0":
            phase_final_ln(P, C, S.x1, S.mix2, I.ln_ffn_g[0:1, :], I.ln_ffn_b[0:1, :], O.y)
            P.end_phase()
            return nc
        phase_inproj_c(P, C, I, S, lambda st: make_ln_loader(P, C, st, S.x1, S.mix2, I.ln_ffn_g[0:1, :], I.ln_ffn_b[0:1, :], S.x2))
        P.end_phase()
        phase_mlstm(P, C, I, O, S, seqs)
        P.end_phase()
        phase_outproj(P, C, I.w_out_c, 16, S.mixT[0:2048, :], S.mix)
        P.end_phase()
        phase_ln_T(P, C, S.x2, S.mix, I.ln_mix_g[1:2, :], I.ln_mix_b[1:2, :], S.x1, S.x1T)
        P.end_phase()
        phase_ffn(P, C, S.x1T, I.w_ffn_gate[1], I.w_ffn_up[1], I.w_ffn_down[1], S.mix2)
        P.end_phase()
        phase_final_ln(P, C, S.x1, S.mix2, I.ln_ffn_g[1:2, :], I.ln_ffn_b[1:2, :], O.y)
        P.end_phase()
    return nc


def phase_x0_copy(P, C, S, load_x0):
    with ExitStack() as st:
        r = Ring([P.sb(st, f"x0c{i}", [128, D], F32) for i in range(3)])
        for (t0, n) in _tiles(C.NTOK, 128):
            b = r.next()
            load_x0(b, t0, n)
            P.store(b, S.x0[t0:t0 + n, :], b[0:n, :])


def phase_inproj_a(P, C, I, O, S, load_x0):
    NTOK = C.NTOK
    scale = DH_A ** -0.5
    with ExitStack() as st:
        TT = 1152
        xT = P.sb(st, "xT", [128, 16, TT + 128], BF16)
        P.op("pool", lambda e: e.memset(xT[:], 0.0), writes=[xT])
        xin = Ring([P.sb(st, f"xin{i}", [128, D], F32) for i in range(2)])
        for b_ in xin.items:
            P.op("pool", lambda e, b_=b_: e.memset(b_[:], 0.0), writes=[b_])
        wring = Ring([P.sb(st, f"w{i}", [128, 16, 512], BF16) for i in range(2)])
        sf = Ring([P.sb(st, f"sf{i}", [128, 512], F32) for i in range(3)])
        sh = Ring([P.sb(st, f"sh{i}", [128, 512], BF16) for i in range(3)])
        sm = Ring([P.sb(st, f"sm{i}", [128, 16], F32) for i in range(3)])
        bfb = P.sb(st, "bfb", [128, 16], F32)
        P.load(bfb, bfb[:], I.b_fgate_a[0:1, :].to_broadcast([128, 16]))
        psT = Ring(C.psum[0:2])
        psG = Ring(C.psum[2:8])
        for (T0, NT) in supertiles(NTOK, TT):
            for (t0, n) in _tiles(NT, 128):
                build_xT(P, C, st, xT, T0 + t0, n, t0, load_x0, xin, psT)

            def ev_qk(dst, mul):
                def f(ps, t0, n, col0, ncol):
                    b = sh.next()
                    P.op("act", lambda e: e.activation(b[0:ncol, 0:n], ps[0:ncol, 0:n], AF.Copy, scale=float(mul)),
                         reads=[ps], writes=[b])
                    P.store(b, dst[col0:col0 + ncol, T0 + t0:T0 + t0 + n], b[0:ncol, 0:n])
                return f

            def ev_ktok(ps, t0, n, col0, ncol):
                b = sf.next()
                P.op("act", lambda e: e.copy(b[0:n, 0:ncol], ps[0:n, 0:ncol]), reads=[ps], writes=[b])
                P.store(b, O.fox_k[T0 + t0:T0 + t0 + n, col0 - 2048:col0 - 2048 + ncol], b[0:n, 0:ncol])

            def ev_v(ps, t0, n, col0, ncol):
                b = sf.next()
                P.op("act", lambda e: e.copy(b[0:n, 0:ncol], ps[0:n, 0:ncol]), reads=[ps], writes=[b])
                P.store(b, O.fox_v[T0 + t0:T0 + t0 + n, col0 - 4096:col0 - 4096 + ncol], b[0:n, 0:ncol])
                b2 = sh.next()
                P.op("dve", lambda e: e.tensor_copy(b2[0:n, 0:ncol], b[0:n, 0:ncol]), reads=[b], writes=[b2])
                P.store(b2, S.vbf[T0 + t0:T0 + t0 + n, col0 - 4096:col0 - 4096 + ncol], b2[0:n, 0:ncol])

            def ev_fg(ps, t0, n, col0, ncol):
                b = sm.next()
                P.op("dve", lambda e: e.tensor_tensor(b[0:n, :], ps[0:n, 0:16], bfb[0:n, :], ALU.add),
                     reads=[ps, bfb], writes=[b])
                P.op("act", lambda e: e.activation(b[0:n, :], b[0:n, :], AF.Exp, scale=-1.0), reads=[b], writes=[b])
                P.op("act", lambda e: e.activation(b[0:n, :], b[0:n, :], AF.Ln, bias=1.0), reads=[b], writes=[b])
                P.op("act", lambda e: e.mul(b[0:n, :], b[0:n, :], -1.0), reads=[b], writes=[b])
                P.store(b, O.fox_logf[T0 + t0:T0 + t0 + n, :], b[0:n, :])
                P.store(b, S.logf[T0 + t0:T0 + t0 + n, :], b[0:n, :])

            def ev_z(ps, t0, n, col0, ncol):
                b = sf.next()
                P.op("act", lambda e: e.copy(b[0:n, 0:ncol], ps[0:n, 0:ncol]), reads=[ps], writes=[b])
                P.store(b, S.z[T0 + t0:T0 + t0 + n, col0 - 6160:col0 - 6160 + ncol], b[0:n, 0:ncol])

            def ev_xbc(ps, t0, n, col0, ncol):
                b = sf.next()
                P.op("act", lambda e: e.copy(b[0:ncol, 0:n], ps[0:ncol, 0:n]), reads=[ps], writes=[b])
                P.store(b, S.xbcT[col0 - 8208:col0 - 8208 + ncol, T0 + t0:T0 + t0 + n], b[0:ncol, 0:n])

            def ev_dt(ps, t0, n, col0, ncol):
                b = sf.next()
                P.op("act", lambda e: e.copy(b[0:n, 0:32], ps[0:n, 0:32]), reads=[ps], writes=[b])
                P.store(b, S.dtraw[T0 + t0:T0 + t0 + n, :], b[0:n, 0:32])

            segs = [
                (0, 2048, "feat", ev_qk(S.qT, scale)),
                (2048, 4096, "feat", lambda ps, t0, n, col0, ncol: ev_qk(S.kT, 1.0)(ps, t0, n, col0 - 2048, ncol)),
                (2048, 4096, "tok", ev_ktok),
                (4096, 6144, "tok", ev_v),
                (6144, 6160, "tok", ev_fg),
                (6160, 8208, "tok", ev_z),
                (8208, 11280, "feat", ev_xbc),
                (11280, 11312, "tok", ev_dt),
            ]
            import os
            if os.environ.get("K_SEGS"):
                segs = [segs[int(i)] for i in os.environ["K_SEGS"].split(",")]
            gemm_segments(P, C, xT, 16, NT, I.w_in_a, segs, wring, psG, 512)


def phase_cache_prep(P, C, I, S):
    with ExitStack() as st:
        kin = Ring([P.sb(st, f"kin{i}", [128, 2048], F32) for i in range(2)])
        vin = Ring([P.sb(st, f"vin{i}", [128, 2048], F32) for i in range(2)])
        kts = Ring([P.sb(st, f"kts{i}", [128, 16, 128], BF16) for i in range(2)])
        vbs = Ring([P.sb(st, f"vbs{i}", [128, 2048], BF16) for i in range(2)])
        psT = Ring(C.psum[0:4])
        for s in range(C.NS):
            for (k0, n) in _tiles(C.PAST, 128):
                kb = kin.next()
                P.load(kb, kb[0:n, :], I.cache_k[s, k0:k0 + n, :])
                ko = kts.next()
                for c4 in range(4):
                    ps = psT.next()
                    for j in range(4):
                        c = c4 * 4 + j
                        P.op("pe", lambda e, ps=ps, kb=kb, c=c, j=j, n=n: e.transpose(
                            ps[0:128, j * 128:j * 128 + n], kb[0:n, c * 128:(c + 1) * 128], C.ident[0:n, 0:n]),
                            reads=[kb, C.ident], writes=[ps])
                    eng = "dve" if c4 % 2 == 0 else "act"
                    if eng == "dve":
                        P.op("dve", lambda e, ps=ps, c4=c4, ko=ko, n=n: e.tensor_copy(
                            ko[:, c4 * 4:(c4 + 1) * 4, 0:n], ps[:].rearrange("p (j m) -> p j m", j=4)[:, :, 0:n]),
                            reads=[ps], writes=[ko])
                    else:
                        P.op("act", lambda e, ps=ps, c4=c4, ko=ko, n=n: e.copy(
                            ko[:, c4 * 4:(c4 + 1) * 4, 0:n], ps[:].rearrange("p (j m) -> p j m", j=4)[:, :, 0:n]),
                            reads=[ps], writes=[ko])
                P.store(ko, S.kTp[s, :, k0:k0 + n].rearrange("(c p) m -> p c m", p=128), ko[:, :, 0:n])
                vb = vin.next()
                P.load(vb, vb[0:n, :], I.cache_v[s, k0:k0 + n, :], q="act")
                vo = vbs.next()
                P.op("pool", lambda e, vo=vo, vb=vb, n=n: e.tensor_copy(vo[0:n, :], vb[0:n, :]), reads=[vb], writes=[vo])
                P.store(vo, S.vp[s, k0:k0 + n, :], vo[0:n, :], q="act")


def phase_attention(P, C, I, O, S, seqs):
    with ExitStack() as st:
        Smax = max(past + T for (_, T, past, _) in seqs)
        NKT = (Smax + 127) // 128
        Tmax = max(T for (_, T, _, _) in seqs)
        F_all = P.sb(st, "F_all", [128, NKT, 16], F32)
        lfin = P.sb(st, "lfin", [128, NKT, 16], F32)
        kTh = Ring([P.sb(st, f"kTh{i}", [128, NKT * 128], BF16) for i in range(2)])
        vh = Ring([P.sb(st, f"vh{i}", [128, NKT, 128], BF16) for i in range(2)])
        qTh = Ring([P.sb(st, f"qTh{i}", [128, Tmax], BF16) for i in range(2)])
        pT = Ring([P.sb(st, f"pT{i}", [128, 512], BF16) for i in range(4)])
        bcol = Ring([P.sb(st, f"bcol{i}", [128, 1], F32) for i in range(6)])
        c_b = P.sb(st, "c_b", [128, 16], F32)
        rl = P.sb(st, "rl", [128, 512], F32)
        for b_ in kTh.items + vh.items + [F_all, lfin]:
            P.op("pool", lambda e, b_=b_: e.memset(b_[:], 0.0), writes=[b_])
        osb = Ring([P.sb(st, f"osb{i}", [128, 512], BF16) for i in range(2)])
        ps_s = Ring(C.psum[0:2])
        ps_o = Ring(C.psum[2:4])
        ps_l = Ring(C.psum[4:6])
        ps_m = Ring(C.psum[6:8])
        ones_part = P.sb(st, "ones_part", [128, 128], BF16)
        P.op("dve", lambda e: e.memset(ones_part[:], 0.0), writes=[ones_part])
        P.op("dve", lambda e: e.memset(ones_part[0:16, :], 1.0), writes=[ones_part])
        for (tok0, T, past, sidx) in seqs:
            Sk = past + T
            ktiles = _tiles(Sk, 128)
            assert all(nk in (128, 16) for (_, nk) in ktiles)
            npast_t = past // 128
            for j0 in range(0, npast_t, 16):
                j1 = min(npast_t, j0 + 16)
                P.load(lfin, lfin[:, j0:j1, :], I.cache_logf[sidx - 1, j0 * 128:j1 * 128, :].rearrange("(j p) h -> p j h", p=128))
            nfull = T // 128
            for j0 in range(0, nfull, 16):
                j1 = min(nfull, j0 + 16)
                P.load(lfin, lfin[:, npast_t + j0:npast_t + j1, :],
                       S.logf[tok0 + j0 * 128:tok0 + j1 * 128, :].rearrange("(j p) h -> p j h", p=128))
            rem = T - nfull * 128
            if rem:
                P.load(lfin, lfin[0:rem, npast_t + nfull, :], S.logf[tok0 + nfull * 128:tok0 + T, :])
            for j, (k0, nk) in enumerate(ktiles):
                ps = ps_m.next()
                P.op("pe", lambda e, ps=ps, j=j, nk=nk: e.matmul(ps[0:128, 0:16], C.tri[:, :], lfin[:, j, :],
                                                                  start=True, stop=(j == 0)),
                     reads=[C.tri, lfin], writes=[ps])
                if j > 0:
                    P.op("pe", lambda e, ps=ps, j=j, nk=nk: e.matmul(ps[0:128, 0:16], C.sellast[:, 0:128], F_all[:, j - 1, :],
                                                                      start=False, stop=True),
                         reads=[C.sellast, F_all], writes=[ps])
                P.op("dve", lambda e, ps=ps, j=j, nk=nk: e.tensor_copy(F_all[0:nk, j, :], ps[0:nk, 0:16]),
                     reads=[ps], writes=[F_all])
            if C.dbgF is not None and sidx == 0:
                P.store(F_all, C.dbgF, F_all[:].rearrange("p j h -> p (j h)"))
                P.store(lfin, C.dbgL, lfin[:].rearrange("p j h -> p (j h)"))
            for h in range(H_A):
                kb = kTh.next()
                vb = vh.next()
                qb = qTh.next()
                r0 = h * 128
                if past:
                    P.load(kb, kb[:, 0:past], S.kTp[sidx - 1, r0:r0 + 128, :])
                    for j0 in range(0, npast_t, 16):
                        j1 = min(npast_t, j0 + 16)
                        P.load(vb, vb[:, j0:j1, :], S.vp[sidx - 1, j0 * 128:j1 * 128, r0:r0 + 128].rearrange("(j p) d -> p j d", p=128),
                               q="act")
                P.load(kb, kb[:, past:past + T], S.kT[r0:r0 + 128, tok0:tok0 + T])
                for j0 in range(0, nfull, 16):
                    j1 = min(nfull, j0 + 16)
                    P.load(vb, vb[:, npast_t + j0:npast_t + j1, :],
                           S.vbf[tok0 + j0 * 128:tok0 + j1 * 128, r0:r0 + 128].rearrange("(j p) d -> p j d", p=128), q="act")
                if rem:
                    P.op("pool", lambda e, vb=vb, jj=npast_t + nfull: e.memset(vb[:, jj, :], 0.0), writes=[vb])
                    P.load(vb, vb[0:rem, npast_t + nfull, :], S.vbf[tok0 + nfull * 128:tok0 + T, r0:r0 + 128], q="act")
                P.load(qb, qb[:, 0:T], S.qT[r0:r0 + 128, tok0:tok0 + T])
                for (q0, nq) in _tiles(T, 512):
                    qa = past + q0
                    ja = qa // 128
                    assert qa % 128 == 0
                    if h == 0 or True:
                        psc = ps_m.next()
                        P.op("pe", lambda e, psc=psc, ja=ja: e.matmul(psc[:, 0:16], C.sel0[:, :], F_all[:, ja, :],
                                                                        start=True, stop=True),
                             reads=[C.sel0, F_all], writes=[psc])
                        P.op("dve", lambda e, psc=psc: e.tensor_copy(c_b[:], psc[:, 0:16]), reads=[psc], writes=[c_b])
                    po = ps_o.next()
                    pl = ps_l.next()
                    jl = [j for j, (k0, nk) in enumerate(ktiles) if k0 <= qa + nq - 1]
                    def stage_a(j):
                        k0, nk = ktiles[j]
                        fs = max(0, k0 - qa)
                        bc = bcol.next()
                        if nk < 128:
                            P.op("dve", lambda e, bc=bc: e.memset(bc[:], -30000.0), writes=[bc])
                        P.op("dve", lambda e, bc=bc, j=j, nk=nk, h=h: e.tensor_tensor(
                            bc[0:nk, :], c_b[0:nk, h:h + 1], F_all[0:nk, j, h:h + 1], ALU.subtract),
                            reads=[c_b, F_all], writes=[bc])
                        pss = ps_s.next()
                        P.op("pe", lambda e, pss=pss, kb=kb, qb=qb, k0=k0, nk=nk, fs=fs, q0=q0, nq=nq: e.matmul(
                            pss[0:128, fs:nq], kb[:, k0:k0 + 128], qb[:, q0 + fs:q0 + nq], start=True, stop=True),
                            reads=[kb, qb], writes=[pss])
                        pt = pT.next()
                        P.op("act", lambda e, pt=pt, pss=pss, bc=bc, nk=nk, fs=fs, nq=nq: e.activation(
                            pt[0:128, fs:nq], pss[0:128, fs:nq], AF.Exp, bias=bc[0:128, :]),
                            reads=[pss, bc], writes=[pt])
                        if k0 >= qa:
                            w = min(nk, nq - fs)
                            P.op("dve", lambda e, pt=pt, nk=nk, fs=fs, w=w: e.tensor_tensor(
                                pt[0:nk, fs:fs + w], pt[0:nk, fs:fs + w], C.tri_bf[0:nk, 0:w], ALU.mult),
                                reads=[pt, C.tri_bf], writes=[pt])
                        return (j, nk, fs, pt)

                    def stage_b(item):
                        j, nk, fs, pt = item
                        first = (j == jl[0])
                        last = (j == jl[-1])
                        P.op("pe", lambda e, po=po, vb=vb, pt=pt, j=j, nk=nk, fs=fs, nq=nq, first=first, last=last: e.matmul(
                            po[:, fs:nq], vb[0:128, j, :], pt[0:128, fs:nq], start=first, stop=last),
                            reads=[vb, pt], writes=[po])
                        P.op("pe", lambda e, pl=pl, pt=pt, nk=nk, fs=fs, nq=nq, first=first, last=last: e.matmul(
                            pl[:, fs:nq], (C.ones_bf if nk == 128 else ones_part)[0:128, :], pt[0:128, fs:nq],
                            start=first, stop=last),
                            reads=[C.ones_bf, ones_part, pt], writes=[pl])

                    pend = None
                    for j in jl:
                        cur = stage_a(j)
                        if pend is not None:
                            stage_b(pend)
                        pend = cur
                    stage_b(pend)
                    P.op("dve", lambda e, pl=pl, nq=nq: e.reciprocal(rl[:, 0:nq], pl[:, 0:nq]), reads=[pl], writes=[rl])
                    ob = osb.next()
                    P.op("dve", lambda e, po=po, ob=ob, nq=nq: e.tensor_tensor(ob[:, 0:nq], po[:, 0:nq], rl[:, 0:nq], ALU.mult),
                         reads=[po, rl], writes=[ob])
                    P.store(ob, S.mixT[r0:r0 + 128, tok0 + q0:tok0 + q0 + nq], ob[:, 0:nq])


def _consts():
    idx = np.arange(128)
    ident = np.eye(128, dtype=np.float32)
    tri = (idx[:, None] <= idx[None, :]).astype(np.float32)
    negtri = np.where(idx[None, :] < idx[:, None], np.float32(NEG), np.float32(0.0)).astype(np.float32)
    return {"c_ident": ident, "c_tri": tri, "c_negtri": negtri}


def make_in_map(inp, core, NS=2):
    f = lambda a: np.ascontiguousarray(a, dtype=np.float32)
    s0, s1 = core * NS, (core + 1) * NS
    past = inp["cache_fox_k"].shape[2]
    m = {
        "x_prompt": f(inp["x_prompt"][core]),
        "x_sample": f(inp["x_sample"][s0:s1].reshape(NS * DEC, D)),
        "meta_tokens": f(inp["meta_tokens"]),
        "cache_fox_k": f(inp["cache_fox_k"][0, s0:s1].reshape(NS, past, 2048)),
        "cache_fox_v": f(inp["cache_fox_v"][0, s0:s1].reshape(NS, past, 2048)),
        "cache_fox_logf": f(inp["cache_fox_logf"][0, s0:s1]),
        "state_ssd_conv": f(inp["state_ssd_conv"][0, s0:s1]),
        "state_ssd": f(inp["state_ssd"][0, s0:s1].reshape(NS, 2048, 128)),
        "state_mlstm_c": f(inp["state_mlstm_c"][0, s0:s1].reshape(NS, 2048, 128)),
        "state_mlstm_n": f(inp["state_mlstm_n"][0, s0:s1]),
        "state_mlstm_m": f(inp["state_mlstm_m"][0, s0:s1]),
        "w_in_a": f(inp["w_in_a"][0]),
        "b_fgate_a": f(inp["b_fgate_a"]),
        "conv_w": f(inp["conv_w"][0]),
        "conv_b": f(inp["conv_b"]),
        "dt_bias": f(inp["dt_bias"]),
        "a_log": f(inp["a_log"]),
        "d_skip": f(inp["d_skip"]),
        "ssd_norm_g": f(inp["ssd_norm_g"]),
        "w_out_a": f(inp["w_out_a"][0]),
        "w_in_c": f(inp["w_in_c"][0]),
        "b_igate_c": f(inp["b_igate_c"]),
        "b_fgate_c": f(inp["b_fgate_c"]),
        "mlstm_norm_g": f(inp["mlstm_norm_g"]),
        "w_out_c": f(inp["w_out_c"][0]),
        "ln_mix_g": f(inp["ln_mix_g"]),
        "ln_mix_b": f(inp["ln_mix_b"]),
        "ln_ffn_g": f(inp["ln_ffn_g"]),
        "ln_ffn_b": f(inp["ln_ffn_b"]),
        "w_ffn_gate": f(inp["w_ffn_gate"]),
        "w_ffn_up": f(inp["w_ffn_up"]),
        "w_ffn_down": f(inp["w_ffn_down"]),
    }
    m.update(_consts())
    return m


def bview(ap, shape):
    return ap.to_broadcast(list(shape))


def phase_ssd_conv(P, C, I, O, S, seqs):
    with ExitStack() as st:
        Tmax = max(T for (_, T, _, _) in seqs)
        Tpad = ((Tmax + 127) // 128) * 128
        cwin = P.sb(st, "cwin", [128, CONV_DIM], F32)
        hin = P.sb(st, "hin", [128, CONV_DIM], F32)
        cw = P.sb(st, "cw", [128, 24, 8], F32)
        hT = P.sb(st, "hT", [128, 24, 4], F32)
        xc = Ring([P.sb(st, f"xc{i}", [128, Tpad + 4], F32) for i in range(2)])
        acc = Ring([P.sb(st, f"acc{i}", [128, Tpad], F32) for i in range(2)])
        accb = Ring([P.sb(st, f"accb{i}", [128, Tpad], BF16) for i in range(2)])
        cst = P.sb(st, "cst", [128, 128], F32)
        convout = P.sb(st, "convout", [128, CONV_DIM], F32)
        stf = Ring([P.sb(st, f"stf{i}", [128, 128], F32) for i in range(10)])
        sth = Ring([P.sb(st, f"sth{i}", [128, 128], BF16) for i in range(8)])
        psT = Ring(C.psum[0:4])
        for b_ in [cwin, hin, cst] + xc.items + acc.items:
            P.op("pool", lambda e, b_=b_: e.memset(b_[:], 0.0), writes=[b_])
        P.load(cwin, cwin[0:4, :], I.conv_w)
        P.load(cwin, cwin[4:5, :], I.conv_b)
        for cc in range(24):
            ps = psT.next()
            P.op("pe", lambda e, ps=ps, cc=cc: e.transpose(ps[:, 0:128], cwin[:, cc * 128:(cc + 1) * 128], C.ident[:, :]),
                 reads=[cwin, C.ident], writes=[ps])
            P.op("dve", lambda e, ps=ps, cc=cc: e.tensor_copy(cw[:, cc, 0:5], ps[:, 0:5]), reads=[ps], writes=[cw])
        for (tok0, T, past, sidx) in seqs:
            if past:
                P.load(hin, hin[0:3, :], I.st_conv[sidx - 1])
                for cc in range(24):
                    ps = psT.next()
                    P.op("pe", lambda e, ps=ps, cc=cc: e.transpose(ps[:, 0:128], hin[:, cc * 128:(cc + 1) * 128], C.ident[:, :]),
                         reads=[hin, C.ident], writes=[ps])
                    P.op("dve", lambda e, ps=ps, cc=cc: e.tensor_copy(hT[:, cc, 0:3], ps[:, 0:3]), reads=[ps], writes=[hT])
            else:
                P.op("dve", lambda e: e.memset(hT[:], 0.0), writes=[hT])
            for cc in range(24):
                xb = xc.next()
                P.load(xb, xb[:, 3:3 + T], S.xbcT[cc * 128:(cc + 1) * 128, tok0:tok0 + T])
                P.op("dve", lambda e, xb=xb, cc=cc: e.tensor_copy(xb[:, 0:3], hT[:, cc, 0:3]), reads=[hT], writes=[xb])
                P.op("dve", lambda e, xb=xb, T=T: e.tensor_copy(cst[:, 0:3], xb[:, T:T + 3]), reads=[xb], writes=[cst])
                ps = psT.next()
                P.op("pe", lambda e, ps=ps: e.transpose(ps[:, 0:128], cst[:, :], C.ident[:, :]), reads=[cst, C.ident], writes=[ps])
                P.op("dve", lambda e, ps=ps, cc=cc: e.tensor_copy(convout[0:3, cc * 128:(cc + 1) * 128], ps[0:3, 0:128]),
                     reads=[ps], writes=[convout])
                a = acc.next()
                P.op("act", lambda e, a=a, xb=xb, cc=cc, T=T: e.activation(a[:, 0:T], xb[:, 0:T], AF.Identity,
                                                                         bias=cw[:, cc, 4:5], scale=cw[:, cc, 0:1]),
                     reads=[xb, cw], writes=[a])
                for i in range(1, 4):
                    P.op("dve", lambda e, a=a, xb=xb, cc=cc, T=T, i=i: e.scalar_tensor_tensor(
                        out=a[:, 0:T], in0=xb[:, i:i + T], scalar=cw[:, cc, i:i + 1], in1=a[:, 0:T], op0=ALU.mult, op1=ALU.add),
                        reads=[a, xb, cw], writes=[a])
                P.op("act", lambda e, a=a, T=T: e.activation(a[:, 0:T], a[:, 0:T], AF.Silu), reads=[a], writes=[a])
                if cc < 20:
                    for (b0, n) in _tiles(T, 128):
                        ps = psT.next()
                        P.op("pe", lambda e, ps=ps, a=a, b0=b0: e.transpose(ps[:, 0:128], a[:, b0:b0 + 128], C.ident[:, :]),
                             reads=[a, C.ident], writes=[ps])
                        if cc < 16:
                            sb_ = stf.next()
                            P.op("act", lambda e, ps=ps, sb_=sb_: e.copy(sb_[:, :], ps[:, 0:128]), reads=[ps], writes=[sb_])
                            P.store(sb_, S.xtok[tok0 + b0:tok0 + b0 + n, cc * 128:(cc + 1) * 128], sb_[0:n, :])
                        else:
                            sb_ = sth.next()
                            P.op("act", lambda e, ps=ps, sb_=sb_: e.copy(sb_[:, :], ps[:, 0:128]), reads=[ps], writes=[sb_])
                            P.store(sb_, S.bmtok[tok0 + b0:tok0 + b0 + n, (cc - 16) * 128:(cc - 15) * 128], sb_[0:n, :])
                if cc >= 16:
                    ab = accb.next()
                    P.op("pool", lambda e, ab=ab, a=a, T=T: e.tensor_copy(ab[:, 0:T], a[:, 0:T]), reads=[a], writes=[ab])
                    P.store(ab, S.bcT[(cc - 16) * 128:(cc - 15) * 128, tok0:tok0 + T], ab[:, 0:T])
            P.store(convout, O.ssd_conv[sidx], convout[0:3, :])


def phase_ssd(P, C, I, O, S, seqs):
    with ExitStack() as st:
        sb = lambda name, shape, dt=F32: P.sb(st, name, shape, dt)
        dtb_b = sb("dtb_b", [128, 32]); a_b = sb("a_b", [128, 32]); D_b = sb("D_b", [128, 32])
        g_b = sb("g_b", [128, 2048])
        P.load(dtb_b, dtb_b[:], bview(I.dt_bias[0:1, :], [128, 32]))
        P.load(a_b, a_b[:], bview(I.a_log[0:1, :], [128, 32]))
        P.load(D_b, D_b[:], bview(I.d_skip[0:1, :], [128, 32]))
        P.load(g_b, g_b[:], bview(I.ssd_norm_g[0:1, :], [128, 2048]))
        P.op("act", lambda e: e.activation(a_b[:], a_b[:], AF.Exp), reads=[a_b], writes=[a_b])
        P.op("act", lambda e: e.mul(a_b[:], a_b[:], -1.0), reads=[a_b], writes=[a_b])
        valid16 = sb("valid16", [128, 1])
        P.op("dve", lambda e: e.memset(valid16[:], 0.0), writes=[valid16])
        P.op("dve", lambda e: e.memset(valid16[0:16, :], 1.0), writes=[valid16])
        S_sb = sb("S_sb", [128, 2048]); S_bf = sb("S_bf", [128, 2048], BF16)
        sio = sb("sio", [128, 16, 128])
        x_c = Ring([sb(f"x_c{i}", [128, 2048]) for i in range(2)])
        z_c = Ring([sb(f"z_c{i}", [128, 2048]) for i in range(2)])
        bm_c = Ring([sb(f"bm_c{i}", [128, 512], BF16) for i in range(2)])
        bcT_c = Ring([sb(f"bcT_c{i}", [128, 8, 128], BF16) for i in range(2)])
        dt_c = Ring([sb(f"dt_c{i}", [128, 32]) for i in range(2)])
        dt = sb("dt", [128, 32]); lndt = sb("lndt", [128, 32]); dta = sb("dta", [128, 32]); A_sb = sb("A_sb", [128, 32])
        bias_all = sb("bias_all", [128, 32]); wend = sb("wend", [128, 32]); cd_b = sb("cd_b", [128, 32]); expA = sb("expA", [128, 32])
        D2 = sb("D2", [128, 32, 128])
        x_bf = sb("x_bf", [128, 2048], BF16); xw = sb("xw", [128, 2048], BF16)
        dec = Ring([sb(f"dec{i}", [128, 128]) for i in range(3)])
        WT = Ring([sb(f"WT{i}", [128, 128], BF16) for i in range(3)])
        y_c = sb("y_c", [128, 2048]); tmpb = sb("tmpb", [128, 2048]); junk = sb("junk", [128, 512])
        ss = sb("ss", [128, 4]); rstd = sb("rstd", [128, 4])
        yT = Ring([sb(f"yT{i}", [128, 16, 128], BF16) for i in range(2)])
        psE = [C.psum[0], C.psum[1]]
        psCB, ps_y, ps_off, ps_st = C.psum[2], C.psum[3], C.psum[4], C.psum[5]
        psm = Ring(C.psum[6:8])
        for b_ in x_c.items + z_c.items + bm_c.items + bcT_c.items + dt_c.items + [sio]:
            P.op("pool", lambda e, b_=b_: e.memset(b_[:], 0.0), writes=[b_])

        def v3(buf, lo, n):
            return buf[:, lo * 64:(lo + n) * 64].rearrange("p (h q) -> p h q", q=64)

        for (tok0, T, past, sidx) in seqs:
            if past:
                P.load(sio, sio[:], I.st_ssd[sidx - 1].rearrange("(c p) n -> p c n", p=128))
                for c in range(16):
                    ps = psm.next()
                    P.op("pe", lambda e, ps=ps, c=c: e.transpose(ps[:, 0:128], sio[:, c, :], C.ident[:, :]),
                         reads=[sio, C.ident], writes=[ps])
                    P.op("dve", lambda e, ps=ps, c=c: e.tensor_copy(S_sb[:, c * 128:(c + 1) * 128], ps[:, 0:128]),
                         reads=[ps], writes=[S_sb])
            else:
                P.op("dve", lambda e: e.memset(S_sb[:], 0.0), writes=[S_sb])
            P.op("act", lambda e: e.copy(S_bf[:], S_sb[:]), reads=[S_sb], writes=[S_bf])
            for (t0, L) in _tiles(T, 128):
                assert L in (128, 16)
                a0 = tok0 + t0
                xb = x_c.next(); zb = z_c.next(); bmb = bm_c.next(); bcb = bcT_c.next(); dtr = dt_c.next()
                if L < 128:
                    for b_ in (xb, zb, bmb, bcb, dtr):
                        P.op("pool", lambda e, b_=b_: e.memset(b_[:], 0.0), writes=[b_])
                P.load(xb, xb[0:L, :], S.xtok[a0:a0 + L, :])
                P.load(zb, zb[0:L, :], S.z[a0:a0 + L, :], q="act")
                P.load(bmb, bmb[0:L, :], S.bmtok[a0:a0 + L, :])
                P.load(bcb, bcb[:, :, 0:L], S.bcT[:, a0:a0 + L].rearrange("(g n) t -> n g t", n=128), q="act")
                P.load(dtr, dtr[0:L, :], S.dtraw[a0:a0 + L, :])
                P.op("dve", lambda e, dtr=dtr: e.tensor_tensor(dt[:], dtr[:], dtb_b[:], ALU.add), reads=[dtr, dtb_b], writes=[dt])
                P.op("act", lambda e: e.activation(dt[:], dt[:], AF.Exp), reads=[dt], writes=[dt])
                P.op("act", lambda e: e.activation(dt[:], dt[:], AF.Ln, bias=1.0), reads=[dt], writes=[dt])
                if L < 128:
                    P.op("dve", lambda e: e.tensor_scalar(dt[:], dt[:], valid16[:, 0:1], None, ALU.mult), reads=[dt, valid16], writes=[dt])
                P.op("dve", lambda e: e.tensor_scalar(lndt[:], dt[:], 1e-30, None, ALU.max), reads=[dt], writes=[lndt])
                P.op("act", lambda e: e.activation(lndt[:], lndt[:], AF.Ln), reads=[lndt], writes=[lndt])
                P.op("dve", lambda e: e.tensor_tensor(dta[:], dt[:], a_b[:], ALU.mult), reads=[dt, a_b], writes=[dta])
                ps = psm.next()
                P.op("pe", lambda e, ps=ps: e.matmul(ps[:, 0:32], C.tri[:, :], dta[:, :], start=True, stop=True),
                     reads=[C.tri, dta], writes=[ps])
                P.op("dve", lambda e, ps=ps: e.tensor_copy(A_sb[:], ps[:, 0:32]), reads=[ps], writes=[A_sb])
                pse = psm.next()
                P.op("pe", lambda e, pse=pse: e.matmul(pse[:, 0:32], C.sellast[:, :], A_sb[:, :], start=True, stop=True),
                     reads=[C.sellast, A_sb], writes=[pse])
                P.op("dve", lambda e: e.tensor_tensor(bias_all[:], lndt[:], A_sb[:], ALU.subtract), reads=[lndt, A_sb], writes=[bias_all])
                P.op("dve", lambda e, pse=pse: e.tensor_tensor(wend[:], pse[:, 0:32], bias_all[:], ALU.add), reads=[pse, bias_all], writes=[wend])
                P.op("act", lambda e: e.activation(wend[:], wend[:], AF.Exp), reads=[wend], writes=[wend])
                P.op("act", lambda e, pse=pse: e.activation(cd_b[:], pse[:, 0:32], AF.Exp), reads=[pse], writes=[cd_b])
                P.op("act", lambda e: e.activation(expA[:], A_sb[:], AF.Exp), reads=[A_sb], writes=[expA])
                P.op("dve", lambda e: e.tensor_tensor(D2[:], bview(C.ident[:, :].unsqueeze(1), [128, 32, 128]),
                                                      bview(A_sb[:, :].unsqueeze(2), [128, 32, 128]), ALU.mult),
                     reads=[C.ident, A_sb], writes=[D2])
                P.op("act", lambda e, xb=xb: e.copy(x_bf[:], xb[:]), reads=[xb], writes=[x_bf])
                P.op("dve", lambda e, xb=xb: e.tensor_tensor(v3(xw, 0, 32), v3(xb, 0, 32), bview(wend[:, :].unsqueeze(2), [128, 32, 64]), ALU.mult),
                     reads=[xb, wend], writes=[xw])
                for g in range(4):
                    for half in range(2):
                        pe_ = psE[half]
                        P.op("pe", lambda e, pe_=pe_, g=g, half=half: e.matmul(
                            pe_[:, 0:512], C.ones_f[:, :], D2[:, g * 8 + half * 4:g * 8 + half * 4 + 4, :].rearrange("p h l -> p (h l)"),
                            start=True, stop=False), reads=[C.ones_f, D2], writes=[pe_])
                        P.op("pe", lambda e, pe_=pe_: e.matmul(
                            pe_[:, 0:512], C.ident[:, :], C.negtri4[:, :], start=False, stop=True),
                            reads=[C.ident, C.negtri4], writes=[pe_])
                    P.op("pe", lambda e, bcb=bcb, g=g: e.matmul(psCB[:, 0:128], bcb[:, g, :], bcb[:, 4 + g, :], start=True, stop=True),
                         reads=[bcb], writes=[psCB])
                    for hh in range(8):
                        h = g * 8 + hh
                        pe_ = psE[hh // 4]
                        d_ = dec.next()
                        P.op("act", lambda e, d_=d_, pe_=pe_, hh=hh, h=h: e.activation(
                            d_[:, :], pe_[:, (hh % 4) * 128:(hh % 4 + 1) * 128], AF.Exp, bias=bias_all[:, h:h + 1]),
                            reads=[pe_, bias_all], writes=[d_])
                        w_ = WT.next()
                        P.op("dve", lambda e, d_=d_, w_=w_: e.tensor_tensor(w_[:, :], d_[:, :], psCB[:, 0:128], ALU.mult),
                             reads=[d_, psCB], writes=[w_])
                        P.op("pe", lambda e, w_=w_, hh=hh, h=h: e.matmul(ps_y[:, hh * 64:(hh + 1) * 64], w_[:, :], x_bf[:, h * 64:(h + 1) * 64],
                                                                          start=True, stop=True), reads=[w_, x_bf], writes=[ps_y])
                    P.op("pe", lambda e, bcb=bcb, g=g: e.matmul(ps_off[:, 0:512], bcb[:, 4 + g, :], S_bf[:, g * 512:(g + 1) * 512],
                                                                  start=True, stop=True), reads=[bcb, S_bf], writes=[ps_off])
                    P.op("dve", lambda e, g=g: e.tensor_tensor(v3(y_c, g * 8, 8), ps_off[:, 0:512].rearrange("p (h q) -> p h q", q=64),
                                                                bview(expA[:, g * 8:(g + 1) * 8].unsqueeze(2), [128, 8, 64]), ALU.mult),
                         reads=[ps_off, expA], writes=[y_c])
                    P.op("dve", lambda e, g=g: e.tensor_tensor(y_c[:, g * 512:(g + 1) * 512], y_c[:, g * 512:(g + 1) * 512], ps_y[:, 0:512], ALU.add),
                         reads=[y_c, ps_y], writes=[y_c])
                    P.op("pe", lambda e, bmb=bmb, g=g: e.matmul(ps_st[:, 0:512], bmb[:, g * 128:(g + 1) * 128], xw[:, g * 512:(g + 1) * 512],
                                                                  start=True, stop=True), reads=[bmb, xw], writes=[ps_st])
                    P.op("dve", lambda e, g=g: e.tensor_tensor(v3(S_sb, g * 8, 8), v3(S_sb, g * 8, 8),
                                                                bview(cd_b[:, g * 8:(g + 1) * 8].unsqueeze(2), [128, 8, 64]), ALU.mult),
                         reads=[S_sb, cd_b], writes=[S_sb])
                    P.op("dve", lambda e, g=g: e.tensor_tensor(S_sb[:, g * 512:(g + 1) * 512], S_sb[:, g * 512:(g + 1) * 512], ps_st[:, 0:512], ALU.add),
                         reads=[S_sb, ps_st], writes=[S_sb])
                    P.op("act", lambda e, g=g: e.copy(S_bf[:, g * 512:(g + 1) * 512], S_sb[:, g * 512:(g + 1) * 512]), reads=[S_sb], writes=[S_bf])
                P.op("pool", lambda e, xb=xb: e.tensor_tensor(v3(tmpb, 0, 32), v3(xb, 0, 32), bview(D_b[:, :].unsqueeze(2), [128, 32, 64]), ALU.mult),
                     reads=[xb, D_b], writes=[tmpb])
                P.op("dve", lambda e: e.tensor_tensor(y_c[:], y_c[:], tmpb[:], ALU.add), reads=[y_c, tmpb], writes=[y_c])
                P.op("act", lambda e, zb=zb: e.activation(tmpb[:], zb[:], AF.Silu), reads=[zb], writes=[tmpb])
                P.op("dve", lambda e: e.tensor_tensor(y_c[:], y_c[:], tmpb[:], ALU.mult), reads=[y_c, tmpb], writes=[y_c])
                for g in range(4):
                    P.op("act", lambda e, g=g: e.activation(junk[:, :], y_c[:, g * 512:(g + 1) * 512], AF.Square, accum_out=ss[:, g:g + 1]),
                         reads=[y_c], writes=[junk, ss])
                P.op("dve", lambda e: e.tensor_scalar(rstd[:], ss[:], 1.0 / 512, 1e-5, ALU.mult, ALU.add), reads=[ss], writes=[rstd])
                P.op("act", lambda e: e.activation(rstd[:], rstd[:], AF.Sqrt), reads=[rstd], writes=[rstd])
                P.op("dve", lambda e: e.reciprocal(rstd[:], rstd[:]), reads=[rstd], writes=[rstd])
                for g in range(4):
                    P.op("dve", lambda e, g=g: e.tensor_scalar(y_c[:, g * 512:(g + 1) * 512], y_c[:, g * 512:(g + 1) * 512], rstd[:, g:g + 1], None, ALU.mult),
                         reads=[y_c, rstd], writes=[y_c])
                P.op("pool", lambda e: e.tensor_tensor(y_c[:], y_c[:], g_b[:], ALU.mult), reads=[y_c, g_b], writes=[y_c])
                yt = yT.next()
                for c4 in range(4):
                    ps = psm.next()
                    for j in range(4):
                        c = c4 * 4 + j
                        P.op("pe", lambda e, ps=ps, c=c, j=j: e.transpose(ps[:, j * 128:(j + 1) * 128], y_c[:, c * 128:(c + 1) * 128], C.ident[:, :]),
                             reads=[y_c, C.ident], writes=[ps])
                    P.op("act", lambda e, ps=ps, c4=c4, yt=yt: e.copy(yt[:, c4 * 4:(c4 + 1) * 4, :], ps[:].rearrange("p (j m) -> p j m", j=4)),
                         reads=[ps], writes=[yt])
                P.store(yt, S.mixT[2048:4096, a0:a0 + L].rearrange("(c p) t -> p c t", p=128), yt[:, :, 0:L])
            for c in range(16):
                ps = psm.next()
                P.op("pe", lambda e, ps=ps, c=c: e.transpose(ps[:, 0:128], S_sb[:, c * 128:(c + 1) * 128], C.ident[:, :]),
                     reads=[S_sb, C.ident], writes=[ps])
                P.op("dve", lambda e, ps=ps, c=c: e.tensor_copy(sio[:, c, :], ps[:, 0:128]), reads=[ps], writes=[sio])
            P.store(sio, O.ssd_state[sidx].rearrange("(c p) n -> p c n", p=128), sio[:])


def phase_outproj(P, C, W, Kc, srcT, dst):
    with ExitStack() as st:
        TT = 1152
        wcols = 512 if Kc <= 16 else 256
        xT = P.sb(st, "xT", [128, Kc, TT + 128], BF16)
        P.op("pool", lambda e: e.memset(xT[:], 0.0), writes=[xT])
        wring = Ring([P.sb(st, f"w{i}", [128, Kc, wcols], BF16) for i in range(2)])
        sf = Ring([P.sb(st, f"sf{i}", [128, 512], F32) for i in range(3)])
        psG = Ring(C.psum[0:8])
        for (T0, NT) in supertiles(C.NTOK, TT):
            for k0 in range(0, Kc, 16):
                P.load(xT, xT[:, k0:k0 + 16, 0:NT], srcT[k0 * 128:(k0 + 16) * 128, T0:T0 + NT].rearrange("(c p) t -> p c t", p=128),
                       q=("sp" if (k0 // 16) % 2 == 0 else "act"))

            def ev(ps, t0, n, col0, ncol, T0=T0):
                b = sf.next()
                P.op("act", lambda e: e.copy(b[0:n, 0:ncol], ps[0:n, 0:ncol]), reads=[ps], writes=[b])
                P.store(b, dst[T0 + t0:T0 + t0 + n, col0:col0 + ncol], b[0:n, 0:ncol])

            gemm_segments(P, C, xT, Kc, NT, W, [(0, 2048, "tok", ev)], wring, psG, wcols)


def make_ln_loader(P, C, st, xa_src, xb_src, g_ap, b_ap, x_dst):
    g_b = P.sb(st, "lng", [128, D], F32)
    b_b = P.sb(st, "lnb", [128, D], F32)
    P.load(g_b, g_b[:], bview(g_ap, [128, D]))
    P.load(b_b, b_b[:], bview(b_ap, [128, D]))
    mring = Ring([P.sb(st, f"lnm{i}", [128, D], F32) for i in range(3)])
    aring = Ring([P.sb(st, f"lna{i}", [128, D], F32) for i in range(3)])
    stats = P.sb(st, "lnstats", [128, 24], F32)
    mv = P.sb(st, "lnmv", [128, 2], F32)
    rstd = P.sb(st, "lnrstd", [128, 1], F32)

    def loader(xb, tok0, n):
        xa = aring.next()
        mb = mring.next()
        if callable(xa_src):
            xa_src(xa, tok0, n)
        else:
            P.load(xa, xa[0:n, :], xa_src[tok0:tok0 + n, :])
        P.load(mb, mb[0:n, :], xb_src[tok0:tok0 + n, :], q="act")
        layer_norm_tile(P, C, (stats, mv, rstd), xa, mb, n, g_b, b_b, xb)
        P.store(xb, x_dst[tok0:tok0 + n, :], xb[0:n, :])

    return loader


def phase_ffn(P, C, xT_src, wg, wu, wd, dst):
    with ExitStack() as st:
        TT = 896
        KF = D_FF // 128
        xT = P.sb(st, "xT", [128, 16, TT], BF16)
        hT = P.sb(st, "hT", [128, KF, TT], BF16)
        P.op("pool", lambda e: e.memset(xT[:], 0.0), writes=[xT])
        P.op("pool", lambda e: e.memset(hT[:], 0.0), writes=[hT])
        wring = Ring([P.sb(st, f"w{i}", [128, KF * 256], BF16) for i in range(3)])
        assert all(((nt + 127) // 128) * 128 <= TT for (_, nt) in supertiles(C.NTOK, TT))
        sf = Ring([P.sb(st, f"sf{i}", [128, 512], F32) for i in range(3)])
        sg = Ring([P.sb(st, f"sg{i}", [128, 512], F32) for i in range(2)])
        psA = Ring(C.psum[0:4])
        psB = Ring(C.psum[4:8])
        wgv = wg.rearrange("(c p) n -> p c n", p=128)
        wuv = wu.rearrange("(c p) n -> p c n", p=128)
        wdv = wd.rearrange("(c p) n -> p c n", p=128)
        for (T0, NT) in supertiles(C.NTOK, TT):
            P.load(xT, xT[:, :, 0:NT], xT_src[:, T0:T0 + NT].rearrange("(c p) t -> p c t", p=128))
            for (f0, nf) in _tiles(D_FF, 512):
                wgb = wring.next()
                wub = wring.next()
                wg3 = wgb[:, 0:16 * 512].rearrange("p (c n) -> p c n", n=512)
                wu3 = wub[:, 0:16 * 512].rearrange("p (c n) -> p c n", n=512)
                P.load(wgb, wg3[:, :, 0:nf], wgv[:, :, f0:f0 + nf], q="pool")
                P.load(wub, wu3[:, :, 0:nf], wuv[:, :, f0:f0 + nf], q="pool")
                for (s0, ns) in _tiles(nf, 128):
                    j = (f0 + s0) // 128
                    for (t0, n) in _tiles(NT, 512):
                        pg = psA.next()
                        pu = psB.next()
                        for c in range(16):
                            P.op("pe", lambda e, pg=pg, c=c, wg3=wg3, s0=s0, t0=t0, n=n: e.matmul(
                                pg[:, 0:n], wg3[:, c, s0:s0 + 128], xT[:, c, t0:t0 + n], start=(c == 0), stop=(c == 15)),
                                reads=[wgb, xT], writes=[pg])
                        for c in range(16):
                            P.op("pe", lambda e, pu=pu, c=c, wu3=wu3, s0=s0, t0=t0, n=n: e.matmul(
                                pu[:, 0:n], wu3[:, c, s0:s0 + 128], xT[:, c, t0:t0 + n], start=(c == 0), stop=(c == 15)),
                                reads=[wub, xT], writes=[pu])
                        g_ = sg.next()
                        P.op("act", lambda e, g_=g_, pg=pg, n=n: e.activation(g_[:, 0:n], pg[:, 0:n], AF.Silu), reads=[pg], writes=[g_])
                        P.op("dve", lambda e, g_=g_, pu=pu, n=n, j=j, t0=t0: e.tensor_tensor(hT[:, j, t0:t0 + n], g_[:, 0:n], pu[:, 0:n], ALU.mult),
                             reads=[g_, pu], writes=[hT])
            for (c0, ncw) in _tiles(D, 256):
                wb = wring.next()
                w3 = wb[:, 0:KF * 256].rearrange("p (c n) -> p c n", n=256)
                for k0 in range(0, KF, 11):
                    P.load(wb, w3[:, k0:k0 + 11, 0:ncw], wdv[:, k0:k0 + 11, c0:c0 + ncw], q="pool")
                for (t0, n) in _tiles(NT, 128):
                    ps = psA.next()
                    for c in range(KF):
                        P.op("pe", lambda e, ps=ps, c=c, t0=t0, w3=w3, ncw=ncw: e.matmul(
                            ps[0:128, 0:ncw], hT[:, c, t0:t0 + 128], w3[:, c, 0:ncw], start=(c == 0), stop=(c == KF - 1)),
                            reads=[hT, wb], writes=[ps])
                    b = sf.next()
                    P.op("act", lambda e, b=b, ps=ps, n=n, ncw=ncw: e.copy(b[0:n, 0:ncw], ps[0:n, 0:ncw]), reads=[ps], writes=[b])
                    P.store(b, dst[T0 + t0:T0 + t0 + n, c0:c0 + ncw], b[0:n, 0:ncw])


def phase_ln_T(P, C, xa_src, xb_src, g_ap, b_ap, x_dst, xT_dst):
    with ExitStack() as st:
        loader = make_ln_loader(P, C, st, xa_src, xb_src, g_ap, b_ap, x_dst)
        xin = Ring([P.sb(st, f"xin{i}", [128, D], F32) for i in range(3)])
        for b_ in xin.items:
            P.op("pool", lambda e, b_=b_: e.memset(b_[:], 0.0), writes=[b_])
        xTs = Ring([P.sb(st, f"xTs{i}", [128, 16, 128], BF16) for i in range(3)])
        psT = Ring(C.psum[0:4])
        for (t0, n) in _tiles(C.NTOK, 128):
            xt = xTs.next()
            build_xT(P, C, st, xt, t0, n, 0, loader, xin, psT)
            P.store(xt, xT_dst[:, t0:t0 + n].rearrange("(c p) t -> p c t", p=128), xt[:, :, 0:n])


def phase_final_ln(P, C, xa_src, xb_src, g_ap, b_ap, dst):
    with ExitStack() as st:
        loader = make_ln_loader(P, C, st, xa_src, xb_src, g_ap, b_ap, dst)
        outr = Ring([P.sb(st, f"lnout{i}", [128, D], F32) for i in range(3)])
        for (t0, n) in _tiles(C.NTOK, 128):
            loader(outr.next(), t0, n)


def phase_inproj_c(P, C, I, S, loader_factory):
    NTOK = C.NTOK
    kscale = DK_C ** -0.5
    with ExitStack() as st:
        TT = 1152
        loader = loader_factory(st)
        xT = P.sb(st, "xT", [128, 16, TT + 128], BF16)
        P.op("pool", lambda e: e.memset(xT[:], 0.0), writes=[xT])
        xin = Ring([P.sb(st, f"xin{i}", [128, D], F32) for i in range(2)])
        for b_ in xin.items:
            P.op("pool", lambda e, b_=b_: e.memset(b_[:], 0.0), writes=[b_])
        wring = Ring([P.sb(st, f"w{i}", [128, 16, 512], BF16) for i in range(2)])
        sf = Ring([P.sb(st, f"sf{i}", [128, 512], F32) for i in range(3)])
        sh = Ring([P.sb(st, f"sh{i}", [128, 512], BF16) for i in range(3)])
        sm = Ring([P.sb(st, f"sm{i}", [128, 16], F32) for i in range(3)])
        gb = P.sb(st, "gb", [128, 16], F32)
        P.load(gb, gb[:, 0:8], bview(I.b_igate_c[0:1, :], [128, 8]))
        P.load(gb, gb[:, 8:16], bview(I.b_fgate_c[0:1, :], [128, 8]))
        psT = Ring(C.psum[0:2])
        psG = Ring(C.psum[2:8])
        for (T0, NT) in supertiles(NTOK, TT):
            for (t0, n) in _tiles(NT, 128):
                build_xT(P, C, st, xT, T0 + t0, n, t0, loader, xin, psT)

            def ev_feat(dst, mul, cbase, T0=T0):
                def f(ps, t0, n, col0, ncol):
                    b = sh.next()
                    P.op("act", lambda e: e.activation(b[0:ncol, 0:n], ps[0:ncol, 0:n], AF.Copy, scale=float(mul)), reads=[ps], writes=[b])
                    P.store(b, dst[col0 - cbase:col0 - cbase + ncol, T0 + t0:T0 + t0 + n], b[0:ncol, 0:n])
                return f

            def ev_tok_bf(dst, mul, cbase, T0=T0):
                def f(ps, t0, n, col0, ncol):
                    b = sh.next()
                    P.op("act", lambda e: e.activation(b[0:n, 0:ncol], ps[0:n, 0:ncol], AF.Copy, scale=float(mul)), reads=[ps], writes=[b])
                    P.store(b, dst[T0 + t0:T0 + t0 + n, col0 - cbase:col0 - cbase + ncol], b[0:n, 0:ncol])
                return f

            def ev_o(ps, t0, n, col0, ncol, T0=T0):
                b = sf.next()
                P.op("act", lambda e: e.copy(b[0:n, 0:ncol], ps[0:n, 0:ncol]), reads=[ps], writes=[b])
                P.store(b, S.z[T0 + t0:T0 + t0 + n, col0 - 4096:col0 - 4096 + ncol], b[0:n, 0:ncol])

            def ev_gates(ps, t0, n, col0, ncol, T0=T0):
                b = sm.next()
                P.op("dve", lambda e: e.tensor_tensor(b[0:n, :], ps[0:n, 0:16], gb[0:n, :], ALU.add), reads=[ps, gb], writes=[b])
                P.op("act", lambda e: e.activation(b[0:n, 8:16], b[0:n, 8:16], AF.Exp, scale=-1.0), reads=[b], writes=[b])
                P.op("act", lambda e: e.activation(b[0:n, 8:16], b[0:n, 8:16], AF.Ln, bias=1.0), reads=[b], writes=[b])
                P.op("act", lambda e: e.mul(b[0:n, 8:16], b[0:n, 8:16], -1.0), reads=[b], writes=[b])
                P.store(b, S.gates[T0 + t0:T0 + t0 + n, :], b[0:n, :])

            segs = [
                (0, 1024, "feat", ev_feat(S.qT, 1.0, 0)),
                (1024, 2048, "feat", ev_feat(S.kT, kscale, 1024)),
                (1024, 2048, "tok", ev_tok_bf(S.ktok, kscale, 1024)),
                (2048, 4096, "tok", ev_tok_bf(S.vbf, 1.0, 2048)),
                (4096, 6144, "tok", ev_o),
                (6144, 6160, "tok", ev_gates),
            ]
            gemm_segments(P, C, xT, 16, NT, I.w_in_c, segs, wring, psG, 512)


def phase_mlstm(P, C, I, O, S, seqs):
    with ExitStack() as st:
        sb = lambda name, shape, dt=F32: P.sb(st, name, shape, dt)
        g_b = sb("g_b", [128, 2048])
        P.load(g_b, g_b[:], bview(I.mlstm_norm_g[0:1, :], [128, 2048]))
        Cs = sb("Cs", [128, 8, 257]); Cs_bf = sb("Cs_bf", [128, 8, 257], BF16)
        m_b = sb("m_b", [128, 8]); m_col = sb("m_col", [128, 1])
        cio = sb("cio", [128, 16, 128])
        npad = sb("npad", [128, 128])
        gt = Ring([sb(f"gt{i}", [128, 16]) for i in range(2)])
        qT_c = Ring([sb(f"qT_c{i}", [128, 8, 128], BF16) for i in range(2)])
        kT_c = Ring([sb(f"kT_c{i}", [128, 8, 128], BF16) for i in range(2)])
        kt_c = Ring([sb(f"kt_c{i}", [128, 1024], BF16) for i in range(2)])
        va_c = Ring([sb(f"va_c{i}", [128, 8, 257], BF16) for i in range(2)])
        o_c = Ring([sb(f"o_c{i}", [128, 2048]) for i in range(2)])
        bcum = sb("bcum", [128, 8]); u = sb("u", [128, 8]); upad = sb("upad", [128, 128]); bpad = sb("bpad", [128, 128])
        uT = sb("uT", [128, 128]); bT = sb("bT", [128, 128]); MT = sb("MT", [128, 128]); ones8 = sb("ones8", [128, 128])
        M_sb = sb("M_sb", [128, 8]); negM = sb("negM", [128, 8]); Mlast = sb("Mlast", [128, 8]); btot = sb("btot", [128, 8])
        ginter = sb("ginter", [128, 8]); wend = sb("wend", [128, 8]); gold = sb("gold", [128, 8]); emr = sb("emr", [128, 8])
        D2m = sb("D2m", [128, 8, 128])
        dec = Ring([sb(f"dec{i}", [128, 128]) for i in range(3)])
        WT = Ring([sb(f"WT{i}", [128, 128], BF16) for i in range(3)])
        numsb = Ring([sb(f"numsb{i}", [128, 257]) for i in range(2)])
        comb = Ring([sb(f"comb{i}", [128, 257]) for i in range(2)])
        rdn = Ring([sb(f"rdn{i}", [128, 1]) for i in range(2)])
        vw = Ring([sb(f"vw{i}", [128, 257], BF16) for i in range(2)])
        h_c = sb("h_c", [128, 2048]); tmpb = sb("tmpb", [128, 2048]); junk = sb("junk", [128, 256])
        ss = sb("ss", [128, 8]); rstd = sb("rstd", [128, 8])
        hTs = Ring([sb(f"hTs{i}", [128, 16, 128], BF16) for i in range(2)])
        psE = [C.psum[0], C.psum[1]]
        ps_s, ps_num, ps_int, ps_c = C.psum[2], C.psum[3], C.psum[4], C.psum[5]
        psm = Ring(C.psum[6:8])
        for b_ in [upad, bpad, MT, npad, cio, m_col] + qT_c.items + kT_c.items + kt_c.items + va_c.items + o_c.items:
            P.op("pool", lambda e, b_=b_: e.memset(b_[:], 0.0), writes=[b_])
        P.op("pool", lambda e: e.memset(ones8[:], 1.0), writes=[ones8])

        for (tok0, T, past, sidx) in seqs:
            if past:
                P.load(cio, cio[:], I.st_c[sidx - 1].rearrange("(c p) d -> p c d", p=128))
                for c in range(16):
                    ps = psm.next()
                    P.op("pe", lambda e, ps=ps, c=c: e.transpose(ps[:, 0:128], cio[:, c, :], C.ident[:, :]), reads=[cio, C.ident], writes=[ps])
                    P.op("dve", lambda e, ps=ps, c=c: e.tensor_copy(Cs[:, c // 2, (c % 2) * 128:(c % 2 + 1) * 128], ps[:, 0:128]),
                         reads=[ps], writes=[Cs])
                P.load(npad, npad[0:8, :], I.st_n[sidx - 1])
                ps = psm.next()
                P.op("pe", lambda e, ps=ps: e.transpose(ps[:, 0:128], npad[:, :], C.ident[:, :]), reads=[npad, C.ident], writes=[ps])
                P.op("dve", lambda e, ps=ps: e.tensor_copy(Cs[:, :, 256], ps[:, 0:8]), reads=[ps], writes=[Cs])
                P.load(m_b, m_b[:], bview(I.st_m[sidx - 1:sidx, :], [128, 8]))
                P.load(m_col, m_col[0:8, :], I.st_m[sidx - 1:sidx, :].rearrange("o h -> h o"))
            else:
                P.op("dve", lambda e: e.memset(Cs[:], 0.0), writes=[Cs])
                P.op("dve", lambda e: e.memset(m_b[:], 0.0), writes=[m_b])
                P.op("dve", lambda e: e.memset(m_col[:], 0.0), writes=[m_col])
            P.op("act", lambda e: e.copy(Cs_bf[:], Cs[:]), reads=[Cs], writes=[Cs_bf])
            for (t0, L) in _tiles(T, 128):
                assert L in (128, 16)
                a0 = tok0 + t0
                g_ = gt.next(); qb = qT_c.next(); kb = kT_c.next(); ktb = kt_c.next(); vab = va_c.next(); ob = o_c.next()
                if L < 128:
                    for b_ in (qb, kb, ktb, vab, ob):
                        P.op("pool", lambda e, b_=b_: e.memset(b_[:], 0.0), writes=[b_])
                P.op("pool", lambda e, g_=g_: e.memset(g_[:, 0:8], NEG), writes=[g_])
                P.op("pool", lambda e, g_=g_: e.memset(g_[:, 8:16], 0.0), writes=[g_])
                P.load(g_, g_[0:L, :], S.gates[a0:a0 + L, :])
                P.load(qb, qb[:, :, 0:L], S.qT[0:1024, a0:a0 + L].rearrange("(h d) t -> d h t", d=128))
                P.load(kb, kb[:, :, 0:L], S.kT[0:1024, a0:a0 + L].rearrange("(h d) t -> d h t", d=128), q="act")
                P.load(ktb, ktb[0:L, :], S.ktok[a0:a0 + L, :])
                P.load(vab, vab[0:L, :, 0:256], S.vbf[a0:a0 + L, :].rearrange("t (h v) -> t h v", v=256), q="act")
                P.op("pool", lambda e, vab=vab: e.memset(vab[:, :, 256:257], 1.0), writes=[vab])
                P.load(ob, ob[0:L, :], S.z[a0:a0 + L, :])
                ps = psm.next()
                P.op("pe", lambda e, ps=ps, g_=g_: e.matmul(ps[:, 0:8], C.tri[:, :], g_[:, 8:16], start=True, stop=True), reads=[C.tri, g_], writes=[ps])
                P.op("dve", lambda e, ps=ps: e.tensor_copy(bcum[:], ps[:, 0:8]), reads=[ps], writes=[bcum])
                P.op("dve", lambda e, g_=g_: e.tensor_tensor(u[:], g_[:, 0:8], bcum[:], ALU.subtract), reads=[g_, bcum], writes=[u])
                P.op("dve", lambda e: e.tensor_copy(upad[:, 0:8], u[:]), reads=[u], writes=[upad])
                P.op("dve", lambda e: e.tensor_copy(bpad[:, 0:8], bcum[:]), reads=[bcum], writes=[bpad])
                ps = psm.next()
                P.op("pe", lambda e, ps=ps: e.transpose(ps[:, 0:128], upad[:, :], C.ident[:, :]), reads=[upad, C.ident], writes=[ps])
                P.op("dve", lambda e, ps=ps: e.tensor_copy(uT[0:8, :], ps[0:8, 0:128]), reads=[ps], writes=[uT])
                ps = psm.next()
                P.op("pe", lambda e, ps=ps: e.transpose(ps[:, 0:128], bpad[:, :], C.ident[:, :]), reads=[bpad, C.ident], writes=[ps])
                P.op("dve", lambda e, ps=ps: e.tensor_copy(bT[0:8, :], ps[0:8, 0:128]), reads=[ps], writes=[bT])
                P.op("dve", lambda e: e.tensor_tensor_scan(MT[0:8, :], ones8[0:8, :], uT[0:8, :], m_col[0:8, 0:1], ALU.mult, ALU.max),
                     reads=[ones8, uT, m_col], writes=[MT])
                ps = psm.next()
                P.op("pe", lambda e, ps=ps: e.transpose(ps[:, 0:128], MT[:, :], C.ident[:, :]), reads=[MT, C.ident], writes=[ps])
                P.op("dve", lambda e, ps=ps: e.tensor_copy(M_sb[:], ps[:, 0:8]), reads=[ps], writes=[M_sb])
                P.op("act", lambda e: e.mul(negM[:], M_sb[:], -1.0), reads=[M_sb], writes=[negM])
                ps = psm.next()
                P.op("pe", lambda e, ps=ps: e.matmul(ps[:, 0:8], C.sellast[:, :], M_sb[:, :], start=True, stop=True), reads=[C.sellast, M_sb], writes=[ps])
                P.op("dve", lambda e, ps=ps: e.tensor_copy(Mlast[:], ps[:, 0:8]), reads=[ps], writes=[Mlast])
                ps = psm.next()
                P.op("pe", lambda e, ps=ps: e.matmul(ps[:, 0:8], C.sellast[:, :], bcum[:, :], start=True, stop=True), reads=[C.sellast, bcum], writes=[ps])
                P.op("dve", lambda e, ps=ps: e.tensor_copy(btot[:], ps[:, 0:8]), reads=[ps], writes=[btot])
                P.op("dve", lambda e: e.tensor_tensor(ginter[:], m_b[:], M_sb[:], ALU.subtract), reads=[m_b, M_sb], writes=[ginter])
                P.op("act", lambda e: e.activation(ginter[:], ginter[:], AF.Exp), reads=[ginter], writes=[ginter])
                P.op("dve", lambda e: e.tensor_tensor(wend[:], u[:], Mlast[:], ALU.subtract), reads=[u, Mlast], writes=[wend])
                P.op("act", lambda e: e.activation(wend[:], wend[:], AF.Exp), reads=[wend], writes=[wend])
                P.op("dve", lambda e: e.tensor_tensor(gold[:], m_b[:], Mlast[:], ALU.subtract), reads=[m_b, Mlast], writes=[gold])
                P.op("act", lambda e: e.activation(gold[:], gold[:], AF.Exp), reads=[gold], writes=[gold])
                P.op("dve", lambda e: e.tensor_tensor(emr[:], bcum[:], M_sb[:], ALU.add), reads=[bcum, M_sb], writes=[emr])
                P.op("act", lambda e: e.activation(emr[:], emr[:], AF.Exp, scale=-1.0), reads=[emr], writes=[emr])
                P.op("dve", lambda e: e.tensor_tensor(D2m[:], bview(C.ident[:, :].unsqueeze(1), [128, 8, 128]),
                                                      bview(negM[:, :].unsqueeze(2), [128, 8, 128]), ALU.mult),
                     reads=[C.ident, negM], writes=[D2m])
                for half in range(2):
                    pe_ = psE[half]
                    P.op("pe", lambda e, pe_=pe_, half=half: e.matmul(pe_[:, 0:512], C.ones_f[:, :],
                                                                        D2m[:, half * 4:half * 4 + 4, :].rearrange("p h l -> p (h l)"),
                                                                        start=True, stop=False), reads=[C.ones_f, D2m], writes=[pe_])
                    P.op("pe", lambda e, pe_=pe_: e.matmul(pe_[:, 0:512], C.ident[:, :], C.negtri4[:, :], start=False, stop=True),
                         reads=[C.ident, C.negtri4], writes=[pe_])
                for h in range(8):
                    pe_ = psE[h // 4]
                    P.op("pe", lambda e, kb=kb, qb=qb, h=h: e.matmul(ps_s[:, 0:128], kb[:, h, :], qb[:, h, :], start=True, stop=True),
                         reads=[kb, qb], writes=[ps_s])
                    d_ = dec.next()
                    P.op("act", lambda e, d_=d_, pe_=pe_, h=h: e.activation(d_[:, :], pe_[:, (h % 4) * 128:(h % 4 + 1) * 128], AF.Exp, bias=u[:, h:h + 1]),
                         reads=[pe_, u], writes=[d_])
                    w_ = WT.next()
                    P.op("dve", lambda e, d_=d_, w_=w_: e.tensor_tensor(w_[:, :], d_[:, :], ps_s[:, 0:128], ALU.mult), reads=[d_, ps_s], writes=[w_])
                    P.op("pe", lambda e, w_=w_, vab=vab, h=h: e.matmul(ps_num[:, 0:257], w_[:, :], vab[:, h, :], start=True, stop=True),
                         reads=[w_, vab], writes=[ps_num])
                    P.op("pe", lambda e, qb=qb, h=h: e.matmul(ps_int[:, 0:257], qb[:, h, :], Cs_bf[:, h, :], start=True, stop=True),
                         reads=[qb, Cs_bf], writes=[ps_int])
                    ns_ = numsb.next()
                    P.op("act", lambda e, ns_=ns_: e.copy(ns_[:, :], ps_num[:, 0:257]), reads=[ps_num], writes=[ns_])
                    cb_ = comb.next()
                    P.op("dve", lambda e, cb_=cb_, ns_=ns_, h=h: e.scalar_tensor_tensor(out=cb_[:, :], in0=ps_int[:, 0:257], scalar=ginter[:, h:h + 1],
                                                                                      in1=ns_[:, :], op0=ALU.mult, op1=ALU.add),
                         reads=[ps_int, ginter, ns_], writes=[cb_])
                    r_ = rdn.next()
                    P.op("act", lambda e, r_=r_, cb_=cb_: e.activation(r_[:, :], cb_[:, 256:257], AF.Abs), reads=[cb_], writes=[r_])
                    P.op("dve", lambda e, r_=r_, h=h: e.tensor_tensor(r_[:, :], r_[:, :], emr[:, h:h + 1], ALU.max), reads=[r_, emr], writes=[r_])
                    P.op("dve", lambda e, r_=r_: e.reciprocal(r_[:, :], r_[:, :]), reads=[r_], writes=[r_])
                    P.op("dve", lambda e, r_=r_, cb_=cb_, h=h: e.tensor_scalar(h_c[:, h * 256:(h + 1) * 256], cb_[:, 0:256], r_[:, 0:1], None, ALU.mult),
                         reads=[cb_, r_], writes=[h_c])
                    v_ = vw.next()
                    P.op("pool", lambda e, v_=v_, vab=vab, h=h: e.tensor_scalar(v_[:, :], vab[:, h, :], wend[:, h:h + 1], None, ALU.mult),
                         reads=[vab, wend], writes=[v_])
                    P.op("pe", lambda e, ktb=ktb, v_=v_, h=h: e.matmul(ps_c[:, 0:257], ktb[:, h * 128:(h + 1) * 128], v_[:, :], start=True, stop=True),
                         reads=[ktb, v_], writes=[ps_c])
                    P.op("dve", lambda e, h=h: e.scalar_tensor_tensor(out=Cs[:, h, :], in0=Cs[:, h, :], scalar=gold[:, h:h + 1], in1=ps_c[:, 0:257],
                                                                       op0=ALU.mult, op1=ALU.add), reads=[Cs, gold, ps_c], writes=[Cs])
                    P.op("act", lambda e, h=h: e.copy(Cs_bf[:, h, :], Cs[:, h, :]), reads=[Cs], writes=[Cs_bf])
                P.op("dve", lambda e: e.tensor_tensor(m_b[:], btot[:], Mlast[:], ALU.add), reads=[btot, Mlast], writes=[m_b])
                P.op("dve", lambda e: e.tensor_tensor(m_col[0:8, :], bT[0:8, 127:128], MT[0:8, 127:128], ALU.add), reads=[bT, MT], writes=[m_col])
                for h in range(8):
                    P.op("act", lambda e, h=h: e.activation(junk[:, :], h_c[:, h * 256:(h + 1) * 256], AF.Square, accum_out=ss[:, h:h + 1]),
                         reads=[h_c], writes=[junk, ss])
                P.op("dve", lambda e: e.tensor_scalar(rstd[:], ss[:], 1.0 / 256, 1e-5, ALU.mult, ALU.add), reads=[ss], writes=[rstd])
                P.op("act", lambda e: e.activation(rstd[:], rstd[:], AF.Sqrt), reads=[rstd], writes=[rstd])
                P.op("dve", lambda e: e.reciprocal(rstd[:], rstd[:]), reads=[rstd], writes=[rstd])
                for h in range(8):
                    P.op("dve", lambda e, h=h: e.tensor_scalar(h_c[:, h * 256:(h + 1) * 256], h_c[:, h * 256:(h + 1) * 256], rstd[:, h:h + 1], None, ALU.mult),
                         reads=[h_c, rstd], writes=[h_c])
                P.op("pool", lambda e: e.tensor_tensor(h_c[:], h_c[:], g_b[:], ALU.mult), reads=[h_c, g_b], writes=[h_c])
                P.op("act", lambda e, ob=ob: e.activation(tmpb[:], ob[:], AF.Sigmoid), reads=[ob], writes=[tmpb])
                P.op("dve", lambda e: e.tensor_tensor(h_c[:], h_c[:], tmpb[:], ALU.mult), reads=[h_c, tmpb], writes=[h_c])
                ht = hTs.next()
                for c4 in range(4):
                    ps = psm.next()
                    for j in range(4):
                        c = c4 * 4 + j
                        P.op("pe", lambda e, ps=ps, c=c, j=j: e.transpose(ps[:, j * 128:(j + 1) * 128], h_c[:, c * 128:(c + 1) * 128], C.ident[:, :]),
                             reads=[h_c, C.ident], writes=[ps])
                    P.op("act", lambda e, ps=ps, c4=c4, ht=ht: e.copy(ht[:, c4 * 4:(c4 + 1) * 4, :], ps[:].rearrange("p (j m) -> p j m", j=4)),
                         reads=[ps], writes=[ht])
                P.store(ht, S.mixT[0:2048, a0:a0 + L].rearrange("(c p) t -> p c t", p=128), ht[:, :, 0:L])
            for c in range(16):
                ps = psm.next()
                P.op("pe", lambda e, ps=ps, c=c: e.transpose(ps[:, 0:128], Cs[:, c // 2, (c % 2) * 128:(c % 2 + 1) * 128], C.ident[:, :]),
                     reads=[Cs, C.ident], writes=[ps])
                P.op("dve", lambda e, ps=ps, c=c: e.tensor_copy(cio[:, c, :], ps[:, 0:128]), reads=[ps], writes=[cio])
            P.store(cio, O.mlstm_c[sidx].rearrange("(c p) d -> p c d", p=128), cio[:])
            P.op("dve", lambda e: e.tensor_copy(npad[:, 0:8], Cs[:, :, 256]), reads=[Cs], writes=[npad])
            ps = psm.next()
            P.op("pe", lambda e, ps=ps: e.transpose(ps[:, 0:128], npad[:, :], C.ident[:, :]), reads=[npad, C.ident], writes=[ps])
            P.op("dve", lambda e, ps=ps: e.tensor_copy(bpad[0:8, :], ps[0:8, 0:128]), reads=[ps], writes=[bpad])
            P.store(bpad, O.mlstm_n[sidx], bpad[0:8, :])
            P.op("dve", lambda e: e.memset(bpad[:], 0.0), writes=[bpad])
            P.op("dve", lambda e: e.memset(npad[:], 0.0), writes=[npad])
            P.store(m_col, O.mlstm_m[sidx:sidx + 1, :].rearrange("o h -> h o"), m_col[0:8, :])


_PROG_CACHE = {}


def kernel(**inputs):
    NCORES = 8
    SEQ = inputs["x_prompt"].shape[1]
    PAST = inputs["cache_fox_k"].shape[2]
    NS = inputs["x_sample"].shape[0] // NCORES
    key = (SEQ, PAST, NS)
    if key not in _PROG_CACHE:
        _PROG_CACHE[key] = build_program(SEQ=SEQ, PAST=PAST, NS=NS)
    nc = _PROG_CACHE[key]
    in_maps = [make_in_map(inputs, c, NS=NS) for c in range(NCORES)]
    res = run_bass_kernel_spmd(nc, in_maps, core_ids=list(range(NCORES)))
    R = res.results
    NP = N_META + SEQ
    B = NCORES
    DB = NCORES * NS
    f32 = np.float32
    y_p = np.empty((B, SEQ, D), f32); y_s = np.empty((DB, DEC, D), f32)
    kp = np.empty((1, B, NP, H_A, DH_A), f32); vp = np.empty((1, B, NP, H_A, DH_A), f32); fp = np.empty((1, B, NP, H_A), f32)
    cvp = np.empty((1, B, 3, CONV_DIM), f32); hp = np.empty((1, B, H_B, P_B, N_B), f32)
    cp = np.empty((1, B, H_C, DV_C, DK_C), f32); np_ = np.empty((1, B, H_C, DK_C), f32); mp = np.empty((1, B, H_C), f32)
    ks = np.empty((1, DB, DEC, H_A, DH_A), f32); vs = np.empty((1, DB, DEC, H_A, DH_A), f32); fs = np.empty((1, DB, DEC, H_A), f32)
    cvs = np.empty((1, DB, 3, CONV_DIM), f32); hs = np.empty((1, DB, H_B, P_B, N_B), f32)
    cs = np.empty((1, DB, H_C, DV_C, DK_C), f32); ns_ = np.empty((1, DB, H_C, DK_C), f32); ms = np.empty((1, DB, H_C), f32)
    for c in range(NCORES):
        r = R[c]
        y = np.asarray(r["o_y"]); y_p[c] = y[N_META:NP]; y_s[c * NS:(c + 1) * NS] = y[NP:].reshape(NS, DEC, D)
        k = np.asarray(r["o_fox_k"]); kp[0, c] = k[:NP].reshape(NP, H_A, DH_A); ks[0, c * NS:(c + 1) * NS] = k[NP:].reshape(NS, DEC, H_A, DH_A)
        v = np.asarray(r["o_fox_v"]); vp[0, c] = v[:NP].reshape(NP, H_A, DH_A); vs[0, c * NS:(c + 1) * NS] = v[NP:].reshape(NS, DEC, H_A, DH_A)
        lf = np.asarray(r["o_fox_logf"]); fp[0, c] = lf[:NP]; fs[0, c * NS:(c + 1) * NS] = lf[NP:].reshape(NS, DEC, H_A)
        cv = np.asarray(r["o_ssd_conv"]); cvp[0, c] = cv[0]; cvs[0, c * NS:(c + 1) * NS] = cv[1:]
        hh = np.asarray(r["o_ssd_state"]).reshape(1 + NS, H_B, P_B, N_B); hp[0, c] = hh[0]; hs[0, c * NS:(c + 1) * NS] = hh[1:]
        cc = np.asarray(r["o_mlstm_c"]).reshape(1 + NS, H_C, DV_C, DK_C); cp[0, c] = cc[0]; cs[0, c * NS:(c + 1) * NS] = cc[1:]
        nn = np.asarray(r["o_mlstm_n"]); np_[0, c] = nn[0]; ns_[0, c * NS:(c + 1) * NS] = nn[1:]
        mm = np.asarray(r["o_mlstm_m"]); mp[0, c] = mm[0]; ms[0, c * NS:(c + 1) * NS] = mm[1:]
    return (y_p, y_s, kp, vp, fp, cvp, hp, cp, np_, mp, ks, vs, fs, cvs, hs, cs, ns_, ms)
```

```python
import math
import numpy as np
from contextlib import ExitStack
import concourse.bass as bass
import concourse.mybir as mybir
from concourse.bass_utils import run_bass_kernel_spmd

F32 = mybir.dt.float32
BF16 = mybir.dt.bfloat16
AF = mybir.ActivationFunctionType
ALU = mybir.AluOpType
AX = mybir.AxisListType

D = 2048
N_META = 16
DEC = 16
H_A, DH_A = 16, 128
H_B, P_B, G_B, N_B = 32, 64, 4, 128
D_SSM = 2048
CONV_DIM = 3072
H_C, DK_C, DV_C = 8, 128, 256
D_FF = 5632
E_A = 11312
E_C = 6160
DEPTH = 2
ALPHA = (2 * DEPTH) ** 0.25
NEG = -1e30

ENGS = ("pe", "act", "dve", "pool", "sp")


class Buf:
    __slots__ = ("name", "t", "last_w", "readers", "is_psum")

    def __init__(self, name, t, is_psum=False):
        self.name = name
        self.t = t
        self.last_w = None
        self.readers = []
        self.is_psum = is_psum

    def __getitem__(self, idx):
        return self.t[idx]


class Op:
    __slots__ = ("eng", "fn", "deps", "is_dma", "key", "sig", "tick", "dcount")

    def __init__(self, eng, fn, is_dma=False, key=None):
        self.eng = eng
        self.fn = fn
        self.deps = []
        self.is_dma = is_dma
        self.key = key
        self.sig = False
        self.tick = 0
        self.dcount = 0


class Prog:
    def __init__(self, nc, stack):
        self.nc = nc
        self.stack = stack
        self.esem = {e: stack.enter_context(nc.semaphore("sem_" + e)) for e in ENGS}
        self.ecnt = {e: 0 for e in ENGS}
        self.ksem = {}
        self.kcnt = {}
        self.ops = {e: [] for e in ENGS}
        self.known = {e: {} for e in ENGS}
        self.bufs = []
        self.nops = 0
        self.uid = 0
        self.dmaq = 0
        self.free_dsems = {"sw": [], "hw": []}
        self.kkind = {}
        self.all_dsems = []
        self.keep_names = {"ident", "tri", "negtri", "negtri4", "tri", "ones", "sel0", "sellast"}

    def sb(self, stack, name, shape, dt):
        self.uid += 1
        nm = f"{name}_{self.uid}"
        t = stack.enter_context(self.nc.sbuf_tensor(nm, list(shape), dt))
        b = Buf(nm, t)
        self.bufs.append(b)
        return b

    def ps(self, stack, name, shape, dt=F32):
        self.uid += 1
        nm = f"{name}_{self.uid}"
        t = stack.enter_context(self.nc.psum_tensor(nm, list(shape), dt))
        b = Buf(nm, t, True)
        self.bufs.append(b)
        return b

    def _deps(self, op, reads, writes):
        for b in reads:
            if b.last_w is not None:
                op.deps.append(b.last_w)
            if b.is_psum:
                for r in b.readers:
                    if r.eng != op.eng:
                        op.deps.append(r)
        for b in writes:
            lw = b.last_w
            if lw is not None:
                if op.is_dma and lw.is_dma and lw.key is op.key:
                    op.deps.extend(lw.deps)
                else:
                    op.deps.append(lw)
            for r in b.readers:
                op.deps.append(r)
        for b in reads:
            b.readers.append(op)
        for b in writes:
            b.last_w = op
            b.readers = []

    def op(self, eng, fn, reads=(), writes=()):
        o = Op(eng, fn)
        self._deps(o, reads, writes)
        self.ops[eng].append(o)
        self.nops += 1
        return o

    def dma(self, out, in_, key, reads=(), writes=(), q=None):
        if q is None:
            q = "sp"
        kind = "sw" if q == "pool" else "hw"
        if key in self.kkind:
            assert self.kkind[key] == kind, key.name
        if key not in self.ksem:
            self.kkind[key] = kind
            if self.free_dsems[kind]:
                self.ksem[key], self.kcnt[key] = self.free_dsems[kind].pop()
            else:
                self.ksem[key] = self.stack.enter_context(self.nc.semaphore(f"dk{len(self.all_dsems)}"))
                self.kcnt[key] = 0
                self.all_dsems.append(self.ksem[key])
        o = Op(q, lambda e, out=out, in_=in_: e.dma_start(out=out, in_=in_), True, key)
        self.kcnt[key] += 16
        o.dcount = self.kcnt[key]
        self._deps(o, reads, writes)
        self.ops[q].append(o)
        self.nops += 1
        return o

    def load(self, buf, dst, src, q=None):
        return self.dma(dst, src, buf, writes=[buf], q=q)

    def store(self, buf, dst, src, q=None):
        return self.dma(dst, src, buf, reads=[buf], q=q)

    def end_phase(self):
        nc = self.nc
        for e in ENGS:
            for o in self.ops[e]:
                for d in o.deps:
                    if not d.is_dma and (d.eng != o.eng or o.eng != "pe"):
                        d.sig = True
        for e in ENGS:
            c = self.ecnt[e]
            for o in self.ops[e]:
                if o.sig:
                    c += 1
                    o.tick = c
            self.ecnt[e] = c
        ops, esem, ksem, known, kcnt = self.ops, self.esem, self.ksem, self.known, self.kcnt

        def replay(eng_obj, name):
            kn = known[name]
            for o in ops[name]:
                need = {}
                for d in o.deps:
                    if d.is_dma:
                        s, v = ksem[d.key], d.dcount
                    else:
                        if d.eng == name and name == "pe":
                            continue
                        s, v = esem[d.eng], d.tick
                    if kn.get(id(s), 0) >= v:
                        continue
                    if need.get(id(s), (None, 0))[1] < v:
                        need[id(s)] = (s, v)
                for sid, (s, v) in need.items():
                    eng_obj.wait_ge(s, v)
                    kn[sid] = v
                ins = o.fn(eng_obj)
                if o.is_dma:
                    ins.then_inc(ksem[o.key], 16)
                elif o.sig:
                    ins.then_inc(esem[name], 1)
            if name == "sp":
                for k, c in kcnt.items():
                    s = ksem[k]
                    if kn.get(id(s), 0) < c:
                        eng_obj.wait_ge(s, c)
                        kn[id(s)] = c

        with nc.Block() as block:
            @block.tensor
            def _(e):
                replay(e, "pe")

            @block.scalar
            def _(e):
                replay(e, "act")

            @block.vector
            def _(e):
                replay(e, "dve")

            @block.gpsimd
            def _(e):
                replay(e, "pool")

            @block.sync
            def _(e):
                replay(e, "sp")

        for e in ENGS:
            self.ops[e] = []
            for e2 in ENGS:
                self.known[e][id(self.esem[e2])] = self.ecnt[e2]
            for k, c in self.kcnt.items():
                self.known[e][id(self.ksem[k])] = c
        for k, c in self.kcnt.items():
            self.free_dsems[self.kkind[k]].append((self.ksem[k], c))
        self.ksem = {}
        self.kcnt = {}
        self.kkind = {}
        for b in self.bufs:
            b.last_w = None
            b.readers = []
        self.bufs = [b for b in self.bufs if b.is_psum or b.name.split("_")[0] in self.keep_names]


class Ring:
    def __init__(self, items):
        self.items = items
        self.i = 0

    def next(self):
        b = self.items[self.i % len(self.items)]
        self.i += 1
        return b


def _tiles(n, step):
    return [(i, min(step, n - i)) for i in range(0, n, step)]


class Ctx:
    pass


def build_xT(P, C, pst, xT, tok0, n, off, loader, xin_ring, psT_ring):
    xb = xin_ring.next()
    loader(xb, tok0, n)
    for c4 in range(4):
        ps = psT_ring.next()
        for j in range(4):
            c = c4 * 4 + j
            P.op("pe", lambda e, ps=ps, xb=xb, c=c, j=j, n=n: e.transpose(
                ps[0:128, j * 128:j * 128 + 128], xb[0:128, c * 128:(c + 1) * 128], C.ident[:, :]),
                reads=[xb, C.ident], writes=[ps])
        P.op("dve", lambda e, ps=ps, c4=c4, n=n, off=off: e.tensor_copy(
            xT[:, c4 * 4:(c4 + 1) * 4, off:off + n],
            ps[:].rearrange("p (j m) -> p j m", j=4)[:, :, 0:n]), reads=[ps], writes=[xT])


def gemm_segments(P, C, xT, Kc, ntok, W, segs, wring, ps_ring, wcols):
    Wv = W.rearrange("(c p) n -> p c n", p=128)
    for (c0, c1, orient, evac) in segs:
        for (cc, ncw) in _tiles(c1 - c0, wcols):
            wb = wring.next()
            P.load(wb, wb[:, 0:Kc, 0:ncw], Wv[:, :, c0 + cc:c0 + cc + ncw], q="pool")
            if orient == "tok":
                for (t0, n) in _tiles(ntok, 128):
                    ps = ps_ring.next()
                    for c in range(Kc):
                        P.op("pe", lambda e, ps=ps, c=c, t0=t0, n=n, wb=wb, ncw=ncw: e.matmul(
                            ps[0:128, 0:ncw], xT[:, c, t0:t0 + 128], wb[:, c, 0:ncw],
                            start=(c == 0), stop=(c == Kc - 1)), reads=[xT, wb], writes=[ps])
                    evac(ps, t0, n, c0 + cc, ncw)
            else:
                for (s0, ns) in _tiles(ncw, 128):
                    for (t0, n) in _tiles(ntok, 512):
                        ps = ps_ring.next()
                        for c in range(Kc):
                            P.op("pe", lambda e, ps=ps, c=c, t0=t0, n=n, wb=wb, s0=s0, ns=ns: e.matmul(
                                ps[0:ns, 0:n], wb[:, c, s0:s0 + ns], xT[:, c, t0:t0 + n],
                                start=(c == 0), stop=(c == Kc - 1)), reads=[xT, wb], writes=[ps])
                        evac(ps, t0, n, c0 + cc + s0, ns)


def supertiles(ntok, tt):
    n128 = (ntok + 127) // 128
    per = tt // 128
    nsup = (n128 + per - 1) // per
    base, extra = divmod(n128, nsup)
    out = []
    a = 0
    for i in range(nsup):
        k = base + (1 if i < extra else 0)
        b = min(ntok, a + k * 128)
        out.append((a, b - a))
        a = b
    return out


def layer_norm_tile(P, C, st_bufs, xa, xb_, n, g_b, b_b, out):
    stats, mv, rstd = st_bufs
    P.op("dve", lambda e: e.scalar_tensor_tensor(out=xa[0:n, :], in0=xa[0:n, :], scalar=float(ALPHA), in1=xb_[0:n, :],
                                                 op0=ALU.mult, op1=ALU.add), reads=[xa, xb_], writes=[xa])
    for k in range(4):
        P.op("dve", lambda e, k=k: e.bn_stats(stats[0:n, k * 6:(k + 1) * 6], xa[0:n, k * 512:(k + 1) * 512]),
             reads=[xa], writes=[stats])
    P.op("dve", lambda e: e.bn_aggr(mv[0:n, :], stats[0:n, :]), reads=[stats], writes=[mv])
    P.op("dve", lambda e: e.tensor_scalar(rstd[0:n, :], mv[0:n, 1:2], 1e-5, None, ALU.add), reads=[mv], writes=[rstd])
    P.op("act", lambda e: e.activation(rstd[0:n, :], rstd[0:n, :], AF.Sqrt), reads=[rstd], writes=[rstd])
    P.op("dve", lambda e: e.reciprocal(rstd[0:n, :], rstd[0:n, :]), reads=[rstd], writes=[rstd])
    P.op("dve", lambda e: e.tensor_scalar(xa[0:n, :], xa[0:n, :], mv[0:n, 0:1], rstd[0:n, 0:1], ALU.subtract, ALU.mult),
         reads=[xa, mv, rstd], writes=[xa])
    P.op("pool", lambda e: e.tensor_tensor(xa[0:n, :], xa[0:n, :], g_b[0:n, :], ALU.mult), reads=[xa, g_b], writes=[xa])
    P.op("pool", lambda e: e.tensor_tensor(out[0:n, :], xa[0:n, :], b_b[0:n, :], ALU.add), reads=[xa, b_b], writes=[out])


def build_program(SEQ=4096, PAST=4096, NS=2, stop_after=None, debug=False):
    nc = bass.Bass("TRN2", target_bir_lowering=False)
    C = Ctx()
    NP = N_META + SEQ
    NTOK = NP + NS * DEC
    C.NP, C.NTOK, C.SEQ, C.PAST, C.NS = NP, NTOK, SEQ, PAST, NS

    early = {"x_prompt", "x_sample", "meta_tokens", "cache_fox_k", "cache_fox_v", "cache_fox_logf", "w_in_a",
             "b_fgate_a", "c_ident", "c_tri", "c_negtri"}
    if stop_after == "l0":
        early |= {"w_out_a", "ln_mix_g", "ln_mix_b", "ln_ffn_g", "ln_ffn_b", "w_ffn_gate", "w_ffn_up", "w_ffn_down"}
    if stop_after in ("ssd", "l0"):
        early |= {"state_ssd_conv", "state_ssd", "conv_w", "conv_b", "dt_bias", "a_log", "d_skip", "ssd_norm_g"}
    C.early = early

    def din(name, shape, dt=F32):
        if stop_after in ("inproj_a", "attn", "cache", "ssd", "l0") and name not in early:
            return None
        return nc.dram_tensor(name, list(shape), dt, kind="ExternalInput").ap()

    def dout(name, shape, dt=F32):
        return nc.dram_tensor(name, list(shape), dt, kind="ExternalOutput").ap()

    def dscr(name, shape, dt=F32):
        if debug:
            return nc.dram_tensor(name, list(shape), dt, kind="ExternalOutput").ap()
        return nc.dram_tensor(name, list(shape), dt, kind="Internal").ap()

    I = Ctx()
    I.x_prompt = din("x_prompt", [SEQ, D])
    I.x_sample = din("x_sample", [NS * DEC, D])
    I.meta = din("meta_tokens", [N_META, D])
    I.cache_k = din("cache_fox_k", [NS, PAST, 2048])
    I.cache_v = din("cache_fox_v", [NS, PAST, 2048])
    I.cache_logf = din("cache_fox_logf", [NS, PAST, 16])
    I.st_conv = din("state_ssd_conv", [NS, 3, CONV_DIM])
    I.st_ssd = din("state_ssd", [NS, 2048, 128])
    I.st_c = din("state_mlstm_c", [NS, 2048, 128])
    I.st_n = din("state_mlstm_n", [NS, 8, 128])
    I.st_m = din("state_mlstm_m", [NS, 8])
    I.w_in_a = din("w_in_a", [D, E_A])
    I.b_fgate_a = din("b_fgate_a", [1, 16])
    I.conv_w = din("conv_w", [4, CONV_DIM])
    I.conv_b = din("conv_b", [1, CONV_DIM])
    I.dt_bias = din("dt_bias", [1, 32])
    I.a_log = din("a_log", [1, 32])
    I.d_skip = din("d_skip", [1, 32])
    I.ssd_norm_g = din("ssd_norm_g", [1, 2048])
    I.w_out_a = din("w_out_a", [4096, D])
    I.w_in_c = din("w_in_c", [D, E_C])
    I.b_igate_c = din("b_igate_c", [1, 8])
    I.b_fgate_c = din("b_fgate_c", [1, 8])
    I.mlstm_norm_g = din("mlstm_norm_g", [1, 2048])
    I.w_out_c = din("w_out_c", [2048, D])
    I.ln_mix_g = din("ln_mix_g", [2, D])
    I.ln_mix_b = din("ln_mix_b", [2, D])
    I.ln_ffn_g = din("ln_ffn_g", [2, D])
    I.ln_ffn_b = din("ln_ffn_b", [2, D])
    I.w_ffn_gate = din("w_ffn_gate", [2, D, D_FF])
    I.w_ffn_up = din("w_ffn_up", [2, D, D_FF])
    I.w_ffn_down = din("w_ffn_down", [2, D_FF, D])
    I.c_ident = din("c_ident", [128, 128])
    I.c_tri = din("c_tri", [128, 128])
    I.c_negtri = din("c_negtri", [128, 128])

    O = Ctx()
    O.y = dout("o_y", [NTOK, D])
    O.fox_k = dout("o_fox_k", [NTOK, 2048])
    O.fox_v = dout("o_fox_v", [NTOK, 2048])
    O.fox_logf = dout("o_fox_logf", [NTOK, 16])
    O.ssd_conv = dout("o_ssd_conv", [1 + NS, 3, CONV_DIM])
    O.ssd_state = dout("o_ssd_state", [1 + NS, 2048, 128])
    O.mlstm_c = dout("o_mlstm_c", [1 + NS, 2048, 128])
    O.mlstm_n = dout("o_mlstm_n", [1 + NS, 8, 128])
    O.mlstm_m = dout("o_mlstm_m", [1 + NS, 8])

    S = Ctx()
    S.qT = dscr("s_qT", [2048, NTOK], BF16)
    S.kT = dscr("s_kT", [2048, NTOK], BF16)
    S.vbf = dscr("s_vbf", [NTOK, 2048], BF16)
    S.kTp = dscr("s_kTp", [NS, 2048, PAST], BF16)
    S.vp = dscr("s_vp", [NS, PAST, 2048], BF16)
    S.logf = dscr("s_logf", [NTOK, 16])
    S.z = dscr("s_z", [NTOK, 2048])
    S.xbcT = dscr("s_xbcT", [CONV_DIM, NTOK])
    S.dtraw = dscr("s_dtraw", [NTOK, 32])
    S.xtok = dscr("s_xtok", [NTOK, 2048])
    S.bmtok = dscr("s_bmtok", [NTOK, 512], BF16)
    S.bcT = dscr("s_bcT", [1024, NTOK], BF16)
    S.mixT = dscr("s_mixT", [4096, NTOK], BF16)
    S.mix = dscr("s_mix", [NTOK, D])
    S.mix2 = dscr("s_mix2", [NTOK, D])
    S.x0 = dscr("s_x0", [NTOK, D])
    S.x1T = dscr("s_x1T", [D, NTOK], BF16)
    S.ktok = dscr("s_ktok", [NTOK, 1024], BF16)
    S.gates = dscr("s_gates", [NTOK, 16])
    S.x1 = dscr("s_x1", [NTOK, D])
    S.x2 = dscr("s_x2", [NTOK, D])

    NKT_ = (max(NP, PAST + DEC) + 127) // 128
    C.dbgF = dscr("s_dbgF", [128, NKT_ * 16]) if debug else None
    C.dbgL = dscr("s_dbgL", [128, NKT_ * 16]) if debug else None
    seqs = [(0, NP, 0, 0)] + [(NP + s * DEC, DEC, PAST, 1 + s) for s in range(NS)]

    with ExitStack() as gst:
        P = Prog(nc, gst)
        C.ident = P.sb(gst, "ident", [128, 128], F32)
        C.tri = P.sb(gst, "tri", [128, 128], F32)
        C.negtri = P.sb(gst, "negtri", [128, 128], F32)
        C.tri_bf = P.sb(gst, "tri_bf", [128, 128], BF16)
        C.ones_bf = P.sb(gst, "ones_bf", [128, 128], BF16)
        C.ones_f = P.sb(gst, "ones_f", [128, 128], F32)
        C.sel0 = P.sb(gst, "sel0", [128, 128], F32)
        P.load(C.ident, C.ident[:], I.c_ident)
        P.load(C.tri, C.tri[:], I.c_tri)
        P.load(C.negtri, C.negtri[:], I.c_negtri)
        P.op("dve", lambda e: e.tensor_copy(C.tri_bf[:], C.tri[:]), reads=[C.tri], writes=[C.tri_bf])
        P.op("dve", lambda e: e.memset(C.ones_bf[:], 1.0), writes=[C.ones_bf])
        P.op("dve", lambda e: e.memset(C.ones_f[:], 1.0), writes=[C.ones_f])
        P.op("dve", lambda e: e.tensor_copy(C.sel0[:], C.ident[:, 0:1].to_broadcast([128, 128])), reads=[C.ident],
             writes=[C.sel0])
        C.negtri4 = P.sb(gst, "negtri4", [128, 512], F32)
        P.op("dve", lambda e: e.tensor_copy(C.negtri4[:].rearrange("p (h l) -> p h l", h=4),
                                            C.negtri[:, :].unsqueeze(1).to_broadcast([128, 4, 128])),
             reads=[C.negtri], writes=[C.negtri4])
        C.sellast = P.sb(gst, "sellast", [128, 128], F32)
        P.op("dve", lambda e: e.tensor_copy(C.sellast[:], C.ident[:, 127:128].to_broadcast([128, 128])), reads=[C.ident],
             writes=[C.sellast])
        psum = [P.ps(gst, f"psb{i}", [128, 512], F32) for i in range(8)]
        C.psum = psum

        def load_x0(xb, tok0, n):
            r = tok0
            done = 0
            while done < n:
                rr = r + done
                if rr < N_META:
                    m = min(n - done, N_META - rr)
                    src = I.meta[rr:rr + m, :]
                elif rr < NP:
                    m = min(n - done, NP - rr)
                    src = I.x_prompt[rr - N_META:rr - N_META + m, :]
                else:
                    m = n - done
                    src = I.x_sample[rr - NP:rr - NP + m, :]
                P.load(xb, xb[done:done + m, :], src)
                done += m

        phase_inproj_a(P, C, I, O, S, load_x0)
        P.end_phase()
        if stop_after == "inproj_a":
            return nc
        phase_cache_prep(P, C, I, S)
        P.end_phase()
        if stop_after == "cache":
            return nc
        phase_attention(P, C, I, O, S, seqs)
        P.end_phase()
        if stop_after == "attn":
            return nc
        phase_ssd_conv(P, C, I, O, S, seqs)
        P.end_phase()
        phase_ssd(P, C, I, O, S, seqs)
        P.end_phase()
        if stop_after == "ssd":
            return nc
        phase_outproj(P, C, I.w_out_a, 32, S.mixT, S.mix)
        P.end_phase()
        phase_ln_T(P, C, load_x0, S.mix, I.ln_mix_g[0:1, :], I.ln_mix_b[0:1, :], S.x1, S.x1T)
        P.end_phase()
        phase_ffn(P, C, S.x1T, I.w_ffn_gate[0], I.w_ffn_up[0], I.w_ffn_down[0], S.mix2)
        P.end_phase()
        if stop_after == "l0":
            phase_final_ln(P, C, S.x1, S.mix2, I.ln_ffn_g[0:1, :], I.ln_ffn_b[0:1, :], O.y)
            P.end_phase()
            return nc
        phase_inproj_c(P, C, I, S, lambda st: make_ln_loader(P, C, st, S.x1, S.mix2, I.ln_ffn_g[0:1, :], I.ln_ffn_b[0:1, :], S.x2))
        P.end_phase()
        phase_mlstm(P, C, I, O, S, seqs)
        P.end_phase()
        phase_outproj(P, C, I.w_out_c, 16, S.mixT[0:2048, :], S.mix)
        P.end_phase()
        phase_ln_T(P, C, S.x2, S.mix, I.ln_mix_g[1:2, :], I.ln_mix_b[1:2, :], S.x1, S.x1T)
        P.end_phase()
        phase_ffn(P, C, S.x1T, I.w_ffn_gate[1], I.w_ffn_up[1], I.w_ffn_down[1], S.mix2)
        P.end_phase()
        phase_final_ln(P, C, S.x1, S.mix2, I.ln_ffn_g[1:2, :], I.ln_ffn_b[1:2, :], O.y)
        P.end_phase()
    return nc


def phase_x0_copy(P, C, S, load_x0):
    with ExitStack() as st:
        r = Ring([P.sb(st, f"x0c{i}", [128, D], F32) for i in range(3)])
        for (t0, n) in _tiles(C.NTOK, 128):
            b = r.next()
            load_x0(b, t0, n)
            P.store(b, S.x0[t0:t0 + n, :], b[0:n, :])


def phase_inproj_a(P, C, I, O, S, load_x0):
    NTOK = C.NTOK
    scale = DH_A ** -0.5
    with ExitStack() as st:
        TT = 1152
        xT = P.sb(st, "xT", [128, 16, TT + 128], BF16)
        P.op("pool", lambda e: e.memset(xT[:], 0.0), writes=[xT])
        xin = Ring([P.sb(st, f"xin{i}", [128, D], F32) for i in range(2)])
        for b_ in xin.items:
            P.op("pool", lambda e, b_=b_: e.memset(b_[:], 0.0), writes=[b_])
        wring = Ring([P.sb(st, f"w{i}", [128, 16, 512], BF16) for i in range(2)])
        sf = Ring([P.sb(st, f"sf{i}", [128, 512], F32) for i in range(3)])
        sh = Ring([P.sb(st, f"sh{i}", [128, 512], BF16) for i in range(3)])
        sm = Ring([P.sb(st, f"sm{i}", [128, 16], F32) for i in range(3)])
        bfb = P.sb(st, "bfb", [128, 16], F32)
        P.load(bfb, bfb[:], I.b_fgate_a[0:1, :].to_broadcast([128, 16]))
        psT = Ring(C.psum[0:2])
        psG = Ring(C.psum[2:8])
        for (T0, NT) in supertiles(NTOK, TT):
            for (t0, n) in _tiles(NT, 128):
                build_xT(P, C, st, xT, T0 + t0, n, t0, load_x0, xin, psT)

            def ev_qk(dst, mul):
                def f(ps, t0, n, col0, ncol):
                    b = sh.next()
                    P.op("act", lambda e: e.activation(b[0:ncol, 0:n], ps[0:ncol, 0:n], AF.Copy, scale=float(mul)),
                         reads=[ps], writes=[b])
                    P.store(b, dst[col0:col0 + ncol, T0 + t0:T0 + t0 + n], b[0:ncol, 0:n])
                return f

            def ev_ktok(ps, t0, n, col0, ncol):
                b = sf.next()
                P.op("act", lambda e: e.copy(b[0:n, 0:ncol], ps[0:n, 0:ncol]), reads=[ps], writes=[b])
                P.store(b, O.fox_k[T0 + t0:T0 + t0 + n, col0 - 2048:col0 - 2048 + ncol], b[0:n, 0:ncol])

            def ev_v(ps, t0, n, col0, ncol):
                b = sf.next()
                P.op("act", lambda e: e.copy(b[0:n, 0:ncol], ps[0:n, 0:ncol]), reads=[ps], writes=[b])
                P.store(b, O.fox_v[T0 + t0:T0 + t0 + n, col0 - 4096:col0 - 4096 + ncol], b[0:n, 0:ncol])
                b2 = sh.next()
                P.op("dve", lambda e: e.tensor_copy(b2[0:n, 0:ncol], b[0:n, 0:ncol]), reads=[b], writes=[b2])
                P.store(b2, S.vbf[T0 + t0:T0 + t0 + n, col0 - 4096:col0 - 4096 + ncol], b2[0:n, 0:ncol])

            def ev_fg(ps, t0, n, col0, ncol):
                b = sm.next()
                P.op("dve", lambda e: e.tensor_tensor(b[0:n, :], ps[0:n, 0:16], bfb[0:n, :], ALU.add),
                     reads=[ps, bfb], writes=[b])
                P.op("act", lambda e: e.activation(b[0:n, :], b[0:n, :], AF.Exp, scale=-1.0), reads=[b], writes=[b])
                P.op("act", lambda e: e.activation(b[0:n, :], b[0:n, :], AF.Ln, bias=1.0), reads=[b], writes=[b])
                P.op("act", lambda e: e.mul(b[0:n, :], b[0:n, :], -1.0), reads=[b], writes=[b])
                P.store(b, O.fox_logf[T0 + t0:T0 + t0 + n, :], b[0:n, :])
                P.store(b, S.logf[T0 + t0:T0 + t0 + n, :], b[0:n, :])

            def ev_z(ps, t0, n, col0, ncol):
                b = sf.next()
                P.op("act", lambda e: e.copy(b[0:n, 0:ncol], ps[0:n, 0:ncol]), reads=[ps], writes=[b])
                P.store(b, S.z[T0 + t0:T0 + t0 + n, col0 - 6160:col0 - 6160 + ncol], b[0:n, 0:ncol])

            def ev_xbc(ps, t0, n, col0, ncol):
                b = sf.next()
                P.op("act", lambda e: e.copy(b[0:ncol, 0:n], ps[0:ncol, 0:n]), reads=[ps], writes=[b])
                P.store(b, S.xbcT[col0 - 8208:col0 - 8208 + ncol, T0 + t0:T0 + t0 + n], b[0:ncol, 0:n])

            def ev_dt(ps, t0, n, col0, ncol):
                b = sf.next()
                P.op("act", lambda e: e.copy(b[0:n, 0:32], ps[0:n, 0:32]), reads=[ps], writes=[b])
                P.store(b, S.dtraw[T0 + t0:T0 + t0 + n, :], b[0:n, 0:32])

            segs = [
                (0, 2048, "feat", ev_qk(S.qT, scale)),
                (2048, 4096, "feat", lambda ps, t0, n, col0, ncol: ev_qk(S.kT, 1.0)(ps, t0, n, col0 - 2048, ncol)),
                (2048, 4096, "tok", ev_ktok),
                (4096, 6144, "tok", ev_v),
                (6144, 6160, "tok", ev_fg),
                (6160, 8208, "tok", ev_z),
                (8208, 11280, "feat", ev_xbc),
                (11280, 11312, "tok", ev_dt),
            ]
            import os
            if os.environ.get("K_SEGS"):
                segs = [segs[int(i)] for i in os.environ["K_SEGS"].split(",")]
            gemm_segments(P, C, xT, 16, NT, I.w_in_a, segs, wring, psG, 512)


def phase_cache_prep(P, C, I, S):
    with ExitStack() as st:
        kin = Ring([P.sb(st, f"kin{i}", [128, 2048], F32) for i in range(2)])
        vin = Ring([P.sb(st, f"vin{i}", [128, 2048], F32) for i in range(2)])
        kts = Ring([P.sb(st, f"kts{i}", [128, 16, 128], BF16) for i in range(2)])
        vbs = Ring([P.sb(st, f"vbs{i}", [128, 2048], BF16) for i in range(2)])
        psT = Ring(C.psum[0:4])
        for s in range(C.NS):
            for (k0, n) in _tiles(C.PAST, 128):
                kb = kin.next()
                P.load(kb, kb[0:n, :], I.cache_k[s, k0:k0 + n, :])
                ko = kts.next()
                for c4 in range(4):
                    ps = psT.next()
                    for j in range(4):
                        c = c4 * 4 + j
                        P.op("pe", lambda e, ps=ps, kb=kb, c=c, j=j, n=n: e.transpose(
                            ps[0:128, j * 128:j * 128 + n], kb[0:n, c * 128:(c + 1) * 128], C.ident[0:n, 0:n]),
                            reads=[kb, C.ident], writes=[ps])
                    eng = "dve" if c4 % 2 == 0 else "act"
                    if eng == "dve":
                        P.op("dve", lambda e, ps=ps, c4=c4, ko=ko, n=n: e.tensor_copy(
                            ko[:, c4 * 4:(c4 + 1) * 4, 0:n], ps[:].rearrange("p (j m) -> p j m", j=4)[:, :, 0:n]),
                            reads=[ps], writes=[ko])
                    else:
                        P.op("act", lambda e, ps=ps, c4=c4, ko=ko, n=n: e.copy(
                            ko[:, c4 * 4:(c4 + 1) * 4, 0:n], ps[:].rearrange("p (j m) -> p j m", j=4)[:, :, 0:n]),
                            reads=[ps], writes=[ko])
                P.store(ko, S.kTp[s, :, k0:k0 + n].rearrange("(c p) m -> p c m", p=128), ko[:, :, 0:n])
                vb = vin.next()
                P.load(vb, vb[0:n, :], I.cache_v[s, k0:k0 + n, :], q="act")
                vo = vbs.next()
                P.op("pool", lambda e, vo=vo, vb=vb, n=n: e.tensor_copy(vo[0:n, :], vb[0:n, :]), reads=[vb], writes=[vo])
                P.store(vo, S.vp[s, k0:k0 + n, :], vo[0:n, :], q="act")


def phase_attention(P, C, I, O, S, seqs):
    with ExitStack() as st:
        Smax = max(past + T for (_, T, past, _) in seqs)
        NKT = (Smax + 127) // 128
        Tmax = max(T for (_, T, _, _) in seqs)
        F_all = P.sb(st, "F_all", [128, NKT, 16], F32)
        lfin = P.sb(st, "lfin", [128, NKT, 16], F32)
        kTh = Ring([P.sb(st, f"kTh{i}", [128, NKT * 128], BF16) for i in range(2)])
        vh = Ring([P.sb(st, f"vh{i}", [128, NKT, 128], BF16) for i in range(2)])
        qTh = Ring([P.sb(st, f"qTh{i}", [128, Tmax], BF16) for i in range(2)])
        pT = Ring([P.sb(st, f"pT{i}", [128, 512], BF16) for i in range(5)])
        bcol = Ring([P.sb(st, f"bcol{i}", [128, 1], F32) for i in range(6)])
        c_b = P.sb(st, "c_b", [128, 16], F32)
        rl = P.sb(st, "rl", [128, 512], F32)
        for b_ in kTh.items + vh.items + [F_all, lfin]:
            P.op("pool", lambda e, b_=b_: e.memset(b_[:], 0.0), writes=[b_])
        osb = Ring([P.sb(st, f"osb{i}", [128, 512], BF16) for i in range(2)])
        ps_s = Ring(C.psum[0:3])
        ps_o = Ring(C.psum[3:5])
        ps_l = Ring(C.psum[5:7])
        ps_m = Ring(C.psum[7:8])
        ones_part = P.sb(st, "ones_part", [128, 128], BF16)
        P.op("dve", lambda e: e.memset(ones_part[:], 0.0), writes=[ones_part])
        P.op("dve", lambda e: e.memset(ones_part[0:16, :], 1.0), writes=[ones_part])
        for (tok0, T, past, sidx) in seqs:
            Sk = past + T
            ktiles = _tiles(Sk, 128)
            assert all(nk in (128, 16) for (_, nk) in ktiles)
            npast_t = past // 128
            for j0 in range(0, npast_t, 16):
                j1 = min(npast_t, j0 + 16)
                P.load(lfin, lfin[:, j0:j1, :], I.cache_logf[sidx - 1, j0 * 128:j1 * 128, :].rearrange("(j p) h -> p j h", p=128))
            nfull = T // 128
            for j0 in range(0, nfull, 16):
                j1 = min(nfull, j0 + 16)
                P.load(lfin, lfin[:, npast_t + j0:npast_t + j1, :],
                       S.logf[tok0 + j0 * 128:tok0 + j1 * 128, :].rearrange("(j p) h -> p j h", p=128))
            rem = T - nfull * 128
            if rem:
                P.load(lfin, lfin[0:rem, npast_t + nfull, :], S.logf[tok0 + nfull * 128:tok0 + T, :])
            for j, (k0, nk) in enumerate(ktiles):
                ps = ps_m.next()
                P.op("pe", lambda e, ps=ps, j=j, nk=nk: e.matmul(ps[0:128, 0:16], C.tri[:, :], lfin[:, j, :],
                                                                  start=True, stop=(j == 0)),
                     reads=[C.tri, lfin], writes=[ps])
                if j > 0:
                    P.op("pe", lambda e, ps=ps, j=j, nk=nk: e.matmul(ps[0:128, 0:16], C.sellast[:, 0:128], F_all[:, j - 1, :],
                                                                      start=False, stop=True),
                         reads=[C.sellast, F_all], writes=[ps])
                P.op("dve", lambda e, ps=ps, j=j, nk=nk: e.tensor_copy(F_all[0:nk, j, :], ps[0:nk, 0:16]),
                     reads=[ps], writes=[F_all])
            if C.dbgF is not None and sidx == 0:
                P.store(F_all, C.dbgF, F_all[:].rearrange("p j h -> p (j h)"))
                P.store(lfin, C.dbgL, lfin[:].rearrange("p j h -> p (j h)"))
            for h in range(H_A):
                kb = kTh.next()
                vb = vh.next()
                qb = qTh.next()
                r0 = h * 128
                if past:
                    P.load(kb, kb[:, 0:past], S.kTp[sidx - 1, r0:r0 + 128, :])
                    for j0 in range(0, npast_t, 16):
                        j1 = min(npast_t, j0 + 16)
                        P.load(vb, vb[:, j0:j1, :], S.vp[sidx - 1, j0 * 128:j1 * 128, r0:r0 + 128].rearrange("(j p) d -> p j d", p=128),
                               q="act")
                P.load(kb, kb[:, past:past + T], S.kT[r0:r0 + 128, tok0:tok0 + T])
                for j0 in range(0, nfull, 16):
                    j1 = min(nfull, j0 + 16)
                    P.load(vb, vb[:, npast_t + j0:npast_t + j1, :],
                           S.vbf[tok0 + j0 * 128:tok0 + j1 * 128, r0:r0 + 128].rearrange("(j p) d -> p j d", p=128), q="act")
                if rem:
                    P.op("pool", lambda e, vb=vb, jj=npast_t + nfull: e.memset(vb[:, jj, :], 0.0), writes=[vb])
                    P.load(vb, vb[0:rem, npast_t + nfull, :], S.vbf[tok0 + nfull * 128:tok0 + T, r0:r0 + 128], q="act")
                P.load(qb, qb[:, 0:T], S.qT[r0:r0 + 128, tok0:tok0 + T])
                for (q0, nq) in _tiles(T, 512):
                    qa = past + q0
                    ja = qa // 128
                    assert qa % 128 == 0
                    if h == 0 or True:
                        psc = ps_m.next()
                        P.op("pe", lambda e, psc=psc, ja=ja: e.matmul(psc[:, 0:16], C.sel0[:, :], F_all[:, ja, :],
                                                                        start=True, stop=True),
                             reads=[C.sel0, F_all], writes=[psc])
                        P.op("dve", lambda e, psc=psc: e.tensor_copy(c_b[:], psc[:, 0:16]), reads=[psc], writes=[c_b])
                    po = ps_o.next()
                    pl = ps_l.next()
                    jl = [j for j, (k0, nk) in enumerate(ktiles) if k0 <= qa + nq - 1]
                    def stage_a(j):
                        k0, nk = ktiles[j]
                        fs = max(0, k0 - qa)
                        bc = bcol.next()
                        if nk < 128:
                            P.op("dve", lambda e, bc=bc: e.memset(bc[:], -30000.0), writes=[bc])
                        P.op("dve", lambda e, bc=bc, j=j, nk=nk, h=h: e.tensor_tensor(
                            bc[0:nk, :], c_b[0:nk, h:h + 1], F_all[0:nk, j, h:h + 1], ALU.subtract),
                            reads=[c_b, F_all], writes=[bc])
                        pss = ps_s.next()
                        P.op("pe", lambda e, pss=pss, kb=kb, qb=qb, k0=k0, nk=nk, fs=fs, q0=q0, nq=nq: e.matmul(
                            pss[0:128, fs:nq], kb[:, k0:k0 + 128], qb[:, q0 + fs:q0 + nq], start=True, stop=True),
                            reads=[kb, qb], writes=[pss])
                        pt = pT.next()
                        P.op("act", lambda e, pt=pt, pss=pss, bc=bc, nk=nk, fs=fs, nq=nq: e.activation(
                            pt[0:128, fs:nq], pss[0:128, fs:nq], AF.Exp, bias=bc[0:128, :]),
                            reads=[pss, bc], writes=[pt])
                        if k0 >= qa:
                            w = min(nk, nq - fs)
                            P.op("dve", lambda e, pt=pt, nk=nk, fs=fs, w=w: e.tensor_tensor(
                                pt[0:nk, fs:fs + w], pt[0:nk, fs:fs + w], C.tri_bf[0:nk, 0:w], ALU.mult),
                                reads=[pt, C.tri_bf], writes=[pt])
                        return (j, nk, fs, pt)

                    def stage_b(item):
                        j, nk, fs, pt = item
                        first = (j == jl[0])
                        last = (j == jl[-1])
                        P.op("pe", lambda e, po=po, vb=vb, pt=pt, j=j, nk=nk, fs=fs, nq=nq, first=first, last=last: e.matmul(
                            po[:, fs:nq], vb[0:128, j, :], pt[0:128, fs:nq], start=first, stop=last),
                            reads=[vb, pt], writes=[po])
                        P.op("pe", lambda e, pl=pl, pt=pt, nk=nk, fs=fs, nq=nq, first=first, last=last: e.matmul(
                            pl[:, fs:nq], (C.ones_bf if nk == 128 else ones_part)[0:128, :], pt[0:128, fs:nq],
                            start=first, stop=last),
                            reads=[C.ones_bf, ones_part, pt], writes=[pl])

                    pend = []
                    for j in jl:
                        pend.append(stage_a(j))
                        if len(pend) > 2:
                            stage_b(pend.pop(0))
                    while pend:
                        stage_b(pend.pop(0))
                    P.op("dve", lambda e, pl=pl, nq=nq: e.reciprocal(rl[:, 0:nq], pl[:, 0:nq]), reads=[pl], writes=[rl])
                    ob = osb.next()
                    P.op("dve", lambda e, po=po, ob=ob, nq=nq: e.tensor_tensor(ob[:, 0:nq], po[:, 0:nq], rl[:, 0:nq], ALU.mult),
                         reads=[po, rl], writes=[ob])
                    P.store(ob, S.mixT[r0:r0 + 128, tok0 + q0:tok0 + q0 + nq], ob[:, 0:nq])


def _consts():
    idx = np.arange(128)
    ident = np.eye(128, dtype=np.float32)
    tri = (idx[:, None] <= idx[None, :]).astype(np.float32)
    negtri = np.where(idx[None, :] < idx[:, None], np.float32(NEG), np.float32(0.0)).astype(np.float32)
    return {"c_ident": ident, "c_tri": tri, "c_negtri": negtri}


def make_in_map(inp, core, NS=2):
    f = lambda a: np.ascontiguousarray(a, dtype=np.float32)
    s0, s1 = core * NS, (core + 1) * NS
    past = inp["cache_fox_k"].shape[2]
    m = {
        "x_prompt": f(inp["x_prompt"][core]),
        "x_sample": f(inp["x_sample"][s0:s1].reshape(NS * DEC, D)),
        "meta_tokens": f(inp["meta_tokens"]),
        "cache_fox_k": f(inp["cache_fox_k"][0, s0:s1].reshape(NS, past, 2048)),
        "cache_fox_v": f(inp["cache_fox_v"][0, s0:s1].reshape(NS, past, 2048)),
        "cache_fox_logf": f(inp["cache_fox_logf"][0, s0:s1]),
        "state_ssd_conv": f(inp["state_ssd_conv"][0, s0:s1]),
        "state_ssd": f(inp["state_ssd"][0, s0:s1].reshape(NS, 2048, 128)),
        "state_mlstm_c": f(inp["state_mlstm_c"][0, s0:s1].reshape(NS, 2048, 128)),
        "state_mlstm_n": f(inp["state_mlstm_n"][0, s0:s1]),
        "state_mlstm_m": f(inp["state_mlstm_m"][0, s0:s1]),
        "w_in_a": f(inp["w_in_a"][0]),
        "b_fgate_a": f(inp["b_fgate_a"]),
        "conv_w": f(inp["conv_w"][0]),
        "conv_b": f(inp["conv_b"]),
        "dt_bias": f(inp["dt_bias"]),
        "a_log": f(inp["a_log"]),
        "d_skip": f(inp["d_skip"]),
        "ssd_norm_g": f(inp["ssd_norm_g"]),
        "w_out_a": f(inp["w_out_a"][0]),
        "w_in_c": f(inp["w_in_c"][0]),
        "b_igate_c": f(inp["b_igate_c"]),
        "b_fgate_c": f(inp["b_fgate_c"]),
        "mlstm_norm_g": f(inp["mlstm_norm_g"]),
        "w_out_c": f(inp["w_out_c"][0]),
        "ln_mix_g": f(inp["ln_mix_g"]),
        "ln_mix_b": f(inp["ln_mix_b"]),
        "ln_ffn_g": f(inp["ln_ffn_g"]),
        "ln_ffn_b": f(inp["ln_ffn_b"]),
        "w_ffn_gate": f(inp["w_ffn_gate"]),
        "w_ffn_up": f(inp["w_ffn_up"]),
        "w_ffn_down": f(inp["w_ffn_down"]),
    }
    m.update(_consts())
    return m


def bview(ap, shape):
    return ap.to_broadcast(list(shape))


def phase_ssd_conv(P, C, I, O, S, seqs):
    with ExitStack() as st:
        Tmax = max(T for (_, T, _, _) in seqs)
        Tpad = ((Tmax + 127) // 128) * 128
        cwin = P.sb(st, "cwin", [128, CONV_DIM], F32)
        hin = P.sb(st, "hin", [128, CONV_DIM], F32)
        cw = P.sb(st, "cw", [128, 24, 8], F32)
        hT = P.sb(st, "hT", [128, 24, 4], F32)
        xc = Ring([P.sb(st, f"xc{i}", [128, Tpad + 4], F32) for i in range(2)])
        acc = Ring([P.sb(st, f"acc{i}", [128, Tpad], F32) for i in range(2)])
        accb = Ring([P.sb(st, f"accb{i}", [128, Tpad], BF16) for i in range(2)])
        cst = P.sb(st, "cst", [128, 128], F32)
        convout = P.sb(st, "convout", [128, CONV_DIM], F32)
        stf = Ring([P.sb(st, f"stf{i}", [128, 128], F32) for i in range(10)])
        sth = Ring([P.sb(st, f"sth{i}", [128, 128], BF16) for i in range(8)])
        psT = Ring(C.psum[0:4])
        for b_ in [cwin, hin, cst] + xc.items + acc.items:
            P.op("pool", lambda e, b_=b_: e.memset(b_[:], 0.0), writes=[b_])
        P.load(cwin, cwin[0:4, :], I.conv_w)
        P.load(cwin, cwin[4:5, :], I.conv_b)
        for cc in range(24):
            ps = psT.next()
            P.op("pe", lambda e, ps=ps, cc=cc: e.transpose(ps[:, 0:128], cwin[:, cc * 128:(cc + 1) * 128], C.ident[:, :]),
                 reads=[cwin, C.ident], writes=[ps])
            P.op("dve", lambda e, ps=ps, cc=cc: e.tensor_copy(cw[:, cc, 0:5], ps[:, 0:5]), reads=[ps], writes=[cw])
        for (tok0, T, past, sidx) in seqs:
            if past:
                P.load(hin, hin[0:3, :], I.st_conv[sidx - 1])
                for cc in range(24):
                    ps = psT.next()
                    P.op("pe", lambda e, ps=ps, cc=cc: e.transpose(ps[:, 0:128], hin[:, cc * 128:(cc + 1) * 128], C.ident[:, :]),
                         reads=[hin, C.ident], writes=[ps])
                    P.op("dve", lambda e, ps=ps, cc=cc: e.tensor_copy(hT[:, cc, 0:3], ps[:, 0:3]), reads=[ps], writes=[hT])
            else:
                P.op("dve", lambda e: e.memset(hT[:], 0.0), writes=[hT])
            for cc in range(24):
                xb = xc.next()
                P.load(xb, xb[:, 3:3 + T], S.xbcT[cc * 128:(cc + 1) * 128, tok0:tok0 + T])
                P.op("dve", lambda e, xb=xb, cc=cc: e.tensor_copy(xb[:, 0:3], hT[:, cc, 0:3]), reads=[hT], writes=[xb])
                P.op("dve", lambda e, xb=xb, T=T: e.tensor_copy(cst[:, 0:3], xb[:, T:T + 3]), reads=[xb], writes=[cst])
                ps = psT.next()
                P.op("pe", lambda e, ps=ps: e.transpose(ps[:, 0:128], cst[:, :], C.ident[:, :]), reads=[cst, C.ident], writes=[ps])
                P.op("dve", lambda e, ps=ps, cc=cc: e.tensor_copy(convout[0:3, cc * 128:(cc + 1) * 128], ps[0:3, 0:128]),
                     reads=[ps], writes=[convout])
                a = acc.next()
                P.op("act", lambda e, a=a, xb=xb, cc=cc, T=T: e.activation(a[:, 0:T], xb[:, 0:T], AF.Identity,
                                                                         bias=cw[:, cc, 4:5], scale=cw[:, cc, 0:1]),
                     reads=[xb, cw], writes=[a])
                for i in range(1, 4):
                    P.op("dve", lambda e, a=a, xb=xb, cc=cc, T=T, i=i: e.scalar_tensor_tensor(
                        out=a[:, 0:T], in0=xb[:, i:i + T], scalar=cw[:, cc, i:i + 1], in1=a[:, 0:T], op0=ALU.mult, op1=ALU.add),
                        reads=[a, xb, cw], writes=[a])
                P.op("act", lambda e, a=a, T=T: e.activation(a[:, 0:T], a[:, 0:T], AF.Silu), reads=[a], writes=[a])
                if cc < 20:
                    for (b0, n) in _tiles(T, 128):
                        ps = psT.next()
                        P.op("pe", lambda e, ps=ps, a=a, b0=b0: e.transpose(ps[:, 0:128], a[:, b0:b0 + 128], C.ident[:, :]),
                             reads=[a, C.ident], writes=[ps])
                        if cc < 16:
                            sb_ = stf.next()
                            P.op("act", lambda e, ps=ps, sb_=sb_: e.copy(sb_[:, :], ps[:, 0:128]), reads=[ps], writes=[sb_])
                            P.store(sb_, S.xtok[tok0 + b0:tok0 + b0 + n, cc * 128:(cc + 1) * 128], sb_[0:n, :])
                        else:
                            sb_ = sth.next()
                            P.op("act", lambda e, ps=ps, sb_=sb_: e.copy(sb_[:, :], ps[:, 0:128]), reads=[ps], writes=[sb_])
                            P.store(sb_, S.bmtok[tok0 + b0:tok0 + b0 + n, (cc - 16) * 128:(cc - 15) * 128], sb_[0:n, :])
                if cc >= 16:
                    ab = accb.next()
                    P.op("pool", lambda e, ab=ab, a=a, T=T: e.tensor_copy(ab[:, 0:T], a[:, 0:T]), reads=[a], writes=[ab])
                    P.store(ab, S.bcT[(cc - 16) * 128:(cc - 15) * 128, tok0:tok0 + T], ab[:, 0:T])
            P.store(convout, O.ssd_conv[sidx], convout[0:3, :])


def phase_ssd(P, C, I, O, S, seqs):
    with ExitStack() as st:
        sb = lambda name, shape, dt=F32: P.sb(st, name, shape, dt)
        dtb_b = sb("dtb_b", [128, 32]); a_b = sb("a_b", [128, 32]); D_b = sb("D_b", [128, 32])
        g_b = sb("g_b", [128, 2048])
        P.load(dtb_b, dtb_b[:], bview(I.dt_bias[0:1, :], [128, 32]))
        P.load(a_b, a_b[:], bview(I.a_log[0:1, :], [128, 32]))
        P.load(D_b, D_b[:], bview(I.d_skip[0:1, :], [128, 32]))
        P.load(g_b, g_b[:], bview(I.ssd_norm_g[0:1, :], [128, 2048]))
        P.op("act", lambda e: e.activation(a_b[:], a_b[:], AF.Exp), reads=[a_b], writes=[a_b])
        P.op("act", lambda e: e.mul(a_b[:], a_b[:], -1.0), reads=[a_b], writes=[a_b])
        valid16 = sb("valid16", [128, 1])
        P.op("dve", lambda e: e.memset(valid16[:], 0.0), writes=[valid16])
        P.op("dve", lambda e: e.memset(valid16[0:16, :], 1.0), writes=[valid16])
        S_sb = sb("S_sb", [128, 2048]); S_bf = sb("S_bf", [128, 2048], BF16)
        sio = sb("sio", [128, 16, 128])
        x_c = Ring([sb(f"x_c{i}", [128, 2048]) for i in range(2)])
        z_c = Ring([sb(f"z_c{i}", [128, 2048]) for i in range(2)])
        bm_c = Ring([sb(f"bm_c{i}", [128, 512], BF16) for i in range(2)])
        bcT_c = Ring([sb(f"bcT_c{i}", [128, 8, 128], BF16) for i in range(2)])
        dt_c = Ring([sb(f"dt_c{i}", [128, 32]) for i in range(2)])
        dt = sb("dt", [128, 32]); lndt = sb("lndt", [128, 32]); dta = sb("dta", [128, 32]); A_sb = sb("A_sb", [128, 32])
        bias_all = sb("bias_all", [128, 32]); wend = sb("wend", [128, 32]); cd_b = sb("cd_b", [128, 32]); expA = sb("expA", [128, 32])
        D2 = sb("D2", [128, 32, 128])
        x_bf = sb("x_bf", [128, 2048], BF16); xw = sb("xw", [128, 2048], BF16)
        dec = Ring([sb(f"dec{i}", [128, 128]) for i in range(3)])
        WT = Ring([sb(f"WT{i}", [128, 128], BF16) for i in range(3)])
        y_c = sb("y_c", [128, 2048]); tmpb = sb("tmpb", [128, 2048]); junk = sb("junk", [128, 512])
        ss = sb("ss", [128, 4]); rstd = sb("rstd", [128, 4])
        yT = Ring([sb(f"yT{i}", [128, 16, 128], BF16) for i in range(2)])
        psE = [C.psum[0], C.psum[1]]
        psCB, ps_y, ps_off, ps_st = C.psum[2], C.psum[3], C.psum[4], C.psum[5]
        psm = Ring(C.psum[6:8])
        for b_ in x_c.items + z_c.items + bm_c.items + bcT_c.items + dt_c.items + [sio]:
            P.op("pool", lambda e, b_=b_: e.memset(b_[:], 0.0), writes=[b_])

        def v3(buf, lo, n):
            return buf[:, lo * 64:(lo + n) * 64].rearrange("p (h q) -> p h q", q=64)

        for (tok0, T, past, sidx) in seqs:
            if past:
                P.load(sio, sio[:], I.st_ssd[sidx - 1].rearrange("(c p) n -> p c n", p=128))
                for c in range(16):
                    ps = psm.next()
                    P.op("pe", lambda e, ps=ps, c=c: e.transpose(ps[:, 0:128], sio[:, c, :], C.ident[:, :]),
                         reads=[sio, C.ident], writes=[ps])
                    P.op("dve", lambda e, ps=ps, c=c: e.tensor_copy(S_sb[:, c * 128:(c + 1) * 128], ps[:, 0:128]),
                         reads=[ps], writes=[S_sb])
            else:
                P.op("dve", lambda e: e.memset(S_sb[:], 0.0), writes=[S_sb])
            P.op("act", lambda e: e.copy(S_bf[:], S_sb[:]), reads=[S_sb], writes=[S_bf])
            for (t0, L) in _tiles(T, 128):
                assert L in (128, 16)
                a0 = tok0 + t0
                xb = x_c.next(); zb = z_c.next(); bmb = bm_c.next(); bcb = bcT_c.next(); dtr = dt_c.next()
                if L < 128:
                    for b_ in (xb, zb, bmb, bcb, dtr):
                        P.op("pool", lambda e, b_=b_: e.memset(b_[:], 0.0), writes=[b_])
                P.load(xb, xb[0:L, :], S.xtok[a0:a0 + L, :])
                P.load(zb, zb[0:L, :], S.z[a0:a0 + L, :], q="act")
                P.load(bmb, bmb[0:L, :], S.bmtok[a0:a0 + L, :])
                P.load(bcb, bcb[:, :, 0:L], S.bcT[:, a0:a0 + L].rearrange("(g n) t -> n g t", n=128), q="act")
                P.load(dtr, dtr[0:L, :], S.dtraw[a0:a0 + L, :])
                P.op("dve", lambda e, dtr=dtr: e.tensor_tensor(dt[:], dtr[:], dtb_b[:], ALU.add), reads=[dtr, dtb_b], writes=[dt])
                P.op("act", lambda e: e.activation(dt[:], dt[:], AF.Exp), reads=[dt], writes=[dt])
                P.op("act", lambda e: e.activation(dt[:], dt[:], AF.Ln, bias=1.0), reads=[dt], writes=[dt])
                if L < 128:
                    P.op("dve", lambda e: e.tensor_scalar(dt[:], dt[:], valid16[:, 0:1], None, ALU.mult), reads=[dt, valid16], writes=[dt])
                P.op("dve", lambda e: e.tensor_scalar(lndt[:], dt[:], 1e-30, None, ALU.max), reads=[dt], writes=[lndt])
                P.op("act", lambda e: e.activation(lndt[:], lndt[:], AF.Ln), reads=[lndt], writes=[lndt])
                P.op("dve", lambda e: e.tensor_tensor(dta[:], dt[:], a_b[:], ALU.mult), reads=[dt, a_b], writes=[dta])
                ps = psm.next()
                P.op("pe", lambda e, ps=ps: e.matmul(ps[:, 0:32], C.tri[:, :], dta[:, :], start=True, stop=True),
                     reads=[C.tri, dta], writes=[ps])
                P.op("dve", lambda e, ps=ps: e.tensor_copy(A_sb[:], ps[:, 0:32]), reads=[ps], writes=[A_sb])
                pse = psm.next()
                P.op("pe", lambda e, pse=pse: e.matmul(pse[:, 0:32], C.sellast[:, :], A_sb[:, :], start=True, stop=True),
                     reads=[C.sellast, A_sb], writes=[pse])
                P.op("dve", lambda e: e.tensor_tensor(bias_all[:], lndt[:], A_sb[:], ALU.subtract), reads=[lndt, A_sb], writes=[bias_all])
                P.op("dve", lambda e, pse=pse: e.tensor_tensor(wend[:], pse[:, 0:32], bias_all[:], ALU.add), reads=[pse, bias_all], writes=[wend])
                P.op("act", lambda e: e.activation(wend[:], wend[:], AF.Exp), reads=[wend], writes=[wend])
                P.op("act", lambda e, pse=pse: e.activation(cd_b[:], pse[:, 0:32], AF.Exp), reads=[pse], writes=[cd_b])
                P.op("act", lambda e: e.activation(expA[:], A_sb[:], AF.Exp), reads=[A_sb], writes=[expA])
                P.op("dve", lambda e: e.tensor_tensor(D2[:], bview(C.ident[:, :].unsqueeze(1), [128, 32, 128]),
                                                      bview(A_sb[:, :].unsqueeze(2), [128, 32, 128]), ALU.mult),
                     reads=[C.ident, A_sb], writes=[D2])
                P.op("act", lambda e, xb=xb: e.copy(x_bf[:], xb[:]), reads=[xb], writes=[x_bf])
                P.op("dve", lambda e, xb=xb: e.tensor_tensor(v3(xw, 0, 32), v3(xb, 0, 32), bview(wend[:, :].unsqueeze(2), [128, 32, 64]), ALU.mult),
                     reads=[xb, wend], writes=[xw])
                for g in range(4):
                    for half in range(2):
                        pe_ = psE[half]
                        P.op("pe", lambda e, pe_=pe_, g=g, half=half: e.matmul(
                            pe_[:, 0:512], C.ones_f[:, :], D2[:, g * 8 + half * 4:g * 8 + half * 4 + 4, :].rearrange("p h l -> p (h l)"),
                            start=True, stop=False), reads=[C.ones_f, D2], writes=[pe_])
                        P.op("pe", lambda e, pe_=pe_: e.matmul(
                            pe_[:, 0:512], C.ident[:, :], C.negtri4[:, :], start=False, stop=True),
                            reads=[C.ident, C.negtri4], writes=[pe_])
                    P.op("pe", lambda e, bcb=bcb, g=g: e.matmul(psCB[:, 0:128], bcb[:, g, :], bcb[:, 4 + g, :], start=True, stop=True),
                         reads=[bcb], writes=[psCB])
                    for hh in range(8):
                        h = g * 8 + hh
                        pe_ = psE[hh // 4]
                        d_ = dec.next()
                        P.op("act", lambda e, d_=d_, pe_=pe_, hh=hh, h=h: e.activation(
                            d_[:, :], pe_[:, (hh % 4) * 128:(hh % 4 + 1) * 128], AF.Exp, bias=bias_all[:, h:h + 1]),
                            reads=[pe_, bias_all], writes=[d_])
                        w_ = WT.next()
                        P.op("dve", lambda e, d_=d_, w_=w_: e.tensor_tensor(w_[:, :], d_[:, :], psCB[:, 0:128], ALU.mult),
                             reads=[d_, psCB], writes=[w_])
                        P.op("pe", lambda e, w_=w_, hh=hh, h=h: e.matmul(ps_y[:, hh * 64:(hh + 1) * 64], w_[:, :], x_bf[:, h * 64:(h + 1) * 64],
                                                                          start=True, stop=True), reads=[w_, x_bf], writes=[ps_y])
                    P.op("pe", lambda e, bcb=bcb, g=g: e.matmul(ps_off[:, 0:512], bcb[:, 4 + g, :], S_bf[:, g * 512:(g + 1) * 512],
                                                                  start=True, stop=True), reads=[bcb, S_bf], writes=[ps_off])
                    P.op("dve", lambda e, g=g: e.tensor_tensor(v3(y_c, g * 8, 8), ps_off[:, 0:512].rearrange("p (h q) -> p h q", q=64),
                                                                bview(expA[:, g * 8:(g + 1) * 8].unsqueeze(2), [128, 8, 64]), ALU.mult),
                         reads=[ps_off, expA], writes=[y_c])
                    P.op("dve", lambda e, g=g: e.tensor_tensor(y_c[:, g * 512:(g + 1) * 512], y_c[:, g * 512:(g + 1) * 512], ps_y[:, 0:512], ALU.add),
                         reads=[y_c, ps_y], writes=[y_c])
                    P.op("pe", lambda e, bmb=bmb, g=g: e.matmul(ps_st[:, 0:512], bmb[:, g * 128:(g + 1) * 128], xw[:, g * 512:(g + 1) * 512],
                                                                  start=True, stop=True), reads=[bmb, xw], writes=[ps_st])
                    P.op("dve", lambda e, g=g: e.tensor_tensor(v3(S_sb, g * 8, 8), v3(S_sb, g * 8, 8),
                                                                bview(cd_b[:, g * 8:(g + 1) * 8].unsqueeze(2), [128, 8, 64]), ALU.mult),
                         reads=[S_sb, cd_b], writes=[S_sb])
                    P.op("dve", lambda e, g=g: e.tensor_tensor(S_sb[:, g * 512:(g + 1) * 512], S_sb[:, g * 512:(g + 1) * 512], ps_st[:, 0:512], ALU.add),
                         reads=[S_sb, ps_st], writes=[S_sb])
                    P.op("act", lambda e, g=g: e.copy(S_bf[:, g * 512:(g + 1) * 512], S_sb[:, g * 512:(g + 1) * 512]), reads=[S_sb], writes=[S_bf])
                P.op("pool", lambda e, xb=xb: e.tensor_tensor(v3(tmpb, 0, 32), v3(xb, 0, 32), bview(D_b[:, :].unsqueeze(2), [128, 32, 64]), ALU.mult),
                     reads=[xb, D_b], writes=[tmpb])
                P.op("dve", lambda e: e.tensor_tensor(y_c[:], y_c[:], tmpb[:], ALU.add), reads=[y_c, tmpb], writes=[y_c])
                P.op("act", lambda e, zb=zb: e.activation(tmpb[:], zb[:], AF.Silu), reads=[zb], writes=[tmpb])
                P.op("dve", lambda e: e.tensor_tensor(y_c[:], y_c[:], tmpb[:], ALU.mult), reads=[y_c, tmpb], writes=[y_c])
                for g in range(4):
                    P.op("act", lambda e, g=g: e.activation(junk[:, :], y_c[:, g * 512:(g + 1) * 512], AF.Square, accum_out=ss[:, g:g + 1]),
                         reads=[y_c], writes=[junk, ss])
                P.op("dve", lambda e: e.tensor_scalar(rstd[:], ss[:], 1.0 / 512, 1e-5, ALU.mult, ALU.add), reads=[ss], writes=[rstd])
                P.op("act", lambda e: e.activation(rstd[:], rstd[:], AF.Sqrt), reads=[rstd], writes=[rstd])
                P.op("dve", lambda e: e.reciprocal(rstd[:], rstd[:]), reads=[rstd], writes=[rstd])
                for g in range(4):
                    P.op("dve", lambda e, g=g: e.tensor_scalar(y_c[:, g * 512:(g + 1) * 512], y_c[:, g * 512:(g + 1) * 512], rstd[:, g:g + 1], None, ALU.mult),
                         reads=[y_c, rstd], writes=[y_c])
                P.op("pool", lambda e: e.tensor_tensor(y_c[:], y_c[:], g_b[:], ALU.mult), reads=[y_c, g_b], writes=[y_c])
                yt = yT.next()
                for c4 in range(4):
                    ps = psm.next()
                    for j in range(4):
                        c = c4 * 4 + j
                        P.op("pe", lambda e, ps=ps, c=c, j=j: e.transpose(ps[:, j * 128:(j + 1) * 128], y_c[:, c * 128:(c + 1) * 128], C.ident[:, :]),
                             reads=[y_c, C.ident], writes=[ps])
                    P.op("act", lambda e, ps=ps, c4=c4, yt=yt: e.copy(yt[:, c4 * 4:(c4 + 1) * 4, :], ps[:].rearrange("p (j m) -> p j m", j=4)),
                         reads=[ps], writes=[yt])
                P.store(yt, S.mixT[2048:4096, a0:a0 + L].rearrange("(c p) t -> p c t", p=128), yt[:, :, 0:L])
            for c in range(16):
                ps = psm.next()
                P.op("pe", lambda e, ps=ps, c=c: e.transpose(ps[:, 0:128], S_sb[:, c * 128:(c + 1) * 128], C.ident[:, :]),
                     reads=[S_sb, C.ident], writes=[ps])
                P.op("dve", lambda e, ps=ps, c=c: e.tensor_copy(sio[:, c, :], ps[:, 0:128]), reads=[ps], writes=[sio])
            P.store(sio, O.ssd_state[sidx].rearrange("(c p) n -> p c n", p=128), sio[:])


def phase_outproj(P, C, W, Kc, srcT, dst):
    with ExitStack() as st:
        TT = 1152
        wcols = 512 if Kc <= 16 else 256
        xT = P.sb(st, "xT", [128, Kc, TT + 128], BF16)
        P.op("pool", lambda e: e.memset(xT[:], 0.0), writes=[xT])
        wring = Ring([P.sb(st, f"w{i}", [128, Kc, wcols], BF16) for i in range(2)])
        sf = Ring([P.sb(st, f"sf{i}", [128, 512], F32) for i in range(3)])
        psG = Ring(C.psum[0:8])
        for (T0, NT) in supertiles(C.NTOK, TT):
            for k0 in range(0, Kc, 16):
                P.load(xT, xT[:, k0:k0 + 16, 0:NT], srcT[k0 * 128:(k0 + 16) * 128, T0:T0 + NT].rearrange("(c p) t -> p c t", p=128),
                       q=("sp" if (k0 // 16) % 2 == 0 else "act"))

            def ev(ps, t0, n, col0, ncol, T0=T0):
                b = sf.next()
                P.op("act", lambda e: e.copy(b[0:n, 0:ncol], ps[0:n, 0:ncol]), reads=[ps], writes=[b])
                P.store(b, dst[T0 + t0:T0 + t0 + n, col0:col0 + ncol], b[0:n, 0:ncol])

            gemm_segments(P, C, xT, Kc, NT, W, [(0, 2048, "tok", ev)], wring, psG, wcols)


def make_ln_loader(P, C, st, xa_src, xb_src, g_ap, b_ap, x_dst):
    g_b = P.sb(st, "lng", [128, D], F32)
    b_b = P.sb(st, "lnb", [128, D], F32)
    P.load(g_b, g_b[:], bview(g_ap, [128, D]))
    P.load(b_b, b_b[:], bview(b_ap, [128, D]))
    mring = Ring([P.sb(st, f"lnm{i}", [128, D], F32) for i in range(3)])
    aring = Ring([P.sb(st, f"lna{i}", [128, D], F32) for i in range(3)])
    stats = P.sb(st, "lnstats", [128, 24], F32)
    mv = P.sb(st, "lnmv", [128, 2], F32)
    rstd = P.sb(st, "lnrstd", [128, 1], F32)

    def loader(xb, tok0, n):
        xa = aring.next()
        mb = mring.next()
        if callable(xa_src):
            xa_src(xa, tok0, n)
        else:
            P.load(xa, xa[0:n, :], xa_src[tok0:tok0 + n, :])
        P.load(mb, mb[0:n, :], xb_src[tok0:tok0 + n, :], q="act")
        layer_norm_tile(P, C, (stats, mv, rstd), xa, mb, n, g_b, b_b, xb)
        P.store(xb, x_dst[tok0:tok0 + n, :], xb[0:n, :])

    return loader


def phase_ffn(P, C, xT_src, wg, wu, wd, dst):
    with ExitStack() as st:
        TT = 896
        KF = D_FF // 128
        xT = P.sb(st, "xT", [128, 16, TT], BF16)
        hT = P.sb(st, "hT", [128, KF, TT], BF16)
        P.op("pool", lambda e: e.memset(xT[:], 0.0), writes=[xT])
        P.op("pool", lambda e: e.memset(hT[:], 0.0), writes=[hT])
        wring = Ring([P.sb(st, f"w{i}", [128, KF * 256], BF16) for i in range(3)])
        assert all(((nt + 127) // 128) * 128 <= TT for (_, nt) in supertiles(C.NTOK, TT))
        sf = Ring([P.sb(st, f"sf{i}", [128, 512], F32) for i in range(3)])
        sg = Ring([P.sb(st, f"sg{i}", [128, 512], F32) for i in range(2)])
        psA = Ring(C.psum[0:4])
        psB = Ring(C.psum[4:8])
        wgv = wg.rearrange("(c p) n -> p c n", p=128)
        wuv = wu.rearrange("(c p) n -> p c n", p=128)
        wdv = wd.rearrange("(c p) n -> p c n", p=128)
        for (T0, NT) in supertiles(C.NTOK, TT):
            P.load(xT, xT[:, :, 0:NT], xT_src[:, T0:T0 + NT].rearrange("(c p) t -> p c t", p=128))
            for (f0, nf) in _tiles(D_FF, 512):
                wgb = wring.next()
                wub = wring.next()
                wg3 = wgb[:, 0:16 * 512].rearrange("p (c n) -> p c n", n=512)
                wu3 = wub[:, 0:16 * 512].rearrange("p (c n) -> p c n", n=512)
                P.load(wgb, wg3[:, :, 0:nf], wgv[:, :, f0:f0 + nf], q="pool")
                P.load(wub, wu3[:, :, 0:nf], wuv[:, :, f0:f0 + nf], q="pool")
                for (s0, ns) in _tiles(nf, 128):
                    j = (f0 + s0) // 128
                    for (t0, n) in _tiles(NT, 512):
                        pg = psA.next()
                        pu = psB.next()
                        for c in range(16):
                            P.op("pe", lambda e, pg=pg, c=c, wg3=wg3, s0=s0, t0=t0, n=n: e.matmul(
                                pg[:, 0:n], wg3[:, c, s0:s0 + 128], xT[:, c, t0:t0 + n], start=(c == 0), stop=(c == 15)),
                                reads=[wgb, xT], writes=[pg])
                        for c in range(16):
                            P.op("pe", lambda e, pu=pu, c=c, wu3=wu3, s0=s0, t0=t0, n=n: e.matmul(
                                pu[:, 0:n], wu3[:, c, s0:s0 + 128], xT[:, c, t0:t0 + n], start=(c == 0), stop=(c == 15)),
                                reads=[wub, xT], writes=[pu])
                        g_ = sg.next()
                        P.op("act", lambda e, g_=g_, pg=pg, n=n: e.activation(g_[:, 0:n], pg[:, 0:n], AF.Silu), reads=[pg], writes=[g_])
                        P.op("dve", lambda e, g_=g_, pu=pu, n=n, j=j, t0=t0: e.tensor_tensor(hT[:, j, t0:t0 + n], g_[:, 0:n], pu[:, 0:n], ALU.mult),
                             reads=[g_, pu], writes=[hT])
            for (c0, ncw) in _tiles(D, 256):
                wb = wring.next()
                w3 = wb[:, 0:KF * 256].rearrange("p (c n) -> p c n", n=256)
                for k0 in range(0, KF, 11):
                    P.load(wb, w3[:, k0:k0 + 11, 0:ncw], wdv[:, k0:k0 + 11, c0:c0 + ncw], q="pool")
                for (t0, n) in _tiles(NT, 128):
                    ps = psA.next()
                    for c in range(KF):
                        P.op("pe", lambda e, ps=ps, c=c, t0=t0, w3=w3, ncw=ncw: e.matmul(
                            ps[0:128, 0:ncw], hT[:, c, t0:t0 + 128], w3[:, c, 0:ncw], start=(c == 0), stop=(c == KF - 1)),
                            reads=[hT, wb], writes=[ps])
                    b = sf.next()
                    P.op("act", lambda e, b=b, ps=ps, n=n, ncw=ncw: e.copy(b[0:n, 0:ncw], ps[0:n, 0:ncw]), reads=[ps], writes=[b])
                    P.store(b, dst[T0 + t0:T0 + t0 + n, c0:c0 + ncw], b[0:n, 0:ncw])


def phase_ln_T(P, C, xa_src, xb_src, g_ap, b_ap, x_dst, xT_dst):
    with ExitStack() as st:
        loader = make_ln_loader(P, C, st, xa_src, xb_src, g_ap, b_ap, x_dst)
        xin = Ring([P.sb(st, f"xin{i}", [128, D], F32) for i in range(3)])
        for b_ in xin.items:
            P.op("pool", lambda e, b_=b_: e.memset(b_[:], 0.0), writes=[b_])
        xTs = Ring([P.sb(st, f"xTs{i}", [128, 16, 128], BF16) for i in range(3)])
        psT = Ring(C.psum[0:4])
        for (t0, n) in _tiles(C.NTOK, 128):
            xt = xTs.next()
            build_xT(P, C, st, xt, t0, n, 0, loader, xin, psT)
            P.store(xt, xT_dst[:, t0:t0 + n].rearrange("(c p) t -> p c t", p=128), xt[:, :, 0:n])


def phase_final_ln(P, C, xa_src, xb_src, g_ap, b_ap, dst):
    with ExitStack() as st:
        loader = make_ln_loader(P, C, st, xa_src, xb_src, g_ap, b_ap, dst)
        outr = Ring([P.sb(st, f"lnout{i}", [128, D], F32) for i in range(3)])
        for (t0, n) in _tiles(C.NTOK, 128):
            loader(outr.next(), t0, n)


def phase_inproj_c(P, C, I, S, loader_factory):
    NTOK = C.NTOK
    kscale = DK_C ** -0.5
    with ExitStack() as st:
        TT = 1152
        loader = loader_factory(st)
        xT = P.sb(st, "xT", [128, 16, TT + 128], BF16)
        P.op("pool", lambda e: e.memset(xT[:], 0.0), writes=[xT])
        xin = Ring([P.sb(st, f"xin{i}", [128, D], F32) for i in range(2)])
        for b_ in xin.items:
            P.op("pool", lambda e, b_=b_: e.memset(b_[:], 0.0), writes=[b_])
        wring = Ring([P.sb(st, f"w{i}", [128, 16, 512], BF16) for i in range(2)])
        sf = Ring([P.sb(st, f"sf{i}", [128, 512], F32) for i in range(3)])
        sh = Ring([P.sb(st, f"sh{i}", [128, 512], BF16) for i in range(3)])
        sm = Ring([P.sb(st, f"sm{i}", [128, 16], F32) for i in range(3)])
        gb = P.sb(st, "gb", [128, 16], F32)
        P.load(gb, gb[:, 0:8], bview(I.b_igate_c[0:1, :], [128, 8]))
        P.load(gb, gb[:, 8:16], bview(I.b_fgate_c[0:1, :], [128, 8]))
        psT = Ring(C.psum[0:2])
        psG = Ring(C.psum[2:8])
        for (T0, NT) in supertiles(NTOK, TT):
            for (t0, n) in _tiles(NT, 128):
                build_xT(P, C, st, xT, T0 + t0, n, t0, loader, xin, psT)

            def ev_feat(dst, mul, cbase, T0=T0):
                def f(ps, t0, n, col0, ncol):
                    b = sh.next()
                    P.op("act", lambda e: e.activation(b[0:ncol, 0:n], ps[0:ncol, 0:n], AF.Copy, scale=float(mul)), reads=[ps], writes=[b])
                    P.store(b, dst[col0 - cbase:col0 - cbase + ncol, T0 + t0:T0 + t0 + n], b[0:ncol, 0:n])
                return f

            def ev_tok_bf(dst, mul, cbase, T0=T0):
                def f(ps, t0, n, col0, ncol):
                    b = sh.next()
                    P.op("act", lambda e: e.activation(b[0:n, 0:ncol], ps[0:n, 0:ncol], AF.Copy, scale=float(mul)), reads=[ps], writes=[b])
                    P.store(b, dst[T0 + t0:T0 + t0 + n, col0 - cbase:col0 - cbase + ncol], b[0:n, 0:ncol])
                return f

            def ev_o(ps, t0, n, col0, ncol, T0=T0):
                b = sf.next()
                P.op("act", lambda e: e.copy(b[0:n, 0:ncol], ps[0:n, 0:ncol]), reads=[ps], writes=[b])
                P.store(b, S.z[T0 + t0:T0 + t0 + n, col0 - 4096:col0 - 4096 + ncol], b[0:n, 0:ncol])

            def ev_gates(ps, t0, n, col0, ncol, T0=T0):
                b = sm.next()
                P.op("dve", lambda e: e.tensor_tensor(b[0:n, :], ps[0:n, 0:16], gb[0:n, :], ALU.add), reads=[ps, gb], writes=[b])
                P.op("act", lambda e: e.activation(b[0:n, 8:16], b[0:n, 8:16], AF.Exp, scale=-1.0), reads=[b], writes=[b])
                P.op("act", lambda e: e.activation(b[0:n, 8:16], b[0:n, 8:16], AF.Ln, bias=1.0), reads=[b], writes=[b])
                P.op("act", lambda e: e.mul(b[0:n, 8:16], b[0:n, 8:16], -1.0), reads=[b], writes=[b])
                P.store(b, S.gates[T0 + t0:T0 + t0 + n, :], b[0:n, :])

            segs = [
                (0, 1024, "feat", ev_feat(S.qT, 1.0, 0)),
                (1024, 2048, "feat", ev_feat(S.kT, kscale, 1024)),
                (1024, 2048, "tok", ev_tok_bf(S.ktok, kscale, 1024)),
                (2048, 4096, "tok", ev_tok_bf(S.vbf, 1.0, 2048)),
                (4096, 6144, "tok", ev_o),
                (6144, 6160, "tok", ev_gates),
            ]
            gemm_segments(P, C, xT, 16, NT, I.w_in_c, segs, wring, psG, 512)


def phase_mlstm(P, C, I, O, S, seqs):
    with ExitStack() as st:
        sb = lambda name, shape, dt=F32: P.sb(st, name, shape, dt)
        g_b = sb("g_b", [128, 2048])
        P.load(g_b, g_b[:], bview(I.mlstm_norm_g[0:1, :], [128, 2048]))
        Cs = sb("Cs", [128, 8, 257]); Cs_bf = sb("Cs_bf", [128, 8, 257], BF16)
        m_b = sb("m_b", [128, 8]); m_col = sb("m_col", [128, 1])
        cio = sb("cio", [128, 16, 128])
        npad = sb("npad", [128, 128])
        gt = Ring([sb(f"gt{i}", [128, 16]) for i in range(2)])
        qT_c = Ring([sb(f"qT_c{i}", [128, 8, 128], BF16) for i in range(2)])
        kT_c = Ring([sb(f"kT_c{i}", [128, 8, 128], BF16) for i in range(2)])
        kt_c = Ring([sb(f"kt_c{i}", [128, 1024], BF16) for i in range(2)])
        va_c = Ring([sb(f"va_c{i}", [128, 8, 257], BF16) for i in range(2)])
        o_c = Ring([sb(f"o_c{i}", [128, 2048]) for i in range(2)])
        bcum = sb("bcum", [128, 8]); u = sb("u", [128, 8]); upad = sb("upad", [128, 128]); bpad = sb("bpad", [128, 128])
        uT = sb("uT", [128, 128]); bT = sb("bT", [128, 128]); MT = sb("MT", [128, 128]); ones8 = sb("ones8", [128, 128])
        M_sb = sb("M_sb", [128, 8]); negM = sb("negM", [128, 8]); Mlast = sb("Mlast", [128, 8]); btot = sb("btot", [128, 8])
        ginter = sb("ginter", [128, 8]); wend = sb("wend", [128, 8]); gold = sb("gold", [128, 8]); emr = sb("emr", [128, 8])
        D2m = sb("D2m", [128, 8, 128])
        dec = Ring([sb(f"dec{i}", [128, 128]) for i in range(3)])
        WT = Ring([sb(f"WT{i}", [128, 128], BF16) for i in range(3)])
        numsb = Ring([sb(f"numsb{i}", [128, 257]) for i in range(2)])
        comb = Ring([sb(f"comb{i}", [128, 257]) for i in range(2)])
        rdn = Ring([sb(f"rdn{i}", [128, 1]) for i in range(2)])
        vw = Ring([sb(f"vw{i}", [128, 257], BF16) for i in range(2)])
        h_c = sb("h_c", [128, 2048]); tmpb = sb("tmpb", [128, 2048]); junk = sb("junk", [128, 256])
        ss = sb("ss", [128, 8]); rstd = sb("rstd", [128, 8])
        hTs = Ring([sb(f"hTs{i}", [128, 16, 128], BF16) for i in range(2)])
        psE = [C.psum[0], C.psum[1]]
        ps_s, ps_num, ps_int, ps_c = C.psum[2], C.psum[3], C.psum[4], C.psum[5]
        psm = Ring(C.psum[6:8])
        for b_ in [upad, bpad, MT, npad, cio, m_col] + qT_c.items + kT_c.items + kt_c.items + va_c.items + o_c.items:
            P.op("pool", lambda e, b_=b_: e.memset(b_[:], 0.0), writes=[b_])
        P.op("pool", lambda e: e.memset(ones8[:], 1.0), writes=[ones8])

        for (tok0, T, past, sidx) in seqs:
            if past:
                P.load(cio, cio[:], I.st_c[sidx - 1].rearrange("(c p) d -> p c d", p=128))
                for c in range(16):
                    ps = psm.next()
                    P.op("pe", lambda e, ps=ps, c=c: e.transpose(ps[:, 0:128], cio[:, c, :], C.ident[:, :]), reads=[cio, C.ident], writes=[ps])
                    P.op("dve", lambda e, ps=ps, c=c: e.tensor_copy(Cs[:, c // 2, (c % 2) * 128:(c % 2 + 1) * 128], ps[:, 0:128]),
                         reads=[ps], writes=[Cs])
                P.load(npad, npad[0:8, :], I.st_n[sidx - 1])
                ps = psm.next()
                P.op("pe", lambda e, ps=ps: e.transpose(ps[:, 0:128], npad[:, :], C.ident[:, :]), reads=[npad, C.ident], writes=[ps])
                P.op("dve", lambda e, ps=ps: e.tensor_copy(Cs[:, :, 256], ps[:, 0:8]), reads=[ps], writes=[Cs])
                P.load(m_b, m_b[:], bview(I.st_m[sidx - 1:sidx, :], [128, 8]))
                P.load(m_col, m_col[0:8, :], I.st_m[sidx - 1:sidx, :].rearrange("o h -> h o"))
            else:
                P.op("dve", lambda e: e.memset(Cs[:], 0.0), writes=[Cs])
                P.op("dve", lambda e: e.memset(m_b[:], 0.0), writes=[m_b])
                P.op("dve", lambda e: e.memset(m_col[:], 0.0), writes=[m_col])
            P.op("act", lambda e: e.copy(Cs_bf[:], Cs[:]), reads=[Cs], writes=[Cs_bf])
            for (t0, L) in _tiles(T, 128):
                assert L in (128, 16)
                a0 = tok0 + t0
                g_ = gt.next(); qb = qT_c.next(); kb = kT_c.next(); ktb = kt_c.next(); vab = va_c.next(); ob = o_c.next()
                if L < 128:
                    for b_ in (qb, kb, ktb, vab, ob):
                        P.op("pool", lambda e, b_=b_: e.memset(b_[:], 0.0), writes=[b_])
                P.op("pool", lambda e, g_=g_: e.memset(g_[:, 0:8], NEG), writes=[g_])
                P.op("pool", lambda e, g_=g_: e.memset(g_[:, 8:16], 0.0), writes=[g_])
                P.load(g_, g_[0:L, :], S.gates[a0:a0 + L, :])
                P.load(qb, qb[:, :, 0:L], S.qT[0:1024, a0:a0 + L].rearrange("(h d) t -> d h t", d=128))
                P.load(kb, kb[:, :, 0:L], S.kT[0:1024, a0:a0 + L].rearrange("(h d) t -> d h t", d=128), q="act")
                P.load(ktb, ktb[0:L, :], S.ktok[a0:a0 + L, :])
                P.load(vab, vab[0:L, :, 0:256], S.vbf[a0:a0 + L, :].rearrange("t (h v) -> t h v", v=256), q="act")
                P.op("pool", lambda e, vab=vab: e.memset(vab[:, :, 256:257], 1.0), writes=[vab])
                P.load(ob, ob[0:L, :], S.z[a0:a0 + L, :])
                ps = psm.next()
                P.op("pe", lambda e, ps=ps, g_=g_: e.matmul(ps[:, 0:8], C.tri[:, :], g_[:, 8:16], start=True, stop=True), reads=[C.tri, g_], writes=[ps])
                P.op("dve", lambda e, ps=ps: e.tensor_copy(bcum[:], ps[:, 0:8]), reads=[ps], writes=[bcum])
                P.op("dve", lambda e, g_=g_: e.tensor_tensor(u[:], g_[:, 0:8], bcum[:], ALU.subtract), reads=[g_, bcum], writes=[u])
                P.op("dve", lambda e: e.tensor_copy(upad[:, 0:8], u[:]), reads=[u], writes=[upad])
                P.op("dve", lambda e: e.tensor_copy(bpad[:, 0:8], bcum[:]), reads=[bcum], writes=[bpad])
                ps = psm.next()
                P.op("pe", lambda e, ps=ps: e.transpose(ps[:, 0:128], upad[:, :], C.ident[:, :]), reads=[upad, C.ident], writes=[ps])
                P.op("dve", lambda e, ps=ps: e.tensor_copy(uT[0:8, :], ps[0:8, 0:128]), reads=[ps], writes=[uT])
                ps = psm.next()
                P.op("pe", lambda e, ps=ps: e.transpose(ps[:, 0:128], bpad[:, :], C.ident[:, :]), reads=[bpad, C.ident], writes=[ps])
                P.op("dve", lambda e, ps=ps: e.tensor_copy(bT[0:8, :], ps[0:8, 0:128]), reads=[ps], writes=[bT])
                P.op("dve", lambda e: e.tensor_tensor_scan(MT[0:8, :], ones8[0:8, :], uT[0:8, :], m_col[0:8, 0:1], ALU.mult, ALU.max),
                     reads=[ones8, uT, m_col], writes=[MT])
                ps = psm.next()
                P.op("pe", lambda e, ps=ps: e.transpose(ps[:, 0:128], MT[:, :], C.ident[:, :]), reads=[MT, C.ident], writes=[ps])
                P.op("dve", lambda e, ps=ps: e.tensor_copy(M_sb[:], ps[:, 0:8]), reads=[ps], writes=[M_sb])
                P.op("act", lambda e: e.mul(negM[:], M_sb[:], -1.0), reads=[M_sb], writes=[negM])
                ps = psm.next()
                P.op("pe", lambda e, ps=ps: e.matmul(ps[:, 0:8], C.sellast[:, :], M_sb[:, :], start=True, stop=True), reads=[C.sellast, M_sb], writes=[ps])
                P.op("dve", lambda e, ps=ps: e.tensor_copy(Mlast[:], ps[:, 0:8]), reads=[ps], writes=[Mlast])
                ps = psm.next()
                P.op("pe", lambda e, ps=ps: e.matmul(ps[:, 0:8], C.sellast[:, :], bcum[:, :], start=True, stop=True), reads=[C.sellast, bcum], writes=[ps])
                P.op("dve", lambda e, ps=ps: e.tensor_copy(btot[:], ps[:, 0:8]), reads=[ps], writes=[btot])
                P.op("dve", lambda e: e.tensor_tensor(ginter[:], m_b[:], M_sb[:], ALU.subtract), reads=[m_b, M_sb], writes=[ginter])
                P.op("act", lambda e: e.activation(ginter[:], ginter[:], AF.Exp), reads=[ginter], writes=[ginter])
                P.op("dve", lambda e: e.tensor_tensor(wend[:], u[:], Mlast[:], ALU.subtract), reads=[u, Mlast], writes=[wend])
                P.op("act", lambda e: e.activation(wend[:], wend[:], AF.Exp), reads=[wend], writes=[wend])
                P.op("dve", lambda e: e.tensor_tensor(gold[:], m_b[:], Mlast[:], ALU.subtract), reads=[m_b, Mlast], writes=[gold])
                P.op("act", lambda e: e.activation(gold[:], gold[:], AF.Exp), reads=[gold], writes=[gold])
                P.op("dve", lambda e: e.tensor_tensor(emr[:], bcum[:], M_sb[:], ALU.add), reads=[bcum, M_sb], writes=[emr])
                P.op("act", lambda e: e.activation(emr[:], emr[:], AF.Exp, scale=-1.0), reads=[emr], writes=[emr])
                P.op("dve", lambda e: e.tensor_tensor(D2m[:], bview(C.ident[:, :].unsqueeze(1), [128, 8, 128]),
                                                      bview(negM[:, :].unsqueeze(2), [128, 8, 128]), ALU.mult),
                     reads=[C.ident, negM], writes=[D2m])
                for half in range(2):
                    pe_ = psE[half]
                    P.op("pe", lambda e, pe_=pe_, half=half: e.matmul(pe_[:, 0:512], C.ones_f[:, :],
                                                                        D2m[:, half * 4:half * 4 + 4, :].rearrange("p h l -> p (h l)"),
                                                                        start=True, stop=False), reads=[C.ones_f, D2m], writes=[pe_])
                    P.op("pe", lambda e, pe_=pe_: e.matmul(pe_[:, 0:512], C.ident[:, :], C.negtri4[:, :], start=False, stop=True),
                         reads=[C.ident, C.negtri4], writes=[pe_])
                for h in range(8):
                    pe_ = psE[h // 4]
                    P.op("pe", lambda e, kb=kb, qb=qb, h=h: e.matmul(ps_s[:, 0:128], kb[:, h, :], qb[:, h, :], start=True, stop=True),
                         reads=[kb, qb], writes=[ps_s])
                    d_ = dec.next()
                    P.op("act", lambda e, d_=d_, pe_=pe_, h=h: e.activation(d_[:, :], pe_[:, (h % 4) * 128:(h % 4 + 1) * 128], AF.Exp, bias=u[:, h:h + 1]),
                         reads=[pe_, u], writes=[d_])
                    w_ = WT.next()
                    P.op("dve", lambda e, d_=d_, w_=w_: e.tensor_tensor(w_[:, :], d_[:, :], ps_s[:, 0:128], ALU.mult), reads=[d_, ps_s], writes=[w_])
                    P.op("pe", lambda e, w_=w_, vab=vab, h=h: e.matmul(ps_num[:, 0:257], w_[:, :], vab[:, h, :], start=True, stop=True),
                         reads=[w_, vab], writes=[ps_num])
                    P.op("pe", lambda e, qb=qb, h=h: e.matmul(ps_int[:, 0:257], qb[:, h, :], Cs_bf[:, h, :], start=True, stop=True),
                         reads=[qb, Cs_bf], writes=[ps_int])
                    ns_ = numsb.next()
                    P.op("act", lambda e, ns_=ns_: e.copy(ns_[:, :], ps_num[:, 0:257]), reads=[ps_num], writes=[ns_])
                    cb_ = comb.next()
                    P.op("dve", lambda e, cb_=cb_, ns_=ns_, h=h: e.scalar_tensor_tensor(out=cb_[:, :], in0=ps_int[:, 0:257], scalar=ginter[:, h:h + 1],
                                                                                      in1=ns_[:, :], op0=ALU.mult, op1=ALU.add),
                         reads=[ps_int, ginter, ns_], writes=[cb_])
                    r_ = rdn.next()
                    P.op("act", lambda e, r_=r_, cb_=cb_: e.activation(r_[:, :], cb_[:, 256:257], AF.Abs), reads=[cb_], writes=[r_])
                    P.op("dve", lambda e, r_=r_, h=h: e.tensor_tensor(r_[:, :], r_[:, :], emr[:, h:h + 1], ALU.max), reads=[r_, emr], writes=[r_])
                    P.op("dve", lambda e, r_=r_: e.reciprocal(r_[:, :], r_[:, :]), reads=[r_], writes=[r_])
                    P.op("dve", lambda e, r_=r_, cb_=cb_, h=h: e.tensor_scalar(h_c[:, h * 256:(h + 1) * 256], cb_[:, 0:256], r_[:, 0:1], None, ALU.mult),
                         reads=[cb_, r_], writes=[h_c])
                    v_ = vw.next()
                    P.op("pool", lambda e, v_=v_, vab=vab, h=h: e.tensor_scalar(v_[:, :], vab[:, h, :], wend[:, h:h + 1], None, ALU.mult),
                         reads=[vab, wend], writes=[v_])
                    P.op("pe", lambda e, ktb=ktb, v_=v_, h=h: e.matmul(ps_c[:, 0:257], ktb[:, h * 128:(h + 1) * 128], v_[:, :], start=True, stop=True),
                         reads=[ktb, v_], writes=[ps_c])
                    P.op("dve", lambda e, h=h: e.scalar_tensor_tensor(out=Cs[:, h, :], in0=Cs[:, h, :], scalar=gold[:, h:h + 1], in1=ps_c[:, 0:257],
                                                                       op0=ALU.mult, op1=ALU.add), reads=[Cs, gold, ps_c], writes=[Cs])
                    P.op("act", lambda e, h=h: e.copy(Cs_bf[:, h, :], Cs[:, h, :]), reads=[Cs], writes=[Cs_bf])
                P.op("dve", lambda e: e.tensor_tensor(m_b[:], btot[:], Mlast[:], ALU.add), reads=[btot, Mlast], writes=[m_b])
                P.op("dve", lambda e: e.tensor_tensor(m_col[0:8, :], bT[0:8, 127:128], MT[0:8, 127:128], ALU.add), reads=[bT, MT], writes=[m_col])
                for h in range(8):
                    P.op("act", lambda e, h=h: e.activation(junk[:, :], h_c[:, h * 256:(h + 1) * 256], AF.Square, accum_out=ss[:, h:h + 1]),
                         reads=[h_c], writes=[junk, ss])
                P.op("dve", lambda e: e.tensor_scalar(rstd[:], ss[:], 1.0 / 256, 1e-5, ALU.mult, ALU.add), reads=[ss], writes=[rstd])
                P.op("act", lambda e: e.activation(rstd[:], rstd[:], AF.Sqrt), reads=[rstd], writes=[rstd])
                P.op("dve", lambda e: e.reciprocal(rstd[:], rstd[:]), reads=[rstd], writes=[rstd])
                for h in range(8):
                    P.op("dve", lambda e, h=h: e.tensor_scalar(h_c[:, h * 256:(h + 1) * 256], h_c[:, h * 256:(h + 1) * 256], rstd[:, h:h + 1], None, ALU.mult),
                         reads=[h_c, rstd], writes=[h_c])
                P.op("pool", lambda e: e.tensor_tensor(h_c[:], h_c[:], g_b[:], ALU.mult), reads=[h_c, g_b], writes=[h_c])
                P.op("act", lambda e, ob=ob: e.activation(tmpb[:], ob[:], AF.Sigmoid), reads=[ob], writes=[tmpb])
                P.op("dve", lambda e: e.tensor_tensor(h_c[:], h_c[:], tmpb[:], ALU.mult), reads=[h_c, tmpb], writes=[h_c])
                ht = hTs.next()
                for c4 in range(4):
                    ps = psm.next()
                    for j in range(4):
                        c = c4 * 4 + j
                        P.op("pe", lambda e, ps=ps, c=c, j=j: e.transpose(ps[:, j * 128:(j + 1) * 128], h_c[:, c * 128:(c + 1) * 128], C.ident[:, :]),
                             reads=[h_c, C.ident], writes=[ps])
                    P.op("act", lambda e, ps=ps, c4=c4, ht=ht: e.copy(ht[:, c4 * 4:(c4 + 1) * 4, :], ps[:].rearrange("p (j m) -> p j m", j=4)),
                         reads=[ps], writes=[ht])
                P.store(ht, S.mixT[0:2048, a0:a0 + L].rearrange("(c p) t -> p c t", p=128), ht[:, :, 0:L])
            for c in range(16):
                ps = psm.next()
                P.op("pe", lambda e, ps=ps, c=c: e.transpose(ps[:, 0:128], Cs[:, c // 2, (c % 2) * 128:(c % 2 + 1) * 128], C.ident[:, :]),
                     reads=[Cs, C.ident], writes=[ps])
                P.op("dve", lambda e, ps=ps, c=c: e.tensor_copy(cio[:, c, :], ps[:, 0:128]), reads=[ps], writes=[cio])
            P.store(cio, O.mlstm_c[sidx].rearrange("(c p) d -> p c d", p=128), cio[:])
            P.op("dve", lambda e: e.tensor_copy(npad[:, 0:8], Cs[:, :, 256]), reads=[Cs], writes=[npad])
            ps = psm.next()
            P.op("pe", lambda e, ps=ps: e.transpose(ps[:, 0:128], npad[:, :], C.ident[:, :]), reads=[npad, C.ident], writes=[ps])
            P.op("dve", lambda e, ps=ps: e.tensor_copy(bpad[0:8, :], ps[0:8, 0:128]), reads=[ps], writes=[bpad])
            P.store(bpad, O.mlstm_n[sidx], bpad[0:8, :])
            P.op("dve", lambda e: e.memset(bpad[:], 0.0), writes=[bpad])
            P.op("dve", lambda e: e.memset(npad[:], 0.0), writes=[npad])
            P.store(m_col, O.mlstm_m[sidx:sidx + 1, :].rearrange("o h -> h o"), m_col[0:8, :])


_PROG_CACHE = {}


def kernel(**inputs):
    NCORES = 8
    SEQ = inputs["x_prompt"].shape[1]
    PAST = inputs["cache_fox_k"].shape[2]
    NS = inputs["x_sample"].shape[0] // NCORES
    key = (SEQ, PAST, NS)
    if key not in _PROG_CACHE:
        _PROG_CACHE[key] = build_program(SEQ=SEQ, PAST=PAST, NS=NS)
    nc = _PROG_CACHE[key]
    in_maps = [make_in_map(inputs, c, NS=NS) for c in range(NCORES)]
    res = run_bass_kernel_spmd(nc, in_maps, core_ids=list(range(NCORES)))
    R = res.results
    NP = N_META + SEQ
    B = NCORES
    DB = NCORES * NS
    f32 = np.float32
    y_p = np.empty((B, SEQ, D), f32); y_s = np.empty((DB, DEC, D), f32)
    kp = np.empty((1, B, NP, H_A, DH_A), f32); vp = np.empty((1, B, NP, H_A, DH_A), f32); fp = np.empty((1, B, NP, H_A), f32)
    cvp = np.empty((1, B, 3, CONV_DIM), f32); hp = np.empty((1, B, H_B, P_B, N_B), f32)
    cp = np.empty((1, B, H_C, DV_C, DK_C), f32); np_ = np.empty((1, B, H_C, DK_C), f32); mp = np.empty((1, B, H_C), f32)
    ks = np.empty((1, DB, DEC, H_A, DH_A), f32); vs = np.empty((1, DB, DEC, H_A, DH_A), f32); fs = np.empty((1, DB, DEC, H_A), f32)
    cvs = np.empty((1, DB, 3, CONV_DIM), f32); hs = np.empty((1, DB, H_B, P_B, N_B), f32)
    cs = np.empty((1, DB, H_C, DV_C, DK_C), f32); ns_ = np.empty((1, DB, H_C, DK_C), f32); ms = np.empty((1, DB, H_C), f32)
    for c in range(NCORES):
        r = R[c]
        y = np.asarray(r["o_y"]); y_p[c] = y[N_META:NP]; y_s[c * NS:(c + 1) * NS] = y[NP:].reshape(NS, DEC, D)
        k = np.asarray(r["o_fox_k"]); kp[0, c] = k[:NP].reshape(NP, H_A, DH_A); ks[0, c * NS:(c + 1) * NS] = k[NP:].reshape(NS, DEC, H_A, DH_A)
        v = np.asarray(r["o_fox_v"]); vp[0, c] = v[:NP].reshape(NP, H_A, DH_A); vs[0, c * NS:(c + 1) * NS] = v[NP:].reshape(NS, DEC, H_A, DH_A)
        lf = np.asarray(r["o_fox_logf"]); fp[0, c] = lf[:NP]; fs[0, c * NS:(c + 1) * NS] = lf[NP:].reshape(NS, DEC, H_A)
        cv = np.asarray(r["o_ssd_conv"]); cvp[0, c] = cv[0]; cvs[0, c * NS:(c + 1) * NS] = cv[1:]
        hh = np.asarray(r["o_ssd_state"]).reshape(1 + NS, H_B, P_B, N_B); hp[0, c] = hh[0]; hs[0, c * NS:(c + 1) * NS] = hh[1:]
        cc = np.asarray(r["o_mlstm_c"]).reshape(1 + NS, H_C, DV_C, DK_C); cp[0, c] = cc[0]; cs[0, c * NS:(c + 1) * NS] = cc[1:]
        nn = np.asarray(r["o_mlstm_n"]); np_[0, c] = nn[0]; ns_[0, c * NS:(c + 1) * NS] = nn[1:]
        mm = np.asarray(r["o_mlstm_m"]); mp[0, c] = mm[0]; ms[0, c * NS:(c + 1) * NS] = mm[1:]
    return (y_p, y_s, kp, vp, fp, cvp, hp, cp, np_, mp, ks, vs, fs, cvs, hs, cs, ns_, ms)
```
